# Optimizing a Trainium2 kernel written in Bass

```python
import math
import jax
import jax.numpy as jnp
from jax import lax
import numpy as np

D_MODEL = 2048
BATCH = 2
SEQ = 4096
DEPTH = 4

GRID_W = 64
CTX_LEN = 256
N_BRANCH = 4
BRANCH_W = D_MODEL // 2
CONV_K = 31
RWKV_HEAD = 64
RWKV_HEADS = BRANCH_W // RWKV_HEAD
DECAY_LORA = 96
AAA_LORA = 96
RWKV_GN_EPS = 64e-5
DIFF_HALF = 64
DIFF_VDIM = 2 * DIFF_HALF
DIFF_HEADS = BRANCH_W // DIFF_VDIM
GQA_HEAD = 128
GQA_HEADS = BRANCH_W // GQA_HEAD
GQA_KV_HEADS = GQA_HEADS // 4
BLOCK_Q = 128
ROPE_THETA = 10000.0
NORM_EPS = 1e-6

IN_SIZES = (
    BRANCH_W, BRANCH_W, BRANCH_W,
    BRANCH_W, BRANCH_W, BRANCH_W, DECAY_LORA, AAA_LORA, BRANCH_W,
    BRANCH_W, BRANCH_W, BRANCH_W, BRANCH_W,
    GQA_HEADS * GQA_HEAD, GQA_KV_HEADS * GQA_HEAD, GQA_KV_HEADS * GQA_HEAD, BRANCH_W,
    N_BRANCH * D_MODEL,
)
D_IN = sum(IN_SIZES)

kernel_name = "hybrid_conv_rwkv7_diffattn_gqa_flow_block"


def _rms_norm(t, g, eps=NORM_EPS):
    tf = t.astype(jnp.float32)
    y = tf * lax.rsqrt(jnp.mean(tf * tf, axis=-1, keepdims=True) + eps)
    return (y * g.astype(jnp.float32)).astype(t.dtype)


def _layer_norm(t, g, b, eps=1e-5):
    tf = t.astype(jnp.float32)
    mu = jnp.mean(tf, axis=-1, keepdims=True)
    var = jnp.mean(jnp.square(tf - mu), axis=-1, keepdims=True)
    return ((tf - mu) * lax.rsqrt(var + eps) * g.astype(jnp.float32) + b.astype(jnp.float32)).astype(t.dtype)


def _axial_rope(pos_row, pos_col, dim):
    half = dim // 2
    inv = ROPE_THETA ** (-jnp.arange(0, half, 2, dtype=jnp.float32) / half)
    ang = jnp.concatenate([pos_row[:, None].astype(jnp.float32) * inv,
                           pos_col[:, None].astype(jnp.float32) * inv], axis=-1)
    return jnp.cos(ang), jnp.sin(ang)


def _rope(t, cos, sin):
    shp = (1, t.shape[1]) + (1,) * (t.ndim - 3) + (cos.shape[-1],)
    cs, sn = cos.reshape(shp), sin.reshape(shp)
    tp = t.astype(jnp.float32).reshape(t.shape[:-1] + (-1, 2))
    t1, t2 = tp[..., 0], tp[..., 1]
    return jnp.stack([t1 * cs - t2 * sn, t1 * sn + t2 * cs], axis=-1).reshape(t.shape).astype(t.dtype)


def _dwconv(u, w, b):
    out = lax.conv_general_dilated(
        u, w[:, None, :].astype(u.dtype), window_strides=(1,),
        padding=[(CONV_K // 2, CONV_K // 2)], dimension_numbers=('NWC', 'WIO', 'NWC'),
        feature_group_count=u.shape[-1])
    return out + b.astype(u.dtype)


def _rev_segments(t, n_ctx):
    return jnp.concatenate([jnp.flip(t[:, :n_ctx], axis=1), jnp.flip(t[:, n_ctx:], axis=1)], axis=1)


def _rwkv7_bidirectional(r, k, v, wl, al, n_ctx, w0, w_up, a0, a_up, k_k, k_a, r_k, gn_g, gn_b):
    f32 = jnp.float32
    B, N = r.shape[:2]
    H, E = RWKV_HEADS, RWKV_HEAD
    out_dtype = r.dtype
    r, k, v = (t.astype(f32).reshape(B, N, H, E) for t in (r, k, v))
    wl, al = wl.astype(f32), al.astype(f32)
    w_pre = w0.astype(f32)[:, None, None, :] + jnp.einsum('bnr,erc->ebnc', wl, w_up.astype(f32))
    decay = jnp.exp(-jnp.exp(-jax.nn.softplus(-w_pre) - 0.5)).reshape(2, B, N, H, E)
    a = jax.nn.sigmoid(a0.astype(f32)[:, None, None, :]
                       + jnp.einsum('bnr,erc->ebnc', al, a_up.astype(f32))).reshape(2, B, N, H, E)
    kk = k * k_k.astype(f32).reshape(H, E)
    kk = kk * lax.rsqrt(jnp.maximum(jnp.sum(kk * kk, axis=-1, keepdims=True), 1e-24))
    k_dir = k[None] * (1.0 + (a - 1.0) * k_a.astype(f32).reshape(H, E))

    def order(t0, t1):
        return jnp.stack([t0, _rev_segments(t1, n_ctx)])

    xs = (order(r, r), order(decay[0], decay[1]), order(k_dir[0], k_dir[1]),
          order(v, v), order(kk, kk), order(a[0], a[1]))
    xs = tuple(jnp.moveaxis(t, 2, 0) for t in xs)

    def step(S, inp):
        r_t, w_t, k_t, v_t, kk_t, a_t = inp
        sa = jnp.einsum('ebhvk,ebhk->ebhv', S, -kk_t)
        S = (S * w_t[..., None, :] + sa[..., :, None] * (kk_t * a_t)[..., None, :]
             + v_t[..., :, None] * k_t[..., None, :])
        return S, jnp.einsum('ebhvk,ebhk->ebhv', S, r_t)

    S0 = jnp.zeros((2, B, H, E, E), f32)
    _, ys = lax.scan(step, S0, xs)
    ys = jnp.moveaxis(ys, 0, 2)
    y = ys[0] + _rev_segments(ys[1], n_ctx)
    mu = jnp.mean(y, axis=-1, keepdims=True)
    var = jnp.mean(jnp.square(y - mu), axis=-1, keepdims=True)
    y = ((y - mu) * lax.rsqrt(var + RWKV_GN_EPS)).reshape(B, N, H * E)
    y = y * gn_g.astype(f32) + gn_b.astype(f32)
    bonus = jnp.sum(jnp.sum(r[None] * k_dir * r_k.astype(f32), axis=-1, keepdims=True), axis=0) * v
    return (y + bonus.reshape(B, N, H * E)).astype(out_dtype)


def _latent_blocks(fn, q_lat):
    B, L = q_lat.shape[:2]
    nb = L // BLOCK_Q
    qb = jnp.swapaxes(q_lat.reshape((B, nb, BLOCK_Q) + q_lat.shape[2:]), 0, 1)
    ob = lax.map(fn, qb)
    return jnp.swapaxes(ob, 0, 1).reshape((B, L) + ob.shape[3:])


def _diff_core(q, k, v, lam):
    s = jnp.einsum('bqhjd,bkhjd->bhjqk', q.astype(jnp.float32), k.astype(jnp.float32)) * DIFF_HALF ** -0.5
    p = jax.nn.softmax(s, axis=-1)
    p = p[:, :, 0] - lam * p[:, :, 1]
    return jnp.einsum('bhqk,bkhd->bqhd', p.astype(v.dtype), v)


def _gqa_core(q, k, v):
    s = jnp.einsum('bqgrd,bkgd->bgrqk', q.astype(jnp.float32), k.astype(jnp.float32)) * GQA_HEAD ** -0.5
    p = jax.nn.softmax(s, axis=-1)
    return jnp.einsum('bgrqk,bkgd->bqgrd', p.astype(v.dtype), v)


def setup_inputs(seed: int = 0) -> dict:
    key = jax.random.key(seed)
    ks = list(jax.random.split(key, 32))
    f32 = jnp.float32
    L, D, W = DEPTH, D_MODEL, BRANCH_W

    def nrm(i, shape, scale):
        return jax.random.normal(ks[i], shape, f32) * scale

    return {
        'x': nrm(0, (BATCH, SEQ, D), 1.0),
        'c': nrm(1, (BATCH, D), 1.0),
        'ctx': nrm(2, (BATCH, CTX_LEN, D), 1.0),
        'c_ctx': nrm(3, (D,), 1.0),
        'norm_pre_g': 1.0 + nrm(4, (L, D), 0.02),
        'norm_post_g': 1.0 + nrm(5, (L, D), 0.02),
        'w_mod': nrm(6, (L, D, 3 * D), 0.5 * D ** -0.5),
        'b_mod': nrm(7, (L, 3 * D), 0.01),
        'w_in': nrm(8, (L, D, D_IN), D ** -0.5),
        'conv_w': nrm(9, (L, CONV_K, W), CONV_K ** -0.5),
        'conv_b': nrm(10, (L, W), 0.01),
        'conv_ln_g': 1.0 + nrm(11, (L, W), 0.02),
        'conv_ln_b': nrm(12, (L, W), 0.01),
        'rwkv_w0': jax.random.uniform(ks[13], (L, 2, W), f32, -6.0, 1.0),
        'rwkv_w_up': nrm(14, (L, 2, DECAY_LORA, W), 0.1 * DECAY_LORA ** -0.5),
        'rwkv_a0': nrm(15, (L, 2, W), 0.1),
        'rwkv_a_up': nrm(16, (L, 2, AAA_LORA, W), 0.1 * AAA_LORA ** -0.5),
        'rwkv_k_k': 0.85 + nrm(17, (L, W), 0.05),
        'rwkv_k_a': 1.0 + nrm(18, (L, W), 0.05),
        'rwkv_r_k': nrm(19, (L, RWKV_HEADS, RWKV_HEAD), 0.1),
        'rwkv_gn_g': 1.0 + nrm(20, (L, W), 0.02),
        'rwkv_gn_b': nrm(21, (L, W), 0.01),
        'diff_lam': nrm(22, (L, 4, DIFF_HALF), 0.1),
        'diff_subln_g': 1.0 + nrm(23, (L, DIFF_VDIM), 0.02),
        'gqa_qk_norm_g': 1.0 + nrm(24, (L, 2, GQA_HEAD), 0.02),
        'w_branch': nrm(25, (L, N_BRANCH, W, D), W ** -0.5),
        'b_gate': nrm(26, (L, N_BRANCH, D), 0.01),
        'w_out': nrm(27, (L, D, D), D ** -0.5),
    }


def reference(x, c, ctx, c_ctx, norm_pre_g, norm_post_g, w_mod, b_mod, w_in, conv_w, conv_b,
              conv_ln_g, conv_ln_b, rwkv_w0, rwkv_w_up, rwkv_a0, rwkv_a_up, rwkv_k_k, rwkv_k_a,
              rwkv_r_k, rwkv_gn_g, rwkv_gn_b, diff_lam, diff_subln_g, gqa_qk_norm_g, w_branch,
              b_gate, w_out):
    B, n_lat = x.shape[0], x.shape[1]
    n_ctx = ctx.shape[1]
    rows = n_lat // GRID_W
    pos_row = jnp.repeat(jnp.arange(rows), GRID_W)
    pos_col = jnp.broadcast_to(jnp.arange(GRID_W)[None, :], (rows, GRID_W)).reshape(-1)
    cos_d, sin_d = _axial_rope(pos_row, pos_col, DIFF_HALF)
    cos_g, sin_g = _axial_rope(pos_row, pos_col, GQA_HEAD)
    split_at = np.cumsum(IN_SIZES)[:-1].tolist()
    n_rep = GQA_HEADS // GQA_KV_HEADS

    x_ctx, x_lat = ctx, x
    for li in range(DEPTH):
        mod_lat = jax.nn.silu(c) @ w_mod[li] + b_mod[li]
        mod_ctx = jax.nn.silu(c_ctx) @ w_mod[li] + b_mod[li]
        sh_l, sc_l, gt_l = jnp.split(mod_lat[:, None, :], 3, axis=-1)
        sh_c, sc_c, gt_c = jnp.split(mod_ctx, 3, axis=-1)
        h = jnp.concatenate([_rms_norm(x_ctx, norm_pre_g[li]) * (1.0 + sc_c) + sh_c,
                             _rms_norm(x_lat, norm_pre_g[li]) * (1.0 + sc_l) + sh_l], axis=1)
        N = h.shape[1]
        (cv_val, cv_glu, cv_gate, rk_r, rk_k, rk_v, rk_wl, rk_al, rk_gate,
         df_q, df_k, df_v, df_gate, gq_q, gq_k, gq_v, gq_gate, merge_logits) = jnp.split(
            h @ w_in[li], split_at, axis=-1)

        u = cv_val * jax.nn.sigmoid(cv_glu)
        u = jnp.concatenate([_dwconv(u[:, :n_ctx], conv_w[li], conv_b[li]),
                             _dwconv(u[:, n_ctx:], conv_w[li], conv_b[li])], axis=1)
        br_conv = jax.nn.silu(_layer_norm(u, conv_ln_g[li], conv_ln_b[li])) * jax.nn.silu(cv_gate)

        br_rwkv = _rwkv7_bidirectional(
            rk_r, rk_k, rk_v, jnp.tanh(rk_wl), rk_al, n_ctx, rwkv_w0[li], rwkv_w_up[li], rwkv_a0[li],
            rwkv_a_up[li], rwkv_k_k[li], rwkv_k_a[li], rwkv_r_k[li], rwkv_gn_g[li], rwkv_gn_b[li]
        ) * jax.nn.silu(rk_gate)

        lam_p = diff_lam[li].astype(jnp.float32)
        lam_init = 0.8 - 0.6 * math.exp(-0.3 * li)
        lam = jnp.exp(jnp.sum(lam_p[0] * lam_p[1])) - jnp.exp(jnp.sum(lam_p[2] * lam_p[3])) + lam_init
        dq = df_q.reshape(B, N, DIFF_HEADS, 2, DIFF_HALF)
        dk = df_k.reshape(B, N, DIFF_HEADS, 2, DIFF_HALF)
        dv = df_v.reshape(B, N, DIFF_HEADS, DIFF_VDIM)
        dk_all = jnp.concatenate([dk[:, :n_ctx], _rope(dk[:, n_ctx:], cos_d, sin_d)], axis=1)
        do_ctx = _diff_core(dq[:, :n_ctx], dk[:, :n_ctx], dv[:, :n_ctx], lam)
        do_lat = _latent_blocks(lambda qb: _diff_core(qb, dk_all, dv, lam), _rope(dq[:, n_ctx:], cos_d, sin_d))
        do = jnp.concatenate([do_ctx, do_lat], axis=1)
        br_diff = (_rms_norm(do, diff_subln_g[li]) * (1.0 - lam_init)).reshape(B, N, BRANCH_W) * jax.nn.silu(df_gate)

        gq = _rms_norm(gq_q.reshape(B, N, GQA_HEADS, GQA_HEAD), gqa_qk_norm_g[li, 0])
        gk = _rms_norm(gq_k.reshape(B, N, GQA_KV_HEADS, GQA_HEAD), gqa_qk_norm_g[li, 1])
        gv = gq_v.reshape(B, N, GQA_KV_HEADS, GQA_HEAD)
        gk_all = jnp.concatenate([gk[:, :n_ctx], _rope(gk[:, n_ctx:], cos_g, sin_g)], axis=1)
        gq_grp = gq.reshape(B, N, GQA_KV_HEADS, n_rep, GQA_HEAD)
        go_ctx = _gqa_core(gq_grp[:, :n_ctx], gk[:, :n_ctx], gv[:, :n_ctx])
        go_lat = _latent_blocks(lambda qb: _gqa_core(qb, gk_all, gv), _rope(gq_grp[:, n_ctx:], cos_g, sin_g))
        br_gqa = jnp.concatenate([go_ctx, go_lat], axis=1).reshape(B, N, GQA_HEADS * GQA_HEAD) * jax.nn.silu(gq_gate)

        br = jnp.stack([br_conv, br_rwkv, br_diff, br_gqa], axis=2)
        gates = jax.nn.sigmoid(merge_logits.reshape(B, N, N_BRANCH, D_MODEL) + b_gate[li])
        merged = jnp.sum(jnp.einsum('bnjc,jcd->bnjd', br, w_branch[li]) * gates, axis=2)
        y = _rms_norm(merged @ w_out[li], norm_post_g[li])
        x_ctx = x_ctx + gt_c * y[:, :n_ctx]
        x_lat = x_lat + gt_l * y[:, n_ctx:]
    return x_lat
```

```python
import contextlib
import math
import numpy as np
import concourse.bass as bass
import concourse.mybir as mybir
from concourse.bass_utils import run_bass_kernel_spmd

F32 = mybir.dt.float32
BF16 = mybir.dt.bfloat16
AF = mybir.ActivationFunctionType
ALU = mybir.AluOpType
AX = mybir.AxisListType

SEM_LIM = 30000
D = 2048
NT = 4352
NTILE = 34
NBLK = 17
NCTX = 256
W = 1024
EPS = 1e-6


class Dep:
    __slots__ = ("name", "w", "rs", "wsem", "wcnt", "rsem", "rcnt", "excl")

    def __init__(self, name="", excl=False):
        self.name = name
        self.excl = excl
        self.w = None
        self.rs = []
        self.wsem = None
        self.wcnt = 0
        self.rsem = None
        self.rcnt = 0


class Sched:
    def __init__(self, nc, stack):
        self.nc = nc
        self.stack = stack
        self.eng = {"pe": nc.tensor, "dve": nc.vector, "act": nc.scalar, "pool": nc.gpsimd, "sp": nc.sync}
        self.sems = {k: [] for k in self.eng}
        self.cnt = {k: 0 for k in self.eng}
        self.known = {k: {} for k in self.eng}
        self.nsem = 0
        self.ninst = 0
        self.deps = []
        self.dticks = []

    def dep(self, name="", excl=False):
        d = Dep(name, excl)
        self.deps.append(d)
        return d

    def pdep(self, name=""):
        return self.dep(name, excl=True)

    def new_sem(self, name):
        self.nsem += 1
        return self.stack.enter_context(self.nc.semaphore(f"{name}_{self.nsem}"))

    def sb(self, name, shape, dt, stack=None):
        return (stack or self.stack).enter_context(self.nc.sbuf_tensor(name, list(shape), dt))

    def ps(self, name, shape, dt=F32, stack=None):
        return (stack or self.stack).enter_context(self.nc.psum_tensor(name, list(shape), dt))

    def _wait(self, e, tick):
        sem, val, src = tick
        kn = self.known[e]
        if kn.get(id(sem), 0) >= val:
            return
        self.eng[e].wait_ge(sem, val)
        kn[id(sem)] = val
        if src in self.sems:
            for s in self.sems[src]:
                if s is sem:
                    break
                kn[id(s)] = SEM_LIM

    def _deps(self, e, reads, writes, dma=False):
        for d in reads:
            if d.w is not None:
                self._wait(e, d.w)
            if d.excl:
                for r in d.rs:
                    if r[2] != e:
                        self._wait(e, r)
        for d in writes:
            if d.w is not None and (dma or d.w[2] != e or e != "pe"):
                self._wait(e, d.w)
            for r in d.rs:
                self._wait(e, r)

    def op(self, e, fn, reads=(), writes=()):
        self._deps(e, reads, writes)
        c = self.cnt[e]
        if c % SEM_LIM == 0:
            self.sems[e].append(self.new_sem(e))
        sem = self.sems[e][-1]
        val = c % SEM_LIM + 1
        self.cnt[e] = c + 1
        ins = fn(self.eng[e])
        ins.then_inc(sem, 1)
        self.ninst += 1
        tick = (sem, val, e)
        for d in reads:
            d.rs.append(tick)
        for d in writes:
            d.w = tick
            d.rs = []
        return tick

    def dma(self, e, out, in_, reads=(), writes=(), **kw):
        self._deps(e, reads, writes, dma=True)
        if writes:
            d0 = writes[0]
            if d0.wsem is None:
                d0.wsem = self.new_sem("dw")
            d0.wcnt += 16
            tick = (d0.wsem, d0.wcnt, "dma")
        else:
            d0 = reads[0]
            if d0.rsem is None:
                d0.rsem = self.new_sem("dr")
            d0.rcnt += 16
            tick = (d0.rsem, d0.rcnt, "dma")
        ins = self.eng[e].dma_start(out=out, in_=in_, **kw)
        ins.then_inc(tick[0], 16)
        self.ninst += 1
        self.dticks.append(tick)
        for d in reads:
            d.rs.append(tick)
        for d in writes:
            d.w = tick
            d.rs = []
        return tick

    def wait_all(self, e, deps):
        for d in deps:
            if d.w is not None:
                self._wait(e, d.w)
            for r in d.rs:
                self._wait(e, r)

    def drain_dma(self, e, keep=0):
        n = len(self.dticks) - keep
        for t in self.dticks[:max(n, 0)]:
            self._wait(e, t)
        self.dticks = self.dticks[max(n, 0):]

    def barrier(self):
        for e in self.eng:
            self.wait_all(e, self.deps)
        for d in self.deps:
            d.rs = d.rs[-8:]


def _mod_prologue(S, nc, cT, wmod, bmod, ncols, modb, dmodb):
    with contextlib.ExitStack() as st:
        ct = S.sb("m_ct", [128, 16, 2], F32, st); dct = S.dep()
        cs = S.sb("m_cs", [128, 16, 2], F32, st); dcs = S.dep()
        rep = S.sb("m_rep", [128, 2, 16, 128], F32, st); drep = S.dep()
        ones1 = S.sb("m_ones", [1, 128], F32, st); dones = S.dep()
        bm = S.sb("m_bm", [1, ncols], F32, st); dbm = S.dep()
        wt = [S.sb(f"m_wt{i}", [128, 16, 512], F32, st) for i in range(2)]
        dwt = [S.dep() for _ in range(2)]
        pm = [S.ps(f"m_pm{i}", [128, 512], F32, st) for i in range(2)]
        dpm = [S.pdep() for _ in range(2)]
        S.dma("sp", ct[:], cT[:, :, :], writes=[dct])
        S.dma("sp", bm[:], bmod[:, :], writes=[dbm])
        S.op("dve", lambda e: e.memset(ones1[:], 1.0), writes=[dones])
        S.op("act", lambda e: e.activation(out=cs[:], in_=ct[:], func=AF.Silu), reads=[dct], writes=[dcs])
        for wh in range(2):
            for kc in range(16):
                S.op("dve", lambda e: e.tensor_copy(out=rep[:, wh, kc, :], in_=cs[:, kc, wh:wh + 1].to_broadcast([128, 128])),
                     reads=[dcs], writes=[drep])
        ng = ncols // 512
        wv = wmod.rearrange("(kc p) n -> p kc n", p=128)
        for g in range(ng):
            b = g % 2
            for h4 in range(4):
                S.dma("sp", wt[b][:, h4 * 4:(h4 + 1) * 4, :], wv[:, h4 * 4:(h4 + 1) * 4, g * 512:(g + 1) * 512], writes=[dwt[b]])
            for wh in range(2):
                for kc in range(16):
                    S.op("pe", lambda e: e.matmul(pm[wh][:], lhsT=rep[:, wh, kc, :], rhs=wt[b][:, kc, :], start=(kc == 0), stop=False),
                         reads=[drep, dwt[b]], writes=[dpm[wh]])
                S.op("pe", lambda e: e.matmul(pm[wh][:], lhsT=ones1[:], rhs=bm[:, g * 512:(g + 1) * 512], start=False, stop=True),
                     reads=[dones, dbm], writes=[dpm[wh]])
                S.op("act", lambda e: e.copy(out=modb[wh][:, g * 512:(g + 1) * 512], in_=pm[wh][:]), reads=[dpm[wh]], writes=[dmodb])
        S.barrier()


def _make_ident(S, st, n, dt, name):
    f = S.sb(name + "_f", [n, n], F32, st)
    df = S.dep()
    S.op("pool", lambda e: e.memset(f[:], 1.0), writes=[df])
    S.op("pool", lambda e: e.affine_select(out=f[:], in_=f[:], pattern=[[-1, n]], compare_op=ALU.is_equal, fill=0.0,
                                           base=0, channel_multiplier=1), reads=[df], writes=[df])
    if dt == F32:
        return f, df
    b = S.sb(name + "_b", [n, n], dt, st)
    db = S.dep()
    S.op("dve", lambda e: e.tensor_copy(out=b[:], in_=f[:]), reads=[df], writes=[db])
    return b, db


def _rope(S, src, dst, cos, sin, G, P, t1, t2, reads, writes, dtmp):
    sv = src.rearrange("p g (i t) -> p g i t", t=2)
    dv = dst.rearrange("p g (i t) -> p g i t", t=2)
    cb = cos.unsqueeze(1).to_broadcast([128, G, P])
    sbb = sin.unsqueeze(1).to_broadcast([128, G, P])
    S.op("dve", lambda e: e.tensor_tensor(out=t1, in0=sv[:, :, :, 0], in1=cb, op=ALU.mult), reads=reads, writes=[dtmp])
    S.op("dve", lambda e: e.tensor_tensor(out=t2, in0=sv[:, :, :, 1], in1=sbb, op=ALU.mult), reads=reads, writes=[dtmp])
    S.op("dve", lambda e: e.tensor_tensor(out=dv[:, :, :, 0], in0=t1, in1=t2, op=ALU.subtract), reads=[dtmp], writes=writes)
    S.op("dve", lambda e: e.tensor_tensor(out=t1, in0=sv[:, :, :, 0], in1=sbb, op=ALU.mult), reads=reads + [dtmp], writes=[dtmp])
    S.op("dve", lambda e: e.tensor_tensor(out=t2, in0=sv[:, :, :, 1], in1=cb, op=ALU.mult), reads=reads + [dtmp], writes=[dtmp])
    S.op("dve", lambda e: e.tensor_tensor(out=dv[:, :, :, 1], in0=t1, in1=t2, op=ALU.add), reads=[dtmp], writes=writes)


def _rstd(S, ss, n, scale, eps, reads, dss):
    S.op("dve", lambda e: e.tensor_scalar(out=ss, in0=ss, scalar1=scale, scalar2=eps, op0=ALU.mult, op1=ALU.add),
         reads=reads + [dss], writes=[dss])
    S.op("act", lambda e: e.activation(out=ss, in_=ss, func=AF.Sqrt), reads=[dss], writes=[dss])
    S.op("dve", lambda e: e.reciprocal(out=ss, in_=ss), reads=[dss], writes=[dss])


def _phase_A1(S, nc, T):
    with contextlib.ExitStack() as st:
        identb, did = _make_ident(S, st, 128, BF16, "a1id")
        modb = [S.sb(f"modb{i}", [128, 4096], F32, st) for i in range(2)]
        dmodb = S.dep()
        _mod_prologue(S, nc, T["cT"], T["wmod"], T["bmod"], 4096, modb, dmodb)
        with contextlib.ExitStack() as st2:
            gb = S.sb("gb", [128, 2048], F32, st2); dgb = S.dep()
            S.dma("sp", gb[:], T["gpre"][0:1, :].to_broadcast([128, 2048]), writes=[dgb])
            for wh in range(2):
                S.op("dve", lambda e: e.scalar_tensor_tensor(out=modb[wh][:, 2048:4096], in0=modb[wh][:, 2048:4096], scalar=1.0,
                                                             in1=gb[:], op0=ALU.add, op1=ALU.mult), reads=[dmodb, dgb], writes=[dmodb])
            S.barrier()
        gqn = S.sb("gqn_sb", [128, 2, 128], F32, st); dgqn = S.dep()
        S.dma("sp", gqn[:].rearrange("p a b -> p (a b)"), T["gqn"][0:1, :].to_broadcast([128, 256]), writes=[dgqn])
        wfm = S.sb("wfm", [128, 16, 704], BF16, st); dwfm = S.dep()
        wtm = S.sb("wtm", [128, 16, 2304], BF16, st); dwtm = S.dep()
        wfv = T["w_fm"].rearrange("(kc p) n -> p kc n", p=128)
        wtv = T["w_tm"].rearrange("(kc p) n -> p kc n", p=128)
        for k4 in range(8):
            S.dma("pool", wfm[:, k4 * 2:(k4 + 1) * 2, :], wfv[:, k4 * 2:(k4 + 1) * 2, :], writes=[dwfm])
        for k4 in range(16):
            S.dma("pool", wtm[:, k4:(k4 + 1), :], wtv[:, k4:(k4 + 1), :], writes=[dwtm])
        xb = [S.sb(f"xb{i}", [128, 2048], F32, st) for i in range(2)]; dxb = [S.dep() for _ in range(2)]
        rpb = [S.sb(f"rp{i}", [128, 192], F32, st) for i in range(2)]; drp = [S.dep() for _ in range(2)]
        hb = S.sb("hb", [128, 2048], BF16, st); dhb = S.dep()
        ss = S.sb("ss", [128, 4], F32, st); dss = S.dep()
        hTb = [S.sb(f"hTb{i}", [128, 16, 256], BF16, st) for i in range(2)]; dhT = [S.dep() for _ in range(2)]
        rkst = S.sb("rkst", [64, 8, 256], F32, st); drkst = S.dep()
        wast = S.sb("wast", [96, 2, 256], F32, st); dwast = S.dep()
        stf = S.sb("stf", [128, 2304], F32, st); dstf = [S.dep() for _ in range(5)]
        gst = S.sb("gst", [128, 768], BF16, st); dgst = S.dep()
        qkd = S.sb("qkd", [128, 8, 64], BF16, st); dqkd = S.dep()
        qkTd = S.sb("qkTd", [64, 8, 128], BF16, st); dqkTd = S.dep()
        qkg = S.sb("qkg", [128, 3, 128], BF16, st); dqkg = S.dep()
        qkgn = S.sb("qkgn", [128, 3, 128], F32, st); dqkgn = S.dep()
        qkTg = S.sb("qkTg", [128, 3, 128], BF16, st); dqkTg = S.dep()
        vdst = S.sb("vdst", [128, 2, 129], BF16, st); dvd = S.dep()
        vgst = S.sb("vgst", [128, 129], BF16, st); dvg = S.dep()
        rt1 = S.sb("rt1", [128, 8, 32], F32, st); rt2 = S.sb("rt2", [128, 8, 32], F32, st); drt = S.dep()
        sqg = S.sb("sqg", [128, 3, 128], F32, st); dsqg = S.dep()
        ssg = S.sb("ssg", [128, 4], F32, st); dssg = S.dep()
        pf = S.ps("pf", [64, 8, 256], F32, st); dpf = S.pdep()
        pw = S.ps("pw", [96, 2, 256], F32, st); dpw = S.pdep()
        pg = [S.ps(f"pg{i}", [128, 512], F32, st) for i in range(2)]; dpg = [S.pdep() for _ in range(2)]
        ptb = S.ps("ptb", [128, 8, 128], BF16, st); dptb = S.pdep()
        S.op("dve", lambda e: e.memset(vdst[:], 1.0), writes=[dvd])
        S.op("dve", lambda e: e.memset(vgst[:], 1.0), writes=[dvg])

        s_rk = T["s_rk"].rearrange("g c t -> c g t")
        s_wa = T["s_wa"].rearrange("j c t -> c j t")
        s_dqk = T["s_dqkT"].rearrange("g c t -> c g t")
        s_gqk = T["s_gqkT"].rearrange("g c t -> c g t")
        hTo = T["hT"].rearrange("kc p t -> p kc t")
        import os
        for blk in range(int(os.environ.get('A1_BLOCKS', NBLK))):
            wh = 0 if blk == 0 else 1
            S.drain_dma("sp", keep=12)
            hT = hTb[blk % 2]; dh = dhT[blk % 2]
            for ti in range(2):
                t = 2 * blk + ti
                tok0 = t * 128
                xt = xb[t % 2]; dx = dxb[t % 2]; rp = rpb[t % 2]; dr = drp[t % 2]
                S.dma("sp", xt[:], T["xf"][tok0:tok0 + 128, :], writes=[dx])
                S.dma("sp", rp[:], T["rope"][tok0:tok0 + 128, :], writes=[dr])
                S.op("act", lambda e: e.activation(out=hb[:], in_=xt[:], func=AF.Square, accum_out=ss[:, 0:1]),
                     reads=[dx], writes=[dhb, dss])
                _rstd(S, ss[:, 0:1], 1, 1.0 / D, EPS, [], dss)
                S.op("dve", lambda e: e.scalar_tensor_tensor(out=xt[:], in0=xt[:], scalar=ss[:, 0:1], in1=modb[wh][:, 2048:4096],
                                                             op0=ALU.mult, op1=ALU.mult), reads=[dx, dss, dmodb], writes=[dx])
                S.op("dve", lambda e: e.tensor_tensor(out=hb[:], in0=xt[:], in1=modb[wh][:, 0:2048], op=ALU.add),
                     reads=[dx, dmodb], writes=[dhb])
                for half in range(2):
                    for j in range(8):
                        kc = half * 8 + j
                        S.op("pe", lambda e: e.transpose(out=ptb[:, j, :], in_=hb[:, kc * 128:(kc + 1) * 128], identity=identb[:]),
                             reads=[dhb, did], writes=[dptb])
                    S.op("act", lambda e: e.copy(out=hT[:, half * 8:(half + 1) * 8, ti * 128:(ti + 1) * 128], in_=ptb[:]),
                         reads=[dptb], writes=[dh])
            S.dma("sp", hTo[:, :, blk * 256:(blk + 1) * 256], hT[:], reads=[dh])
            for g in range(8):
                for kc in range(16):
                    S.op("pe", lambda e: e.matmul(pf[:, g, :], lhsT=wfm[:, kc, g * 64:(g + 1) * 64], rhs=hT[:, kc, :],
                                                  start=(kc == 0), stop=(kc == 15)), reads=[dwfm, dh], writes=[dpf])
            S.op("act", lambda e: e.copy(out=rkst[:], in_=pf[:]), reads=[dpf], writes=[drkst])
            S.dma("sp", s_rk[:, :, blk * 256:(blk + 1) * 256], rkst[:], reads=[drkst])
            for j in range(2):
                for kc in range(16):
                    S.op("pe", lambda e: e.matmul(pw[:, j, :], lhsT=wfm[:, kc, 512 + j * 96:512 + (j + 1) * 96], rhs=hT[:, kc, :],
                                                  start=(kc == 0), stop=(kc == 15)), reads=[dwfm, dh], writes=[dpw])
            S.op("act", lambda e: e.activation(out=wast[:, 0, :], in_=pw[:, 0, :], func=AF.Tanh), reads=[dpw], writes=[dwast])
            S.op("act", lambda e: e.copy(out=wast[:, 1, :], in_=pw[:, 1, :]), reads=[dpw], writes=[dwast])
            S.dma("sp", s_wa[:, :, blk * 256:(blk + 1) * 256], wast[:], reads=[dwast])
            for ti in range(2):
                t = 2 * blk + ti
                tok0 = t * 128
                rp = rpb[t % 2]; dr = drp[t % 2]
                for gi in range(5):
                    ncol = 512 if gi < 4 else 256
                    p = pg[gi % 2]; dp = dpg[gi % 2]
                    for kc in range(16):
                        S.op("pe", lambda e: e.matmul(p[:, 0:ncol], lhsT=hT[:, kc, ti * 128:(ti + 1) * 128],
                                                      rhs=wtm[:, kc, gi * 512:gi * 512 + ncol], start=(kc == 0), stop=(kc == 15)),
                             reads=[dwtm, dh], writes=[dp])
                    S.op("act", lambda e: e.copy(out=stf[:, gi * 512:gi * 512 + ncol], in_=p[:, 0:ncol]), reads=[dp], writes=[dstf[gi]])
                S.dma("sp", T["s_v"][tok0:tok0 + 128, :], stf[:, 0:256], reads=[dstf[0]])
                S.op("act", lambda e: e.activation(out=gst[:, 0:256], in_=stf[:, 256:512], func=AF.Silu), reads=[dstf[0]], writes=[dgst])
                _rope(S, stf[:, 512:1024].rearrange("p (g d) -> p g d", g=8), qkd[:], rp[:, 0:32], rp[:, 32:64], 8, 32,
                      rt1[:], rt2[:], [dstf[1], dr], [dqkd], drt)
                for g in range(8):
                    S.op("pe", lambda e: e.transpose(out=ptb[0:64, g, :], in_=qkd[:, g, :], identity=identb[:]),
                         reads=[dqkd, did], writes=[dptb])
                S.op("act", lambda e: e.copy(out=qkTd[:], in_=ptb[0:64, :, :]), reads=[dptb], writes=[dqkTd])
                S.dma("sp", s_dqk[:, :, tok0:tok0 + 128], qkTd[:], reads=[dqkTd])
                S.op("act", lambda e: e.copy(out=vdst[:, :, 0:128], in_=stf[:, 1024:1280].rearrange("p (h d) -> p h d", h=2)),
                     reads=[dstf[2]], writes=[dvd])
                S.dma("sp", T["s_dv"][tok0:tok0 + 128, :, :], vdst[:], reads=[dvd])
                S.op("act", lambda e: e.activation(out=gst[:, 256:512], in_=stf[:, 1280:1536], func=AF.Silu), reads=[dstf[2]], writes=[dgst])
                src3 = stf[:, 1536:1920].rearrange("p (g d) -> p g d", g=3)
                S.op("dve", lambda e: e.tensor_tensor(out=sqg[:], in0=src3, in1=src3, op=ALU.mult), reads=[dstf[3]], writes=[dsqg])
                S.op("dve", lambda e: e.tensor_reduce(out=ssg[:, 0:3], in_=sqg[:], axis=AX.X, op=ALU.add), reads=[dsqg], writes=[dssg])
                _rstd(S, ssg[:, 0:3], 3, 1.0 / 128, EPS, [], dssg)
                for i in range(3):
                    S.op("dve", lambda e: e.scalar_tensor_tensor(out=qkgn[:, i, :], in0=src3[:, i, :], scalar=ssg[:, i:i + 1],
                                                                 in1=gqn[:, (0 if i < 2 else 1), :], op0=ALU.mult, op1=ALU.mult),
                         reads=[dstf[3], dssg, dgqn], writes=[dqkgn])
                _rope(S, qkgn[:], qkg[:], rp[:, 64:128], rp[:, 128:192], 3, 64,
                      rt1[:].rearrange("p a b -> p (a b)")[:, 0:192].rearrange("p (a b) -> p a b", a=3),
                      rt2[:].rearrange("p a b -> p (a b)")[:, 0:192].rearrange("p (a b) -> p a b", a=3),
                      [dqkgn, dr], [dqkg], drt)
                for g in range(3):
                    S.op("pe", lambda e: e.transpose(out=ptb[:, g, :], in_=qkg[:, g, :], identity=identb[:]),
                         reads=[dqkg, did], writes=[dptb])
                S.op("act", lambda e: e.copy(out=qkTg[:], in_=ptb[:, 0:3, :]), reads=[dptb], writes=[dqkTg])
                S.dma("sp", s_gqk[:, :, tok0:tok0 + 128], qkTg[:], reads=[dqkTg])
                S.op("act", lambda e: e.copy(out=vgst[:, 0:128], in_=stf[:, 1920:2048]), reads=[dstf[3]], writes=[dvg])
                S.dma("sp", T["s_gv"][tok0:tok0 + 128, :], vgst[:], reads=[dvg])
                S.op("act", lambda e: e.activation(out=gst[:, 512:768], in_=stf[:, 2048:2304], func=AF.Silu), reads=[dstf[4]], writes=[dgst])
                S.dma("sp", T["s_gate"][tok0:tok0 + 128, :], gst[:], reads=[dgst])
        S.barrier()


def _phase_A2(S, nc, T):
    with contextlib.ExitStack() as st:
        KTd = S.sb("KTd", [64, 4, NT], BF16, st); dKTd = S.dep()
        Vd = S.sb("Vd", [128, NTILE, 2, 129], BF16, st); dVd = S.dep()
        KTg = S.sb("KTg", [128, NT], BF16, st); dKTg = S.dep()
        Vg = S.sb("Vg", [128, NTILE, 129], BF16, st); dVg = S.dep()
        for g in range(4):
            S.dma("sp", KTd[:, g, :], T["s_dqkT"][4 + g, :, :], writes=[dKTd])
        S.dma("sp", KTg[:], T["s_gqkT"][2, :, :], writes=[dKTg])
        for c in range(2):
            S.dma("sp", Vd[:, c * 17:(c + 1) * 17, :, :], T["s_dv"].rearrange("(t p) h d -> p t h d", p=128)[:, c * 17:(c + 1) * 17, :, :], writes=[dVd])
        S.dma("sp", Vg[:], T["s_gv"].rearrange("(t p) d -> p t d", p=128), writes=[dVg])
        lamp = S.sb("lamp_sb", [128, 4, 64], F32, st); dlam = S.dep()
        lam = S.sb("lam", [128, 8], F32, st)
        S.dma("sp", lamp[:].rearrange("p a b -> p (a b)"), T["lamp"][0:1, :].to_broadcast([128, 256]), writes=[dlam])
        S.dma("sp", lam[:, 4:5], T["lami"][0:1, 0:1].to_broadcast([128, 1]), writes=[dlam])
        prod = S.sb("lprod", [128, 2, 64], F32, st)
        S.op("dve", lambda e: e.tensor_tensor(out=prod[:, 0, :], in0=lamp[:, 0, :], in1=lamp[:, 1, :], op=ALU.mult), reads=[dlam], writes=[dlam])
        S.op("dve", lambda e: e.tensor_tensor(out=prod[:, 1, :], in0=lamp[:, 2, :], in1=lamp[:, 3, :], op=ALU.mult), reads=[dlam], writes=[dlam])
        S.op("dve", lambda e: e.tensor_reduce(out=lam[:, 0:2], in_=prod[:], axis=AX.X, op=ALU.add), reads=[dlam], writes=[dlam])
        S.op("act", lambda e: e.activation(out=lam[:, 2:4], in_=lam[:, 0:2], func=AF.Exp), reads=[dlam], writes=[dlam])
        S.op("dve", lambda e: e.tensor_tensor(out=lam[:, 5:6], in0=lam[:, 2:3], in1=lam[:, 3:4], op=ALU.subtract), reads=[dlam], writes=[dlam])
        S.op("dve", lambda e: e.tensor_tensor(out=lam[:, 6:7], in0=lam[:, 5:6], in1=lam[:, 4:5], op=ALU.add), reads=[dlam], writes=[dlam])
        gsub = S.sb("gsub", [128, 128], F32, st); dgsub = S.dep()
        S.dma("sp", gsub[:], T["subg"][0:1, :].to_broadcast([128, 128]), writes=[dgsub])
        S.op("dve", lambda e: e.tensor_scalar(out=lam[:, 7:8], in0=lam[:, 4:5], scalar1=-1.0, scalar2=1.0, op0=ALU.mult, op1=ALU.add),
             reads=[dlam], writes=[dlam])
        S.op("dve", lambda e: e.tensor_scalar(out=gsub[:], in0=gsub[:], scalar1=lam[:, 7:8], scalar2=None, op0=ALU.mult),
             reads=[dlam, dgsub], writes=[dgsub])

        QTd = [S.sb(f"QTd{i}", [64, 4, 256], BF16, st) for i in range(2)]; dQd = [S.dep() for _ in range(2)]
        QTg = [S.sb(f"QTg{i}", [128, 2, 256], BF16, st) for i in range(2)]; dQg = [S.dep() for _ in range(2)]
        gt = [S.sb(f"gt{i}", [128, 2, 768], BF16, st) for i in range(2)]; dgt = [S.dep() for _ in range(2)]
        PT = [S.sb(f"PT{i}", [128, 2, 256], BF16, st) for i in range(2)]; dPT = [S.dep() for _ in range(2)]
        pss = [S.ps(f"pss{i}", [128, 2, 256], F32, st) for i in range(2)]; dpss = [S.pdep() for _ in range(2)]
        acc = [[S.ps(f"acc{j}{q}", [128, 512], F32, st) for q in range(2)] for j in range(2)]
        dacc = [[S.pdep() for q in range(2)] for j in range(2)]
        rs = S.sb("rs", [128, 8], F32, st); drs = S.dep()
        o0 = S.sb("o0", [128, 128], F32, st); do0 = S.dep()
        dd = S.sb("dd", [128, 128], F32, st); ddd = S.dep()
        junk = S.sb("junk", [128, 128], F32, st); djunk = S.dep()
        brs = [S.sb(f"brs{i}", [128, 512], BF16, st) for i in range(2)]; dbrs = [S.dep() for _ in range(2)]
        s_dqk = T["s_dqkT"].rearrange("g c t -> c g t")
        s_gqk = T["s_gqkT"].rearrange("g c t -> c g t")
        it = 0
        for qb in range(NBLK):
            b2 = qb % 2
            q0 = qb * 256
            S.drain_dma("sp", keep=8)
            S.dma("sp", QTd[b2][:], s_dqk[:, 0:4, q0:q0 + 256], writes=[dQd[b2]])
            S.dma("sp", QTg[b2][:], s_gqk[:, 0:2, q0:q0 + 256], writes=[dQg[b2]])
            S.dma("sp", gt[b2][:], T["s_gate"].rearrange("(t p) n -> p t n", p=128)[:, 2 * qb:2 * qb + 2, :], writes=[dgt[b2]])
            kts = list(range(2)) if qb == 0 else list(range(NTILE))
            for u in range(3):
                for ki, kt in enumerate(kts):
                    ps_ = pss[it % 2]; dps_ = dpss[it % 2]; pt_ = PT[it % 2]; dpt_ = dPT[it % 2]
                    it += 1
                    for j in range(2):
                        if u < 2:
                            S.op("pe", lambda e: e.matmul(ps_[:, j, :], lhsT=KTd[:, u * 2 + j, kt * 128:(kt + 1) * 128], rhs=QTd[b2][:, u * 2 + j, :],
                                                          start=True, stop=True), reads=[dKTd, dQd[b2]], writes=[dps_])
                        else:
                            S.op("pe", lambda e: e.matmul(ps_[:, j, :], lhsT=KTg[:, kt * 128:(kt + 1) * 128], rhs=QTg[b2][:, j, :],
                                                          start=True, stop=True), reads=[dKTg, dQg[b2]], writes=[dps_])
                    sc = (64 ** -0.5) if u < 2 else (128 ** -0.5)
                    S.op("act", lambda e: e.activation(out=pt_[:], in_=ps_[:], func=AF.Exp, scale=sc), reads=[dps_], writes=[dpt_])
                    for j in range(2):
                        for q in range(2):
                            rhs = Vd[:, kt, u, :] if u < 2 else Vg[:, kt, :]
                            S.op("pe", lambda e: e.matmul(acc[j][q][:, 0:129], lhsT=pt_[:, j, q * 128:(q + 1) * 128], rhs=rhs,
                                                          start=(ki == 0), stop=(ki == len(kts) - 1)),
                                 reads=[dpt_, dVd if u < 2 else dVg], writes=[dacc[j][q]])
                for q in range(2):
                    bs = brs[q]; dbs = dbrs[q]
                    if u < 2:
                        S.op("dve", lambda e: e.reciprocal(out=rs[:, 0:1], in_=acc[0][q][:, 128:129]), reads=[dacc[0][q]], writes=[drs])
                        S.op("dve", lambda e: e.reciprocal(out=rs[:, 1:2], in_=acc[1][q][:, 128:129]), reads=[dacc[1][q]], writes=[drs])
                        S.op("dve", lambda e: e.scalar_tensor_tensor(out=rs[:, 2:3], in0=rs[:, 1:2], scalar=-1.0, in1=lam[:, 6:7],
                                                                     op0=ALU.mult, op1=ALU.mult), reads=[drs, dlam], writes=[drs])
                        S.op("dve", lambda e: e.tensor_scalar(out=o0[:], in0=acc[0][q][:, 0:128], scalar1=rs[:, 0:1], scalar2=None, op0=ALU.mult),
                             reads=[dacc[0][q], drs], writes=[do0])
                        S.op("dve", lambda e: e.scalar_tensor_tensor(out=dd[:], in0=acc[1][q][:, 0:128], scalar=rs[:, 2:3], in1=o0[:],
                                                                     op0=ALU.mult, op1=ALU.add), reads=[dacc[1][q], drs, do0], writes=[ddd])
                        S.op("act", lambda e: e.activation(out=junk[:], in_=dd[:], func=AF.Square, accum_out=rs[:, 3:4]),
                             reads=[ddd], writes=[djunk, drs])
                        _rstd(S, rs[:, 3:4], 1, 1.0 / 128, EPS, [], drs)
                        S.op("dve", lambda e: e.scalar_tensor_tensor(out=o0[:], in0=dd[:], scalar=rs[:, 3:4], in1=gsub[:],
                                                                     op0=ALU.mult, op1=ALU.mult), reads=[ddd, drs, dgsub], writes=[do0])
                        S.op("dve", lambda e: e.tensor_tensor(out=bs[:, u * 128:(u + 1) * 128], in0=o0[:],
                                                              in1=gt[b2][:, q, 256 + u * 128:256 + (u + 1) * 128], op=ALU.mult),
                             reads=[do0, dgt[b2]], writes=[dbs])
                    else:
                        for j in range(2):
                            S.op("dve", lambda e: e.reciprocal(out=rs[:, 4 + j:5 + j], in_=acc[j][q][:, 128:129]), reads=[dacc[j][q]], writes=[drs])
                            S.op("dve", lambda e: e.scalar_tensor_tensor(out=bs[:, 256 + j * 128:256 + (j + 1) * 128], in0=acc[j][q][:, 0:128],
                                                                         scalar=rs[:, 4 + j:5 + j], in1=gt[b2][:, q, 512 + j * 128:512 + (j + 1) * 128],
                                                                         op0=ALU.mult, op1=ALU.mult), reads=[dacc[j][q], drs, dgt[b2]], writes=[dbs])
                        S.dma("sp", T["br_att"][q0 + q * 128:q0 + (q + 1) * 128, :], bs[:], reads=[dbs])
        S.barrier()


def _phase_A3(S, nc, T):
    NCH = NT // 64
    with contextlib.ExitStack() as st:
        identf, didf = _make_ident(S, st, 64, F32, "a3id")
        ones64 = S.sb("ones64", [64, 64], F32, st); dconst = S.dep()
        S.op("dve", lambda e: e.memset(ones64[:], 1.0), writes=[dconst])
        mk = S.sb("mk", [64, 4, 64], F32, st)
        S.op("pool", lambda e: e.memset(mk[:], 1.0), writes=[dconst])
        for i, (stp, cm, cmp_) in enumerate([(1, -1, ALU.is_gt), (1, -1, ALU.is_ge), (-1, 1, ALU.is_gt), (-1, 1, ALU.is_ge)]):
            S.op("pool", lambda e: e.affine_select(out=mk[:, i, :], in_=mk[:, i, :], pattern=[[stp, 64]], compare_op=cmp_, fill=0.0,
                                                   base=0, channel_multiplier=cm), reads=[dconst], writes=[dconst])
        MSK = S.sb("MSK", [64, 2, 320], F32, st)
        for e_ in range(2):
            order = [0, 1, 0, 1, 2] if e_ == 0 else [2, 3, 2, 3, 0]
            for j, m in enumerate(order):
                S.op("dve", lambda e: e.tensor_copy(out=MSK[:, e_, j * 64:(j + 1) * 64], in_=mk[:, m, :]), reads=[dconst], writes=[dconst])
        rmask = S.sb("rmask", [64, 16, 64], F32, st)
        S.op("dve", lambda e: e.memset(rmask[:], 1.0), writes=[dconst])
        S.op("dve", lambda e: e.memset(rmask[:, :, 0:1], 0.0), reads=[dconst], writes=[dconst])
        prm = S.sb("prm", [64, 7, 4], F32, st)
        S.dma("sp", prm[:], T["rprm"][:, :, :], writes=[dconst])
        omk = S.sb("omk", [64, 4], F32, st)
        S.op("dve", lambda e: e.tensor_scalar(out=omk[:], in0=prm[:, 5, :], scalar1=-1.0, scalar2=1.0, op0=ALU.mult, op1=ALU.add),
             reads=[dconst], writes=[dconst])
        wup = S.sb("wup_sb", [96, 2, 256], F32, st)
        aup = S.sb("aup_sb", [96, 2, 256], F32, st)
        S.dma("sp", wup[:], T["wup"].rearrange("e r c -> r e c"), writes=[dconst])
        S.dma("sp", aup[:], T["aup"].rearrange("e r c -> r e c"), writes=[dconst])
        gng = S.sb("gng", [64, 2, 256], F32, st)
        S.dma("sp", gng[:].rearrange("p a b -> p (a b)"), T["gn"][0:1, :].to_broadcast([64, 512]), writes=[dconst])

        yacc = S.sb("yacc", [64, NCH, 256], F32, st); dy = S.dep()
        bacc = S.sb("bacc", [64, NCH, 4], F32, st); dba = S.dep()
        ST = S.sb("ST", [64, 4, 64], F32, st); dST = S.dep()
        PSA = S.ps("PSA", [64, 8, 512], F32, st); dB = [S.pdep() for _ in range(8)]

        def t4(name):
            return S.sb(name, [64, 4, 256], F32, st), S.dep()
        rk_in = S.sb("rk_in", [64, 8, 256], F32, st); drk = S.dep()
        wa_in = S.sb("wa_in", [96, 2, 256], F32, st); dwa = S.dep()
        v_in, dv = t4("v_in")
        lw, dlw = t4("lw"); aa, daa = t4("aa"); kk, dkk = t4("kk"); kd, dkd = t4("kd"); be, dbe = t4("be")
        L, dL = t4("L"); Ld, dLd = t4("Ld"); tmp, dtmp = t4("tmp")
        E1, dE1 = t4("E1"); E2, dE2 = t4("E2"); E3, dE3 = t4("E3"); E4, dE4 = t4("E4")
        bt, dbt = t4("bt"); kt_, dkt = t4("kt_"); bh, dbh = be, dbe; kh, dkh = kd, dkd
        AR = S.sb("AR", [64, 4, 4, 2, 64], F32, st); dAR = S.dep()
        ltot = S.sb("ltot", [64, 4, 4], F32, st); dlt = S.dep()
        pc = S.sb("pc", [64, 4, 4], F32, st); dpc = S.dep()
        bkT = S.sb("bkT", [64, 4, 4, 2, 64], F32, st); dbk = S.dep()
        GM = S.sb("GM", [64, 4, 4, 320], F32, st); dGM = S.dep()
        XX0 = S.sb("XX0", [64, 16, 2, 64], F32, st); XX = [XX0, XX0]; dXX0 = S.dep(); dXX = [dXX0, dXX0]
        Tm = S.sb("Tm", [64, 16, 64], F32, st); dTm = S.dep()
        WT = S.sb("WT", [64, 4, 64], F32, st); dWT = S.dep()
        UT = S.sb("UT", [64, 4, 64], F32, st); dUT = S.dep()
        tS = S.sb("tS", [64, 4, 64], F32, st); dtS = S.dep()

        s_rk = T["s_rk"].rearrange("g c t -> c g t")
        s_wa = T["s_wa"].rearrange("j c t -> c j t")
        v4 = lambda ap: ap.rearrange("p h (c s) -> p h c s", s=64)
        fl = lambda ap: ap.rearrange("p h t -> p (h t)")

        for e_ in range(2):
            S.op("dve", lambda e: e.memset(ST[:], 0.0), reads=[dST], writes=[dST])
            blocks = list(range(NBLK)) if e_ == 0 else [0] + list(range(NBLK - 1, 0, -1))
            corder = [0, 1, 2, 3] if e_ == 0 else [3, 2, 1, 0]
            import os
            blocks = blocks[:int(os.environ.get('A3_BLOCKS', NBLK))]
            for blk in blocks:
                t0 = blk * 256
                S.drain_dma("sp", keep=4)
                S.dma("sp", rk_in[:], s_rk[:, :, t0:t0 + 256], writes=[drk])
                S.dma("sp", wa_in[:], s_wa[:, :, t0:t0 + 256], writes=[dwa])
                S.dma("sp", v_in[:], T["s_v"][t0:t0 + 256, :].rearrange("(c s) n -> s c n", s=64), writes=[dv])
                r_ = rk_in[:, 0:4, :]; k_ = rk_in[:, 4:8, :]
                pwp = PSA[:, 0:2, :].rearrange("p b (h t) -> p (b h) t", h=2)
                pap = PSA[:, 2:4, :].rearrange("p b (h t) -> p (b h) t", h=2)
                for h in range(4):
                    S.op("pe", lambda e: e.matmul(pwp[:, h, :], lhsT=wup[:, e_, h * 64:(h + 1) * 64], rhs=wa_in[:, 0, :], start=True, stop=True),
                         reads=[dconst, dwa], writes=[dB[h // 2]])
                for h in range(4):
                    S.op("pe", lambda e: e.matmul(pap[:, h, :], lhsT=aup[:, e_, h * 64:(h + 1) * 64], rhs=wa_in[:, 1, :], start=True, stop=True),
                         reads=[dconst, dwa], writes=[dB[2 + h // 2]])
                for h in range(4):
                    S.op("act", lambda e: e.activation(out=lw[:, h, :], in_=pwp[:, h, :], func=AF.Sigmoid, bias=prm[:, e_, h:h + 1]),
                         reads=[dB[h // 2], dconst], writes=[dlw])
                for h in range(4):
                    S.op("act", lambda e: e.activation(out=aa[:, h, :], in_=pap[:, h, :], func=AF.Sigmoid, bias=prm[:, 2 + e_, h:h + 1]),
                         reads=[dB[2 + h // 2], dconst], writes=[daa])
                S.op("dve", lambda e: e.tensor_scalar(out=lw[:], in0=lw[:], scalar1=-0.6065306597126334, scalar2=None, op0=ALU.mult),
                     reads=[dlw], writes=[dlw])
                S.op("dve", lambda e: e.tensor_tensor(out=kk[:], in0=k_, in1=prm[:, 4, :].unsqueeze(2).to_broadcast([64, 4, 256]), op=ALU.mult),
                     reads=[drk, dconst], writes=[dkk])
                S.op("dve", lambda e: e.tensor_tensor(out=tmp[:], in0=kk[:], in1=kk[:], op=ALU.mult), reads=[dkk], writes=[dtmp])
                for half in range(2):
                    S.op("pe", lambda e: e.matmul(PSA[:, 4 + half, :], lhsT=ones64[:], rhs=fl(tmp[:])[:, half * 512:(half + 1) * 512],
                                                  start=True, stop=True), reads=[dconst, dtmp], writes=[dB[4 + half]])
                S.op("dve", lambda e: e.tensor_scalar(out=fl(tmp[:]), in0=PSA[:, 4:6, :].rearrange("p b t -> p (b t)"), scalar1=1e-24, scalar2=None,
                                                      op0=ALU.max), reads=[dB[4], dB[5]], writes=[dtmp])
                S.op("act", lambda e: e.activation(out=tmp[:], in_=tmp[:], func=AF.Sqrt), reads=[dtmp], writes=[dtmp])
                S.op("dve", lambda e: e.reciprocal(out=tmp[:], in_=tmp[:]), reads=[dtmp], writes=[dtmp])
                S.op("dve", lambda e: e.tensor_tensor(out=kk[:], in0=kk[:], in1=tmp[:], op=ALU.mult), reads=[dkk, dtmp], writes=[dkk])
                S.op("dve", lambda e: e.tensor_tensor(out=tmp[:], in0=aa[:], in1=prm[:, 5, :].unsqueeze(2).to_broadcast([64, 4, 256]), op=ALU.mult),
                     reads=[daa, dconst], writes=[dtmp])
                S.op("dve", lambda e: e.tensor_tensor(out=tmp[:], in0=tmp[:], in1=omk[:].unsqueeze(2).to_broadcast([64, 4, 256]), op=ALU.add),
                     reads=[dtmp, dconst], writes=[dtmp])
                S.op("dve", lambda e: e.tensor_tensor(out=kd[:], in0=tmp[:], in1=k_, op=ALU.mult), reads=[dtmp, drk], writes=[dkd])
                S.op("pool", lambda e: e.tensor_tensor(out=be[:], in0=kk[:], in1=aa[:], op=ALU.mult), reads=[dkk, daa], writes=[dbe])
                S.op("dve", lambda e: e.tensor_tensor_scan(out=fl(L[:]), data0=rmask[:].rearrange("p a b -> p (a b)"), data1=fl(lw[:]),
                                                           initial=0.0, op0=ALU.mult, op1=ALU.add), reads=[dlw, dconst], writes=[dL])
                S.op("dve", lambda e: e.tensor_copy(out=ltot[:], in_=v4(L[:])[:, :, :, 63]), reads=[dL], writes=[dlt])
                if e_ == 0:
                    Lc, dLc = L, dL
                else:
                    S.op("dve", lambda e: e.tensor_tensor(out=tmp[:], in0=lw[:], in1=L[:], op=ALU.subtract), reads=[dlw, dL], writes=[dtmp])
                    S.op("dve", lambda e: e.tensor_tensor(out=v4(Ld[:]), in0=v4(tmp[:]), in1=ltot[:].unsqueeze(3).to_broadcast([64, 4, 4, 64]),
                                                          op=ALU.add), reads=[dtmp, dlt], writes=[dLd])
                    Lc, dLc = Ld, dLd
                S.op("pool", lambda e: e.tensor_tensor(out=E1[:], in0=Lc[:], in1=lw[:], op=ALU.subtract), reads=[dLc, dlw], writes=[dE1])
                S.op("act", lambda e: e.activation(out=E1[:], in_=E1[:], func=AF.Exp), reads=[dE1], writes=[dE1])
                S.op("act", lambda e: e.activation(out=E2[:], in_=Lc[:], func=AF.Exp), reads=[dLc], writes=[dE2])
                S.op("act", lambda e: e.activation(out=E3[:], in_=Lc[:], func=AF.Exp, scale=-1.0), reads=[dLc], writes=[dE3])
                S.op("dve", lambda e: e.tensor_tensor(out=v4(tmp[:]), in0=ltot[:].unsqueeze(3).to_broadcast([64, 4, 4, 64]), in1=v4(Lc[:]),
                                                      op=ALU.subtract), reads=[dlt, dLc], writes=[dtmp])
                S.op("act", lambda e: e.activation(out=E4[:], in_=tmp[:], func=AF.Exp), reads=[dtmp], writes=[dE4])
                S.op("act", lambda e: e.activation(out=pc[:], in_=ltot[:], func=AF.Exp), reads=[dlt], writes=[dpc])
                S.op("dve", lambda e: e.scalar_tensor_tensor(out=AR[:, :, :, 0, :], in0=v4(kk[:]), scalar=-1.0, in1=v4(E1[:]),
                                                             op0=ALU.mult, op1=ALU.mult), reads=[dkk, dE1], writes=[dAR])
                S.op("dve", lambda e: e.tensor_tensor(out=AR[:, :, :, 1, :], in0=v4(r_), in1=v4(E2[:]), op=ALU.mult), reads=[drk, dE2], writes=[dAR])
                S.op("dve", lambda e: e.tensor_tensor(out=kt_[:], in0=kd[:], in1=E3[:], op=ALU.mult), reads=[dkd, dE3], writes=[dkt])
                S.op("pool", lambda e: e.tensor_tensor(out=bt[:], in0=be[:], in1=E3[:], op=ALU.mult), reads=[dbe, dE3], writes=[dbt])
                S.op("dve", lambda e: e.tensor_tensor(out=tmp[:], in0=r_, in1=kd[:], op=ALU.mult), reads=[drk, dkd], writes=[dtmp])
                S.op("dve", lambda e: e.tensor_tensor(out=tmp[:], in0=tmp[:], in1=prm[:, 6, :].unsqueeze(2).to_broadcast([64, 4, 256]), op=ALU.mult),
                     reads=[dtmp, dconst], writes=[dtmp])
                pbo2 = PSA[:, 6, 0:32].rearrange("p (c h w) -> p c h w", h=4, w=2)
                pbo = pbo2[:, :, :, 0]
                for c in range(4):
                    for h in range(4):
                        S.op("pe", lambda e: e.matmul(pbo2[:, c, h, :], lhsT=tmp[:, h, c * 64:(c + 1) * 64], rhs=ones64[:, 0:2], start=True, stop=True),
                             reads=[dtmp, dconst], writes=[dB[6]])
                S.op("dve", lambda e: e.tensor_tensor(out=bh[:], in0=be[:], in1=E4[:], op=ALU.mult), reads=[dbe, dE4], writes=[dbh])
                S.op("pool", lambda e: e.tensor_tensor(out=kh[:], in0=kd[:], in1=E4[:], op=ALU.mult), reads=[dkd, dE4], writes=[dkh])
                bsl = bacc[:, blk * 4:(blk + 1) * 4, :]
                if e_ == 0:
                    S.op("dve", lambda e: e.tensor_copy(out=bsl, in_=pbo), reads=[dB[6]], writes=[dba])
                else:
                    S.op("dve", lambda e: e.tensor_tensor(out=bsl, in0=bsl, in1=pbo, op=ALU.add), reads=[dB[6], dba], writes=[dba])
                ptr = PSA[:, 0:4, :].rearrange("p b (i s) -> p (b i) s", s=64)
                for c in range(4):
                    for h in range(4):
                        for w_, (src, dsrc) in enumerate([(bh, dbh), (kh, dkh)]):
                            idx = (c * 4 + h) * 2 + w_
                            S.op("pe", lambda e: e.transpose(out=ptr[:, idx, :], in_=src[:, h, c * 64:(c + 1) * 64], identity=identf[:]),
                                 reads=[dsrc, didf], writes=[dB[idx // 8]])
                for half in range(2):
                    S.op("act", lambda e: e.copy(out=bkT[:, half * 2:(half + 1) * 2].rearrange("p c h w s -> p (c h w s)"),
                                                 in_=PSA[:, half * 2:(half + 1) * 2, :].rearrange("p b t -> p (b t)")),
                         reads=[dB[half * 2], dB[half * 2 + 1]], writes=[dbk])
                for c in range(0 if 'g' in os.environ.get('A3_SKIP', '') else 4):
                    bb = (c % 2) * 4
                    cs = slice(c * 64, (c + 1) * 64)
                    for h in range(4):
                        arh = AR[:, h, c, :, :].rearrange("p a s -> p (a s)")
                        S.op("pe", lambda e: e.matmul(PSA[:, bb + h, 0:128], lhsT=bt[:, h, cs], rhs=arh, start=True, stop=True),
                             reads=[dbt, dAR], writes=[dB[bb + h]])
                        S.op("pe", lambda e: e.matmul(PSA[:, bb + h, 128:256], lhsT=kt_[:, h, cs], rhs=arh, start=True, stop=True),
                             reads=[dkt, dAR], writes=[dB[bb + h]])
                        S.op("pe", lambda e: e.matmul(PSA[:, bb + h, 256:320], lhsT=AR[:, h, c, 0, :], rhs=bt[:, h, cs], start=True, stop=True),
                             reads=[dbt, dAR], writes=[dB[bb + h]])
                    S.op("dve", lambda e: e.tensor_tensor(out=GM[:, c, :, :], in0=PSA[:, bb:bb + 4, 0:320],
                                                          in1=MSK[:, e_, :].unsqueeze(1).to_broadcast([64, 4, 320]), op=ALU.mult),
                         reads=[dB[bb], dB[bb + 1], dB[bb + 2], dB[bb + 3], dconst], writes=[dGM])
                GMf = GM[:].rearrange("p c h n -> p (c h) n")
                S.op("dve", lambda e: e.tensor_tensor(out=Tm[:], in0=GMf[:, :, 0:64], in1=identf[:].unsqueeze(1).to_broadcast([64, 16, 64]), op=ALU.add),
                     reads=[dGM, didf], writes=[dTm])
                pxx = PSA[:, 0:4, :].rearrange("p b (i w s) -> p (b i) w s", w=2, s=64)
                ptm = PSA[:, 4:6, :].rearrange("p b (i s) -> p (b i) s", s=64)
                for it_ in range(0 if 'd' in os.environ.get('A3_SKIP', '') else 5):
                    pp = it_ % 2
                    for idx in range(16):
                        if it_ == 0:
                            Xc = GMf[:, idx, 0:64]; XTc = GMf[:, idx, 256:320]; dsrc = dGM
                        else:
                            Xc = XX[1 - pp][:, idx, 0, :]; XTc = XX[1 - pp][:, idx, 1, :]; dsrc = dXX[1 - pp]
                        S.op("pe", lambda e: e.matmul(pxx[:, idx, 0, :], lhsT=XTc, rhs=Xc, start=True, stop=True), reads=[dsrc], writes=[dB[idx // 4]])
                        S.op("pe", lambda e: e.matmul(pxx[:, idx, 1, :], lhsT=Xc, rhs=XTc, start=True, stop=True), reads=[dsrc], writes=[dB[idx // 4]])
                    S.op("act", lambda e: e.copy(out=XX[pp][:].rearrange("p i w s -> p (i w s)"), in_=PSA[:, 0:4, :].rearrange("p b t -> p (b t)")),
                         reads=[dB[0], dB[1], dB[2], dB[3]], writes=[dXX[pp]])
                    for idx in range(16):
                        S.op("pe", lambda e: e.matmul(ptm[:, idx, :], lhsT=XX[pp][:, idx, 1, :], rhs=Tm[:, idx, :], start=True, stop=True),
                             reads=[dXX[pp], dTm], writes=[dB[4 + idx // 8]])
                    S.op("dve", lambda e: e.tensor_tensor(out=Tm[:].rearrange("p i s -> p (i s)"), in0=Tm[:].rearrange("p i s -> p (i s)"),
                                                          in1=PSA[:, 4:6, :].rearrange("p b t -> p (b t)"), op=ALU.add),
                         reads=[dB[4], dB[5], dTm], writes=[dTm])
                pW = PSA[:, 6, 0:256].rearrange("p (h s) -> p h s", s=64)
                pU = PSA[:, 7, 0:256].rearrange("p (h s) -> p h s", s=64)
                pYS = PSA[:, 6, :].rearrange("p (h w s) -> p h w s", w=2, s=64)
                for c in ([] if 's' in os.environ.get('A3_SKIP', '') else corder):
                    gc = blk * 4 + c
                    for h in range(4):
                        vh = v_in[:, c, h * 64:(h + 1) * 64]
                        S.op("pe", lambda e: e.matmul(pW[:, h, :], lhsT=AR[:, h, c, 0, :], rhs=ST[:, h, :], start=True, stop=False),
                             reads=[dAR, dST], writes=[dB[6]])
                        S.op("pe", lambda e: e.matmul(pW[:, h, :], lhsT=GM[:, c, h, 128:192], rhs=vh, start=False, stop=True),
                             reads=[dGM, dv], writes=[dB[6]])
                    S.op("act", lambda e: e.copy(out=WT[:], in_=pW), reads=[dB[6]], writes=[dWT])
                    for h in range(4):
                        S.op("pe", lambda e: e.matmul(pU[:, h, :], lhsT=Tm[:, c * 4 + h, :], rhs=WT[:, h, :], start=True, stop=True),
                             reads=[dTm, dWT], writes=[dB[7]])
                    S.op("act", lambda e: e.copy(out=UT[:], in_=pU), reads=[dB[7]], writes=[dUT])
                    if 'y' in os.environ.get('A3_SKIP', ''):
                        continue
                    for h in range(4):
                        vh = v_in[:, c, h * 64:(h + 1) * 64]
                        S.op("pe", lambda e: e.matmul(pYS[:, h, 0, :], lhsT=AR[:, h, c, 1, :], rhs=ST[:, h, :], start=True, stop=False),
                             reads=[dAR, dST], writes=[dB[6]])
                        S.op("pe", lambda e: e.matmul(pYS[:, h, 0, :], lhsT=GM[:, c, h, 64:128], rhs=UT[:, h, :], start=False, stop=False),
                             reads=[dGM, dUT], writes=[dB[6]])
                        S.op("pe", lambda e: e.matmul(pYS[:, h, 0, :], lhsT=GM[:, c, h, 192:256], rhs=vh, start=False, stop=True),
                             reads=[dGM, dv], writes=[dB[6]])
                        S.op("pe", lambda e: e.matmul(pYS[:, h, 1, :], lhsT=bkT[:, c, h, 0, :], rhs=UT[:, h, :], start=True, stop=False),
                             reads=[dbk, dUT], writes=[dB[6]])
                        S.op("pe", lambda e: e.matmul(pYS[:, h, 1, :], lhsT=bkT[:, c, h, 1, :], rhs=vh, start=False, stop=True),
                             reads=[dbk, dv], writes=[dB[6]])
                    ysl = yacc[:, gc, :].rearrange("p (h s) -> p h s", s=64)
                    if e_ == 0:
                        S.op("dve", lambda e: e.tensor_copy(out=ysl, in_=pYS[:, :, 0, :]), reads=[dB[6]], writes=[dy])
                    else:
                        S.op("dve", lambda e: e.tensor_tensor(out=ysl, in0=ysl, in1=pYS[:, :, 0, :], op=ALU.add), reads=[dB[6], dy], writes=[dy])
                    S.op("dve", lambda e: e.tensor_tensor(out=tS[:], in0=ST[:], in1=pc[:, :, c:c + 1].to_broadcast([64, 4, 64]), op=ALU.mult),
                         reads=[dST, dpc], writes=[dtS])
                    S.op("dve", lambda e: e.tensor_tensor(out=ST[:], in0=tS[:], in1=pYS[:, :, 1, :], op=ALU.add), reads=[dtS, dB[6]], writes=[dST])
        gtb_ = fl(E3[:]).bitcast(BF16)[:, 0:1024].rearrange("p (c n) -> p c n", n=256); dgtb = dE3
        obr_ = fl(E4[:]).bitcast(BF16)[:, 0:1024].rearrange("p (c n) -> p c n", n=256); dobr = dE4
        st1 = S.sb("st1", [64, 4, 16], F32, st); dst1 = S.dep()
        v16 = lambda ap: ap.rearrange("p c (h s) -> p (c h) s", s=64)
        for blk in range(NBLK):
            t0 = blk * 256
            S.drain_dma("sp", keep=4)
            yb = yacc[:, blk * 4:(blk + 1) * 4, :]
            S.dma("sp", v_in[:], T["s_v"][t0:t0 + 256, :].rearrange("(c s) n -> s c n", s=64), writes=[dv])
            S.dma("sp", gtb_, T["s_gate"][t0:t0 + 256, 0:256].rearrange("(c s) n -> s c n", s=64), writes=[dgtb])
            S.op("dve", lambda e: e.tensor_reduce(out=st1[:, 0, :], in_=v16(yb), axis=AX.X, op=ALU.add), reads=[dy], writes=[dst1])
            S.op("dve", lambda e: e.tensor_tensor(out=tmp[:], in0=yb, in1=yb, op=ALU.mult), reads=[dy], writes=[dtmp])
            S.op("dve", lambda e: e.tensor_reduce(out=st1[:, 1, :], in_=v16(tmp[:]), axis=AX.X, op=ALU.add), reads=[dtmp], writes=[dst1])
            S.op("dve", lambda e: e.tensor_scalar(out=st1[:, 0:2, :], in0=st1[:, 0:2, :], scalar1=1.0 / 64, scalar2=None, op0=ALU.mult),
                 reads=[dst1], writes=[dst1])
            S.op("dve", lambda e: e.tensor_tensor(out=st1[:, 2, :], in0=st1[:, 0, :], in1=st1[:, 0, :], op=ALU.mult), reads=[dst1], writes=[dst1])
            S.op("dve", lambda e: e.tensor_tensor(out=st1[:, 3, :], in0=st1[:, 1, :], in1=st1[:, 2, :], op=ALU.subtract), reads=[dst1], writes=[dst1])
            _rstd(S, st1[:, 3, :], 16, 1.0, 64e-5, [], dst1)
            S.op("dve", lambda e: e.tensor_tensor(out=v16(tmp[:]), in0=v16(yb), in1=st1[:, 0, :].unsqueeze(2).to_broadcast([64, 16, 64]), op=ALU.subtract),
                 reads=[dy, dst1], writes=[dtmp])
            S.op("dve", lambda e: e.tensor_tensor(out=v16(tmp[:]), in0=v16(tmp[:]), in1=st1[:, 3, :].unsqueeze(2).to_broadcast([64, 16, 64]), op=ALU.mult),
                 reads=[dtmp, dst1], writes=[dtmp])
            S.op("dve", lambda e: e.tensor_tensor(out=tmp[:], in0=tmp[:], in1=gng[:, 0, :].unsqueeze(1).to_broadcast([64, 4, 256]), op=ALU.mult),
                 reads=[dtmp, dconst], writes=[dtmp])
            S.op("dve", lambda e: e.tensor_tensor(out=tmp[:], in0=tmp[:], in1=gng[:, 1, :].unsqueeze(1).to_broadcast([64, 4, 256]), op=ALU.add),
                 reads=[dtmp, dconst], writes=[dtmp])
            bv = bacc[:, blk * 4:(blk + 1) * 4, :].rearrange("p c h -> p (c h)").unsqueeze(2).to_broadcast([64, 16, 64])
            S.op("dve", lambda e: e.tensor_tensor(out=v16(E1[:]), in0=v16(v_in[:]), in1=bv, op=ALU.mult), reads=[dv, dba], writes=[dE1])
            S.op("dve", lambda e: e.tensor_tensor(out=tmp[:], in0=tmp[:], in1=E1[:], op=ALU.add), reads=[dtmp, dE1], writes=[dtmp])
            S.op("dve", lambda e: e.tensor_tensor(out=obr_, in0=tmp[:], in1=gtb_, op=ALU.mult), reads=[dtmp, dgtb], writes=[dobr])
            S.dma("sp", T["br_rwkv"][t0:t0 + 256, :].rearrange("(c s) n -> s c n", s=64), obr_, reads=[dobr])
        S.barrier()


def build_A(phases="123"):
    nc = bass.Bass("TRN2", target_bir_lowering=False)
    T = {}

    def din(name, shape, dt=F32):
        T[name] = nc.dram_tensor(name, list(shape), dt, kind="ExternalInput").ap()

    def dscr(name, shape, dt):
        T[name] = nc.dram_tensor(name, list(shape), dt, kind="Internal").ap()

    def dout(name, shape, dt):
        T[name] = nc.dram_tensor(name, list(shape), dt, kind="ExternalOutput").ap()

    din("xf", [NT, D]); din("cT", [128, 16, 2]); din("wmod", [D, 4096]); din("bmod", [1, 4096]); din("gpre", [1, D])
    din("gqn", [1, 256]); din("w_fm", [D, 704]); din("w_tm", [D, 2304]); din("rope", [NT, 192])
    din("lamp", [1, 256]); din("lami", [1, 1]); din("subg", [1, 128]); din("rprm", [64, 7, 4])
    din("wup", [2, 96, 256]); din("aup", [2, 96, 256]); din("gn", [1, 512])
    dscr("s_rk", [8, 64, NT], F32); dscr("s_wa", [2, 96, NT], F32); dscr("s_v", [NT, 256], F32)
    dscr("s_dqkT", [8, 64, NT], BF16); dscr("s_gqkT", [3, 128, NT], BF16)
    dscr("s_dv", [NT, 2, 129], BF16); dscr("s_gv", [NT, 129], BF16); dscr("s_gate", [NT, 768], BF16)
    dout("hT", [16, 128, NT], BF16); dout("br_att", [NT, 512], BF16); dout("br_rwkv", [NT, 256], BF16)
    with contextlib.ExitStack() as st:
        S = Sched(nc, st)
        if "1" in phases:
            _phase_A1(S, nc, T)
        if "2" in phases:
            _phase_A2(S, nc, T)
        if "3" in phases:
            _phase_A3(S, nc, T)
        S.barrier()
        print("build_A: ninst", S.ninst, "nsem", S.nsem, {k: v for k, v in S.cnt.items()})
    return nc


OFF = {"cv_val": 0, "cv_glu": 1024, "cv_gate": 2048, "rk_r": 3072, "rk_k": 4096, "rk_v": 5120, "rk_wl": 6144, "rk_al": 6240,
       "rk_gate": 6336, "df_q": 7360, "df_k": 8384, "df_v": 9408, "df_gate": 10432, "gq_q": 11456, "gq_k": 12480, "gq_v": 12736,
       "gq_gate": 12992, "merge": 14016}


def _rope_table():
    tab = np.zeros((NT, 192), np.float32)
    tab[:, 0:32] = 1.0
    tab[:, 64:128] = 1.0
    t = np.arange(4096)
    row = (t // 64).astype(np.float32); col = (t % 64).astype(np.float32)
    for half, c0, s0 in ((32, 0, 32), (64, 64, 128)):
        inv = (10000.0 ** (-np.arange(0, half, 2, dtype=np.float32) / half)).astype(np.float32)
        ang = np.concatenate([row[:, None] * inv, col[:, None] * inv], axis=-1).astype(np.float32)
        tab[NCTX:, c0:c0 + half] = np.cos(ang)
        tab[NCTX:, s0:s0 + half] = np.sin(ang)
    return tab


def _cT(c_ctx, cb):
    both = np.stack([c_ctx, cb], axis=-1)
    return np.ascontiguousarray(both.reshape(16, 128, 2).transpose(1, 0, 2))


def inputs_A(inp, li, b, q, xfull, rope):
    w_in = inp["w_in"][li]
    cs = lambda name, a, n: w_in[:, OFF[name] + a:OFF[name] + a + n]
    kv = q // 2
    w_fm = np.concatenate([cs("rk_r", 256 * q, 256), cs("rk_k", 256 * q, 256), cs("rk_wl", 0, 96), cs("rk_al", 0, 96)], axis=1)
    w_tm = np.concatenate([cs("rk_v", 256 * q, 256), cs("rk_gate", 256 * q, 256),
                           cs("df_q", 256 * q, 256), cs("df_k", 256 * q, 256), cs("df_v", 256 * q, 256), cs("df_gate", 256 * q, 256),
                           cs("gq_q", 256 * q, 256), cs("gq_k", 128 * kv, 128), cs("gq_v", 128 * kv, 128), cs("gq_gate", 256 * q, 256)], axis=1)
    sl = slice(256 * q, 256 * q + 256)
    hm = lambda v: v[sl].reshape(4, 64).T
    rprm = np.stack([hm(inp["rwkv_w0"][li, 0]), hm(inp["rwkv_w0"][li, 1]), hm(inp["rwkv_a0"][li, 0]), hm(inp["rwkv_a0"][li, 1]),
                     hm(inp["rwkv_k_k"][li]), hm(inp["rwkv_k_a"][li]), hm(inp["rwkv_r_k"][li].reshape(-1))], axis=1)
    lam_init = 0.8 - 0.6 * math.exp(-0.3 * li)
    return {
        "xf": np.ascontiguousarray(xfull[b]), "cT": _cT(inp["c_ctx"], inp["c"][b]),
        "wmod": np.ascontiguousarray(inp["w_mod"][li][:, 0:4096]), "bmod": np.ascontiguousarray(inp["b_mod"][li][None, 0:4096]),
        "gpre": np.ascontiguousarray(inp["norm_pre_g"][li][None]), "gqn": np.ascontiguousarray(inp["gqa_qk_norm_g"][li].reshape(1, 256)),
        "w_fm": np.ascontiguousarray(w_fm), "w_tm": np.ascontiguousarray(w_tm), "rope": rope,
        "lamp": np.ascontiguousarray(inp["diff_lam"][li].reshape(1, 256)), "lami": np.full((1, 1), lam_init, np.float32),
        "subg": np.ascontiguousarray(inp["diff_subln_g"][li][None]), "rprm": np.ascontiguousarray(rprm.astype(np.float32)),
        "wup": np.ascontiguousarray(inp["rwkv_w_up"][li][:, :, sl]), "aup": np.ascontiguousarray(inp["rwkv_a_up"][li][:, :, sl]),
        "gn": np.ascontiguousarray(np.concatenate([inp["rwkv_gn_g"][li][sl], inp["rwkv_gn_b"][li][sl]])[None]),
    }


NOWN = 1088
NEXT = 1152
MBLK = [(0, 64, 15), (64, 384, 109), (448, 384, 493), (832, 256, 877)]
LNBLK = [(0, 384), (384, 384), (768, 320)]


def build_B():
    nc = bass.Bass("TRN2", target_bir_lowering=False)
    T = {}

    def din(name, shape, dt=F32):
        T[name] = nc.dram_tensor(name, list(shape), dt, kind="ExternalInput").ap()

    din("hTx", [128, 16, NEXT], BF16); din("mask", [1, NEXT]); din("brT", [128, 3, 8, NOWN], BF16); din("x_own", [NOWN, D])
    din("w_cv", [D, 3072]); din("cvp", [128, 8, 34]); din("Wl", [D, 8192]); din("Wb", [4, W, D]); din("Wout", [D, D])
    din("bg", [128, 4, 16]); din("cT", [128, 16, 2]); din("wmodg", [D, D]); din("bmodg", [1, D]); din("gpost", [1, D])
    T["s_cv"] = nc.dram_tensor("s_cv", [8, 128, NOWN], BF16, kind="Internal").ap()
    T["xo"] = nc.dram_tensor("xo", [NOWN, D], F32, kind="ExternalOutput").ap()
    with contextlib.ExitStack() as st0:
        S = Sched(nc, st0)
        big = S.sb("big", [128, 16 * NOWN], BF16, st0); dbig = S.dep()
        mergedT = big[:].rearrange("p (c t) -> p c t", t=NOWN)
        conv_all = big[:].bitcast(F32).rearrange("p (c t) -> p c t", t=NOWN)
        with contextlib.ExitStack() as stm:
            hTx = S.sb("hTx_sb", [128, 16, NEXT], BF16, stm); dhTx = S.dep()
            for k4 in range(4):
                S.dma("sp", hTx[:, k4 * 4:(k4 + 1) * 4, :], T["hTx"][:, k4 * 4:(k4 + 1) * 4, :], writes=[dhTx])
            with contextlib.ExitStack() as st:
                maskb = S.sb("maskb", [128, NEXT], F32, st); dmask = S.dep()
                S.dma("sp", maskb[:], T["mask"][0:1, :].to_broadcast([128, NEXT]), writes=[dmask])
                cvp = S.sb("cvp_sb", [128, 8, 34], F32, st); dcvp = S.dep()
                S.dma("sp", cvp[:], T["cvp"][:, :, :], writes=[dcvp])
                ones = S.sb("ones128", [128, 128], F32, st); dones = S.dep()
                S.op("dve", lambda e: e.memset(ones[:], 1.0), writes=[dones])
                wck = [S.sb(f"wck{k}", [128, 16, 128], BF16, st) for k in range(3)]; dwck = [S.dep() for _ in range(3)]
                u = S.sb("u", [128, NEXT], F32, st); du = S.dep()
                sg = S.sb("sg", [128, 384], F32, st); dsg = S.dep()
                cgx = S.sb("cgx", [128, 8, NEXT], BF16, st); dcg = S.dep()
                sqt = S.sb("sqt", [128, NOWN], F32, st); dsq = S.dep()
                meanb = S.sb("meanb", [128, NOWN], F32, st); dmean = S.dep()
                rstdb = S.sb("rstdb", [128, NOWN], F32, st); drstd = S.dep()
                cst = S.sb("cst", [128, NOWN], BF16, st); dcst = S.dep()
                pcv = [S.ps(f"pcv{i}", [128, 512], F32, st) for i in range(6)]; dpcv = [S.pdep() for _ in range(6)]
                wcv = T["w_cv"].rearrange("(kc p) n -> p kc n", p=128)
                for cc in range(8):
                    for k in range(3):
                        c0 = k * 1024 + cc * 128
                        for k4 in range(2):
                            S.dma("pool", wck[k][:, k4 * 8:(k4 + 1) * 8, :], wcv[:, k4 * 8:(k4 + 1) * 8, c0:c0 + 128], writes=[dwck[k]])
                    for tb in range(3):
                        ts_ = slice(tb * 384, (tb + 1) * 384)
                        pgl, dgl = pcv[0 + tb % 2], dpcv[0 + tb % 2]
                        pv, dpv = pcv[2 + tb % 2], dpcv[2 + tb % 2]
                        pgt, dgt_ = pcv[4 + tb % 2], dpcv[4 + tb % 2]
                        for (k, p_, dp_) in ((1, pgl, dgl), (0, pv, dpv), (2, pgt, dgt_)):
                            for kc in range(16):
                                S.op("pe", lambda e: e.matmul(p_[:, 0:384], lhsT=wck[k][:, kc, :], rhs=hTx[:, kc, ts_], start=(kc == 0), stop=(kc == 15)),
                                     reads=[dwck[k], dhTx], writes=[dp_])
                        S.op("act", lambda e: e.activation(out=sg[:], in_=pgl[:, 0:384], func=AF.Sigmoid), reads=[dgl], writes=[dsg])
                        S.op("dve", lambda e: e.tensor_tensor(out=sg[:], in0=sg[:], in1=maskb[:, ts_], op=ALU.mult), reads=[dsg, dmask], writes=[dsg])
                        S.op("dve", lambda e: e.tensor_tensor(out=u[:, ts_], in0=pv[:, 0:384], in1=sg[:], op=ALU.mult), reads=[dpv, dsg], writes=[du])
                        S.op("act", lambda e: e.activation(out=cgx[:, cc, ts_], in_=pgt[:, 0:384], func=AF.Silu), reads=[dgt_], writes=[dcg])
                    for (o0_, n_, e0_) in ((0, 64, 0), (64, 1024, 94)):
                        acc = conv_all[:, cc, o0_:o0_ + n_]
                        S.op("dve", lambda e: e.tensor_scalar(out=acc, in0=u[:, e0_:e0_ + n_], scalar1=cvp[:, cc, 0:1], scalar2=cvp[:, cc, 31:32],
                                                              op0=ALU.mult, op1=ALU.add), reads=[du, dcvp], writes=[dbig])
                        for j in range(1, 31):
                            S.op("dve", lambda e: e.scalar_tensor_tensor(out=acc, in0=u[:, e0_ + j:e0_ + j + n_], scalar=cvp[:, cc, j:j + 1], in1=acc,
                                                                         op0=ALU.mult, op1=ALU.add), reads=[du, dcvp, dbig], writes=[dbig])
                for cc in range(8):
                    S.op("dve", lambda e: e.tensor_tensor(out=sqt[:], in0=conv_all[:, cc, :], in1=conv_all[:, cc, :], op=ALU.mult), reads=[dbig], writes=[dsq])
                    for bi, (o_, n_) in enumerate(LNBLK):
                        S.op("pe", lambda e: e.matmul(pcv[bi][:, 0:n_], lhsT=ones[:], rhs=conv_all[:, cc, o_:o_ + n_], start=(cc == 0), stop=(cc == 7)),
                             reads=[dones, dbig], writes=[dpcv[bi]])
                        S.op("pe", lambda e: e.matmul(pcv[3 + bi][:, 0:n_], lhsT=ones[:], rhs=sqt[:, o_:o_ + n_], start=(cc == 0), stop=(cc == 7)),
                             reads=[dones, dsq], writes=[dpcv[3 + bi]])
                for bi, (o_, n_) in enumerate(LNBLK):
                    S.op("dve", lambda e: e.tensor_scalar(out=meanb[:, o_:o_ + n_], in0=pcv[bi][:, 0:n_], scalar1=1.0 / W, scalar2=None, op0=ALU.mult),
                         reads=[dpcv[bi]], writes=[dmean])
                    S.op("dve", lambda e: e.tensor_scalar(out=rstdb[:, o_:o_ + n_], in0=pcv[3 + bi][:, 0:n_], scalar1=1.0 / W, scalar2=None, op0=ALU.mult),
                         reads=[dpcv[3 + bi]], writes=[drstd])
                S.op("dve", lambda e: e.tensor_tensor(out=sqt[:], in0=meanb[:], in1=meanb[:], op=ALU.mult), reads=[dmean], writes=[dsq])
                S.op("dve", lambda e: e.tensor_tensor(out=rstdb[:], in0=rstdb[:], in1=sqt[:], op=ALU.subtract), reads=[drstd, dsq], writes=[drstd])
                _rstd(S, rstdb[:], NOWN, 1.0, 1e-5, [], drstd)
                for cc in range(8):
                    cv = conv_all[:, cc, :]
                    S.op("dve", lambda e: e.tensor_tensor(out=cv, in0=cv, in1=meanb[:], op=ALU.subtract), reads=[dbig, dmean], writes=[dbig])
                    S.op("dve", lambda e: e.tensor_tensor(out=cv, in0=cv, in1=rstdb[:], op=ALU.mult), reads=[dbig, drstd], writes=[dbig])
                    S.op("act", lambda e: e.activation(out=sqt[:], in_=cv, func=AF.Silu, bias=cvp[:, cc, 33:34], scale=cvp[:, cc, 32:33]),
                         reads=[dbig, dcvp], writes=[dsq])
                    S.op("dve", lambda e: e.tensor_tensor(out=cst[:, 0:64], in0=sqt[:, 0:64], in1=cgx[:, cc, 15:79], op=ALU.mult), reads=[dsq, dcg], writes=[dcst])
                    S.op("dve", lambda e: e.tensor_tensor(out=cst[:, 64:NOWN], in0=sqt[:, 64:NOWN], in1=cgx[:, cc, 109:1133], op=ALU.mult), reads=[dsq, dcg], writes=[dcst])
                    S.dma("sp", T["s_cv"][cc, :, :], cst[:], reads=[dcst])
                S.barrier()
            with contextlib.ExitStack() as st:
                brT4 = S.sb("brT4", [128, 4, 8, NOWN], BF16, st); dbr = S.dep()
                S.dma("sp", brT4[:, 0, :, :], T["s_cv"].rearrange("c p t -> p c t"), writes=[dbr])
                for j in range(3):
                    S.dma("sp", brT4[:, 1 + j, :, :], T["brT"][:, j, :, :], writes=[dbr])
                bg = S.sb("bg_sb", [128, 4, 16], F32, st); dbg = S.dep()
                S.dma("sp", bg[:], T["bg"][:, :, :], writes=[dbg])
                wl = [S.sb(f"wl{i}", [128, 16, 4, 128], BF16, st) for i in range(2)]; dwl = [S.dep() for _ in range(2)]
                wb = [S.sb(f"wb{i}", [128, 8, 4, 128], BF16, st) for i in range(2)]; dwb = [S.dep() for _ in range(2)]
                gsb = S.sb("gsb", [128, 384], F32, st); dgs = S.dep()
                macc = S.sb("macc", [128, 384], F32, st); dma_ = S.dep()
                mtmp = S.sb("mtmp", [128, 384], F32, st); dmt = S.dep()
                pl = [S.ps(f"pl{i}", [128, 512], F32, st) for i in range(2)]; dpl = [S.pdep() for _ in range(2)]
                pp = [S.ps(f"pp{i}", [128, 512], F32, st) for i in range(2)]; dpp = [S.pdep() for _ in range(2)]
                Wlv = T["Wl"].rearrange("(kc p) n -> p kc n", p=128)
                it = 0
                for dc in range(16):
                    b2 = dc % 2
                    for j in range(4):
                        c0 = j * D + dc * 128
                        S.dma("pool", wl[b2][:, :, j, :], Wlv[:, :, c0:c0 + 128], writes=[dwl[b2]])
                        S.dma("pool", wb[b2][:, :, j, :], T["Wb"][j].rearrange("(cc p) n -> p cc n", p=128)[:, :, dc * 128:(dc + 1) * 128], writes=[dwb[b2]])
                    for (o_, n_, e_) in MBLK:
                        for j in range(4):
                            p1, d1 = pl[it % 2], dpl[it % 2]
                            p2, d2 = pp[it % 2], dpp[it % 2]
                            it += 1
                            for kc in range(16):
                                S.op("pe", lambda e: e.matmul(p1[:, 0:n_], lhsT=wl[b2][:, kc, j, :], rhs=hTx[:, kc, e_:e_ + n_], start=(kc == 0), stop=(kc == 15)),
                                     reads=[dwl[b2], dhTx], writes=[d1])
                            for cc in range(8):
                                S.op("pe", lambda e: e.matmul(p2[:, 0:n_], lhsT=wb[b2][:, cc, j, :], rhs=brT4[:, j, cc, o_:o_ + n_], start=(cc == 0), stop=(cc == 7)),
                                     reads=[dwb[b2], dbr], writes=[d2])
                            S.op("act", lambda e: e.activation(out=gsb[:, 0:n_], in_=p1[:, 0:n_], func=AF.Sigmoid, bias=bg[:, j, dc:dc + 1]),
                                 reads=[d1, dbg], writes=[dgs])
                            if j == 0:
                                S.op("dve", lambda e: e.tensor_tensor(out=macc[:, 0:n_], in0=p2[:, 0:n_], in1=gsb[:, 0:n_], op=ALU.mult), reads=[d2, dgs], writes=[dma_])
                            else:
                                S.op("dve", lambda e: e.tensor_tensor(out=mtmp[:, 0:n_], in0=p2[:, 0:n_], in1=gsb[:, 0:n_], op=ALU.mult), reads=[d2, dgs], writes=[dmt])
                                dst = macc[:, 0:n_] if j < 3 else mergedT[:, dc, o_:o_ + n_]
                                S.op("dve", lambda e: e.tensor_tensor(out=dst, in0=macc[:, 0:n_], in1=mtmp[:, 0:n_], op=ALU.add),
                                     reads=[dma_, dmt], writes=[dma_ if j < 3 else dbig])
                S.barrier()
        with contextlib.ExitStack() as st:
            modb = [S.sb(f"modg{i}", [128, D], F32, st) for i in range(2)]; dmodb = S.dep()
            _mod_prologue(S, nc, T["cT"], T["wmodg"], T["bmodg"], D, modb, dmodb)
            gpb = S.sb("gpb", [128, D], F32, st); dgp = S.dep()
            S.dma("sp", gpb[:], T["gpost"][0:1, :].to_broadcast([128, D]), writes=[dgp])
            for wh in range(2):
                S.op("dve", lambda e: e.tensor_tensor(out=modb[wh][:], in0=modb[wh][:], in1=gpb[:], op=ALU.mult), reads=[dmodb, dgp], writes=[dmodb])
            Wo = S.sb("Wo", [128, 16, D], BF16, st); dWo = S.dep()
            Wov = T["Wout"].rearrange("(kc p) n -> p kc n", p=128)
            for kc in range(16):
                S.dma("pool", Wo[:, kc:kc + 1, :], Wov[:, kc:kc + 1, :], writes=[dWo])
            xt = [S.sb(f"xt{i}", [128, D], F32, st) for i in range(2)]; dxt = [S.dep() for _ in range(2)]
            yb = S.sb("yb", [128, D], F32, st); dyb = S.dep()
            jk = S.sb("jk", [128, D], BF16, st); djk = S.dep()
            ss = S.sb("ss3", [128, 4], F32, st); dss = S.dep()
            py = [S.ps(f"py{i}", [128, 512], F32, st) for i in range(2)]; dpy = [S.pdep() for _ in range(2)]
            tiles = [(0, 64, 0)] + [(64 + 128 * i, 128, 1) for i in range(8)]
            for ti, (o_, n_, wh) in enumerate(tiles):
                S.drain_dma("sp", keep=4)
                x_ = xt[ti % 2]; dx_ = dxt[ti % 2]
                S.dma("sp", x_[0:n_, :], T["x_own"][o_:o_ + n_, :], writes=[dx_])
                for cg in range(4):
                    p_, dp_ = py[cg % 2], dpy[cg % 2]
                    for kc in range(16):
                        S.op("pe", lambda e: e.matmul(p_[0:n_, :], lhsT=mergedT[:, kc, o_:o_ + n_], rhs=Wo[:, kc, cg * 512:(cg + 1) * 512], start=(kc == 0), stop=(kc == 15)),
                             reads=[dbig, dWo], writes=[dp_])
                    S.op("act", lambda e: e.copy(out=yb[0:n_, cg * 512:(cg + 1) * 512], in_=p_[0:n_, :]), reads=[dp_], writes=[dyb])
                S.op("act", lambda e: e.activation(out=jk[0:n_, :], in_=yb[0:n_, :], func=AF.Square, accum_out=ss[0:n_, 0:1]), reads=[dyb], writes=[djk, dss])
                _rstd(S, ss[0:n_, 0:1], 1, 1.0 / D, EPS, [], dss)
                S.op("dve", lambda e: e.scalar_tensor_tensor(out=yb[0:n_, :], in0=yb[0:n_, :], scalar=ss[0:n_, 0:1], in1=modb[wh][0:n_, :],
                                                             op0=ALU.mult, op1=ALU.mult), reads=[dyb, dss, dmodb], writes=[dyb])
                S.op("dve", lambda e: e.tensor_tensor(out=yb[0:n_, :], in0=yb[0:n_, :], in1=x_[0:n_, :], op=ALU.add), reads=[dyb, dx_], writes=[dyb])
                S.dma("sp", T["xo"][o_:o_ + n_, :], yb[0:n_, :], reads=[dyb])
            S.barrier()
        print("build_B: ninst", S.ninst, "nsem", S.nsem, {k: v for k, v in S.cnt.items()})
    return nc


def _bf16(a):
    import ml_dtypes
    return np.ascontiguousarray(np.asarray(a).astype(ml_dtypes.bfloat16))


def _tok_maps(q):
    ctx_idx = np.arange(64 * q - 15, 64 * q + 79)
    lat_idx = np.arange(1024 * q - 15, 1024 * q + 1039)
    tok = np.full(NEXT, -1, np.int64)
    v = (ctx_idx >= 0) & (ctx_idx < NCTX)
    tok[0:94][v] = ctx_idx[v]
    v2 = (lat_idx >= 0) & (lat_idx < 4096)
    tok[94:94 + 1054][v2] = NCTX + lat_idx[v2]
    own = np.concatenate([np.arange(64 * q, 64 * q + 64), NCTX + np.arange(1024 * q, 1024 * q + 1024)])
    return tok, own


def shared_B(inp, li):
    w_in = inp["w_in"][li]
    cvp = np.concatenate([inp["conv_w"][li].T, inp["conv_b"][li][:, None], inp["conv_ln_g"][li][:, None], inp["conv_ln_b"][li][:, None]], axis=1)
    return {
        "w_cv": np.ascontiguousarray(w_in[:, 0:3072]),
        "cvp": np.ascontiguousarray(cvp.reshape(8, 128, 34).transpose(1, 0, 2).astype(np.float32)),
        "Wl": np.ascontiguousarray(w_in[:, OFF["merge"]:OFF["merge"] + 8192]),
        "Wb": np.ascontiguousarray(inp["w_branch"][li]),
        "Wout": np.ascontiguousarray(inp["w_out"][li]),
        "bg": np.ascontiguousarray(inp["b_gate"][li].reshape(4, 16, 128).transpose(2, 0, 1)),
        "wmodg": np.ascontiguousarray(inp["w_mod"][li][:, 4096:6144]),
        "bmodg": np.ascontiguousarray(inp["b_mod"][li][None, 4096:6144]),
        "gpost": np.ascontiguousarray(inp["norm_post_g"][li][None]),
    }


def inputs_B(inp, shared, b, q, xfull, hT_b, br_b):
    tok, own = _tok_maps(q)
    valid = tok >= 0
    hTx = np.zeros((16, 128, NEXT), hT_b.dtype)
    hTx[:, :, valid] = hT_b[:, :, tok[valid]]
    brT = br_b[own].reshape(NOWN, 3, 8, 128).transpose(3, 1, 2, 0)
    m = dict(shared)
    m.update({
        "hTx": np.ascontiguousarray(hTx.transpose(1, 0, 2)), "mask": valid.astype(np.float32)[None],
        "brT": np.ascontiguousarray(brT), "x_own": np.ascontiguousarray(xfull[b][own]),
        "cT": _cT(inp["c_ctx"], inp["c"][b]),
    })
    return m


def _gather_A(resA):
    out = []
    for b in range(2):
        hT_b = np.asarray(resA[4 * b]["hT"])
        parts = []
        for j in range(3):
            cols = []
            for q in range(4):
                r = resA[4 * b + q]
                if j == 0:
                    cols.append(np.asarray(r["br_rwkv"]))
                else:
                    cols.append(np.asarray(r["br_att"])[:, (j - 1) * 256:j * 256])
            parts.append(np.concatenate(cols, axis=1))
        out.append((hT_b, np.stack(parts, axis=1)))
    return out


_NC = {}


def kernel(**inputs):
    inp = {k: np.asarray(v) for k, v in inputs.items()}
    if "A" not in _NC:
        _NC["A"] = build_A()
        _NC["B"] = build_B()
    rope = _rope_table()
    xfull = np.concatenate([inp["ctx"], inp["x"]], axis=1).astype(np.float32)
    cores = list(range(8))
    for li in range(4):
        mapsA = [inputs_A(inp, li, c // 4, c % 4, xfull, rope) for c in cores]
        resA = run_bass_kernel_spmd(_NC["A"], mapsA, core_ids=cores).results
        del mapsA
        gA = _gather_A(resA)
        del resA
        sh = shared_B(inp, li)
        mapsB = [inputs_B(inp, sh, c // 4, c % 4, xfull, gA[c // 4][0], gA[c // 4][1]) for c in cores]
        resB = run_bass_kernel_spmd(_NC["B"], mapsB, core_ids=cores).results
        del mapsB
        xnew = np.empty_like(xfull)
        for c in cores:
            _, own = _tok_maps(c % 4)
            xnew[c // 4][own] = np.asarray(resB[c]["xo"])
        xfull = xnew
    return np.ascontiguousarray(xfull[:, NCTX:, :]).astype(np.float32)
```

```python
import contextlib
import math
import numpy as np
import concourse.bass as bass
import concourse.mybir as mybir
from concourse.bass_utils import run_bass_kernel_spmd

F32 = mybir.dt.float32
BF16 = mybir.dt.bfloat16
AF = mybir.ActivationFunctionType
ALU = mybir.AluOpType
AX = mybir.AxisListType

SEM_LIM = 30000
D = 2048
NT = 4352
NTILE = 34
NBLK = 17
NCTX = 256
W = 1024
EPS = 1e-6


class Dep:
    __slots__ = ("name", "w", "rs", "wsem", "wcnt", "rsem", "rcnt", "excl")

    def __init__(self, name="", excl=False):
        self.name = name
        self.excl = excl
        self.w = None
        self.rs = []
        self.wsem = None
        self.wcnt = 0
        self.rsem = None
        self.rcnt = 0


class Sched:
    def __init__(self, nc, stack):
        self.nc = nc
        self.stack = stack
        self.eng = {"pe": nc.tensor, "dve": nc.vector, "act": nc.scalar, "pool": nc.gpsimd, "sp": nc.sync}
        self.sems = {k: [] for k in self.eng}
        self.cnt = {k: 0 for k in self.eng}
        self.known = {k: {} for k in self.eng}
        self.nsem = 0
        self.ninst = 0
        self.deps = []
        self.dticks = []

    def dep(self, name="", excl=False):
        d = Dep(name, excl)
        self.deps.append(d)
        return d

    def pdep(self, name=""):
        return self.dep(name, excl=True)

    def new_sem(self, name):
        self.nsem += 1
        return self.stack.enter_context(self.nc.semaphore(f"{name}_{self.nsem}"))

    def sb(self, name, shape, dt, stack=None):
        return (stack or self.stack).enter_context(self.nc.sbuf_tensor(name, list(shape), dt))

    def ps(self, name, shape, dt=F32, stack=None):
        return (stack or self.stack).enter_context(self.nc.psum_tensor(name, list(shape), dt))

    def _wait(self, e, tick):
        sem, val, src = tick
        kn = self.known[e]
        if kn.get(id(sem), 0) >= val:
            return
        self.eng[e].wait_ge(sem, val)
        kn[id(sem)] = val
        if src in self.sems:
            for s in self.sems[src]:
                if s is sem:
                    break
                kn[id(s)] = SEM_LIM

    def _deps(self, e, reads, writes, dma=False):
        for d in reads:
            if d.w is not None:
                self._wait(e, d.w)
            if d.excl:
                for r in d.rs:
                    if r[2] != e:
                        self._wait(e, r)
        for d in writes:
            if d.w is not None and (dma or d.w[2] != e or e != "pe"):
                self._wait(e, d.w)
            for r in d.rs:
                self._wait(e, r)

    def op(self, e, fn, reads=(), writes=()):
        self._deps(e, reads, writes)
        c = self.cnt[e]
        if c % SEM_LIM == 0:
            self.sems[e].append(self.new_sem(e))
        sem = self.sems[e][-1]
        val = c % SEM_LIM + 1
        self.cnt[e] = c + 1
        ins = fn(self.eng[e])
        ins.then_inc(sem, 1)
        self.ninst += 1
        tick = (sem, val, e)
        for d in reads:
            d.rs.append(tick)
        for d in writes:
            d.w = tick
            d.rs = []
        return tick

    def dma(self, e, out, in_, reads=(), writes=(), **kw):
        self._deps(e, reads, writes, dma=True)
        if writes:
            d0 = writes[0]
            if d0.wsem is None:
                d0.wsem = self.new_sem("dw")
            d0.wcnt += 16
            tick = (d0.wsem, d0.wcnt, "dma")
        else:
            d0 = reads[0]
            if d0.rsem is None:
                d0.rsem = self.new_sem("dr")
            d0.rcnt += 16
            tick = (d0.rsem, d0.rcnt, "dma")
        ins = self.eng[e].dma_start(out=out, in_=in_, **kw)
        ins.then_inc(tick[0], 16)
        self.ninst += 1
        self.dticks.append(tick)
        for d in reads:
            d.rs.append(tick)
        for d in writes:
            d.w = tick
            d.rs = []
        return tick

    def wait_all(self, e, deps):
        for d in deps:
            if d.w is not None:
                self._wait(e, d.w)
            for r in d.rs:
                self._wait(e, r)

    def drain_dma(self, e, keep=0):
        n = len(self.dticks) - keep
        for t in self.dticks[:max(n, 0)]:
            self._wait(e, t)
        self.dticks = self.dticks[max(n, 0):]

    def barrier(self):
        for e in self.eng:
            self.wait_all(e, self.deps)
        for d in self.deps:
            d.rs = d.rs[-8:]


def _mod_prologue(S, nc, cT, wmod, bmod, ncols, modb, dmodb):
    with contextlib.ExitStack() as st:
        ct = S.sb("m_ct", [128, 16, 2], F32, st); dct = S.dep()
        cs = S.sb("m_cs", [128, 16, 2], F32, st); dcs = S.dep()
        rep = S.sb("m_rep", [128, 2, 16, 128], F32, st); drep = S.dep()
        ones1 = S.sb("m_ones", [1, 128], F32, st); dones = S.dep()
        bm = S.sb("m_bm", [1, ncols], F32, st); dbm = S.dep()
        wt = [S.sb(f"m_wt{i}", [128, 16, 512], F32, st) for i in range(2)]
        dwt = [S.dep() for _ in range(2)]
        pm = [S.ps(f"m_pm{i}", [128, 512], F32, st) for i in range(2)]
        dpm = [S.pdep() for _ in range(2)]
        S.dma("sp", ct[:], cT[:, :, :], writes=[dct])
        S.dma("sp", bm[:], bmod[:, :], writes=[dbm])
        S.op("dve", lambda e: e.memset(ones1[:], 1.0), writes=[dones])
        S.op("act", lambda e: e.activation(out=cs[:], in_=ct[:], func=AF.Silu), reads=[dct], writes=[dcs])
        for wh in range(2):
            for kc in range(16):
                S.op("dve", lambda e: e.tensor_copy(out=rep[:, wh, kc, :], in_=cs[:, kc, wh:wh + 1].to_broadcast([128, 128])),
                     reads=[dcs], writes=[drep])
        ng = ncols // 512
        wv = wmod.rearrange("(kc p) n -> p kc n", p=128)
        for g in range(ng):
            b = g % 2
            for h4 in range(4):
                S.dma("sp", wt[b][:, h4 * 4:(h4 + 1) * 4, :], wv[:, h4 * 4:(h4 + 1) * 4, g * 512:(g + 1) * 512], writes=[dwt[b]])
            for wh in range(2):
                for kc in range(16):
                    S.op("pe", lambda e: e.matmul(pm[wh][:], lhsT=rep[:, wh, kc, :], rhs=wt[b][:, kc, :], start=(kc == 0), stop=False),
                         reads=[drep, dwt[b]], writes=[dpm[wh]])
                S.op("pe", lambda e: e.matmul(pm[wh][:], lhsT=ones1[:], rhs=bm[:, g * 512:(g + 1) * 512], start=False, stop=True),
                     reads=[dones, dbm], writes=[dpm[wh]])
                S.op("act", lambda e: e.copy(out=modb[wh][:, g * 512:(g + 1) * 512], in_=pm[wh][:]), reads=[dpm[wh]], writes=[dmodb])
        S.barrier()


def _make_ident(S, st, n, dt, name):
    f = S.sb(name + "_f", [n, n], F32, st)
    df = S.dep()
    S.op("pool", lambda e: e.memset(f[:], 1.0), writes=[df])
    S.op("pool", lambda e: e.affine_select(out=f[:], in_=f[:], pattern=[[-1, n]], compare_op=ALU.is_equal, fill=0.0,
                                           base=0, channel_multiplier=1), reads=[df], writes=[df])
    if dt == F32:
        return f, df
    b = S.sb(name + "_b", [n, n], dt, st)
    db = S.dep()
    S.op("dve", lambda e: e.tensor_copy(out=b[:], in_=f[:]), reads=[df], writes=[db])
    return b, db


def _rope(S, src, dst, cos, sin, G, P, t1, t2, reads, writes, dtmp):
    sv = src.rearrange("p g (i t) -> p g i t", t=2)
    dv = dst.rearrange("p g (i t) -> p g i t", t=2)
    cb = cos.unsqueeze(1).to_broadcast([128, G, P])
    sbb = sin.unsqueeze(1).to_broadcast([128, G, P])
    S.op("dve", lambda e: e.tensor_tensor(out=t1, in0=sv[:, :, :, 0], in1=cb, op=ALU.mult), reads=reads, writes=[dtmp])
    S.op("dve", lambda e: e.tensor_tensor(out=t2, in0=sv[:, :, :, 1], in1=sbb, op=ALU.mult), reads=reads, writes=[dtmp])
    S.op("dve", lambda e: e.tensor_tensor(out=dv[:, :, :, 0], in0=t1, in1=t2, op=ALU.subtract), reads=[dtmp], writes=writes)
    S.op("dve", lambda e: e.tensor_tensor(out=t1, in0=sv[:, :, :, 0], in1=sbb, op=ALU.mult), reads=reads + [dtmp], writes=[dtmp])
    S.op("dve", lambda e: e.tensor_tensor(out=t2, in0=sv[:, :, :, 1], in1=cb, op=ALU.mult), reads=reads + [dtmp], writes=[dtmp])
    S.op("dve", lambda e: e.tensor_tensor(out=dv[:, :, :, 1], in0=t1, in1=t2, op=ALU.add), reads=[dtmp], writes=writes)


def _rstd(S, ss, n, scale, eps, reads, dss):
    S.op("dve", lambda e: e.tensor_scalar(out=ss, in0=ss, scalar1=scale, scalar2=eps, op0=ALU.mult, op1=ALU.add),
         reads=reads + [dss], writes=[dss])
    S.op("act", lambda e: e.activation(out=ss, in_=ss, func=AF.Sqrt), reads=[dss], writes=[dss])
    S.op("dve", lambda e: e.reciprocal(out=ss, in_=ss), reads=[dss], writes=[dss])


def _phase_A1(S, nc, T):
    with contextlib.ExitStack() as st:
        identb, did = _make_ident(S, st, 128, BF16, "a1id")
        modb = [S.sb(f"modb{i}", [128, 4096], F32, st) for i in range(2)]
        dmodb = S.dep()
        _mod_prologue(S, nc, T["cT"], T["wmod"], T["bmod"], 4096, modb, dmodb)
        with contextlib.ExitStack() as st2:
            gb = S.sb("gb", [128, 2048], F32, st2); dgb = S.dep()
            S.dma("sp", gb[:], T["gpre"][0:1, :].to_broadcast([128, 2048]), writes=[dgb])
            for wh in range(2):
                S.op("dve", lambda e: e.scalar_tensor_tensor(out=modb[wh][:, 2048:4096], in0=modb[wh][:, 2048:4096], scalar=1.0,
                                                             in1=gb[:], op0=ALU.add, op1=ALU.mult), reads=[dmodb, dgb], writes=[dmodb])
            S.barrier()
        gqn = S.sb("gqn_sb", [128, 2, 128], F32, st); dgqn = S.dep()
        S.dma("sp", gqn[:].rearrange("p a b -> p (a b)"), T["gqn"][0:1, :].to_broadcast([128, 256]), writes=[dgqn])
        wfm = S.sb("wfm", [128, 16, 704], BF16, st); dwfm = S.dep()
        wtm = S.sb("wtm", [128, 16, 2304], BF16, st); dwtm = S.dep()
        wfv = T["w_fm"].rearrange("(kc p) n -> p kc n", p=128)
        wtv = T["w_tm"].rearrange("(kc p) n -> p kc n", p=128)
        for k4 in range(8):
            S.dma("pool", wfm[:, k4 * 2:(k4 + 1) * 2, :], wfv[:, k4 * 2:(k4 + 1) * 2, :], writes=[dwfm])
        for k4 in range(16):
            S.dma("pool", wtm[:, k4:(k4 + 1), :], wtv[:, k4:(k4 + 1), :], writes=[dwtm])
        xb = [S.sb(f"xb{i}", [128, 2048], F32, st) for i in range(2)]; dxb = [S.dep() for _ in range(2)]
        rpb = [S.sb(f"rp{i}", [128, 192], F32, st) for i in range(2)]; drp = [S.dep() for _ in range(2)]
        hb = S.sb("hb", [128, 2048], BF16, st); dhb = S.dep()
        ss = S.sb("ss", [128, 4], F32, st); dss = S.dep()
        hTb = [S.sb(f"hTb{i}", [128, 16, 256], BF16, st) for i in range(2)]; dhT = [S.dep() for _ in range(2)]
        rkst = S.sb("rkst", [64, 8, 256], F32, st); drkst = S.dep()
        wast = S.sb("wast", [96, 2, 256], F32, st); dwast = S.dep()
        stf = S.sb("stf", [128, 2304], F32, st); dstf = [S.dep() for _ in range(5)]
        gst = S.sb("gst", [128, 768], BF16, st); dgst = S.dep()
        qkd = S.sb("qkd", [128, 8, 64], BF16, st); dqkd = S.dep()
        qkTd = S.sb("qkTd", [64, 8, 128], BF16, st); dqkTd = S.dep()
        qkg = S.sb("qkg", [128, 3, 128], BF16, st); dqkg = S.dep()
        qkgn = S.sb("qkgn", [128, 3, 128], F32, st); dqkgn = S.dep()
        qkTg = S.sb("qkTg", [128, 3, 128], BF16, st); dqkTg = S.dep()
        vdst = S.sb("vdst", [128, 2, 129], BF16, st); dvd = S.dep()
        vgst = S.sb("vgst", [128, 129], BF16, st); dvg = S.dep()
        rt1 = S.sb("rt1", [128, 8, 32], F32, st); rt2 = S.sb("rt2", [128, 8, 32], F32, st); drt = S.dep()
        sqg = S.sb("sqg", [128, 3, 128], F32, st); dsqg = S.dep()
        ssg = S.sb("ssg", [128, 4], F32, st); dssg = S.dep()
        pf = S.ps("pf", [64, 8, 256], F32, st); dpf = S.pdep()
        pw = S.ps("pw", [96, 2, 256], F32, st); dpw = S.pdep()
        pg = [S.ps(f"pg{i}", [128, 512], F32, st) for i in range(2)]; dpg = [S.pdep() for _ in range(2)]
        ptb = S.ps("ptb", [128, 8, 128], BF16, st); dptb = S.pdep()
        S.op("dve", lambda e: e.memset(vdst[:], 1.0), writes=[dvd])
        S.op("dve", lambda e: e.memset(vgst[:], 1.0), writes=[dvg])

        s_rk = T["s_rk"].rearrange("g c t -> c g t")
        s_wa = T["s_wa"].rearrange("j c t -> c j t")
        s_dqk = T["s_dqkT"].rearrange("g c t -> c g t")
        s_gqk = T["s_gqkT"].rearrange("g c t -> c g t")
        hTo = T["hT"].rearrange("kc p t -> p kc t")
        import os
        nblk = int(os.environ.get('A1_BLOCKS', NBLK))

        def prep(blk):
            wh = 0 if blk == 0 else 1
            S.drain_dma("sp", keep=12)
            hT = hTb[blk % 2]; dh = dhT[blk % 2]
            for ti in range(2):
                t = 2 * blk + ti
                tok0 = t * 128
                xt = xb[t % 2]; dx = dxb[t % 2]; rp = rpb[t % 2]; dr = drp[t % 2]
                S.dma("sp", xt[:], T["xf"][tok0:tok0 + 128, :], writes=[dx])
                S.op("act", lambda e: e.activation(out=hb[:], in_=xt[:], func=AF.Square, accum_out=ss[:, 0:1]),
                     reads=[dx], writes=[dhb, dss])
                _rstd(S, ss[:, 0:1], 1, 1.0 / D, EPS, [], dss)
                S.op("dve", lambda e: e.scalar_tensor_tensor(out=xt[:], in0=xt[:], scalar=ss[:, 0:1], in1=modb[wh][:, 2048:4096],
                                                             op0=ALU.mult, op1=ALU.mult), reads=[dx, dss, dmodb], writes=[dx])
                S.op("dve", lambda e: e.tensor_tensor(out=hb[:], in0=xt[:], in1=modb[wh][:, 0:2048], op=ALU.add),
                     reads=[dx, dmodb], writes=[dhb])
                for half in range(2):
                    for j in range(8):
                        kc = half * 8 + j
                        S.op("pe", lambda e: e.transpose(out=ptb[:, j, :], in_=hb[:, kc * 128:(kc + 1) * 128], identity=identb[:]),
                             reads=[dhb, did], writes=[dptb])
                    S.op("act", lambda e: e.copy(out=hT[:, half * 8:(half + 1) * 8, ti * 128:(ti + 1) * 128], in_=ptb[:]),
                         reads=[dptb], writes=[dh])
            S.dma("sp", hTo[:, :, blk * 256:(blk + 1) * 256], hT[:], reads=[dh])

        def mm(blk):
            hT = hTb[blk % 2]; dh = dhT[blk % 2]
            for g in range(8):
                for kc in range(16):
                    S.op("pe", lambda e: e.matmul(pf[:, g, :], lhsT=wfm[:, kc, g * 64:(g + 1) * 64], rhs=hT[:, kc, :],
                                                  start=(kc == 0), stop=(kc == 15)), reads=[dwfm, dh], writes=[dpf])
            S.op("act", lambda e: e.copy(out=rkst[:], in_=pf[:]), reads=[dpf], writes=[drkst])
            S.dma("sp", s_rk[:, :, blk * 256:(blk + 1) * 256], rkst[:], reads=[drkst])
            for j in range(2):
                for kc in range(16):
                    S.op("pe", lambda e: e.matmul(pw[:, j, :], lhsT=wfm[:, kc, 512 + j * 96:512 + (j + 1) * 96], rhs=hT[:, kc, :],
                                                  start=(kc == 0), stop=(kc == 15)), reads=[dwfm, dh], writes=[dpw])
            S.op("act", lambda e: e.activation(out=wast[:, 0, :], in_=pw[:, 0, :], func=AF.Tanh), reads=[dpw], writes=[dwast])
            S.op("act", lambda e: e.copy(out=wast[:, 1, :], in_=pw[:, 1, :]), reads=[dpw], writes=[dwast])
            S.dma("sp", s_wa[:, :, blk * 256:(blk + 1) * 256], wast[:], reads=[dwast])
            for ti in range(2):
                t = 2 * blk + ti
                tok0 = t * 128
                rp = rpb[t % 2]; dr = drp[t % 2]
                S.dma("sp", rp[:], T["rope"][tok0:tok0 + 128, :], writes=[dr])
                for gi in range(5):
                    ncol = 512 if gi < 4 else 256
                    p = pg[gi % 2]; dp = dpg[gi % 2]
                    for kc in range(16):
                        S.op("pe", lambda e: e.matmul(p[:, 0:ncol], lhsT=hT[:, kc, ti * 128:(ti + 1) * 128],
                                                      rhs=wtm[:, kc, gi * 512:gi * 512 + ncol], start=(kc == 0), stop=(kc == 15)),
                             reads=[dwtm, dh], writes=[dp])
                    S.op("act", lambda e: e.copy(out=stf[:, gi * 512:gi * 512 + ncol], in_=p[:, 0:ncol]), reads=[dp], writes=[dstf[gi]])
                S.dma("sp", T["s_v"][tok0:tok0 + 128, :], stf[:, 0:256], reads=[dstf[0]])
                S.op("act", lambda e: e.activation(out=gst[:, 0:256], in_=stf[:, 256:512], func=AF.Silu), reads=[dstf[0]], writes=[dgst])
                _rope(S, stf[:, 512:1024].rearrange("p (g d) -> p g d", g=8), qkd[:], rp[:, 0:32], rp[:, 32:64], 8, 32,
                      rt1[:], rt2[:], [dstf[1], dr], [dqkd], drt)
                for g in range(8):
                    S.op("pe", lambda e: e.transpose(out=ptb[0:64, g, :], in_=qkd[:, g, :], identity=identb[:]),
                         reads=[dqkd, did], writes=[dptb])
                S.op("act", lambda e: e.copy(out=qkTd[:], in_=ptb[0:64, :, :]), reads=[dptb], writes=[dqkTd])
                S.dma("sp", s_dqk[:, :, tok0:tok0 + 128], qkTd[:], reads=[dqkTd])
                S.op("act", lambda e: e.copy(out=vdst[:, :, 0:128], in_=stf[:, 1024:1280].rearrange("p (h d) -> p h d", h=2)),
                     reads=[dstf[2]], writes=[dvd])
                S.dma("sp", T["s_dv"][tok0:tok0 + 128, :, :], vdst[:], reads=[dvd])
                S.op("act", lambda e: e.activation(out=gst[:, 256:512], in_=stf[:, 1280:1536], func=AF.Silu), reads=[dstf[2]], writes=[dgst])
                src3 = stf[:, 1536:1920].rearrange("p (g d) -> p g d", g=3)
                S.op("dve", lambda e: e.tensor_tensor(out=sqg[:], in0=src3, in1=src3, op=ALU.mult), reads=[dstf[3]], writes=[dsqg])
                S.op("dve", lambda e: e.tensor_reduce(out=ssg[:, 0:3], in_=sqg[:], axis=AX.X, op=ALU.add), reads=[dsqg], writes=[dssg])
                _rstd(S, ssg[:, 0:3], 3, 1.0 / 128, EPS, [], dssg)
                for i in range(3):
                    S.op("dve", lambda e: e.scalar_tensor_tensor(out=qkgn[:, i, :], in0=src3[:, i, :], scalar=ssg[:, i:i + 1],
                                                                 in1=gqn[:, (0 if i < 2 else 1), :], op0=ALU.mult, op1=ALU.mult),
                         reads=[dstf[3], dssg, dgqn], writes=[dqkgn])
                _rope(S, qkgn[:], qkg[:], rp[:, 64:128], rp[:, 128:192], 3, 64,
                      rt1[:].rearrange("p a b -> p (a b)")[:, 0:192].rearrange("p (a b) -> p a b", a=3),
                      rt2[:].rearrange("p a b -> p (a b)")[:, 0:192].rearrange("p (a b) -> p a b", a=3),
                      [dqkgn, dr], [dqkg], drt)
                for g in range(3):
                    S.op("pe", lambda e: e.transpose(out=ptb[:, g, :], in_=qkg[:, g, :], identity=identb[:]),
                         reads=[dqkg, did], writes=[dptb])
                S.op("act", lambda e: e.copy(out=qkTg[:], in_=ptb[:, 0:3, :]), reads=[dptb], writes=[dqkTg])
                S.dma("sp", s_gqk[:, :, tok0:tok0 + 128], qkTg[:], reads=[dqkTg])
                S.op("act", lambda e: e.copy(out=vgst[:, 0:128], in_=stf[:, 1920:2048]), reads=[dstf[3]], writes=[dvg])
                S.dma("sp", T["s_gv"][tok0:tok0 + 128, :], vgst[:], reads=[dvg])
                S.op("act", lambda e: e.activation(out=gst[:, 512:768], in_=stf[:, 2048:2304], func=AF.Silu), reads=[dstf[4]], writes=[dgst])
                S.dma("sp", T["s_gate"][tok0:tok0 + 128, :], gst[:], reads=[dgst])

        if nblk > 0:
            prep(0)
        for blk in range(nblk):
            if blk + 1 < nblk:
                prep(blk + 1)
            mm(blk)
        S.barrier()


def _phase_A2(S, nc, T):
    with contextlib.ExitStack() as st:
        KTd = S.sb("KTd", [64, 4, NT], BF16, st); dKTd = S.dep()
        Vd = S.sb("Vd", [128, NTILE, 2, 129], BF16, st); dVd = S.dep()
        KTg = S.sb("KTg", [128, NT], BF16, st); dKTg = S.dep()
        Vg = S.sb("Vg", [128, NTILE, 129], BF16, st); dVg = S.dep()
        for g in range(4):
            S.dma("sp", KTd[:, g, :], T["s_dqkT"][4 + g, :, :], writes=[dKTd])
        S.dma("sp", KTg[:], T["s_gqkT"][2, :, :], writes=[dKTg])
        for c in range(2):
            S.dma("sp", Vd[:, c * 17:(c + 1) * 17, :, :], T["s_dv"].rearrange("(t p) h d -> p t h d", p=128)[:, c * 17:(c + 1) * 17, :, :], writes=[dVd])
        S.dma("sp", Vg[:], T["s_gv"].rearrange("(t p) d -> p t d", p=128), writes=[dVg])
        lamp = S.sb("lamp_sb", [128, 4, 64], F32, st); dlam = S.dep()
        lam = S.sb("lam", [128, 8], F32, st)
        S.dma("sp", lamp[:].rearrange("p a b -> p (a b)"), T["lamp"][0:1, :].to_broadcast([128, 256]), writes=[dlam])
        S.dma("sp", lam[:, 4:5], T["lami"][0:1, 0:1].to_broadcast([128, 1]), writes=[dlam])
        prod = S.sb("lprod", [128, 2, 64], F32, st)
        S.op("dve", lambda e: e.tensor_tensor(out=prod[:, 0, :], in0=lamp[:, 0, :], in1=lamp[:, 1, :], op=ALU.mult), reads=[dlam], writes=[dlam])
        S.op("dve", lambda e: e.tensor_tensor(out=prod[:, 1, :], in0=lamp[:, 2, :], in1=lamp[:, 3, :], op=ALU.mult), reads=[dlam], writes=[dlam])
        S.op("dve", lambda e: e.tensor_reduce(out=lam[:, 0:2], in_=prod[:], axis=AX.X, op=ALU.add), reads=[dlam], writes=[dlam])
        S.op("act", lambda e: e.activation(out=lam[:, 2:4], in_=lam[:, 0:2], func=AF.Exp), reads=[dlam], writes=[dlam])
        S.op("dve", lambda e: e.tensor_tensor(out=lam[:, 5:6], in0=lam[:, 2:3], in1=lam[:, 3:4], op=ALU.subtract), reads=[dlam], writes=[dlam])
        S.op("dve", lambda e: e.tensor_tensor(out=lam[:, 6:7], in0=lam[:, 5:6], in1=lam[:, 4:5], op=ALU.add), reads=[dlam], writes=[dlam])
        gsub = S.sb("gsub", [128, 128], F32, st); dgsub = S.dep()
        S.dma("sp", gsub[:], T["subg"][0:1, :].to_broadcast([128, 128]), writes=[dgsub])
        S.op("dve", lambda e: e.tensor_scalar(out=lam[:, 7:8], in0=lam[:, 4:5], scalar1=-1.0, scalar2=1.0, op0=ALU.mult, op1=ALU.add),
             reads=[dlam], writes=[dlam])
        S.op("dve", lambda e: e.tensor_scalar(out=gsub[:], in0=gsub[:], scalar1=lam[:, 7:8], scalar2=None, op0=ALU.mult),
             reads=[dlam, dgsub], writes=[dgsub])

        QTd = [S.sb(f"QTd{i}", [64, 4, 256], BF16, st) for i in range(2)]; dQd = [S.dep() for _ in range(2)]
        QTg = [S.sb(f"QTg{i}", [128, 2, 256], BF16, st) for i in range(2)]; dQg = [S.dep() for _ in range(2)]
        gt = [S.sb(f"gt{i}", [128, 2, 768], BF16, st) for i in range(2)]; dgt = [S.dep() for _ in range(2)]
        PT = [S.sb(f"PT{i}", [128, 2, 256], BF16, st) for i in range(2)]; dPT = [S.dep() for _ in range(2)]
        pss = [S.ps(f"pss{i}", [128, 2, 256], F32, st) for i in range(2)]; dpss = [S.pdep() for _ in range(2)]
        acc = [[S.ps(f"acc{j}{q}", [128, 512], F32, st) for q in range(2)] for j in range(2)]
        dacc = [[S.pdep() for q in range(2)] for j in range(2)]
        rs = S.sb("rs", [128, 8], F32, st); drs = S.dep()
        o0 = S.sb("o0", [128, 128], F32, st); do0 = S.dep()
        dd = S.sb("dd", [128, 128], F32, st); ddd = S.dep()
        junk = S.sb("junk", [128, 128], F32, st); djunk = S.dep()
        brs = [S.sb(f"brs{i}", [128, 512], BF16, st) for i in range(2)]; dbrs = [S.dep() for _ in range(2)]
        s_dqk = T["s_dqkT"].rearrange("g c t -> c g t")
        s_gqk = T["s_gqkT"].rearrange("g c t -> c g t")
        steps = []
        for qb in range(NBLK):
            kts = list(range(2)) if qb == 0 else list(range(NTILE))
            for u in range(3):
                for ki, kt in enumerate(kts):
                    steps.append((qb, u, ki, kt, len(kts)))

        def emit_S(i):
            qb, u, ki, kt, nk = steps[i]
            b2 = qb % 2
            q0 = qb * 256
            if u == 0 and ki == 0:
                S.drain_dma("sp", keep=8)
                S.dma("sp", QTd[b2][:], s_dqk[:, 0:4, q0:q0 + 256], writes=[dQd[b2]])
                S.dma("sp", QTg[b2][:], s_gqk[:, 0:2, q0:q0 + 256], writes=[dQg[b2]])
                S.dma("sp", gt[b2][:], T["s_gate"].rearrange("(t p) n -> p t n", p=128)[:, 2 * qb:2 * qb + 2, :], writes=[dgt[b2]])
            ps_ = pss[i % 2]; dps_ = dpss[i % 2]
            for j in range(2):
                if u < 2:
                    S.op("pe", lambda e: e.matmul(ps_[:, j, :], lhsT=KTd[:, u * 2 + j, kt * 128:(kt + 1) * 128], rhs=QTd[b2][:, u * 2 + j, :],
                                                  start=True, stop=True), reads=[dKTd, dQd[b2]], writes=[dps_])
                else:
                    S.op("pe", lambda e: e.matmul(ps_[:, j, :], lhsT=KTg[:, kt * 128:(kt + 1) * 128], rhs=QTg[b2][:, j, :],
                                                  start=True, stop=True), reads=[dKTg, dQg[b2]], writes=[dps_])

        def emit_EPV(i):
            qb, u, ki, kt, nk = steps[i]
            ps_ = pss[i % 2]; dps_ = dpss[i % 2]; pt_ = PT[i % 2]; dpt_ = dPT[i % 2]
            sc = (64 ** -0.5) if u < 2 else (128 ** -0.5)
            S.op("act", lambda e: e.activation(out=pt_[:], in_=ps_[:], func=AF.Exp, scale=sc), reads=[dps_], writes=[dpt_])
            for j in range(2):
                for q in range(2):
                    rhs = Vd[:, kt, u, :] if u < 2 else Vg[:, kt, :]
                    S.op("pe", lambda e: e.matmul(acc[j][q][:, 0:129], lhsT=pt_[:, j, q * 128:(q + 1) * 128], rhs=rhs,
                                                  start=(ki == 0), stop=(ki == nk - 1)),
                         reads=[dpt_, dVd if u < 2 else dVg], writes=[dacc[j][q]])

        def finalize(qb, u):
            b2 = qb % 2
            q0 = qb * 256
            for q in range(2):
                bs = brs[q]; dbs = dbrs[q]
                if u < 2:
                    S.op("dve", lambda e: e.reciprocal(out=rs[:, 0:1], in_=acc[0][q][:, 128:129]), reads=[dacc[0][q]], writes=[drs])
                    S.op("dve", lambda e: e.reciprocal(out=rs[:, 1:2], in_=acc[1][q][:, 128:129]), reads=[dacc[1][q]], writes=[drs])
                    S.op("dve", lambda e: e.scalar_tensor_tensor(out=rs[:, 2:3], in0=rs[:, 1:2], scalar=-1.0, in1=lam[:, 6:7],
                                                                 op0=ALU.mult, op1=ALU.mult), reads=[drs, dlam], writes=[drs])
                    S.op("dve", lambda e: e.tensor_scalar(out=o0[:], in0=acc[0][q][:, 0:128], scalar1=rs[:, 0:1], scalar2=None, op0=ALU.mult),
                         reads=[dacc[0][q], drs], writes=[do0])
                    S.op("dve", lambda e: e.scalar_tensor_tensor(out=dd[:], in0=acc[1][q][:, 0:128], scalar=rs[:, 2:3], in1=o0[:],
                                                                 op0=ALU.mult, op1=ALU.add), reads=[dacc[1][q], drs, do0], writes=[ddd])
                    S.op("act", lambda e: e.activation(out=junk[:], in_=dd[:], func=AF.Square, accum_out=rs[:, 3:4]),
                         reads=[ddd], writes=[djunk, drs])
                    _rstd(S, rs[:, 3:4], 1, 1.0 / 128, EPS, [], drs)
                    S.op("dve", lambda e: e.scalar_tensor_tensor(out=o0[:], in0=dd[:], scalar=rs[:, 3:4], in1=gsub[:],
                                                                 op0=ALU.mult, op1=ALU.mult), reads=[ddd, drs, dgsub], writes=[do0])
                    S.op("dve", lambda e: e.tensor_tensor(out=bs[:, u * 128:(u + 1) * 128], in0=o0[:],
                                                          in1=gt[b2][:, q, 256 + u * 128:256 + (u + 1) * 128], op=ALU.mult),
                         reads=[do0, dgt[b2]], writes=[dbs])
                else:
                    for j in range(2):
                        S.op("dve", lambda e: e.reciprocal(out=rs[:, 4 + j:5 + j], in_=acc[j][q][:, 128:129]), reads=[dacc[j][q]], writes=[drs])
                        S.op("dve", lambda e: e.scalar_tensor_tensor(out=bs[:, 256 + j * 128:256 + (j + 1) * 128], in0=acc[j][q][:, 0:128],
                                                                     scalar=rs[:, 4 + j:5 + j], in1=gt[b2][:, q, 512 + j * 128:512 + (j + 1) * 128],
                                                                     op0=ALU.mult, op1=ALU.mult), reads=[dacc[j][q], drs, dgt[b2]], writes=[dbs])
                    S.dma("sp", T["br_att"][q0 + q * 128:q0 + (q + 1) * 128, :], bs[:], reads=[dbs])

        emit_S(0)
        for i in range(len(steps)):
            if i + 1 < len(steps):
                emit_S(i + 1)
            emit_EPV(i)
            qb, u, ki, kt, nk = steps[i]
            if ki == nk - 1:
                finalize(qb, u)
        S.barrier()


def _phase_A3(S, nc, T):
    NCH = NT // 64
    with contextlib.ExitStack() as st:
        identf, didf = _make_ident(S, st, 64, F32, "a3id")
        ones64 = S.sb("ones64", [64, 64], F32, st); dconst = S.dep()
        S.op("dve", lambda e: e.memset(ones64[:], 1.0), writes=[dconst])
        mk = S.sb("mk", [64, 4, 64], F32, st)
        S.op("pool", lambda e: e.memset(mk[:], 1.0), writes=[dconst])
        for i, (stp, cm, cmp_) in enumerate([(1, -1, ALU.is_gt), (1, -1, ALU.is_ge), (-1, 1, ALU.is_gt), (-1, 1, ALU.is_ge)]):
            S.op("pool", lambda e: e.affine_select(out=mk[:, i, :], in_=mk[:, i, :], pattern=[[stp, 64]], compare_op=cmp_, fill=0.0,
                                                   base=0, channel_multiplier=cm), reads=[dconst], writes=[dconst])
        MSK = S.sb("MSK", [64, 2, 320], F32, st)
        for e_ in range(2):
            order = [0, 1, 0, 1, 2] if e_ == 0 else [2, 3, 2, 3, 0]
            for j, m in enumerate(order):
                S.op("dve", lambda e: e.tensor_copy(out=MSK[:, e_, j * 64:(j + 1) * 64], in_=mk[:, m, :]), reads=[dconst], writes=[dconst])
        rmask = S.sb("rmask", [64, 16, 64], F32, st)
        S.op("dve", lambda e: e.memset(rmask[:], 1.0), writes=[dconst])
        S.op("dve", lambda e: e.memset(rmask[:, :, 0:1], 0.0), reads=[dconst], writes=[dconst])
        prm = S.sb("prm", [64, 7, 4], F32, st)
        S.dma("sp", prm[:], T["rprm"][:, :, :], writes=[dconst])
        omk = S.sb("omk", [64, 4], F32, st)
        S.op("dve", lambda e: e.tensor_scalar(out=omk[:], in0=prm[:, 5, :], scalar1=-1.0, scalar2=1.0, op0=ALU.mult, op1=ALU.add),
             reads=[dconst], writes=[dconst])
        wup = S.sb("wup_sb", [96, 2, 256], F32, st)
        aup = S.sb("aup_sb", [96, 2, 256], F32, st)
        S.dma("sp", wup[:], T["wup"].rearrange("e r c -> r e c"), writes=[dconst])
        S.dma("sp", aup[:], T["aup"].rearrange("e r c -> r e c"), writes=[dconst])
        gng = S.sb("gng", [64, 2, 256], F32, st)
        S.dma("sp", gng[:].rearrange("p a b -> p (a b)"), T["gn"][0:1, :].to_broadcast([64, 512]), writes=[dconst])

        yacc = S.sb("yacc", [64, NCH, 256], F32, st); dy = S.dep()
        bacc = S.sb("bacc", [64, NCH, 4], F32, st); dba = S.dep()
        ST = S.sb("ST", [64, 4, 64], F32, st); dST = S.dep()
        PSA = S.ps("PSA", [64, 8, 512], F32, st); dB = [S.pdep() for _ in range(8)]

        def t4(name):
            return S.sb(name, [64, 4, 256], F32, st), S.dep()
        rk_in = S.sb("rk_in", [64, 8, 256], F32, st); drk = S.dep()
        wa_in = S.sb("wa_in", [96, 2, 256], F32, st); dwa = S.dep()
        v_in, dv = t4("v_in")
        lw, dlw = t4("lw"); aa, daa = t4("aa"); kk, dkk = t4("kk"); kd, dkd = t4("kd"); be, dbe = t4("be")
        L, dL = t4("L"); Ld, dLd = t4("Ld"); tmp, dtmp = t4("tmp")
        E1, dE1 = t4("E1"); E2, dE2 = t4("E2"); E3, dE3 = t4("E3"); E4, dE4 = t4("E4")
        bt, dbt = t4("bt"); kt_, dkt = t4("kt_"); bh, dbh = be, dbe; kh, dkh = kd, dkd
        AR = S.sb("AR", [64, 4, 4, 2, 64], F32, st); dAR = S.dep()
        ltot = S.sb("ltot", [64, 4, 4], F32, st); dlt = S.dep()
        pc = S.sb("pc", [64, 4, 4], F32, st); dpc = S.dep()
        bkT = S.sb("bkT", [64, 4, 4, 2, 64], F32, st); dbk = S.dep()
        GM = S.sb("GM", [64, 4, 4, 320], F32, st); dGM = S.dep()
        XX0 = S.sb("XX0", [64, 16, 2, 64], F32, st); XX = [XX0, XX0]; dXX0 = S.dep(); dXX = [dXX0, dXX0]
        Tm = S.sb("Tm", [64, 16, 64], F32, st); dTm = S.dep()
        WT = S.sb("WT", [64, 4, 64], F32, st); dWT = S.dep()
        UT = S.sb("UT", [64, 4, 64], F32, st); dUT = S.dep()
        tS = S.sb("tS", [64, 4, 64], F32, st); dtS = S.dep()

        s_rk = T["s_rk"].rearrange("g c t -> c g t")
        s_wa = T["s_wa"].rearrange("j c t -> c j t")
        v4 = lambda ap: ap.rearrange("p h (c s) -> p h c s", s=64)
        fl = lambda ap: ap.rearrange("p h t -> p (h t)")

        for e_ in range(2):
            S.op("dve", lambda e: e.memset(ST[:], 0.0), reads=[dST], writes=[dST])
            blocks = list(range(NBLK)) if e_ == 0 else [0] + list(range(NBLK - 1, 0, -1))
            corder = [0, 1, 2, 3] if e_ == 0 else [3, 2, 1, 0]
            import os
            blocks = blocks[:int(os.environ.get('A3_BLOCKS', NBLK))]
            for blk in blocks:
                t0 = blk * 256
                S.drain_dma("sp", keep=4)
                S.dma("sp", rk_in[:], s_rk[:, :, t0:t0 + 256], writes=[drk])
                S.dma("sp", wa_in[:], s_wa[:, :, t0:t0 + 256], writes=[dwa])
                S.dma("sp", v_in[:], T["s_v"][t0:t0 + 256, :].rearrange("(c s) n -> s c n", s=64), writes=[dv])
                r_ = rk_in[:, 0:4, :]; k_ = rk_in[:, 4:8, :]
                pwp = PSA[:, 0:2, :].rearrange("p b (h t) -> p (b h) t", h=2)
                pap = PSA[:, 2:4, :].rearrange("p b (h t) -> p (b h) t", h=2)
                for h in range(4):
                    S.op("pe", lambda e: e.matmul(pwp[:, h, :], lhsT=wup[:, e_, h * 64:(h + 1) * 64], rhs=wa_in[:, 0, :], start=True, stop=True),
                         reads=[dconst, dwa], writes=[dB[h // 2]])
                for h in range(4):
                    S.op("pe", lambda e: e.matmul(pap[:, h, :], lhsT=aup[:, e_, h * 64:(h + 1) * 64], rhs=wa_in[:, 1, :], start=True, stop=True),
                         reads=[dconst, dwa], writes=[dB[2 + h // 2]])
                for h in range(4):
                    S.op("act", lambda e: e.activation(out=lw[:, h, :], in_=pwp[:, h, :], func=AF.Sigmoid, bias=prm[:, e_, h:h + 1]),
                         reads=[dB[h // 2], dconst], writes=[dlw])
                for h in range(4):
                    S.op("act", lambda e: e.activation(out=aa[:, h, :], in_=pap[:, h, :], func=AF.Sigmoid, bias=prm[:, 2 + e_, h:h + 1]),
                         reads=[dB[2 + h // 2], dconst], writes=[daa])
                S.op("dve", lambda e: e.tensor_scalar(out=lw[:], in0=lw[:], scalar1=-0.6065306597126334, scalar2=None, op0=ALU.mult),
                     reads=[dlw], writes=[dlw])
                S.op("dve", lambda e: e.tensor_tensor(out=kk[:], in0=k_, in1=prm[:, 4, :].unsqueeze(2).to_broadcast([64, 4, 256]), op=ALU.mult),
                     reads=[drk, dconst], writes=[dkk])
                S.op("dve", lambda e: e.tensor_tensor(out=tmp[:], in0=kk[:], in1=kk[:], op=ALU.mult), reads=[dkk], writes=[dtmp])
                for half in range(2):
                    S.op("pe", lambda e: e.matmul(PSA[:, 4 + half, :], lhsT=ones64[:], rhs=fl(tmp[:])[:, half * 512:(half + 1) * 512],
                                                  start=True, stop=True), reads=[dconst, dtmp], writes=[dB[4 + half]])
                S.op("dve", lambda e: e.tensor_scalar(out=fl(tmp[:]), in0=PSA[:, 4:6, :].rearrange("p b t -> p (b t)"), scalar1=1e-24, scalar2=None,
                                                      op0=ALU.max), reads=[dB[4], dB[5]], writes=[dtmp])
                S.op("act", lambda e: e.activation(out=tmp[:], in_=tmp[:], func=AF.Sqrt), reads=[dtmp], writes=[dtmp])
                S.op("dve", lambda e: e.reciprocal(out=tmp[:], in_=tmp[:]), reads=[dtmp], writes=[dtmp])
                S.op("dve", lambda e: e.tensor_tensor(out=kk[:], in0=kk[:], in1=tmp[:], op=ALU.mult), reads=[dkk, dtmp], writes=[dkk])
                S.op("dve", lambda e: e.tensor_tensor(out=tmp[:], in0=aa[:], in1=prm[:, 5, :].unsqueeze(2).to_broadcast([64, 4, 256]), op=ALU.mult),
                     reads=[daa, dconst], writes=[dtmp])
                S.op("dve", lambda e: e.tensor_tensor(out=tmp[:], in0=tmp[:], in1=omk[:].unsqueeze(2).to_broadcast([64, 4, 256]), op=ALU.add),
                     reads=[dtmp, dconst], writes=[dtmp])
                S.op("dve", lambda e: e.tensor_tensor(out=kd[:], in0=tmp[:], in1=k_, op=ALU.mult), reads=[dtmp, drk], writes=[dkd])
                S.op("pool", lambda e: e.tensor_tensor(out=be[:], in0=kk[:], in1=aa[:], op=ALU.mult), reads=[dkk, daa], writes=[dbe])
                S.op("dve", lambda e: e.tensor_tensor_scan(out=fl(L[:]), data0=rmask[:].rearrange("p a b -> p (a b)"), data1=fl(lw[:]),
                                                           initial=0.0, op0=ALU.mult, op1=ALU.add), reads=[dlw, dconst], writes=[dL])
                S.op("dve", lambda e: e.tensor_copy(out=ltot[:], in_=v4(L[:])[:, :, :, 63]), reads=[dL], writes=[dlt])
                if e_ == 0:
                    Lc, dLc = L, dL
                else:
                    S.op("dve", lambda e: e.tensor_tensor(out=tmp[:], in0=lw[:], in1=L[:], op=ALU.subtract), reads=[dlw, dL], writes=[dtmp])
                    S.op("dve", lambda e: e.tensor_tensor(out=v4(Ld[:]), in0=v4(tmp[:]), in1=ltot[:].unsqueeze(3).to_broadcast([64, 4, 4, 64]),
                                                          op=ALU.add), reads=[dtmp, dlt], writes=[dLd])
                    Lc, dLc = Ld, dLd
                S.op("pool", lambda e: e.tensor_tensor(out=E1[:], in0=Lc[:], in1=lw[:], op=ALU.subtract), reads=[dLc, dlw], writes=[dE1])
                S.op("act", lambda e: e.activation(out=E1[:], in_=E1[:], func=AF.Exp), reads=[dE1], writes=[dE1])
                S.op("act", lambda e: e.activation(out=E2[:], in_=Lc[:], func=AF.Exp), reads=[dLc], writes=[dE2])
                S.op("act", lambda e: e.activation(out=E3[:], in_=Lc[:], func=AF.Exp, scale=-1.0), reads=[dLc], writes=[dE3])
                S.op("dve", lambda e: e.tensor_tensor(out=v4(tmp[:]), in0=ltot[:].unsqueeze(3).to_broadcast([64, 4, 4, 64]), in1=v4(Lc[:]),
                                                      op=ALU.subtract), reads=[dlt, dLc], writes=[dtmp])
                S.op("act", lambda e: e.activation(out=E4[:], in_=tmp[:], func=AF.Exp), reads=[dtmp], writes=[dE4])
                S.op("act", lambda e: e.activation(out=pc[:], in_=ltot[:], func=AF.Exp), reads=[dlt], writes=[dpc])
                S.op("dve", lambda e: e.scalar_tensor_tensor(out=AR[:, :, :, 0, :], in0=v4(kk[:]), scalar=-1.0, in1=v4(E1[:]),
                                                             op0=ALU.mult, op1=ALU.mult), reads=[dkk, dE1], writes=[dAR])
                S.op("dve", lambda e: e.tensor_tensor(out=AR[:, :, :, 1, :], in0=v4(r_), in1=v4(E2[:]), op=ALU.mult), reads=[drk, dE2], writes=[dAR])
                S.op("dve", lambda e: e.tensor_tensor(out=kt_[:], in0=kd[:], in1=E3[:], op=ALU.mult), reads=[dkd, dE3], writes=[dkt])
                S.op("pool", lambda e: e.tensor_tensor(out=bt[:], in0=be[:], in1=E3[:], op=ALU.mult), reads=[dbe, dE3], writes=[dbt])
                S.op("dve", lambda e: e.tensor_tensor(out=tmp[:], in0=r_, in1=kd[:], op=ALU.mult), reads=[drk, dkd], writes=[dtmp])
                S.op("dve", lambda e: e.tensor_tensor(out=tmp[:], in0=tmp[:], in1=prm[:, 6, :].unsqueeze(2).to_broadcast([64, 4, 256]), op=ALU.mult),
                     reads=[dtmp, dconst], writes=[dtmp])
                pbo2 = PSA[:, 6, 0:32].rearrange("p (c h w) -> p c h w", h=4, w=2)
                pbo = pbo2[:, :, :, 0]
                for c in range(4):
                    for h in range(4):
                        S.op("pe", lambda e: e.matmul(pbo2[:, c, h, :], lhsT=tmp[:, h, c * 64:(c + 1) * 64], rhs=ones64[:, 0:2], start=True, stop=True),
                             reads=[dtmp, dconst], writes=[dB[6]])
                S.op("dve", lambda e: e.tensor_tensor(out=bh[:], in0=be[:], in1=E4[:], op=ALU.mult), reads=[dbe, dE4], writes=[dbh])
                S.op("pool", lambda e: e.tensor_tensor(out=kh[:], in0=kd[:], in1=E4[:], op=ALU.mult), reads=[dkd, dE4], writes=[dkh])
                bsl = bacc[:, blk * 4:(blk + 1) * 4, :]
                if e_ == 0:
                    S.op("dve", lambda e: e.tensor_copy(out=bsl, in_=pbo), reads=[dB[6]], writes=[dba])
                else:
                    S.op("dve", lambda e: e.tensor_tensor(out=bsl, in0=bsl, in1=pbo, op=ALU.add), reads=[dB[6], dba], writes=[dba])
                ptr = PSA[:, 0:4, :].rearrange("p b (i s) -> p (b i) s", s=64)
                for c in range(4):
                    for h in range(4):
                        for w_, (src, dsrc) in enumerate([(bh, dbh), (kh, dkh)]):
                            idx = (c * 4 + h) * 2 + w_
                            S.op("pe", lambda e: e.transpose(out=ptr[:, idx, :], in_=src[:, h, c * 64:(c + 1) * 64], identity=identf[:]),
                                 reads=[dsrc, didf], writes=[dB[idx // 8]])
                for half in range(2):
                    S.op("act", lambda e: e.copy(out=bkT[:, half * 2:(half + 1) * 2].rearrange("p c h w s -> p (c h w s)"),
                                                 in_=PSA[:, half * 2:(half + 1) * 2, :].rearrange("p b t -> p (b t)")),
                         reads=[dB[half * 2], dB[half * 2 + 1]], writes=[dbk])
                for c in range(0 if 'g' in os.environ.get('A3_SKIP', '') else 4):
                    bb = (c % 2) * 4
                    cs = slice(c * 64, (c + 1) * 64)
                    for h in range(4):
                        arh = AR[:, h, c, :, :].rearrange("p a s -> p (a s)")
                        S.op("pe", lambda e: e.matmul(PSA[:, bb + h, 0:128], lhsT=bt[:, h, cs], rhs=arh, start=True, stop=True),
                             reads=[dbt, dAR], writes=[dB[bb + h]])
                        S.op("pe", lambda e: e.matmul(PSA[:, bb + h, 128:256], lhsT=kt_[:, h, cs], rhs=arh, start=True, stop=True),
                             reads=[dkt, dAR], writes=[dB[bb + h]])
                        S.op("pe", lambda e: e.matmul(PSA[:, bb + h, 256:320], lhsT=AR[:, h, c, 0, :], rhs=bt[:, h, cs], start=True, stop=True),
                             reads=[dbt, dAR], writes=[dB[bb + h]])
                    S.op("dve", lambda e: e.tensor_tensor(out=GM[:, c, :, :], in0=PSA[:, bb:bb + 4, 0:320],
                                                          in1=MSK[:, e_, :].unsqueeze(1).to_broadcast([64, 4, 320]), op=ALU.mult),
                         reads=[dB[bb], dB[bb + 1], dB[bb + 2], dB[bb + 3], dconst], writes=[dGM])
                GMf = GM[:].rearrange("p c h n -> p (c h) n")
                S.op("dve", lambda e: e.tensor_tensor(out=Tm[:], in0=GMf[:, :, 0:64], in1=identf[:].unsqueeze(1).to_broadcast([64, 16, 64]), op=ALU.add),
                     reads=[dGM, didf], writes=[dTm])
                pxx = PSA[:, 0:4, :].rearrange("p b (i w s) -> p (b i) w s", w=2, s=64)
                ptm = PSA[:, 4:6, :].rearrange("p b (i s) -> p (b i) s", s=64)
                for it_ in range(0 if 'd' in os.environ.get('A3_SKIP', '') else 5):
                    pp = it_ % 2
                    for idx in range(16):
                        if it_ == 0:
                            Xc = GMf[:, idx, 0:64]; XTc = GMf[:, idx, 256:320]; dsrc = dGM
                        else:
                            Xc = XX[1 - pp][:, idx, 0, :]; XTc = XX[1 - pp][:, idx, 1, :]; dsrc = dXX[1 - pp]
                        S.op("pe", lambda e: e.matmul(pxx[:, idx, 0, :], lhsT=XTc, rhs=Xc, start=True, stop=True), reads=[dsrc], writes=[dB[idx // 4]])
                        S.op("pe", lambda e: e.matmul(pxx[:, idx, 1, :], lhsT=Xc, rhs=XTc, start=True, stop=True), reads=[dsrc], writes=[dB[idx // 4]])
                    S.op("act", lambda e: e.copy(out=XX[pp][:].rearrange("p i w s -> p (i w s)"), in_=PSA[:, 0:4, :].rearrange("p b t -> p (b t)")),
                         reads=[dB[0], dB[1], dB[2], dB[3]], writes=[dXX[pp]])
                    for idx in range(16):
                        S.op("pe", lambda e: e.matmul(ptm[:, idx, :], lhsT=XX[pp][:, idx, 1, :], rhs=Tm[:, idx, :], start=True, stop=True),
                             reads=[dXX[pp], dTm], writes=[dB[4 + idx // 8]])
                    S.op("dve", lambda e: e.tensor_tensor(out=Tm[:].rearrange("p i s -> p (i s)"), in0=Tm[:].rearrange("p i s -> p (i s)"),
                                                          in1=PSA[:, 4:6, :].rearrange("p b t -> p (b t)"), op=ALU.add),
                         reads=[dB[4], dB[5], dTm], writes=[dTm])
                pW = PSA[:, 6, 0:256].rearrange("p (h s) -> p h s", s=64)
                pU = PSA[:, 7, 0:256].rearrange("p (h s) -> p h s", s=64)
                pYS = PSA[:, 6, :].rearrange("p (h w s) -> p h w s", w=2, s=64)
                for c in ([] if 's' in os.environ.get('A3_SKIP', '') else corder):
                    gc = blk * 4 + c
                    for h in range(4):
                        vh = v_in[:, c, h * 64:(h + 1) * 64]
                        S.op("pe", lambda e: e.matmul(pW[:, h, :], lhsT=AR[:, h, c, 0, :], rhs=ST[:, h, :], start=True, stop=False),
                             reads=[dAR, dST], writes=[dB[6]])
                        S.op("pe", lambda e: e.matmul(pW[:, h, :], lhsT=GM[:, c, h, 128:192], rhs=vh, start=False, stop=True),
                             reads=[dGM, dv], writes=[dB[6]])
                    S.op("act", lambda e: e.copy(out=WT[:], in_=pW), reads=[dB[6]], writes=[dWT])
                    for h in range(4):
                        S.op("pe", lambda e: e.matmul(pU[:, h, :], lhsT=Tm[:, c * 4 + h, :], rhs=WT[:, h, :], start=True, stop=True),
                             reads=[dTm, dWT], writes=[dB[7]])
                    S.op("act", lambda e: e.copy(out=UT[:], in_=pU), reads=[dB[7]], writes=[dUT])
                    if 'y' in os.environ.get('A3_SKIP', ''):
                        continue
                    for h in range(4):
                        vh = v_in[:, c, h * 64:(h + 1) * 64]
                        S.op("pe", lambda e: e.matmul(pYS[:, h, 0, :], lhsT=AR[:, h, c, 1, :], rhs=ST[:, h, :], start=True, stop=False),
                             reads=[dAR, dST], writes=[dB[6]])
                        S.op("pe", lambda e: e.matmul(pYS[:, h, 0, :], lhsT=GM[:, c, h, 64:128], rhs=UT[:, h, :], start=False, stop=False),
                             reads=[dGM, dUT], writes=[dB[6]])
                        S.op("pe", lambda e: e.matmul(pYS[:, h, 0, :], lhsT=GM[:, c, h, 192:256], rhs=vh, start=False, stop=True),
                             reads=[dGM, dv], writes=[dB[6]])
                        S.op("pe", lambda e: e.matmul(pYS[:, h, 1, :], lhsT=bkT[:, c, h, 0, :], rhs=UT[:, h, :], start=True, stop=False),
                             reads=[dbk, dUT], writes=[dB[6]])
                        S.op("pe", lambda e: e.matmul(pYS[:, h, 1, :], lhsT=bkT[:, c, h, 1, :], rhs=vh, start=False, stop=True),
                             reads=[dbk, dv], writes=[dB[6]])
                    ysl = yacc[:, gc, :].rearrange("p (h s) -> p h s", s=64)
                    if e_ == 0:
                        S.op("dve", lambda e: e.tensor_copy(out=ysl, in_=pYS[:, :, 0, :]), reads=[dB[6]], writes=[dy])
                    else:
                        S.op("dve", lambda e: e.tensor_tensor(out=ysl, in0=ysl, in1=pYS[:, :, 0, :], op=ALU.add), reads=[dB[6], dy], writes=[dy])
                    S.op("dve", lambda e: e.tensor_tensor(out=tS[:], in0=ST[:], in1=pc[:, :, c:c + 1].to_broadcast([64, 4, 64]), op=ALU.mult),
                         reads=[dST, dpc], writes=[dtS])
                    S.op("dve", lambda e: e.tensor_tensor(out=ST[:], in0=tS[:], in1=pYS[:, :, 1, :], op=ALU.add), reads=[dtS, dB[6]], writes=[dST])
        gtb_ = fl(E3[:]).bitcast(BF16)[:, 0:1024].rearrange("p (c n) -> p c n", n=256); dgtb = dE3
        obr_ = fl(E4[:]).bitcast(BF16)[:, 0:1024].rearrange("p (c n) -> p c n", n=256); dobr = dE4
        st1 = S.sb("st1", [64, 4, 16], F32, st); dst1 = S.dep()
        v16 = lambda ap: ap.rearrange("p c (h s) -> p (c h) s", s=64)
        for blk in range(NBLK):
            t0 = blk * 256
            S.drain_dma("sp", keep=4)
            yb = yacc[:, blk * 4:(blk + 1) * 4, :]
            S.dma("sp", v_in[:], T["s_v"][t0:t0 + 256, :].rearrange("(c s) n -> s c n", s=64), writes=[dv])
            S.dma("sp", gtb_, T["s_gate"][t0:t0 + 256, 0:256].rearrange("(c s) n -> s c n", s=64), writes=[dgtb])
            S.op("dve", lambda e: e.tensor_reduce(out=st1[:, 0, :], in_=v16(yb), axis=AX.X, op=ALU.add), reads=[dy], writes=[dst1])
            S.op("dve", lambda e: e.tensor_tensor(out=tmp[:], in0=yb, in1=yb, op=ALU.mult), reads=[dy], writes=[dtmp])
            S.op("dve", lambda e: e.tensor_reduce(out=st1[:, 1, :], in_=v16(tmp[:]), axis=AX.X, op=ALU.add), reads=[dtmp], writes=[dst1])
            S.op("dve", lambda e: e.tensor_scalar(out=st1[:, 0:2, :], in0=st1[:, 0:2, :], scalar1=1.0 / 64, scalar2=None, op0=ALU.mult),
                 reads=[dst1], writes=[dst1])
            S.op("dve", lambda e: e.tensor_tensor(out=st1[:, 2, :], in0=st1[:, 0, :], in1=st1[:, 0, :], op=ALU.mult), reads=[dst1], writes=[dst1])
            S.op("dve", lambda e: e.tensor_tensor(out=st1[:, 3, :], in0=st1[:, 1, :], in1=st1[:, 2, :], op=ALU.subtract), reads=[dst1], writes=[dst1])
            _rstd(S, st1[:, 3, :], 16, 1.0, 64e-5, [], dst1)
            S.op("dve", lambda e: e.tensor_tensor(out=v16(tmp[:]), in0=v16(yb), in1=st1[:, 0, :].unsqueeze(2).to_broadcast([64, 16, 64]), op=ALU.subtract),
                 reads=[dy, dst1], writes=[dtmp])
            S.op("dve", lambda e: e.tensor_tensor(out=v16(tmp[:]), in0=v16(tmp[:]), in1=st1[:, 3, :].unsqueeze(2).to_broadcast([64, 16, 64]), op=ALU.mult),
                 reads=[dtmp, dst1], writes=[dtmp])
            S.op("dve", lambda e: e.tensor_tensor(out=tmp[:], in0=tmp[:], in1=gng[:, 0, :].unsqueeze(1).to_broadcast([64, 4, 256]), op=ALU.mult),
                 reads=[dtmp, dconst], writes=[dtmp])
            S.op("dve", lambda e: e.tensor_tensor(out=tmp[:], in0=tmp[:], in1=gng[:, 1, :].unsqueeze(1).to_broadcast([64, 4, 256]), op=ALU.add),
                 reads=[dtmp, dconst], writes=[dtmp])
            bv = bacc[:, blk * 4:(blk + 1) * 4, :].rearrange("p c h -> p (c h)").unsqueeze(2).to_broadcast([64, 16, 64])
            S.op("dve", lambda e: e.tensor_tensor(out=v16(E1[:]), in0=v16(v_in[:]), in1=bv, op=ALU.mult), reads=[dv, dba], writes=[dE1])
            S.op("dve", lambda e: e.tensor_tensor(out=tmp[:], in0=tmp[:], in1=E1[:], op=ALU.add), reads=[dtmp, dE1], writes=[dtmp])
            S.op("dve", lambda e: e.tensor_tensor(out=obr_, in0=tmp[:], in1=gtb_, op=ALU.mult), reads=[dtmp, dgtb], writes=[dobr])
            S.dma("sp", T["br_rwkv"][t0:t0 + 256, :].rearrange("(c s) n -> s c n", s=64), obr_, reads=[dobr])
        S.barrier()


def build_A(phases="123"):
    nc = bass.Bass("TRN2", target_bir_lowering=False)
    T = {}

    def din(name, shape, dt=F32):
        T[name] = nc.dram_tensor(name, list(shape), dt, kind="ExternalInput").ap()

    def dscr(name, shape, dt):
        T[name] = nc.dram_tensor(name, list(shape), dt, kind="Internal").ap()

    def dout(name, shape, dt):
        T[name] = nc.dram_tensor(name, list(shape), dt, kind="ExternalOutput").ap()

    din("xf", [NT, D]); din("cT", [128, 16, 2]); din("wmod", [D, 4096]); din("bmod", [1, 4096]); din("gpre", [1, D])
    din("gqn", [1, 256]); din("w_fm", [D, 704]); din("w_tm", [D, 2304]); din("rope", [NT, 192])
    din("lamp", [1, 256]); din("lami", [1, 1]); din("subg", [1, 128]); din("rprm", [64, 7, 4])
    din("wup", [2, 96, 256]); din("aup", [2, 96, 256]); din("gn", [1, 512])
    dscr("s_rk", [8, 64, NT], F32); dscr("s_wa", [2, 96, NT], F32); dscr("s_v", [NT, 256], F32)
    dscr("s_dqkT", [8, 64, NT], BF16); dscr("s_gqkT", [3, 128, NT], BF16)
    dscr("s_dv", [NT, 2, 129], BF16); dscr("s_gv", [NT, 129], BF16); dscr("s_gate", [NT, 768], BF16)
    dout("hT", [16, 128, NT], BF16); dout("br_att", [NT, 512], BF16); dout("br_rwkv", [NT, 256], BF16)
    with contextlib.ExitStack() as st:
        S = Sched(nc, st)
        if "1" in phases:
            _phase_A1(S, nc, T)
        if "2" in phases:
            _phase_A2(S, nc, T)
        if "3" in phases:
            _phase_A3(S, nc, T)
        S.barrier()
        print("build_A: ninst", S.ninst, "nsem", S.nsem, {k: v for k, v in S.cnt.items()})
    return nc


OFF = {"cv_val": 0, "cv_glu": 1024, "cv_gate": 2048, "rk_r": 3072, "rk_k": 4096, "rk_v": 5120, "rk_wl": 6144, "rk_al": 6240,
       "rk_gate": 6336, "df_q": 7360, "df_k": 8384, "df_v": 9408, "df_gate": 10432, "gq_q": 11456, "gq_k": 12480, "gq_v": 12736,
       "gq_gate": 12992, "merge": 14016}


def _rope_table():
    tab = np.zeros((NT, 192), np.float32)
    tab[:, 0:32] = 1.0
    tab[:, 64:128] = 1.0
    t = np.arange(4096)
    row = (t // 64).astype(np.float32); col = (t % 64).astype(np.float32)
    for half, c0, s0 in ((32, 0, 32), (64, 64, 128)):
        inv = (10000.0 ** (-np.arange(0, half, 2, dtype=np.float32) / half)).astype(np.float32)
        ang = np.concatenate([row[:, None] * inv, col[:, None] * inv], axis=-1).astype(np.float32)
        tab[NCTX:, c0:c0 + half] = np.cos(ang)
        tab[NCTX:, s0:s0 + half] = np.sin(ang)
    return tab


def _cT(c_ctx, cb):
    both = np.stack([c_ctx, cb], axis=-1)
    return np.ascontiguousarray(both.reshape(16, 128, 2).transpose(1, 0, 2))


def inputs_A(inp, li, b, q, xfull, rope):
    w_in = inp["w_in"][li]
    cs = lambda name, a, n: w_in[:, OFF[name] + a:OFF[name] + a + n]
    kv = q // 2
    w_fm = np.concatenate([cs("rk_r", 256 * q, 256), cs("rk_k", 256 * q, 256), cs("rk_wl", 0, 96), cs("rk_al", 0, 96)], axis=1)
    w_tm = np.concatenate([cs("rk_v", 256 * q, 256), cs("rk_gate", 256 * q, 256),
                           cs("df_q", 256 * q, 256), cs("df_k", 256 * q, 256), cs("df_v", 256 * q, 256), cs("df_gate", 256 * q, 256),
                           cs("gq_q", 256 * q, 256), cs("gq_k", 128 * kv, 128), cs("gq_v", 128 * kv, 128), cs("gq_gate", 256 * q, 256)], axis=1)
    sl = slice(256 * q, 256 * q + 256)
    hm = lambda v: v[sl].reshape(4, 64).T
    rprm = np.stack([hm(inp["rwkv_w0"][li, 0]), hm(inp["rwkv_w0"][li, 1]), hm(inp["rwkv_a0"][li, 0]), hm(inp["rwkv_a0"][li, 1]),
                     hm(inp["rwkv_k_k"][li]), hm(inp["rwkv_k_a"][li]), hm(inp["rwkv_r_k"][li].reshape(-1))], axis=1)
    lam_init = 0.8 - 0.6 * math.exp(-0.3 * li)
    return {
        "xf": np.ascontiguousarray(xfull[b]), "cT": _cT(inp["c_ctx"], inp["c"][b]),
        "wmod": np.ascontiguousarray(inp["w_mod"][li][:, 0:4096]), "bmod": np.ascontiguousarray(inp["b_mod"][li][None, 0:4096]),
        "gpre": np.ascontiguousarray(inp["norm_pre_g"][li][None]), "gqn": np.ascontiguousarray(inp["gqa_qk_norm_g"][li].reshape(1, 256)),
        "w_fm": np.ascontiguousarray(w_fm), "w_tm": np.ascontiguousarray(w_tm), "rope": rope,
        "lamp": np.ascontiguousarray(inp["diff_lam"][li].reshape(1, 256)), "lami": np.full((1, 1), lam_init, np.float32),
        "subg": np.ascontiguousarray(inp["diff_subln_g"][li][None]), "rprm": np.ascontiguousarray(rprm.astype(np.float32)),
        "wup": np.ascontiguousarray(inp["rwkv_w_up"][li][:, :, sl]), "aup": np.ascontiguousarray(inp["rwkv_a_up"][li][:, :, sl]),
        "gn": np.ascontiguousarray(np.concatenate([inp["rwkv_gn_g"][li][sl], inp["rwkv_gn_b"][li][sl]])[None]),
    }


NOWN = 1088
NEXT = 1152
MBLK = [(0, 64, 15), (64, 384, 109), (448, 384, 493), (832, 256, 877)]
LNBLK = [(0, 384), (384, 384), (768, 320)]


def build_B():
    nc = bass.Bass("TRN2", target_bir_lowering=False)
    T = {}

    def din(name, shape, dt=F32):
        T[name] = nc.dram_tensor(name, list(shape), dt, kind="ExternalInput").ap()

    din("hTx", [128, 16, NEXT], BF16); din("mask", [1, NEXT]); din("brT", [128, 3, 8, NOWN], BF16); din("x_own", [NOWN, D])
    din("w_cv", [D, 3072]); din("cvp", [128, 8, 34]); din("Wl", [D, 8192]); din("Wb", [4, W, D]); din("Wout", [D, D])
    din("bg", [128, 4, 16]); din("cT", [128, 16, 2]); din("wmodg", [D, D]); din("bmodg", [1, D]); din("gpost", [1, D])
    T["s_cv"] = nc.dram_tensor("s_cv", [8, 128, NOWN], BF16, kind="Internal").ap()
    T["xo"] = nc.dram_tensor("xo", [NOWN, D], F32, kind="ExternalOutput").ap()
    with contextlib.ExitStack() as st0:
        S = Sched(nc, st0)
        big = S.sb("big", [128, 16 * NOWN], BF16, st0); dbig = S.dep()
        mergedT = big[:].rearrange("p (c t) -> p c t", t=NOWN)
        conv_all = big[:].bitcast(F32).rearrange("p (c t) -> p c t", t=NOWN)
        with contextlib.ExitStack() as stm:
            hTx = S.sb("hTx_sb", [128, 16, NEXT], BF16, stm); dhTx = S.dep()
            for k4 in range(4):
                S.dma("sp", hTx[:, k4 * 4:(k4 + 1) * 4, :], T["hTx"][:, k4 * 4:(k4 + 1) * 4, :], writes=[dhTx])
            with contextlib.ExitStack() as st:
                maskb = S.sb("maskb", [128, NEXT], F32, st); dmask = S.dep()
                S.dma("sp", maskb[:], T["mask"][0:1, :].to_broadcast([128, NEXT]), writes=[dmask])
                cvp = S.sb("cvp_sb", [128, 8, 34], F32, st); dcvp = S.dep()
                S.dma("sp", cvp[:], T["cvp"][:, :, :], writes=[dcvp])
                ones = S.sb("ones128", [128, 128], F32, st); dones = S.dep()
                S.op("dve", lambda e: e.memset(ones[:], 1.0), writes=[dones])
                wck = [S.sb(f"wck{k}", [128, 16, 128], BF16, st) for k in range(3)]; dwck = [S.dep() for _ in range(3)]
                u = S.sb("u", [128, NEXT], F32, st); du = S.dep()
                sg = S.sb("sg", [128, 384], F32, st); dsg = S.dep()
                cgx = S.sb("cgx", [128, 8, NEXT], BF16, st); dcg = S.dep()
                sqt = S.sb("sqt", [128, NOWN], F32, st); dsq = S.dep()
                meanb = S.sb("meanb", [128, NOWN], F32, st); dmean = S.dep()
                rstdb = S.sb("rstdb", [128, NOWN], F32, st); drstd = S.dep()
                cst = S.sb("cst", [128, NOWN], BF16, st); dcst = S.dep()
                pcv = [S.ps(f"pcv{i}", [128, 512], F32, st) for i in range(6)]; dpcv = [S.pdep() for _ in range(6)]
                wcv = T["w_cv"].rearrange("(kc p) n -> p kc n", p=128)
                for cc in range(8):
                    for k in range(3):
                        c0 = k * 1024 + cc * 128
                        for k4 in range(2):
                            S.dma("pool", wck[k][:, k4 * 8:(k4 + 1) * 8, :], wcv[:, k4 * 8:(k4 + 1) * 8, c0:c0 + 128], writes=[dwck[k]])
                    for tb in range(3):
                        ts_ = slice(tb * 384, (tb + 1) * 384)
                        pgl, dgl = pcv[0 + tb % 2], dpcv[0 + tb % 2]
                        pv, dpv = pcv[2 + tb % 2], dpcv[2 + tb % 2]
                        pgt, dgt_ = pcv[4 + tb % 2], dpcv[4 + tb % 2]
                        for (k, p_, dp_) in ((1, pgl, dgl), (0, pv, dpv), (2, pgt, dgt_)):
                            for kc in range(16):
                                S.op("pe", lambda e: e.matmul(p_[:, 0:384], lhsT=wck[k][:, kc, :], rhs=hTx[:, kc, ts_], start=(kc == 0), stop=(kc == 15)),
                                     reads=[dwck[k], dhTx], writes=[dp_])
                        S.op("act", lambda e: e.activation(out=sg[:], in_=pgl[:, 0:384], func=AF.Sigmoid), reads=[dgl], writes=[dsg])
                        S.op("dve", lambda e: e.tensor_tensor(out=sg[:], in0=sg[:], in1=maskb[:, ts_], op=ALU.mult), reads=[dsg, dmask], writes=[dsg])
                        S.op("dve", lambda e: e.tensor_tensor(out=u[:, ts_], in0=pv[:, 0:384], in1=sg[:], op=ALU.mult), reads=[dpv, dsg], writes=[du])
                        S.op("act", lambda e: e.activation(out=cgx[:, cc, ts_], in_=pgt[:, 0:384], func=AF.Silu), reads=[dgt_], writes=[dcg])
                    for (o0_, n_, e0_) in ((0, 64, 0), (64, 1024, 94)):
                        acc = conv_all[:, cc, o0_:o0_ + n_]
                        S.op("dve", lambda e: e.tensor_scalar(out=acc, in0=u[:, e0_:e0_ + n_], scalar1=cvp[:, cc, 0:1], scalar2=cvp[:, cc, 31:32],
                                                              op0=ALU.mult, op1=ALU.add), reads=[du, dcvp], writes=[dbig])
                        for j in range(1, 31):
                            S.op("dve", lambda e: e.scalar_tensor_tensor(out=acc, in0=u[:, e0_ + j:e0_ + j + n_], scalar=cvp[:, cc, j:j + 1], in1=acc,
                                                                         op0=ALU.mult, op1=ALU.add), reads=[du, dcvp, dbig], writes=[dbig])
                for cc in range(8):
                    S.op("dve", lambda e: e.tensor_tensor(out=sqt[:], in0=conv_all[:, cc, :], in1=conv_all[:, cc, :], op=ALU.mult), reads=[dbig], writes=[dsq])
                    for bi, (o_, n_) in enumerate(LNBLK):
                        S.op("pe", lambda e: e.matmul(pcv[bi][:, 0:n_], lhsT=ones[:], rhs=conv_all[:, cc, o_:o_ + n_], start=(cc == 0), stop=(cc == 7)),
                             reads=[dones, dbig], writes=[dpcv[bi]])
                        S.op("pe", lambda e: e.matmul(pcv[3 + bi][:, 0:n_], lhsT=ones[:], rhs=sqt[:, o_:o_ + n_], start=(cc == 0), stop=(cc == 7)),
                             reads=[dones, dsq], writes=[dpcv[3 + bi]])
                for bi, (o_, n_) in enumerate(LNBLK):
                    S.op("dve", lambda e: e.tensor_scalar(out=meanb[:, o_:o_ + n_], in0=pcv[bi][:, 0:n_], scalar1=1.0 / W, scalar2=None, op0=ALU.mult),
                         reads=[dpcv[bi]], writes=[dmean])
                    S.op("dve", lambda e: e.tensor_scalar(out=rstdb[:, o_:o_ + n_], in0=pcv[3 + bi][:, 0:n_], scalar1=1.0 / W, scalar2=None, op0=ALU.mult),
                         reads=[dpcv[3 + bi]], writes=[drstd])
                S.op("dve", lambda e: e.tensor_tensor(out=sqt[:], in0=meanb[:], in1=meanb[:], op=ALU.mult), reads=[dmean], writes=[dsq])
                S.op("dve", lambda e: e.tensor_tensor(out=rstdb[:], in0=rstdb[:], in1=sqt[:], op=ALU.subtract), reads=[drstd, dsq], writes=[drstd])
                _rstd(S, rstdb[:], NOWN, 1.0, 1e-5, [], drstd)
                for cc in range(8):
                    cv = conv_all[:, cc, :]
                    S.op("dve", lambda e: e.tensor_tensor(out=cv, in0=cv, in1=meanb[:], op=ALU.subtract), reads=[dbig, dmean], writes=[dbig])
                    S.op("dve", lambda e: e.tensor_tensor(out=cv, in0=cv, in1=rstdb[:], op=ALU.mult), reads=[dbig, drstd], writes=[dbig])
                    S.op("act", lambda e: e.activation(out=sqt[:], in_=cv, func=AF.Silu, bias=cvp[:, cc, 33:34], scale=cvp[:, cc, 32:33]),
                         reads=[dbig, dcvp], writes=[dsq])
                    S.op("dve", lambda e: e.tensor_tensor(out=cst[:, 0:64], in0=sqt[:, 0:64], in1=cgx[:, cc, 15:79], op=ALU.mult), reads=[dsq, dcg], writes=[dcst])
                    S.op("dve", lambda e: e.tensor_tensor(out=cst[:, 64:NOWN], in0=sqt[:, 64:NOWN], in1=cgx[:, cc, 109:1133], op=ALU.mult), reads=[dsq, dcg], writes=[dcst])
                    S.dma("sp", T["s_cv"][cc, :, :], cst[:], reads=[dcst])
                S.barrier()
            with contextlib.ExitStack() as st:
                brT4 = S.sb("brT4", [128, 4, 8, NOWN], BF16, st); dbr = S.dep()
                S.dma("sp", brT4[:, 0, :, :], T["s_cv"].rearrange("c p t -> p c t"), writes=[dbr])
                for j in range(3):
                    S.dma("sp", brT4[:, 1 + j, :, :], T["brT"][:, j, :, :], writes=[dbr])
                bg = S.sb("bg_sb", [128, 4, 16], F32, st); dbg = S.dep()
                S.dma("sp", bg[:], T["bg"][:, :, :], writes=[dbg])
                wl = [S.sb(f"wl{i}", [128, 16, 4, 128], BF16, st) for i in range(2)]; dwl = [S.dep() for _ in range(2)]
                wb = [S.sb(f"wb{i}", [128, 8, 4, 128], BF16, st) for i in range(2)]; dwb = [S.dep() for _ in range(2)]
                gsb = S.sb("gsb", [128, 384], F32, st); dgs = S.dep()
                macc = S.sb("macc", [128, 384], F32, st); dma_ = S.dep()
                mtmp = S.sb("mtmp", [128, 384], F32, st); dmt = S.dep()
                pl = [S.ps(f"pl{i}", [128, 512], F32, st) for i in range(2)]; dpl = [S.pdep() for _ in range(2)]
                pp = [S.ps(f"pp{i}", [128, 512], F32, st) for i in range(2)]; dpp = [S.pdep() for _ in range(2)]
                Wlv = T["Wl"].rearrange("(kc p) n -> p kc n", p=128)
                it = 0
                for dc in range(16):
                    b2 = dc % 2
                    for j in range(4):
                        c0 = j * D + dc * 128
                        S.dma("pool", wl[b2][:, :, j, :], Wlv[:, :, c0:c0 + 128], writes=[dwl[b2]])
                        S.dma("pool", wb[b2][:, :, j, :], T["Wb"][j].rearrange("(cc p) n -> p cc n", p=128)[:, :, dc * 128:(dc + 1) * 128], writes=[dwb[b2]])
                    for (o_, n_, e_) in MBLK:
                        for j in range(4):
                            p1, d1 = pl[it % 2], dpl[it % 2]
                            p2, d2 = pp[it % 2], dpp[it % 2]
                            it += 1
                            for kc in range(16):
                                S.op("pe", lambda e: e.matmul(p1[:, 0:n_], lhsT=wl[b2][:, kc, j, :], rhs=hTx[:, kc, e_:e_ + n_], start=(kc == 0), stop=(kc == 15)),
                                     reads=[dwl[b2], dhTx], writes=[d1])
                            for cc in range(8):
                                S.op("pe", lambda e: e.matmul(p2[:, 0:n_], lhsT=wb[b2][:, cc, j, :], rhs=brT4[:, j, cc, o_:o_ + n_], start=(cc == 0), stop=(cc == 7)),
                                     reads=[dwb[b2], dbr], writes=[d2])
                            S.op("act", lambda e: e.activation(out=gsb[:, 0:n_], in_=p1[:, 0:n_], func=AF.Sigmoid, bias=bg[:, j, dc:dc + 1]),
                                 reads=[d1, dbg], writes=[dgs])
                            if j == 0:
                                S.op("dve", lambda e: e.tensor_tensor(out=macc[:, 0:n_], in0=p2[:, 0:n_], in1=gsb[:, 0:n_], op=ALU.mult), reads=[d2, dgs], writes=[dma_])
                            else:
                                S.op("dve", lambda e: e.tensor_tensor(out=mtmp[:, 0:n_], in0=p2[:, 0:n_], in1=gsb[:, 0:n_], op=ALU.mult), reads=[d2, dgs], writes=[dmt])
                                dst = macc[:, 0:n_] if j < 3 else mergedT[:, dc, o_:o_ + n_]
                                S.op("dve", lambda e: e.tensor_tensor(out=dst, in0=macc[:, 0:n_], in1=mtmp[:, 0:n_], op=ALU.add),
                                     reads=[dma_, dmt], writes=[dma_ if j < 3 else dbig])
                S.barrier()
        with contextlib.ExitStack() as st:
            modb = [S.sb(f"modg{i}", [128, D], F32, st) for i in range(2)]; dmodb = S.dep()
            _mod_prologue(S, nc, T["cT"], T["wmodg"], T["bmodg"], D, modb, dmodb)
            gpb = S.sb("gpb", [128, D], F32, st); dgp = S.dep()
            S.dma("sp", gpb[:], T["gpost"][0:1, :].to_broadcast([128, D]), writes=[dgp])
            for wh in range(2):
                S.op("dve", lambda e: e.tensor_tensor(out=modb[wh][:], in0=modb[wh][:], in1=gpb[:], op=ALU.mult), reads=[dmodb, dgp], writes=[dmodb])
            Wo = S.sb("Wo", [128, 16, D], BF16, st); dWo = S.dep()
            Wov = T["Wout"].rearrange("(kc p) n -> p kc n", p=128)
            for kc in range(16):
                S.dma("pool", Wo[:, kc:kc + 1, :], Wov[:, kc:kc + 1, :], writes=[dWo])
            xt = [S.sb(f"xt{i}", [128, D], F32, st) for i in range(2)]; dxt = [S.dep() for _ in range(2)]
            yb = S.sb("yb", [128, D], F32, st); dyb = S.dep()
            jk = S.sb("jk", [128, D], BF16, st); djk = S.dep()
            ss = S.sb("ss3", [128, 4], F32, st); dss = S.dep()
            py = [S.ps(f"py{i}", [128, 512], F32, st) for i in range(2)]; dpy = [S.pdep() for _ in range(2)]
            tiles = [(0, 64, 0)] + [(64 + 128 * i, 128, 1) for i in range(8)]
            for ti, (o_, n_, wh) in enumerate(tiles):
                S.drain_dma("sp", keep=4)
                x_ = xt[ti % 2]; dx_ = dxt[ti % 2]
                S.dma("sp", x_[0:n_, :], T["x_own"][o_:o_ + n_, :], writes=[dx_])
                for cg in range(4):
                    p_, dp_ = py[cg % 2], dpy[cg % 2]
                    for kc in range(16):
                        S.op("pe", lambda e: e.matmul(p_[0:n_, :], lhsT=mergedT[:, kc, o_:o_ + n_], rhs=Wo[:, kc, cg * 512:(cg + 1) * 512], start=(kc == 0), stop=(kc == 15)),
                             reads=[dbig, dWo], writes=[dp_])
                    S.op("act", lambda e: e.copy(out=yb[0:n_, cg * 512:(cg + 1) * 512], in_=p_[0:n_, :]), reads=[dp_], writes=[dyb])
                S.op("act", lambda e: e.activation(out=jk[0:n_, :], in_=yb[0:n_, :], func=AF.Square, accum_out=ss[0:n_, 0:1]), reads=[dyb], writes=[djk, dss])
                _rstd(S, ss[0:n_, 0:1], 1, 1.0 / D, EPS, [], dss)
                S.op("dve", lambda e: e.scalar_tensor_tensor(out=yb[0:n_, :], in0=yb[0:n_, :], scalar=ss[0:n_, 0:1], in1=modb[wh][0:n_, :],
                                                             op0=ALU.mult, op1=ALU.mult), reads=[dyb, dss, dmodb], writes=[dyb])
                S.op("dve", lambda e: e.tensor_tensor(out=yb[0:n_, :], in0=yb[0:n_, :], in1=x_[0:n_, :], op=ALU.add), reads=[dyb, dx_], writes=[dyb])
                S.dma("sp", T["xo"][o_:o_ + n_, :], yb[0:n_, :], reads=[dyb])
            S.barrier()
        print("build_B: ninst", S.ninst, "nsem", S.nsem, {k: v for k, v in S.cnt.items()})
    return nc


def _bf16(a):
    import ml_dtypes
    return np.ascontiguousarray(np.asarray(a).astype(ml_dtypes.bfloat16))


def _tok_maps(q):
    ctx_idx = np.arange(64 * q - 15, 64 * q + 79)
    lat_idx = np.arange(1024 * q - 15, 1024 * q + 1039)
    tok = np.full(NEXT, -1, np.int64)
    v = (ctx_idx >= 0) & (ctx_idx < NCTX)
    tok[0:94][v] = ctx_idx[v]
    v2 = (lat_idx >= 0) & (lat_idx < 4096)
    tok[94:94 + 1054][v2] = NCTX + lat_idx[v2]
    own = np.concatenate([np.arange(64 * q, 64 * q + 64), NCTX + np.arange(1024 * q, 1024 * q + 1024)])
    return tok, own


def shared_B(inp, li):
    w_in = inp["w_in"][li]
    cvp = np.concatenate([inp["conv_w"][li].T, inp["conv_b"][li][:, None], inp["conv_ln_g"][li][:, None], inp["conv_ln_b"][li][:, None]], axis=1)
    return {
        "w_cv": np.ascontiguousarray(w_in[:, 0:3072]),
        "cvp": np.ascontiguousarray(cvp.reshape(8, 128, 34).transpose(1, 0, 2).astype(np.float32)),
        "Wl": np.ascontiguousarray(w_in[:, OFF["merge"]:OFF["merge"] + 8192]),
        "Wb": np.ascontiguousarray(inp["w_branch"][li]),
        "Wout": np.ascontiguousarray(inp["w_out"][li]),
        "bg": np.ascontiguousarray(inp["b_gate"][li].reshape(4, 16, 128).transpose(2, 0, 1)),
        "wmodg": np.ascontiguousarray(inp["w_mod"][li][:, 4096:6144]),
        "bmodg": np.ascontiguousarray(inp["b_mod"][li][None, 4096:6144]),
        "gpost": np.ascontiguousarray(inp["norm_post_g"][li][None]),
    }


def inputs_B(inp, shared, b, q, xfull, hT_b, br_b):
    tok, own = _tok_maps(q)
    valid = tok >= 0
    hTx = np.zeros((16, 128, NEXT), hT_b.dtype)
    hTx[:, :, valid] = hT_b[:, :, tok[valid]]
    brT = br_b[own].reshape(NOWN, 3, 8, 128).transpose(3, 1, 2, 0)
    m = dict(shared)
    m.update({
        "hTx": np.ascontiguousarray(hTx.transpose(1, 0, 2)), "mask": valid.astype(np.float32)[None],
        "brT": np.ascontiguousarray(brT), "x_own": np.ascontiguousarray(xfull[b][own]),
        "cT": _cT(inp["c_ctx"], inp["c"][b]),
    })
    return m


def _gather_A(resA):
    out = []
    for b in range(2):
        hT_b = np.asarray(resA[4 * b]["hT"])
        parts = []
        for j in range(3):
            cols = []
            for q in range(4):
                r = resA[4 * b + q]
                if j == 0:
                    cols.append(np.asarray(r["br_rwkv"]))
                else:
                    cols.append(np.asarray(r["br_att"])[:, (j - 1) * 256:j * 256])
            parts.append(np.concatenate(cols, axis=1))
        out.append((hT_b, np.stack(parts, axis=1)))
    return out


_NC = {}


def kernel(**inputs):
    inp = {k: np.asarray(v) for k, v in inputs.items()}
    if "A" not in _NC:
        _NC["A"] = build_A()
        _NC["B"] = build_B()
    rope = _rope_table()
    xfull = np.concatenate([inp["ctx"], inp["x"]], axis=1).astype(np.float32)
    cores = list(range(8))
    for li in range(4):
        mapsA = [inputs_A(inp, li, c // 4, c % 4, xfull, rope) for c in cores]
        resA = run_bass_kernel_spmd(_NC["A"], mapsA, core_ids=cores).results
        del mapsA
        gA = _gather_A(resA)
        del resA
        sh = shared_B(inp, li)
        mapsB = [inputs_B(inp, sh, c // 4, c % 4, xfull, gA[c // 4][0], gA[c // 4][1]) for c in cores]
        resB = run_bass_kernel_spmd(_NC["B"], mapsB, core_ids=cores).results
        del mapsB
        xnew = np.empty_like(xfull)
        for c in cores:
            _, own = _tok_maps(c % 4)
            xnew[c // 4][own] = np.asarray(resB[c]["xo"])
        xfull = xnew
    return np.ascontiguousarray(xfull[:, NCTX:, :]).astype(np.float32)
```

```python
import contextlib
import math
import numpy as np
import concourse.bass as bass
import concourse.mybir as mybir
from concourse.bass_utils import run_bass_kernel_spmd

F32 = mybir.dt.float32
BF16 = mybir.dt.bfloat16
AF = mybir.ActivationFunctionType
ALU = mybir.AluOpType
AX = mybir.AxisListType

SEM_LIM = 30000
D = 2048
NT = 4352
NTILE = 34
NBLK = 17
NCTX = 256
W = 1024
EPS = 1e-6


class Dep:
    __slots__ = ("name", "w", "rs", "wsem", "wcnt", "rsem", "rcnt", "excl")

    def __init__(self, name="", excl=False):
        self.name = name
        self.excl = excl
        self.w = None
        self.rs = []
        self.wsem = None
        self.wcnt = 0
        self.rsem = None
        self.rcnt = 0


class Sched:
    def __init__(self, nc, stack):
        self.nc = nc
        self.stack = stack
        self.eng = {"pe": nc.tensor, "dve": nc.vector, "act": nc.scalar, "pool": nc.gpsimd, "sp": nc.sync}
        self.sems = {k: [] for k in self.eng}
        self.cnt = {k: 0 for k in self.eng}
        self.known = {k: {} for k in self.eng}
        self.nsem = 0
        self.ninst = 0
        self.deps = []
        self.dticks = []

    def dep(self, name="", excl=False):
        d = Dep(name, excl)
        self.deps.append(d)
        return d

    def pdep(self, name=""):
        return self.dep(name, excl=True)

    def new_sem(self, name):
        self.nsem += 1
        return self.stack.enter_context(self.nc.semaphore(f"{name}_{self.nsem}"))

    def sb(self, name, shape, dt, stack=None):
        return (stack or self.stack).enter_context(self.nc.sbuf_tensor(name, list(shape), dt))

    def ps(self, name, shape, dt=F32, stack=None):
        return (stack or self.stack).enter_context(self.nc.psum_tensor(name, list(shape), dt))

    def _wait(self, e, tick):
        sem, val, src = tick
        kn = self.known[e]
        if kn.get(id(sem), 0) >= val:
            return
        self.eng[e].wait_ge(sem, val)
        kn[id(sem)] = val
        if src in self.sems:
            for s in self.sems[src]:
                if s is sem:
                    break
                kn[id(s)] = SEM_LIM

    def _deps(self, e, reads, writes, dma=False):
        for d in reads:
            if d.w is not None:
                self._wait(e, d.w)
            if d.excl:
                for r in d.rs:
                    if r[2] != e:
                        self._wait(e, r)
        for d in writes:
            if d.w is not None and (dma or d.w[2] != e or e != "pe"):
                self._wait(e, d.w)
            for r in d.rs:
                self._wait(e, r)

    def op(self, e, fn, reads=(), writes=()):
        self._deps(e, reads, writes)
        c = self.cnt[e]
        if c % SEM_LIM == 0:
            self.sems[e].append(self.new_sem(e))
        sem = self.sems[e][-1]
        val = c % SEM_LIM + 1
        self.cnt[e] = c + 1
        ins = fn(self.eng[e])
        ins.then_inc(sem, 1)
        self.ninst += 1
        tick = (sem, val, e)
        for d in reads:
            d.rs.append(tick)
        for d in writes:
            d.w = tick
            d.rs = []
        return tick

    def dma(self, e, out, in_, reads=(), writes=(), **kw):
        self._deps(e, reads, writes, dma=True)
        if writes:
            d0 = writes[0]
            if d0.wsem is None:
                d0.wsem = self.new_sem("dw")
            d0.wcnt += 16
            tick = (d0.wsem, d0.wcnt, "dma")
        else:
            d0 = reads[0]
            if d0.rsem is None:
                d0.rsem = self.new_sem("dr")
            d0.rcnt += 16
            tick = (d0.rsem, d0.rcnt, "dma")
        ins = self.eng[e].dma_start(out=out, in_=in_, **kw)
        ins.then_inc(tick[0], 16)
        self.ninst += 1
        self.dticks.append(tick)
        for d in reads:
            d.rs.append(tick)
        for d in writes:
            d.w = tick
            d.rs = []
        return tick

    def wait_all(self, e, deps):
        for d in deps:
            if d.w is not None:
                self._wait(e, d.w)
            for r in d.rs:
                self._wait(e, r)

    def drain_dma(self, e, keep=0):
        n = len(self.dticks) - keep
        for t in self.dticks[:max(n, 0)]:
            self._wait(e, t)
        self.dticks = self.dticks[max(n, 0):]

    def barrier(self):
        for e in self.eng:
            self.wait_all(e, self.deps)
        for d in self.deps:
            d.rs = d.rs[-8:]


def _mod_prologue(S, nc, cT, wmod, bmod, ncols, modb, dmodb):
    with contextlib.ExitStack() as st:
        ct = S.sb("m_ct", [128, 16, 2], F32, st); dct = S.dep()
        cs = S.sb("m_cs", [128, 16, 2], F32, st); dcs = S.dep()
        rep = S.sb("m_rep", [128, 2, 16, 128], F32, st); drep = S.dep()
        ones1 = S.sb("m_ones", [1, 128], F32, st); dones = S.dep()
        bm = S.sb("m_bm", [1, ncols], F32, st); dbm = S.dep()
        wt = [S.sb(f"m_wt{i}", [128, 16, 512], F32, st) for i in range(2)]
        dwt = [S.dep() for _ in range(2)]
        pm = [S.ps(f"m_pm{i}", [128, 512], F32, st) for i in range(2)]
        dpm = [S.pdep() for _ in range(2)]
        S.dma("sp", ct[:], cT[:, :, :], writes=[dct])
        S.dma("sp", bm[:], bmod[:, :], writes=[dbm])
        S.op("dve", lambda e: e.memset(ones1[:], 1.0), writes=[dones])
        S.op("act", lambda e: e.activation(out=cs[:], in_=ct[:], func=AF.Silu), reads=[dct], writes=[dcs])
        for wh in range(2):
            for kc in range(16):
                S.op("dve", lambda e: e.tensor_copy(out=rep[:, wh, kc, :], in_=cs[:, kc, wh:wh + 1].to_broadcast([128, 128])),
                     reads=[dcs], writes=[drep])
        ng = ncols // 512
        wv = wmod.rearrange("(kc p) n -> p kc n", p=128)
        for g in range(ng):
            b = g % 2
            for h4 in range(4):
                S.dma("sp", wt[b][:, h4 * 4:(h4 + 1) * 4, :], wv[:, h4 * 4:(h4 + 1) * 4, g * 512:(g + 1) * 512], writes=[dwt[b]])
            for wh in range(2):
                for kc in range(16):
                    S.op("pe", lambda e: e.matmul(pm[wh][:], lhsT=rep[:, wh, kc, :], rhs=wt[b][:, kc, :], start=(kc == 0), stop=False),
                         reads=[drep, dwt[b]], writes=[dpm[wh]])
                S.op("pe", lambda e: e.matmul(pm[wh][:], lhsT=ones1[:], rhs=bm[:, g * 512:(g + 1) * 512], start=False, stop=True),
                     reads=[dones, dbm], writes=[dpm[wh]])
                S.op("act", lambda e: e.copy(out=modb[wh][:, g * 512:(g + 1) * 512], in_=pm[wh][:]), reads=[dpm[wh]], writes=[dmodb])
        S.barrier()


def _make_ident(S, st, n, dt, name):
    f = S.sb(name + "_f", [n, n], F32, st)
    df = S.dep()
    S.op("pool", lambda e: e.memset(f[:], 1.0), writes=[df])
    S.op("pool", lambda e: e.affine_select(out=f[:], in_=f[:], pattern=[[-1, n]], compare_op=ALU.is_equal, fill=0.0,
                                           base=0, channel_multiplier=1), reads=[df], writes=[df])
    if dt == F32:
        return f, df
    b = S.sb(name + "_b", [n, n], dt, st)
    db = S.dep()
    S.op("dve", lambda e: e.tensor_copy(out=b[:], in_=f[:]), reads=[df], writes=[db])
    return b, db


def _rope(S, src, dst, cos, sin, G, P, t1, t2, reads, writes, dtmp):
    sv = src.rearrange("p g (i t) -> p g i t", t=2)
    dv = dst.rearrange("p g (i t) -> p g i t", t=2)
    cb = cos.unsqueeze(1).to_broadcast([128, G, P])
    sbb = sin.unsqueeze(1).to_broadcast([128, G, P])
    S.op("dve", lambda e: e.tensor_tensor(out=t1, in0=sv[:, :, :, 0], in1=cb, op=ALU.mult), reads=reads, writes=[dtmp])
    S.op("dve", lambda e: e.tensor_tensor(out=t2, in0=sv[:, :, :, 1], in1=sbb, op=ALU.mult), reads=reads, writes=[dtmp])
    S.op("dve", lambda e: e.tensor_tensor(out=dv[:, :, :, 0], in0=t1, in1=t2, op=ALU.subtract), reads=[dtmp], writes=writes)
    S.op("dve", lambda e: e.tensor_tensor(out=t1, in0=sv[:, :, :, 0], in1=sbb, op=ALU.mult), reads=reads + [dtmp], writes=[dtmp])
    S.op("dve", lambda e: e.tensor_tensor(out=t2, in0=sv[:, :, :, 1], in1=cb, op=ALU.mult), reads=reads + [dtmp], writes=[dtmp])
    S.op("dve", lambda e: e.tensor_tensor(out=dv[:, :, :, 1], in0=t1, in1=t2, op=ALU.add), reads=[dtmp], writes=writes)


def _rstd(S, ss, n, scale, eps, reads, dss):
    S.op("dve", lambda e: e.tensor_scalar(out=ss, in0=ss, scalar1=scale, scalar2=eps, op0=ALU.mult, op1=ALU.add),
         reads=reads + [dss], writes=[dss])
    S.op("act", lambda e: e.activation(out=ss, in_=ss, func=AF.Sqrt), reads=[dss], writes=[dss])
    S.op("dve", lambda e: e.reciprocal(out=ss, in_=ss), reads=[dss], writes=[dss])


def _phase_A1(S, nc, T):
    with contextlib.ExitStack() as st:
        identb, did = _make_ident(S, st, 128, BF16, "a1id")
        modb = [S.sb(f"modb{i}", [128, 4096], F32, st) for i in range(2)]
        dmodb = S.dep()
        _mod_prologue(S, nc, T["cT"], T["wmod"], T["bmod"], 4096, modb, dmodb)
        with contextlib.ExitStack() as st2:
            gb = S.sb("gb", [128, 2048], F32, st2); dgb = S.dep()
            S.dma("sp", gb[:], T["gpre"][0:1, :].to_broadcast([128, 2048]), writes=[dgb])
            for wh in range(2):
                S.op("dve", lambda e: e.scalar_tensor_tensor(out=modb[wh][:, 2048:4096], in0=modb[wh][:, 2048:4096], scalar=1.0,
                                                             in1=gb[:], op0=ALU.add, op1=ALU.mult), reads=[dmodb, dgb], writes=[dmodb])
            S.barrier()
        gqn = S.sb("gqn_sb", [128, 2, 128], F32, st); dgqn = S.dep()
        S.dma("sp", gqn[:].rearrange("p a b -> p (a b)"), T["gqn"][0:1, :].to_broadcast([128, 256]), writes=[dgqn])
        wfm = S.sb("wfm", [128, 16, 704], BF16, st); dwfm = S.dep()
        wtm = S.sb("wtm", [128, 16, 2304], BF16, st); dwtm = S.dep()
        wfv = T["w_fm"].rearrange("(kc p) n -> p kc n", p=128)
        wtv = T["w_tm"].rearrange("(kc p) n -> p kc n", p=128)
        for k4 in range(8):
            S.dma("pool", wfm[:, k4 * 2:(k4 + 1) * 2, :], wfv[:, k4 * 2:(k4 + 1) * 2, :], writes=[dwfm])
        for k4 in range(16):
            S.dma("pool", wtm[:, k4:(k4 + 1), :], wtv[:, k4:(k4 + 1), :], writes=[dwtm])
        xb = [S.sb(f"xb{i}", [128, 2048], F32, st) for i in range(2)]; dxb = [S.dep() for _ in range(2)]
        rpb = [S.sb(f"rp{i}", [128, 192], F32, st) for i in range(2)]; drp = [S.dep() for _ in range(2)]
        hb = S.sb("hb", [128, 2048], BF16, st); dhb = S.dep()
        ss = S.sb("ss", [128, 4], F32, st); dss = S.dep()
        hTb = [S.sb(f"hTb{i}", [128, 16, 256], BF16, st) for i in range(2)]; dhT = [S.dep() for _ in range(2)]
        rkst = S.sb("rkst", [64, 8, 256], F32, st); drkst = S.dep()
        wast = S.sb("wast", [96, 2, 256], F32, st); dwast = S.dep()
        stf = S.sb("stf", [128, 2304], F32, st); dstf = [S.dep() for _ in range(5)]
        gst = S.sb("gst", [128, 768], BF16, st); dgst = S.dep()
        qkd = S.sb("qkd", [128, 8, 64], BF16, st); dqkd = S.dep()
        qkTd = S.sb("qkTd", [64, 8, 128], BF16, st); dqkTd = S.dep()
        qkg = S.sb("qkg", [128, 3, 128], BF16, st); dqkg = S.dep()
        qkgn = S.sb("qkgn", [128, 3, 128], F32, st); dqkgn = S.dep()
        qkTg = S.sb("qkTg", [128, 3, 128], BF16, st); dqkTg = S.dep()
        vdst = S.sb("vdst", [128, 2, 129], BF16, st); dvd = S.dep()
        vgst = S.sb("vgst", [128, 129], BF16, st); dvg = S.dep()
        rt1 = S.sb("rt1", [128, 8, 32], F32, st); rt2 = S.sb("rt2", [128, 8, 32], F32, st); drt = S.dep()
        sqg = S.sb("sqg", [128, 3, 128], F32, st); dsqg = S.dep()
        ssg = S.sb("ssg", [128, 4], F32, st); dssg = S.dep()
        pf = S.ps("pf", [64, 8, 256], F32, st); dpf = S.pdep()
        pw = S.ps("pw", [96, 2, 256], F32, st); dpw = S.pdep()
        pg = [S.ps(f"pg{i}", [128, 512], F32, st) for i in range(2)]; dpg = [S.pdep() for _ in range(2)]
        ptb = S.ps("ptb", [128, 8, 128], BF16, st); dptb = S.pdep()
        S.op("dve", lambda e: e.memset(vdst[:], 1.0), writes=[dvd])
        S.op("dve", lambda e: e.memset(vgst[:], 1.0), writes=[dvg])

        s_rk = T["s_rk"].rearrange("g c t -> c g t")
        s_wa = T["s_wa"].rearrange("j c t -> c j t")
        s_dqk = T["s_dqkT"].rearrange("g c t -> c g t")
        s_gqk = T["s_gqkT"].rearrange("g c t -> c g t")
        hTo = T["hT"].rearrange("kc p t -> p kc t")
        import os
        nblk = int(os.environ.get('A1_BLOCKS', NBLK))

        def prep(blk):
            wh = 0 if blk == 0 else 1
            S.drain_dma("sp", keep=12)
            hT = hTb[blk % 2]; dh = dhT[blk % 2]
            for ti in range(2):
                t = 2 * blk + ti
                tok0 = t * 128
                xt = xb[t % 2]; dx = dxb[t % 2]; rp = rpb[t % 2]; dr = drp[t % 2]
                S.dma("sp", xt[:], T["xf"][tok0:tok0 + 128, :], writes=[dx])
                S.op("act", lambda e: e.activation(out=hb[:], in_=xt[:], func=AF.Square, accum_out=ss[:, 0:1]),
                     reads=[dx], writes=[dhb, dss])
                _rstd(S, ss[:, 0:1], 1, 1.0 / D, EPS, [], dss)
                S.op("dve", lambda e: e.scalar_tensor_tensor(out=xt[:], in0=xt[:], scalar=ss[:, 0:1], in1=modb[wh][:, 2048:4096],
                                                             op0=ALU.mult, op1=ALU.mult), reads=[dx, dss, dmodb], writes=[dx])
                S.op("dve", lambda e: e.tensor_tensor(out=hb[:], in0=xt[:], in1=modb[wh][:, 0:2048], op=ALU.add),
                     reads=[dx, dmodb], writes=[dhb])
                for half in range(2):
                    for j in range(8):
                        kc = half * 8 + j
                        S.op("pe", lambda e: e.transpose(out=ptb[:, j, :], in_=hb[:, kc * 128:(kc + 1) * 128], identity=identb[:]),
                             reads=[dhb, did], writes=[dptb])
                    S.op("act", lambda e: e.copy(out=hT[:, half * 8:(half + 1) * 8, ti * 128:(ti + 1) * 128], in_=ptb[:]),
                         reads=[dptb], writes=[dh])
            S.dma("sp", hTo[:, :, blk * 256:(blk + 1) * 256], hT[:], reads=[dh])

        def mm(blk):
            hT = hTb[blk % 2]; dh = dhT[blk % 2]
            for g in range(8):
                for kc in range(16):
                    S.op("pe", lambda e: e.matmul(pf[:, g, :], lhsT=wfm[:, kc, g * 64:(g + 1) * 64], rhs=hT[:, kc, :],
                                                  start=(kc == 0), stop=(kc == 15)), reads=[dwfm, dh], writes=[dpf])
            S.op("act", lambda e: e.copy(out=rkst[:], in_=pf[:]), reads=[dpf], writes=[drkst])
            S.dma("sp", s_rk[:, :, blk * 256:(blk + 1) * 256], rkst[:], reads=[drkst])
            for j in range(2):
                for kc in range(16):
                    S.op("pe", lambda e: e.matmul(pw[:, j, :], lhsT=wfm[:, kc, 512 + j * 96:512 + (j + 1) * 96], rhs=hT[:, kc, :],
                                                  start=(kc == 0), stop=(kc == 15)), reads=[dwfm, dh], writes=[dpw])
            S.op("act", lambda e: e.activation(out=wast[:, 0, :], in_=pw[:, 0, :], func=AF.Tanh), reads=[dpw], writes=[dwast])
            S.op("act", lambda e: e.copy(out=wast[:, 1, :], in_=pw[:, 1, :]), reads=[dpw], writes=[dwast])
            S.dma("sp", s_wa[:, :, blk * 256:(blk + 1) * 256], wast[:], reads=[dwast])
            for ti in range(2):
                t = 2 * blk + ti
                tok0 = t * 128
                rp = rpb[t % 2]; dr = drp[t % 2]
                S.dma("sp", rp[:], T["rope"][tok0:tok0 + 128, :], writes=[dr])
                for gi in range(5):
                    ncol = 512 if gi < 4 else 256
                    p = pg[gi % 2]; dp = dpg[gi % 2]
                    for kc in range(16):
                        S.op("pe", lambda e: e.matmul(p[:, 0:ncol], lhsT=hT[:, kc, ti * 128:(ti + 1) * 128],
                                                      rhs=wtm[:, kc, gi * 512:gi * 512 + ncol], start=(kc == 0), stop=(kc == 15)),
                             reads=[dwtm, dh], writes=[dp])
                    S.op("act", lambda e: e.copy(out=stf[:, gi * 512:gi * 512 + ncol], in_=p[:, 0:ncol]), reads=[dp], writes=[dstf[gi]])
                S.dma("sp", T["s_v"][tok0:tok0 + 128, :], stf[:, 0:256], reads=[dstf[0]])
                S.op("act", lambda e: e.activation(out=gst[:, 0:256], in_=stf[:, 256:512], func=AF.Silu), reads=[dstf[0]], writes=[dgst])
                _rope(S, stf[:, 512:1024].rearrange("p (g d) -> p g d", g=8), qkd[:], rp[:, 0:32], rp[:, 32:64], 8, 32,
                      rt1[:], rt2[:], [dstf[1], dr], [dqkd], drt)
                for g in range(8):
                    S.op("pe", lambda e: e.transpose(out=ptb[0:64, g, :], in_=qkd[:, g, :], identity=identb[:]),
                         reads=[dqkd, did], writes=[dptb])
                S.op("act", lambda e: e.copy(out=qkTd[:], in_=ptb[0:64, :, :]), reads=[dptb], writes=[dqkTd])
                S.dma("sp", s_dqk[:, :, tok0:tok0 + 128], qkTd[:], reads=[dqkTd])
                S.op("act", lambda e: e.copy(out=vdst[:, :, 0:128], in_=stf[:, 1024:1280].rearrange("p (h d) -> p h d", h=2)),
                     reads=[dstf[2]], writes=[dvd])
                S.dma("sp", T["s_dv"][tok0:tok0 + 128, :, :], vdst[:], reads=[dvd])
                S.op("act", lambda e: e.activation(out=gst[:, 256:512], in_=stf[:, 1280:1536], func=AF.Silu), reads=[dstf[2]], writes=[dgst])
                src3 = stf[:, 1536:1920].rearrange("p (g d) -> p g d", g=3)
                S.op("dve", lambda e: e.tensor_tensor(out=sqg[:], in0=src3, in1=src3, op=ALU.mult), reads=[dstf[3]], writes=[dsqg])
                S.op("dve", lambda e: e.tensor_reduce(out=ssg[:, 0:3], in_=sqg[:], axis=AX.X, op=ALU.add), reads=[dsqg], writes=[dssg])
                _rstd(S, ssg[:, 0:3], 3, 1.0 / 128, EPS, [], dssg)
                for i in range(3):
                    S.op("dve", lambda e: e.scalar_tensor_tensor(out=qkgn[:, i, :], in0=src3[:, i, :], scalar=ssg[:, i:i + 1],
                                                                 in1=gqn[:, (0 if i < 2 else 1), :], op0=ALU.mult, op1=ALU.mult),
                         reads=[dstf[3], dssg, dgqn], writes=[dqkgn])
                _rope(S, qkgn[:], qkg[:], rp[:, 64:128], rp[:, 128:192], 3, 64,
                      rt1[:].rearrange("p a b -> p (a b)")[:, 0:192].rearrange("p (a b) -> p a b", a=3),
                      rt2[:].rearrange("p a b -> p (a b)")[:, 0:192].rearrange("p (a b) -> p a b", a=3),
                      [dqkgn, dr], [dqkg], drt)
                for g in range(3):
                    S.op("pe", lambda e: e.transpose(out=ptb[:, g, :], in_=qkg[:, g, :], identity=identb[:]),
                         reads=[dqkg, did], writes=[dptb])
                S.op("act", lambda e: e.copy(out=qkTg[:], in_=ptb[:, 0:3, :]), reads=[dptb], writes=[dqkTg])
                S.dma("sp", s_gqk[:, :, tok0:tok0 + 128], qkTg[:], reads=[dqkTg])
                S.op("act", lambda e: e.copy(out=vgst[:, 0:128], in_=stf[:, 1920:2048]), reads=[dstf[3]], writes=[dvg])
                S.dma("sp", T["s_gv"][tok0:tok0 + 128, :], vgst[:], reads=[dvg])
                S.op("act", lambda e: e.activation(out=gst[:, 512:768], in_=stf[:, 2048:2304], func=AF.Silu), reads=[dstf[4]], writes=[dgst])
                S.dma("sp", T["s_gate"][tok0:tok0 + 128, :], gst[:], reads=[dgst])

        if nblk > 0:
            prep(0)
        for blk in range(nblk):
            if blk + 1 < nblk:
                prep(blk + 1)
            mm(blk)
        S.barrier()


def _phase_A2(S, nc, T):
    with contextlib.ExitStack() as st:
        KTd = S.sb("KTd", [64, 4, NT], BF16, st); dKTd = S.dep()
        Vd = S.sb("Vd", [128, NTILE, 2, 129], BF16, st); dVd = S.dep()
        KTg = S.sb("KTg", [128, NT], BF16, st); dKTg = S.dep()
        Vg = S.sb("Vg", [128, NTILE, 129], BF16, st); dVg = S.dep()
        for g in range(4):
            S.dma("sp", KTd[:, g, :], T["s_dqkT"][4 + g, :, :], writes=[dKTd])
        S.dma("sp", KTg[:], T["s_gqkT"][2, :, :], writes=[dKTg])
        for c in range(2):
            S.dma("sp", Vd[:, c * 17:(c + 1) * 17, :, :], T["s_dv"].rearrange("(t p) h d -> p t h d", p=128)[:, c * 17:(c + 1) * 17, :, :], writes=[dVd])
        S.dma("sp", Vg[:], T["s_gv"].rearrange("(t p) d -> p t d", p=128), writes=[dVg])
        lamp = S.sb("lamp_sb", [128, 4, 64], F32, st); dlam = S.dep()
        lam = S.sb("lam", [128, 8], F32, st)
        S.dma("sp", lamp[:].rearrange("p a b -> p (a b)"), T["lamp"][0:1, :].to_broadcast([128, 256]), writes=[dlam])
        S.dma("sp", lam[:, 4:5], T["lami"][0:1, 0:1].to_broadcast([128, 1]), writes=[dlam])
        prod = S.sb("lprod", [128, 2, 64], F32, st)
        S.op("dve", lambda e: e.tensor_tensor(out=prod[:, 0, :], in0=lamp[:, 0, :], in1=lamp[:, 1, :], op=ALU.mult), reads=[dlam], writes=[dlam])
        S.op("dve", lambda e: e.tensor_tensor(out=prod[:, 1, :], in0=lamp[:, 2, :], in1=lamp[:, 3, :], op=ALU.mult), reads=[dlam], writes=[dlam])
        S.op("dve", lambda e: e.tensor_reduce(out=lam[:, 0:2], in_=prod[:], axis=AX.X, op=ALU.add), reads=[dlam], writes=[dlam])
        S.op("act", lambda e: e.activation(out=lam[:, 2:4], in_=lam[:, 0:2], func=AF.Exp), reads=[dlam], writes=[dlam])
        S.op("dve", lambda e: e.tensor_tensor(out=lam[:, 5:6], in0=lam[:, 2:3], in1=lam[:, 3:4], op=ALU.subtract), reads=[dlam], writes=[dlam])
        S.op("dve", lambda e: e.tensor_tensor(out=lam[:, 6:7], in0=lam[:, 5:6], in1=lam[:, 4:5], op=ALU.add), reads=[dlam], writes=[dlam])
        gsub = S.sb("gsub", [128, 128], F32, st); dgsub = S.dep()
        S.dma("sp", gsub[:], T["subg"][0:1, :].to_broadcast([128, 128]), writes=[dgsub])
        S.op("dve", lambda e: e.tensor_scalar(out=lam[:, 7:8], in0=lam[:, 4:5], scalar1=-1.0, scalar2=1.0, op0=ALU.mult, op1=ALU.add),
             reads=[dlam], writes=[dlam])
        S.op("dve", lambda e: e.tensor_scalar(out=gsub[:], in0=gsub[:], scalar1=lam[:, 7:8], scalar2=None, op0=ALU.mult),
             reads=[dlam, dgsub], writes=[dgsub])

        QTd = [S.sb(f"QTd{i}", [64, 4, 256], BF16, st) for i in range(2)]; dQd = [S.dep() for _ in range(2)]
        QTg = [S.sb(f"QTg{i}", [128, 2, 256], BF16, st) for i in range(2)]; dQg = [S.dep() for _ in range(2)]
        gt = [S.sb(f"gt{i}", [128, 2, 768], BF16, st) for i in range(2)]; dgt = [S.dep() for _ in range(2)]
        PT = [S.sb(f"PT{i}", [128, 2, 256], BF16, st) for i in range(3)]; dPT = [S.dep() for _ in range(3)]
        pss = [S.ps(f"pss{i}", [128, 2, 256], F32, st) for i in range(3)]; dpss = [S.pdep() for _ in range(3)]
        acc = [[S.ps(f"acc{j}{q}", [128, 512], F32, st) for q in range(2)] for j in range(2)]
        dacc = [[S.pdep() for q in range(2)] for j in range(2)]
        rs = S.sb("rs", [128, 8], F32, st); drs = S.dep()
        o0 = S.sb("o0", [128, 128], F32, st); do0 = S.dep()
        dd = S.sb("dd", [128, 128], F32, st); ddd = S.dep()
        junk = S.sb("junk", [128, 128], F32, st); djunk = S.dep()
        brs = [S.sb(f"brs{i}", [128, 512], BF16, st) for i in range(2)]; dbrs = [S.dep() for _ in range(2)]
        s_dqk = T["s_dqkT"].rearrange("g c t -> c g t")
        s_gqk = T["s_gqkT"].rearrange("g c t -> c g t")
        steps = []
        for qb in range(NBLK):
            kts = list(range(2)) if qb == 0 else list(range(NTILE))
            for u in range(3):
                for ki, kt in enumerate(kts):
                    steps.append((qb, u, ki, kt, len(kts)))

        def emit_S(i):
            qb, u, ki, kt, nk = steps[i]
            b2 = qb % 2
            q0 = qb * 256
            if u == 0 and ki == 0:
                S.drain_dma("sp", keep=8)
                S.dma("sp", QTd[b2][:], s_dqk[:, 0:4, q0:q0 + 256], writes=[dQd[b2]])
                S.dma("sp", QTg[b2][:], s_gqk[:, 0:2, q0:q0 + 256], writes=[dQg[b2]])
                S.dma("sp", gt[b2][:], T["s_gate"].rearrange("(t p) n -> p t n", p=128)[:, 2 * qb:2 * qb + 2, :], writes=[dgt[b2]])
            ps_ = pss[i % 3]; dps_ = dpss[i % 3]
            for j in range(2):
                if u < 2:
                    S.op("pe", lambda e: e.matmul(ps_[:, j, :], lhsT=KTd[:, u * 2 + j, kt * 128:(kt + 1) * 128], rhs=QTd[b2][:, u * 2 + j, :],
                                                  start=True, stop=True), reads=[dKTd, dQd[b2]], writes=[dps_])
                else:
                    S.op("pe", lambda e: e.matmul(ps_[:, j, :], lhsT=KTg[:, kt * 128:(kt + 1) * 128], rhs=QTg[b2][:, j, :],
                                                  start=True, stop=True), reads=[dKTg, dQg[b2]], writes=[dps_])

        def emit_EPV(i):
            qb, u, ki, kt, nk = steps[i]
            ps_ = pss[i % 3]; dps_ = dpss[i % 3]; pt_ = PT[i % 3]; dpt_ = dPT[i % 3]
            sc = (64 ** -0.5) if u < 2 else (128 ** -0.5)
            S.op("act", lambda e: e.activation(out=pt_[:], in_=ps_[:], func=AF.Exp, scale=sc), reads=[dps_], writes=[dpt_])
            for j in range(2):
                for q in range(2):
                    rhs = Vd[:, kt, u, :] if u < 2 else Vg[:, kt, :]
                    S.op("pe", lambda e: e.matmul(acc[j][q][:, 0:129], lhsT=pt_[:, j, q * 128:(q + 1) * 128], rhs=rhs,
                                                  start=(ki == 0), stop=(ki == nk - 1)),
                         reads=[dpt_, dVd if u < 2 else dVg], writes=[dacc[j][q]])

        def finalize(qb, u):
            b2 = qb % 2
            q0 = qb * 256
            for q in range(2):
                bs = brs[q]; dbs = dbrs[q]
                if u < 2:
                    S.op("dve", lambda e: e.reciprocal(out=rs[:, 0:1], in_=acc[0][q][:, 128:129]), reads=[dacc[0][q]], writes=[drs])
                    S.op("dve", lambda e: e.reciprocal(out=rs[:, 1:2], in_=acc[1][q][:, 128:129]), reads=[dacc[1][q]], writes=[drs])
                    S.op("dve", lambda e: e.scalar_tensor_tensor(out=rs[:, 2:3], in0=rs[:, 1:2], scalar=-1.0, in1=lam[:, 6:7],
                                                                 op0=ALU.mult, op1=ALU.mult), reads=[drs, dlam], writes=[drs])
                    S.op("dve", lambda e: e.tensor_scalar(out=o0[:], in0=acc[0][q][:, 0:128], scalar1=rs[:, 0:1], scalar2=None, op0=ALU.mult),
                         reads=[dacc[0][q], drs], writes=[do0])
                    S.op("dve", lambda e: e.scalar_tensor_tensor(out=dd[:], in0=acc[1][q][:, 0:128], scalar=rs[:, 2:3], in1=o0[:],
                                                                 op0=ALU.mult, op1=ALU.add), reads=[dacc[1][q], drs, do0], writes=[ddd])
                    S.op("act", lambda e: e.activation(out=junk[:], in_=dd[:], func=AF.Square, accum_out=rs[:, 3:4]),
                         reads=[ddd], writes=[djunk, drs])
                    _rstd(S, rs[:, 3:4], 1, 1.0 / 128, EPS, [], drs)
                    S.op("dve", lambda e: e.scalar_tensor_tensor(out=o0[:], in0=dd[:], scalar=rs[:, 3:4], in1=gsub[:],
                                                                 op0=ALU.mult, op1=ALU.mult), reads=[ddd, drs, dgsub], writes=[do0])
                    S.op("dve", lambda e: e.tensor_tensor(out=bs[:, u * 128:(u + 1) * 128], in0=o0[:],
                                                          in1=gt[b2][:, q, 256 + u * 128:256 + (u + 1) * 128], op=ALU.mult),
                         reads=[do0, dgt[b2]], writes=[dbs])
                else:
                    for j in range(2):
                        S.op("dve", lambda e: e.reciprocal(out=rs[:, 4 + j:5 + j], in_=acc[j][q][:, 128:129]), reads=[dacc[j][q]], writes=[drs])
                        S.op("dve", lambda e: e.scalar_tensor_tensor(out=bs[:, 256 + j * 128:256 + (j + 1) * 128], in0=acc[j][q][:, 0:128],
                                                                     scalar=rs[:, 4 + j:5 + j], in1=gt[b2][:, q, 512 + j * 128:512 + (j + 1) * 128],
                                                                     op0=ALU.mult, op1=ALU.mult), reads=[dacc[j][q], drs, dgt[b2]], writes=[dbs])
                    S.dma("sp", T["br_att"][q0 + q * 128:q0 + (q + 1) * 128, :], bs[:], reads=[dbs])

        emit_S(0)
        emit_S(1)
        for i in range(len(steps)):
            if i + 2 < len(steps):
                emit_S(i + 2)
            emit_EPV(i)
            qb, u, ki, kt, nk = steps[i]
            if ki == nk - 1:
                finalize(qb, u)
        S.barrier()


def _phase_A3(S, nc, T):
    NCH = NT // 64
    with contextlib.ExitStack() as st:
        identf, didf = _make_ident(S, st, 64, F32, "a3id")
        ones64 = S.sb("ones64", [64, 64], F32, st); dconst = S.dep()
        S.op("dve", lambda e: e.memset(ones64[:], 1.0), writes=[dconst])
        mk = S.sb("mk", [64, 4, 64], F32, st)
        S.op("pool", lambda e: e.memset(mk[:], 1.0), writes=[dconst])
        for i, (stp, cm, cmp_) in enumerate([(1, -1, ALU.is_gt), (1, -1, ALU.is_ge), (-1, 1, ALU.is_gt), (-1, 1, ALU.is_ge)]):
            S.op("pool", lambda e: e.affine_select(out=mk[:, i, :], in_=mk[:, i, :], pattern=[[stp, 64]], compare_op=cmp_, fill=0.0,
                                                   base=0, channel_multiplier=cm), reads=[dconst], writes=[dconst])
        MSK = S.sb("MSK", [64, 2, 320], F32, st)
        for e_ in range(2):
            order = [0, 1, 0, 1, 2] if e_ == 0 else [2, 3, 2, 3, 0]
            for j, m in enumerate(order):
                S.op("dve", lambda e: e.tensor_copy(out=MSK[:, e_, j * 64:(j + 1) * 64], in_=mk[:, m, :]), reads=[dconst], writes=[dconst])
        rmask = S.sb("rmask", [64, 16, 64], F32, st)
        S.op("dve", lambda e: e.memset(rmask[:], 1.0), writes=[dconst])
        S.op("dve", lambda e: e.memset(rmask[:, :, 0:1], 0.0), reads=[dconst], writes=[dconst])
        prm = S.sb("prm", [64, 7, 4], F32, st)
        S.dma("sp", prm[:], T["rprm"][:, :, :], writes=[dconst])
        omk = S.sb("omk", [64, 4], F32, st)
        S.op("dve", lambda e: e.tensor_scalar(out=omk[:], in0=prm[:, 5, :], scalar1=-1.0, scalar2=1.0, op0=ALU.mult, op1=ALU.add),
             reads=[dconst], writes=[dconst])
        wup = S.sb("wup_sb", [96, 2, 256], F32, st)
        aup = S.sb("aup_sb", [96, 2, 256], F32, st)
        S.dma("sp", wup[:], T["wup"].rearrange("e r c -> r e c"), writes=[dconst])
        S.dma("sp", aup[:], T["aup"].rearrange("e r c -> r e c"), writes=[dconst])
        gng = S.sb("gng", [64, 2, 256], F32, st)
        S.dma("sp", gng[:].rearrange("p a b -> p (a b)"), T["gn"][0:1, :].to_broadcast([64, 512]), writes=[dconst])

        yacc = S.sb("yacc", [64, NCH, 256], F32, st); dy = S.dep()
        bacc = S.sb("bacc", [64, NCH, 4], F32, st); dba = S.dep()
        ST = S.sb("ST", [64, 4, 64], F32, st); dST = S.dep()
        PSA = S.ps("PSA", [64, 8, 512], F32, st); dB = [S.pdep() for _ in range(8)]

        def t4(name):
            return S.sb(name, [64, 4, 256], F32, st), S.dep()
        rk_in = S.sb("rk_in", [64, 8, 256], F32, st); drk = S.dep()
        wa_in = S.sb("wa_in", [96, 2, 256], F32, st); dwa = S.dep()
        v_in, dv = t4("v_in")
        lw, dlw = t4("lw"); aa, daa = t4("aa"); kk, dkk = t4("kk"); kd, dkd = t4("kd"); be, dbe = t4("be")
        L, dL = t4("L"); Ld, dLd = t4("Ld"); tmp, dtmp = t4("tmp")
        E1, dE1 = t4("E1"); E2, dE2 = t4("E2"); E3, dE3 = t4("E3"); E4, dE4 = t4("E4")
        bt, dbt = t4("bt"); kt_, dkt = t4("kt_"); bh, dbh = be, dbe; kh, dkh = kd, dkd
        AR = S.sb("AR", [64, 4, 4, 2, 64], F32, st); dAR = S.dep()
        ltot = S.sb("ltot", [64, 4, 4], F32, st); dlt = S.dep()
        pc = S.sb("pc", [64, 4, 4], F32, st); dpc = S.dep()
        bkT = S.sb("bkT", [64, 4, 4, 2, 64], F32, st); dbk = S.dep()
        GM = S.sb("GM", [64, 4, 4, 320], F32, st); dGM = S.dep()
        XX0 = S.sb("XX0", [64, 16, 2, 64], F32, st); XX = [XX0, XX0]; dXX0 = S.dep(); dXX = [dXX0, dXX0]
        Tm = S.sb("Tm", [64, 16, 64], F32, st); dTm = S.dep()
        WT = S.sb("WT", [64, 4, 64], F32, st); dWT = S.dep()
        UT = S.sb("UT", [64, 4, 64], F32, st); dUT = S.dep()
        tS = S.sb("tS", [64, 4, 64], F32, st); dtS = S.dep()

        s_rk = T["s_rk"].rearrange("g c t -> c g t")
        s_wa = T["s_wa"].rearrange("j c t -> c j t")
        v4 = lambda ap: ap.rearrange("p h (c s) -> p h c s", s=64)
        fl = lambda ap: ap.rearrange("p h t -> p (h t)")

        for e_ in range(2):
            S.op("dve", lambda e: e.memset(ST[:], 0.0), reads=[dST], writes=[dST])
            blocks = list(range(NBLK)) if e_ == 0 else [0] + list(range(NBLK - 1, 0, -1))
            corder = [0, 1, 2, 3] if e_ == 0 else [3, 2, 1, 0]
            import os
            blocks = blocks[:int(os.environ.get('A3_BLOCKS', NBLK))]
            for blk in blocks:
                t0 = blk * 256
                S.drain_dma("sp", keep=4)
                S.dma("sp", rk_in[:], s_rk[:, :, t0:t0 + 256], writes=[drk])
                S.dma("sp", wa_in[:], s_wa[:, :, t0:t0 + 256], writes=[dwa])
                S.dma("sp", v_in[:], T["s_v"][t0:t0 + 256, :].rearrange("(c s) n -> s c n", s=64), writes=[dv])
                r_ = rk_in[:, 0:4, :]; k_ = rk_in[:, 4:8, :]
                pwp = PSA[:, 0:2, :].rearrange("p b (h t) -> p (b h) t", h=2)
                pap = PSA[:, 2:4, :].rearrange("p b (h t) -> p (b h) t", h=2)
                for h in range(4):
                    S.op("pe", lambda e: e.matmul(pwp[:, h, :], lhsT=wup[:, e_, h * 64:(h + 1) * 64], rhs=wa_in[:, 0, :], start=True, stop=True),
                         reads=[dconst, dwa], writes=[dB[h // 2]])
                for h in range(4):
                    S.op("pe", lambda e: e.matmul(pap[:, h, :], lhsT=aup[:, e_, h * 64:(h + 1) * 64], rhs=wa_in[:, 1, :], start=True, stop=True),
                         reads=[dconst, dwa], writes=[dB[2 + h // 2]])
                for h in range(4):
                    S.op("act", lambda e: e.activation(out=lw[:, h, :], in_=pwp[:, h, :], func=AF.Sigmoid, bias=prm[:, e_, h:h + 1]),
                         reads=[dB[h // 2], dconst], writes=[dlw])
                for h in range(4):
                    S.op("act", lambda e: e.activation(out=aa[:, h, :], in_=pap[:, h, :], func=AF.Sigmoid, bias=prm[:, 2 + e_, h:h + 1]),
                         reads=[dB[2 + h // 2], dconst], writes=[daa])
                S.op("dve", lambda e: e.tensor_scalar(out=lw[:], in0=lw[:], scalar1=-0.6065306597126334, scalar2=None, op0=ALU.mult),
                     reads=[dlw], writes=[dlw])
                S.op("dve", lambda e: e.tensor_tensor(out=kk[:], in0=k_, in1=prm[:, 4, :].unsqueeze(2).to_broadcast([64, 4, 256]), op=ALU.mult),
                     reads=[drk, dconst], writes=[dkk])
                S.op("dve", lambda e: e.tensor_tensor(out=tmp[:], in0=kk[:], in1=kk[:], op=ALU.mult), reads=[dkk], writes=[dtmp])
                for half in range(2):
                    S.op("pe", lambda e: e.matmul(PSA[:, 4 + half, :], lhsT=ones64[:], rhs=fl(tmp[:])[:, half * 512:(half + 1) * 512],
                                                  start=True, stop=True), reads=[dconst, dtmp], writes=[dB[4 + half]])
                S.op("dve", lambda e: e.tensor_scalar(out=fl(tmp[:]), in0=PSA[:, 4:6, :].rearrange("p b t -> p (b t)"), scalar1=1e-24, scalar2=None,
                                                      op0=ALU.max), reads=[dB[4], dB[5]], writes=[dtmp])
                S.op("act", lambda e: e.activation(out=tmp[:], in_=tmp[:], func=AF.Sqrt), reads=[dtmp], writes=[dtmp])
                S.op("dve", lambda e: e.reciprocal(out=tmp[:], in_=tmp[:]), reads=[dtmp], writes=[dtmp])
                S.op("dve", lambda e: e.tensor_tensor(out=kk[:], in0=kk[:], in1=tmp[:], op=ALU.mult), reads=[dkk, dtmp], writes=[dkk])
                S.op("dve", lambda e: e.tensor_tensor(out=tmp[:], in0=aa[:], in1=prm[:, 5, :].unsqueeze(2).to_broadcast([64, 4, 256]), op=ALU.mult),
                     reads=[daa, dconst], writes=[dtmp])
                S.op("dve", lambda e: e.tensor_tensor(out=tmp[:], in0=tmp[:], in1=omk[:].unsqueeze(2).to_broadcast([64, 4, 256]), op=ALU.add),
                     reads=[dtmp, dconst], writes=[dtmp])
                S.op("dve", lambda e: e.tensor_tensor(out=kd[:], in0=tmp[:], in1=k_, op=ALU.mult), reads=[dtmp, drk], writes=[dkd])
                S.op("pool", lambda e: e.tensor_tensor(out=be[:], in0=kk[:], in1=aa[:], op=ALU.mult), reads=[dkk, daa], writes=[dbe])
                S.op("dve", lambda e: e.tensor_tensor_scan(out=fl(L[:]), data0=rmask[:].rearrange("p a b -> p (a b)"), data1=fl(lw[:]),
                                                           initial=0.0, op0=ALU.mult, op1=ALU.add), reads=[dlw, dconst], writes=[dL])
                S.op("dve", lambda e: e.tensor_copy(out=ltot[:], in_=v4(L[:])[:, :, :, 63]), reads=[dL], writes=[dlt])
                if e_ == 0:
                    Lc, dLc = L, dL
                else:
                    S.op("dve", lambda e: e.tensor_tensor(out=tmp[:], in0=lw[:], in1=L[:], op=ALU.subtract), reads=[dlw, dL], writes=[dtmp])
                    S.op("dve", lambda e: e.tensor_tensor(out=v4(Ld[:]), in0=v4(tmp[:]), in1=ltot[:].unsqueeze(3).to_broadcast([64, 4, 4, 64]),
                                                          op=ALU.add), reads=[dtmp, dlt], writes=[dLd])
                    Lc, dLc = Ld, dLd
                S.op("pool", lambda e: e.tensor_tensor(out=E1[:], in0=Lc[:], in1=lw[:], op=ALU.subtract), reads=[dLc, dlw], writes=[dE1])
                S.op("act", lambda e: e.activation(out=E1[:], in_=E1[:], func=AF.Exp), reads=[dE1], writes=[dE1])
                S.op("act", lambda e: e.activation(out=E2[:], in_=Lc[:], func=AF.Exp), reads=[dLc], writes=[dE2])
                S.op("act", lambda e: e.activation(out=E3[:], in_=Lc[:], func=AF.Exp, scale=-1.0), reads=[dLc], writes=[dE3])
                S.op("dve", lambda e: e.tensor_tensor(out=v4(tmp[:]), in0=ltot[:].unsqueeze(3).to_broadcast([64, 4, 4, 64]), in1=v4(Lc[:]),
                                                      op=ALU.subtract), reads=[dlt, dLc], writes=[dtmp])
                S.op("act", lambda e: e.activation(out=E4[:], in_=tmp[:], func=AF.Exp), reads=[dtmp], writes=[dE4])
                S.op("act", lambda e: e.activation(out=pc[:], in_=ltot[:], func=AF.Exp), reads=[dlt], writes=[dpc])
                S.op("dve", lambda e: e.scalar_tensor_tensor(out=AR[:, :, :, 0, :], in0=v4(kk[:]), scalar=-1.0, in1=v4(E1[:]),
                                                             op0=ALU.mult, op1=ALU.mult), reads=[dkk, dE1], writes=[dAR])
                S.op("dve", lambda e: e.tensor_tensor(out=AR[:, :, :, 1, :], in0=v4(r_), in1=v4(E2[:]), op=ALU.mult), reads=[drk, dE2], writes=[dAR])
                S.op("dve", lambda e: e.tensor_tensor(out=kt_[:], in0=kd[:], in1=E3[:], op=ALU.mult), reads=[dkd, dE3], writes=[dkt])
                S.op("pool", lambda e: e.tensor_tensor(out=bt[:], in0=be[:], in1=E3[:], op=ALU.mult), reads=[dbe, dE3], writes=[dbt])
                S.op("dve", lambda e: e.tensor_tensor(out=tmp[:], in0=r_, in1=kd[:], op=ALU.mult), reads=[drk, dkd], writes=[dtmp])
                S.op("dve", lambda e: e.tensor_tensor(out=tmp[:], in0=tmp[:], in1=prm[:, 6, :].unsqueeze(2).to_broadcast([64, 4, 256]), op=ALU.mult),
                     reads=[dtmp, dconst], writes=[dtmp])
                pbo2 = PSA[:, 6, 0:32].rearrange("p (c h w) -> p c h w", h=4, w=2)
                pbo = pbo2[:, :, :, 0]
                for c in range(4):
                    for h in range(4):
                        S.op("pe", lambda e: e.matmul(pbo2[:, c, h, :], lhsT=tmp[:, h, c * 64:(c + 1) * 64], rhs=ones64[:, 0:2], start=True, stop=True),
                             reads=[dtmp, dconst], writes=[dB[6]])
                S.op("dve", lambda e: e.tensor_tensor(out=bh[:], in0=be[:], in1=E4[:], op=ALU.mult), reads=[dbe, dE4], writes=[dbh])
                S.op("pool", lambda e: e.tensor_tensor(out=kh[:], in0=kd[:], in1=E4[:], op=ALU.mult), reads=[dkd, dE4], writes=[dkh])
                bsl = bacc[:, blk * 4:(blk + 1) * 4, :]
                if e_ == 0:
                    S.op("dve", lambda e: e.tensor_copy(out=bsl, in_=pbo), reads=[dB[6]], writes=[dba])
                else:
                    S.op("dve", lambda e: e.tensor_tensor(out=bsl, in0=bsl, in1=pbo, op=ALU.add), reads=[dB[6], dba], writes=[dba])
                ptr = PSA[:, 0:4, :].rearrange("p b (i s) -> p (b i) s", s=64)
                for c in range(4):
                    for h in range(4):
                        for w_, (src, dsrc) in enumerate([(bh, dbh), (kh, dkh)]):
                            idx = (c * 4 + h) * 2 + w_
                            S.op("pe", lambda e: e.transpose(out=ptr[:, idx, :], in_=src[:, h, c * 64:(c + 1) * 64], identity=identf[:]),
                                 reads=[dsrc, didf], writes=[dB[idx // 8]])
                for half in range(2):
                    S.op("act", lambda e: e.copy(out=bkT[:, half * 2:(half + 1) * 2].rearrange("p c h w s -> p (c h w s)"),
                                                 in_=PSA[:, half * 2:(half + 1) * 2, :].rearrange("p b t -> p (b t)")),
                         reads=[dB[half * 2], dB[half * 2 + 1]], writes=[dbk])
                for c in range(0 if 'g' in os.environ.get('A3_SKIP', '') else 4):
                    bb = (c % 2) * 4
                    cs = slice(c * 64, (c + 1) * 64)
                    for h in range(4):
                        arh = AR[:, h, c, :, :].rearrange("p a s -> p (a s)")
                        S.op("pe", lambda e: e.matmul(PSA[:, bb + h, 0:128], lhsT=bt[:, h, cs], rhs=arh, start=True, stop=True),
                             reads=[dbt, dAR], writes=[dB[bb + h]])
                        S.op("pe", lambda e: e.matmul(PSA[:, bb + h, 128:256], lhsT=kt_[:, h, cs], rhs=arh, start=True, stop=True),
                             reads=[dkt, dAR], writes=[dB[bb + h]])
                        S.op("pe", lambda e: e.matmul(PSA[:, bb + h, 256:320], lhsT=AR[:, h, c, 0, :], rhs=bt[:, h, cs], start=True, stop=True),
                             reads=[dbt, dAR], writes=[dB[bb + h]])
                    S.op("dve", lambda e: e.tensor_tensor(out=GM[:, c, :, :], in0=PSA[:, bb:bb + 4, 0:320],
                                                          in1=MSK[:, e_, :].unsqueeze(1).to_broadcast([64, 4, 320]), op=ALU.mult),
                         reads=[dB[bb], dB[bb + 1], dB[bb + 2], dB[bb + 3], dconst], writes=[dGM])
                GMf = GM[:].rearrange("p c h n -> p (c h) n")
                S.op("dve", lambda e: e.tensor_tensor(out=Tm[:], in0=GMf[:, :, 0:64], in1=identf[:].unsqueeze(1).to_broadcast([64, 16, 64]), op=ALU.add),
                     reads=[dGM, didf], writes=[dTm])
                pxx = PSA[:, 0:4, :].rearrange("p b (i w s) -> p (b i) w s", w=2, s=64)
                ptm = PSA[:, 4:6, :].rearrange("p b (i s) -> p (b i) s", s=64)
                for it_ in range(0 if 'd' in os.environ.get('A3_SKIP', '') else 5):
                    pp = it_ % 2
                    for idx in range(16):
                        if it_ == 0:
                            Xc = GMf[:, idx, 0:64]; XTc = GMf[:, idx, 256:320]; dsrc = dGM
                        else:
                            Xc = XX[1 - pp][:, idx, 0, :]; XTc = XX[1 - pp][:, idx, 1, :]; dsrc = dXX[1 - pp]
                        S.op("pe", lambda e: e.matmul(pxx[:, idx, 0, :], lhsT=XTc, rhs=Xc, start=True, stop=True), reads=[dsrc], writes=[dB[idx // 4]])
                        S.op("pe", lambda e: e.matmul(pxx[:, idx, 1, :], lhsT=Xc, rhs=XTc, start=True, stop=True), reads=[dsrc], writes=[dB[idx // 4]])
                    S.op("act", lambda e: e.copy(out=XX[pp][:].rearrange("p i w s -> p (i w s)"), in_=PSA[:, 0:4, :].rearrange("p b t -> p (b t)")),
                         reads=[dB[0], dB[1], dB[2], dB[3]], writes=[dXX[pp]])
                    for idx in range(16):
                        S.op("pe", lambda e: e.matmul(ptm[:, idx, :], lhsT=XX[pp][:, idx, 1, :], rhs=Tm[:, idx, :], start=True, stop=True),
                             reads=[dXX[pp], dTm], writes=[dB[4 + idx // 8]])
                    S.op("dve", lambda e: e.tensor_tensor(out=Tm[:].rearrange("p i s -> p (i s)"), in0=Tm[:].rearrange("p i s -> p (i s)"),
                                                          in1=PSA[:, 4:6, :].rearrange("p b t -> p (b t)"), op=ALU.add),
                         reads=[dB[4], dB[5], dTm], writes=[dTm])
                pW = PSA[:, 6, 0:256].rearrange("p (h s) -> p h s", s=64)
                pU = PSA[:, 7, 0:256].rearrange("p (h s) -> p h s", s=64)
                pYS = PSA[:, 6, :].rearrange("p (h w s) -> p h w s", w=2, s=64)
                for c in ([] if 's' in os.environ.get('A3_SKIP', '') else corder):
                    gc = blk * 4 + c
                    for h in range(4):
                        vh = v_in[:, c, h * 64:(h + 1) * 64]
                        S.op("pe", lambda e: e.matmul(pW[:, h, :], lhsT=AR[:, h, c, 0, :], rhs=ST[:, h, :], start=True, stop=False),
                             reads=[dAR, dST], writes=[dB[6]])
                        S.op("pe", lambda e: e.matmul(pW[:, h, :], lhsT=GM[:, c, h, 128:192], rhs=vh, start=False, stop=True),
                             reads=[dGM, dv], writes=[dB[6]])
                    S.op("act", lambda e: e.copy(out=WT[:], in_=pW), reads=[dB[6]], writes=[dWT])
                    for h in range(4):
                        S.op("pe", lambda e: e.matmul(pU[:, h, :], lhsT=Tm[:, c * 4 + h, :], rhs=WT[:, h, :], start=True, stop=True),
                             reads=[dTm, dWT], writes=[dB[7]])
                    S.op("act", lambda e: e.copy(out=UT[:], in_=pU), reads=[dB[7]], writes=[dUT])
                    if 'y' in os.environ.get('A3_SKIP', ''):
                        continue
                    for h in range(4):
                        vh = v_in[:, c, h * 64:(h + 1) * 64]
                        S.op("pe", lambda e: e.matmul(pYS[:, h, 0, :], lhsT=AR[:, h, c, 1, :], rhs=ST[:, h, :], start=True, stop=False),
                             reads=[dAR, dST], writes=[dB[6]])
                        S.op("pe", lambda e: e.matmul(pYS[:, h, 0, :], lhsT=GM[:, c, h, 64:128], rhs=UT[:, h, :], start=False, stop=False),
                             reads=[dGM, dUT], writes=[dB[6]])
                        S.op("pe", lambda e: e.matmul(pYS[:, h, 0, :], lhsT=GM[:, c, h, 192:256], rhs=vh, start=False, stop=True),
                             reads=[dGM, dv], writes=[dB[6]])
                        S.op("pe", lambda e: e.matmul(pYS[:, h, 1, :], lhsT=bkT[:, c, h, 0, :], rhs=UT[:, h, :], start=True, stop=False),
                             reads=[dbk, dUT], writes=[dB[6]])
                        S.op("pe", lambda e: e.matmul(pYS[:, h, 1, :], lhsT=bkT[:, c, h, 1, :], rhs=vh, start=False, stop=True),
                             reads=[dbk, dv], writes=[dB[6]])
                    ysl = yacc[:, gc, :].rearrange("p (h s) -> p h s", s=64)
                    if e_ == 0:
                        S.op("dve", lambda e: e.tensor_copy(out=ysl, in_=pYS[:, :, 0, :]), reads=[dB[6]], writes=[dy])
                    else:
                        S.op("dve", lambda e: e.tensor_tensor(out=ysl, in0=ysl, in1=pYS[:, :, 0, :], op=ALU.add), reads=[dB[6], dy], writes=[dy])
                    S.op("dve", lambda e: e.tensor_tensor(out=tS[:], in0=ST[:], in1=pc[:, :, c:c + 1].to_broadcast([64, 4, 64]), op=ALU.mult),
                         reads=[dST, dpc], writes=[dtS])
                    S.op("dve", lambda e: e.tensor_tensor(out=ST[:], in0=tS[:], in1=pYS[:, :, 1, :], op=ALU.add), reads=[dtS, dB[6]], writes=[dST])
        gtb_ = fl(E3[:]).bitcast(BF16)[:, 0:1024].rearrange("p (c n) -> p c n", n=256); dgtb = dE3
        obr_ = fl(E4[:]).bitcast(BF16)[:, 0:1024].rearrange("p (c n) -> p c n", n=256); dobr = dE4
        st1 = S.sb("st1", [64, 4, 16], F32, st); dst1 = S.dep()
        v16 = lambda ap: ap.rearrange("p c (h s) -> p (c h) s", s=64)
        for blk in range(NBLK):
            t0 = blk * 256
            S.drain_dma("sp", keep=4)
            yb = yacc[:, blk * 4:(blk + 1) * 4, :]
            S.dma("sp", v_in[:], T["s_v"][t0:t0 + 256, :].rearrange("(c s) n -> s c n", s=64), writes=[dv])
            S.dma("sp", gtb_, T["s_gate"][t0:t0 + 256, 0:256].rearrange("(c s) n -> s c n", s=64), writes=[dgtb])
            S.op("dve", lambda e: e.tensor_reduce(out=st1[:, 0, :], in_=v16(yb), axis=AX.X, op=ALU.add), reads=[dy], writes=[dst1])
            S.op("dve", lambda e: e.tensor_tensor(out=tmp[:], in0=yb, in1=yb, op=ALU.mult), reads=[dy], writes=[dtmp])
            S.op("dve", lambda e: e.tensor_reduce(out=st1[:, 1, :], in_=v16(tmp[:]), axis=AX.X, op=ALU.add), reads=[dtmp], writes=[dst1])
            S.op("dve", lambda e: e.tensor_scalar(out=st1[:, 0:2, :], in0=st1[:, 0:2, :], scalar1=1.0 / 64, scalar2=None, op0=ALU.mult),
                 reads=[dst1], writes=[dst1])
            S.op("dve", lambda e: e.tensor_tensor(out=st1[:, 2, :], in0=st1[:, 0, :], in1=st1[:, 0, :], op=ALU.mult), reads=[dst1], writes=[dst1])
            S.op("dve", lambda e: e.tensor_tensor(out=st1[:, 3, :], in0=st1[:, 1, :], in1=st1[:, 2, :], op=ALU.subtract), reads=[dst1], writes=[dst1])
            _rstd(S, st1[:, 3, :], 16, 1.0, 64e-5, [], dst1)
            S.op("dve", lambda e: e.tensor_tensor(out=v16(tmp[:]), in0=v16(yb), in1=st1[:, 0, :].unsqueeze(2).to_broadcast([64, 16, 64]), op=ALU.subtract),
                 reads=[dy, dst1], writes=[dtmp])
            S.op("dve", lambda e: e.tensor_tensor(out=v16(tmp[:]), in0=v16(tmp[:]), in1=st1[:, 3, :].unsqueeze(2).to_broadcast([64, 16, 64]), op=ALU.mult),
                 reads=[dtmp, dst1], writes=[dtmp])
            S.op("dve", lambda e: e.tensor_tensor(out=tmp[:], in0=tmp[:], in1=gng[:, 0, :].unsqueeze(1).to_broadcast([64, 4, 256]), op=ALU.mult),
                 reads=[dtmp, dconst], writes=[dtmp])
            S.op("dve", lambda e: e.tensor_tensor(out=tmp[:], in0=tmp[:], in1=gng[:, 1, :].unsqueeze(1).to_broadcast([64, 4, 256]), op=ALU.add),
                 reads=[dtmp, dconst], writes=[dtmp])
            bv = bacc[:, blk * 4:(blk + 1) * 4, :].rearrange("p c h -> p (c h)").unsqueeze(2).to_broadcast([64, 16, 64])
            S.op("dve", lambda e: e.tensor_tensor(out=v16(E1[:]), in0=v16(v_in[:]), in1=bv, op=ALU.mult), reads=[dv, dba], writes=[dE1])
            S.op("dve", lambda e: e.tensor_tensor(out=tmp[:], in0=tmp[:], in1=E1[:], op=ALU.add), reads=[dtmp, dE1], writes=[dtmp])
            S.op("dve", lambda e: e.tensor_tensor(out=obr_, in0=tmp[:], in1=gtb_, op=ALU.mult), reads=[dtmp, dgtb], writes=[dobr])
            S.dma("sp", T["br_rwkv"][t0:t0 + 256, :].rearrange("(c s) n -> s c n", s=64), obr_, reads=[dobr])
        S.barrier()


def _phase_A3i(S, nc, T):
    NCH = NT // 64
    BT = 128
    NCB = 2
    NHB = NT // BT
    NI = 4 * NCB
    with contextlib.ExitStack() as st:
        identf, didf = _make_ident(S, st, 64, F32, "a3id")
        ones64 = S.sb("ones64", [64, 64], F32, st); dconst = S.dep()
        S.op("dve", lambda e: e.memset(ones64[:], 1.0), writes=[dconst])
        mk = S.sb("mk", [64, 4, 64], F32, st)
        S.op("pool", lambda e: e.memset(mk[:], 1.0), writes=[dconst])
        for i, (stp, cm, cmp_) in enumerate([(1, -1, ALU.is_gt), (1, -1, ALU.is_ge), (-1, 1, ALU.is_gt), (-1, 1, ALU.is_ge)]):
            S.op("pool", lambda e: e.affine_select(out=mk[:, i, :], in_=mk[:, i, :], pattern=[[stp, 64]], compare_op=cmp_, fill=0.0,
                                                   base=0, channel_multiplier=cm), reads=[dconst], writes=[dconst])
        MSK = S.sb("MSK", [64, 2, 320], F32, st)
        for e_ in range(2):
            order = [0, 1, 0, 1, 2] if e_ == 0 else [2, 3, 2, 3, 0]
            for j, m in enumerate(order):
                S.op("dve", lambda e: e.tensor_copy(out=MSK[:, e_, j * 64:(j + 1) * 64], in_=mk[:, m, :]), reads=[dconst], writes=[dconst])
        rmask = S.sb("rmask", [64, NI, 64], F32, st)
        S.op("dve", lambda e: e.memset(rmask[:], 1.0), writes=[dconst])
        S.op("dve", lambda e: e.memset(rmask[:, :, 0:1], 0.0), reads=[dconst], writes=[dconst])
        prm = S.sb("prm", [64, 7, 4], F32, st)
        S.dma("sp", prm[:], T["rprm"][:, :, :], writes=[dconst])
        omk = S.sb("omk", [64, 4], F32, st)
        S.op("dve", lambda e: e.tensor_scalar(out=omk[:], in0=prm[:, 5, :], scalar1=-1.0, scalar2=1.0, op0=ALU.mult, op1=ALU.add),
             reads=[dconst], writes=[dconst])
        wup = S.sb("wup_sb", [96, 2, 256], F32, st)
        aup = S.sb("aup_sb", [96, 2, 256], F32, st)
        S.dma("sp", wup[:], T["wup"].rearrange("e r c -> r e c"), writes=[dconst])
        S.dma("sp", aup[:], T["aup"].rearrange("e r c -> r e c"), writes=[dconst])
        gng = S.sb("gng", [64, 2, 256], F32, st)
        S.dma("sp", gng[:].rearrange("p a b -> p (a b)"), T["gn"][0:1, :].to_broadcast([64, 512]), writes=[dconst])

        yacc = S.sb("yacc", [64, NCH, 256], F32, st); dy = S.dep()
        bacc = S.sb("bacc", [64, NCH, 4], F32, st); dba = S.dep()
        S.op("dve", lambda e: e.memset(yacc[:], 0.0), writes=[dy])
        S.op("dve", lambda e: e.memset(bacc[:], 0.0), writes=[dba])
        PSA = S.ps("PSA", [64, 8, 512], F32, st); dB = [S.pdep() for _ in range(8)]

        s_rk = T["s_rk"].rearrange("g c t -> c g t")
        s_wa = T["s_wa"].rearrange("j c t -> c j t")
        v4 = lambda ap: ap.rearrange("p h (c s) -> p h c s", s=64)
        fl = lambda ap: ap.rearrange("p h t -> p (h t)")

        class Bset:
            pass

        def mkset(e_):
            B = Bset()
            sf = f"_{e_}"

            def t4(name):
                return S.sb(name + sf, [64, 4, BT], F32, st), S.dep()
            B.ST = S.sb("ST" + sf, [64, 4, 64], F32, st); B.dST = S.dep()
            B.rk_in = S.sb("rk_in" + sf, [64, 8, BT], F32, st); B.drk = S.dep()
            B.wa_in = S.sb("wa_in" + sf, [96, 2, BT], F32, st); B.dwa = S.dep()
            B.v_in = S.sb("v_in" + sf, [64, NCB, 256], F32, st); B.dv = S.dep()
            B.lw, B.dlw = t4("lw"); B.aa, B.daa = t4("aa"); B.kk, B.dkk = t4("kk"); B.kd, B.dkd = t4("kd"); B.be, B.dbe = t4("be")
            B.L, B.dL = t4("L"); B.Ld, B.dLd = t4("Ld"); B.tmp, B.dtmp = t4("tmp")
            B.E1, B.dE1 = t4("E1"); B.E2, B.dE2 = t4("E2"); B.E3, B.dE3 = t4("E3"); B.E4, B.dE4 = t4("E4")
            B.bt, B.dbt = t4("bt"); B.kt_, B.dkt = t4("kt_")
            B.AR = S.sb("AR" + sf, [64, 4, NCB, 2, 64], F32, st); B.dAR = S.dep()
            B.ltot = S.sb("ltot" + sf, [64, 4, NCB], F32, st); B.dlt = S.dep()
            B.pc = S.sb("pc" + sf, [64, 4, NCB], F32, st); B.dpc = S.dep()
            B.bkT = S.sb("bkT" + sf, [64, NCB, 4, 2, 64], F32, st); B.dbk = S.dep()
            B.GM = S.sb("GM" + sf, [64, NCB, 4, 320], F32, st); B.dGM = S.dep()
            B.XX0 = S.sb("XX0" + sf, [64, NI, 2, 64], F32, st); B.dXX = S.dep()
            B.Tm = S.sb("Tm" + sf, [64, NI, 64], F32, st); B.dTm = S.dep()
            B.WT = S.sb("WT" + sf, [64, 4, 64], F32, st); B.dWT = S.dep()
            B.UT = S.sb("UT" + sf, [64, 4, 64], F32, st); B.dUT = S.dep()
            B.tS = S.sb("tS" + sf, [64, 4, 64], F32, st); B.dtS = S.dep()
            return B

        def block_gen(e_, B, blk):
            pb = 4 * e_
            t0 = blk * BT
            corder = list(range(NCB)) if e_ == 0 else list(range(NCB - 1, -1, -1))
            S.drain_dma("sp", keep=8)
            S.dma("sp", B.rk_in[:], s_rk[:, :, t0:t0 + BT], writes=[B.drk])
            S.dma("sp", B.wa_in[:], s_wa[:, :, t0:t0 + BT], writes=[B.dwa])
            S.dma("sp", B.v_in[:], T["s_v"][t0:t0 + BT, :].rearrange("(c s) n -> s c n", s=64), writes=[B.dv])
            r_ = B.rk_in[:, 0:4, :]; k_ = B.rk_in[:, 4:8, :]
            lw, aa, kk, kd, be, L, Ld, tmp = B.lw, B.aa, B.kk, B.kd, B.be, B.L, B.Ld, B.tmp
            E1, E2, E3, E4, bt, kt_, AR, GM, Tm, XX0 = B.E1, B.E2, B.E3, B.E4, B.bt, B.kt_, B.AR, B.GM, B.Tm, B.XX0
            bh, kh = be, kd
            pwp = PSA[:, pb, :].rearrange("p (h t) -> p h t", h=4)
            pap = PSA[:, pb + 1, :].rearrange("p (h t) -> p h t", h=4)
            for h in range(4):
                S.op("pe", lambda e: e.matmul(pwp[:, h, :], lhsT=wup[:, e_, h * 64:(h + 1) * 64], rhs=B.wa_in[:, 0, :], start=True, stop=True),
                     reads=[dconst, B.dwa], writes=[dB[pb]])
            for h in range(4):
                S.op("pe", lambda e: e.matmul(pap[:, h, :], lhsT=aup[:, e_, h * 64:(h + 1) * 64], rhs=B.wa_in[:, 1, :], start=True, stop=True),
                     reads=[dconst, B.dwa], writes=[dB[pb + 1]])
            S.op("dve", lambda e: e.tensor_tensor(out=kk[:], in0=k_, in1=prm[:, 4, :].unsqueeze(2).to_broadcast([64, 4, BT]), op=ALU.mult),
                 reads=[B.drk, dconst], writes=[B.dkk])
            S.op("dve", lambda e: e.tensor_tensor(out=tmp[:], in0=kk[:], in1=kk[:], op=ALU.mult), reads=[B.dkk], writes=[B.dtmp])
            S.op("pe", lambda e: e.matmul(PSA[:, pb + 2, :], lhsT=ones64[:], rhs=fl(tmp[:]), start=True, stop=True),
                 reads=[dconst, B.dtmp], writes=[dB[pb + 2]])
            yield
            for h in range(4):
                S.op("act", lambda e: e.activation(out=lw[:, h, :], in_=pwp[:, h, :], func=AF.Sigmoid, bias=prm[:, e_, h:h + 1]),
                     reads=[dB[pb], dconst], writes=[B.dlw])
            for h in range(4):
                S.op("act", lambda e: e.activation(out=aa[:, h, :], in_=pap[:, h, :], func=AF.Sigmoid, bias=prm[:, 2 + e_, h:h + 1]),
                     reads=[dB[pb + 1], dconst], writes=[B.daa])
            S.op("dve", lambda e: e.tensor_scalar(out=fl(tmp[:]), in0=PSA[:, pb + 2, :], scalar1=1e-24, scalar2=None, op0=ALU.max),
                 reads=[dB[pb + 2]], writes=[B.dtmp])
            yield
            S.op("act", lambda e: e.activation(out=tmp[:], in_=tmp[:], func=AF.Sqrt), reads=[B.dtmp], writes=[B.dtmp])
            S.op("dve", lambda e: e.tensor_scalar(out=lw[:], in0=lw[:], scalar1=-0.6065306597126334, scalar2=None, op0=ALU.mult),
                 reads=[B.dlw], writes=[B.dlw])
            yield
            S.op("dve", lambda e: e.reciprocal(out=tmp[:], in_=tmp[:]), reads=[B.dtmp], writes=[B.dtmp])
            S.op("dve", lambda e: e.tensor_tensor(out=kk[:], in0=kk[:], in1=tmp[:], op=ALU.mult), reads=[B.dkk, B.dtmp], writes=[B.dkk])
            S.op("dve", lambda e: e.tensor_tensor(out=tmp[:], in0=aa[:], in1=prm[:, 5, :].unsqueeze(2).to_broadcast([64, 4, BT]), op=ALU.mult),
                 reads=[B.daa, dconst], writes=[B.dtmp])
            S.op("dve", lambda e: e.tensor_tensor(out=tmp[:], in0=tmp[:], in1=omk[:].unsqueeze(2).to_broadcast([64, 4, BT]), op=ALU.add),
                 reads=[B.dtmp, dconst], writes=[B.dtmp])
            S.op("dve", lambda e: e.tensor_tensor(out=kd[:], in0=tmp[:], in1=k_, op=ALU.mult), reads=[B.dtmp, B.drk], writes=[B.dkd])
            S.op("pool", lambda e: e.tensor_tensor(out=be[:], in0=kk[:], in1=aa[:], op=ALU.mult), reads=[B.dkk, B.daa], writes=[B.dbe])
            yield
            S.op("dve", lambda e: e.tensor_tensor_scan(out=fl(L[:]), data0=rmask[:].rearrange("p a b -> p (a b)"), data1=fl(lw[:]),
                                                       initial=0.0, op0=ALU.mult, op1=ALU.add), reads=[B.dlw, dconst], writes=[B.dL])
            S.op("dve", lambda e: e.tensor_copy(out=B.ltot[:], in_=v4(L[:])[:, :, :, 63]), reads=[B.dL], writes=[B.dlt])
            if e_ == 0:
                Lc, dLc = L, B.dL
            else:
                S.op("dve", lambda e: e.tensor_tensor(out=tmp[:], in0=lw[:], in1=L[:], op=ALU.subtract), reads=[B.dlw, B.dL], writes=[B.dtmp])
                S.op("dve", lambda e: e.tensor_tensor(out=v4(Ld[:]), in0=v4(tmp[:]), in1=B.ltot[:].unsqueeze(3).to_broadcast([64, 4, NCB, 64]),
                                                      op=ALU.add), reads=[B.dtmp, B.dlt], writes=[B.dLd])
                Lc, dLc = Ld, B.dLd
            S.op("pool", lambda e: e.tensor_tensor(out=E1[:], in0=Lc[:], in1=lw[:], op=ALU.subtract), reads=[dLc, B.dlw], writes=[B.dE1])
            S.op("dve", lambda e: e.tensor_tensor(out=v4(tmp[:]), in0=B.ltot[:].unsqueeze(3).to_broadcast([64, 4, NCB, 64]), in1=v4(Lc[:]),
                                                  op=ALU.subtract), reads=[B.dlt, dLc], writes=[B.dtmp])
            yield
            S.op("act", lambda e: e.activation(out=E1[:], in_=E1[:], func=AF.Exp), reads=[B.dE1], writes=[B.dE1])
            S.op("act", lambda e: e.activation(out=E2[:], in_=Lc[:], func=AF.Exp), reads=[dLc], writes=[B.dE2])
            S.op("act", lambda e: e.activation(out=E3[:], in_=Lc[:], func=AF.Exp, scale=-1.0), reads=[dLc], writes=[B.dE3])
            S.op("act", lambda e: e.activation(out=E4[:], in_=tmp[:], func=AF.Exp), reads=[B.dtmp], writes=[B.dE4])
            S.op("act", lambda e: e.activation(out=B.pc[:], in_=B.ltot[:], func=AF.Exp), reads=[B.dlt], writes=[B.dpc])
            yield
            S.op("dve", lambda e: e.scalar_tensor_tensor(out=AR[:, :, :, 0, :], in0=v4(kk[:]), scalar=-1.0, in1=v4(E1[:]),
                                                         op0=ALU.mult, op1=ALU.mult), reads=[B.dkk, B.dE1], writes=[B.dAR])
            S.op("dve", lambda e: e.tensor_tensor(out=AR[:, :, :, 1, :], in0=v4(r_), in1=v4(E2[:]), op=ALU.mult), reads=[B.drk, B.dE2], writes=[B.dAR])
            S.op("dve", lambda e: e.tensor_tensor(out=kt_[:], in0=kd[:], in1=E3[:], op=ALU.mult), reads=[B.dkd, B.dE3], writes=[B.dkt])
            S.op("pool", lambda e: e.tensor_tensor(out=bt[:], in0=be[:], in1=E3[:], op=ALU.mult), reads=[B.dbe, B.dE3], writes=[B.dbt])
            S.op("dve", lambda e: e.tensor_tensor(out=tmp[:], in0=r_, in1=kd[:], op=ALU.mult), reads=[B.drk, B.dkd], writes=[B.dtmp])
            S.op("dve", lambda e: e.tensor_tensor(out=tmp[:], in0=tmp[:], in1=prm[:, 6, :].unsqueeze(2).to_broadcast([64, 4, BT]), op=ALU.mult),
                 reads=[B.dtmp, dconst], writes=[B.dtmp])
            pbo2 = PSA[:, pb + 3, 0:NCB * 8].rearrange("p (c h w) -> p c h w", h=4, w=2)
            pbo = pbo2[:, :, :, 0]
            for c in range(NCB):
                for h in range(4):
                    S.op("pe", lambda e: e.matmul(pbo2[:, c, h, :], lhsT=tmp[:, h, c * 64:(c + 1) * 64], rhs=ones64[:, 0:2], start=True, stop=True),
                         reads=[B.dtmp, dconst], writes=[dB[pb + 3]])
            yield
            S.op("dve", lambda e: e.tensor_tensor(out=bh[:], in0=be[:], in1=E4[:], op=ALU.mult), reads=[B.dbe, B.dE4], writes=[B.dbe])
            S.op("pool", lambda e: e.tensor_tensor(out=kh[:], in0=kd[:], in1=E4[:], op=ALU.mult), reads=[B.dkd, B.dE4], writes=[B.dkd])
            bsl = bacc[:, blk * NCB:(blk + 1) * NCB, :]
            S.op("dve", lambda e: e.tensor_tensor(out=bsl, in0=bsl, in1=pbo, op=ALU.add), reads=[dB[pb + 3], dba], writes=[dba])
            yield
            ptr = PSA[:, pb:pb + 2, :].rearrange("p b (i s) -> p (b i) s", s=64)
            for c in range(NCB):
                for h in range(4):
                    for w_, (src, dsrc) in enumerate([(bh, B.dbe), (kh, B.dkd)]):
                        idx = (c * 4 + h) * 2 + w_
                        S.op("pe", lambda e: e.transpose(out=ptr[:, idx, :], in_=src[:, h, c * 64:(c + 1) * 64], identity=identf[:]),
                             reads=[dsrc, didf], writes=[dB[pb + idx // 8]])
            yield
            S.op("act", lambda e: e.copy(out=B.bkT[:].rearrange("p c h w s -> p (c h w s)"),
                                         in_=PSA[:, pb:pb + 2, :].rearrange("p b t -> p (b t)")),
                 reads=[dB[pb], dB[pb + 1]], writes=[B.dbk])
            yield
            for c in range(NCB):
                cs = slice(c * 64, (c + 1) * 64)
                for h in range(4):
                    arh = AR[:, h, c, :, :].rearrange("p a s -> p (a s)")
                    S.op("pe", lambda e: e.matmul(PSA[:, pb + h, 0:128], lhsT=bt[:, h, cs], rhs=arh, start=True, stop=True),
                         reads=[B.dbt, B.dAR], writes=[dB[pb + h]])
                    S.op("pe", lambda e: e.matmul(PSA[:, pb + h, 128:256], lhsT=kt_[:, h, cs], rhs=arh, start=True, stop=True),
                         reads=[B.dkt, B.dAR], writes=[dB[pb + h]])
                    S.op("pe", lambda e: e.matmul(PSA[:, pb + h, 256:320], lhsT=AR[:, h, c, 0, :], rhs=bt[:, h, cs], start=True, stop=True),
                         reads=[B.dbt, B.dAR], writes=[dB[pb + h]])
                yield
                S.op("dve", lambda e: e.tensor_tensor(out=GM[:, c, :, :], in0=PSA[:, pb:pb + 4, 0:320],
                                                      in1=MSK[:, e_, :].unsqueeze(1).to_broadcast([64, 4, 320]), op=ALU.mult),
                     reads=[dB[pb], dB[pb + 1], dB[pb + 2], dB[pb + 3], dconst], writes=[B.dGM])
                yield
            GMf = GM[:].rearrange("p c h n -> p (c h) n")
            S.op("dve", lambda e: e.tensor_tensor(out=Tm[:], in0=GMf[:, :, 0:64], in1=identf[:].unsqueeze(1).to_broadcast([64, NI, 64]), op=ALU.add),
                 reads=[B.dGM, didf], writes=[B.dTm])
            pxx = PSA[:, pb:pb + 2, :].rearrange("p b (i w s) -> p (b i) w s", w=2, s=64)
            ptm = PSA[:, pb + 2, :].rearrange("p (i s) -> p i s", s=64)
            for it_ in range(5):
                for idx in range(NI):
                    if it_ == 0:
                        Xc = GMf[:, idx, 0:64]; XTc = GMf[:, idx, 256:320]; dsrc = B.dGM
                    else:
                        Xc = XX0[:, idx, 0, :]; XTc = XX0[:, idx, 1, :]; dsrc = B.dXX
                    S.op("pe", lambda e: e.matmul(pxx[:, idx, 0, :], lhsT=XTc, rhs=Xc, start=True, stop=True), reads=[dsrc], writes=[dB[pb + idx // 4]])
                    S.op("pe", lambda e: e.matmul(pxx[:, idx, 1, :], lhsT=Xc, rhs=XTc, start=True, stop=True), reads=[dsrc], writes=[dB[pb + idx // 4]])
                yield
                S.op("act", lambda e: e.copy(out=XX0[:].rearrange("p i w s -> p (i w s)"), in_=PSA[:, pb:pb + 2, :].rearrange("p b t -> p (b t)")),
                     reads=[dB[pb], dB[pb + 1]], writes=[B.dXX])
                yield
                for idx in range(NI):
                    S.op("pe", lambda e: e.matmul(ptm[:, idx, :], lhsT=XX0[:, idx, 1, :], rhs=Tm[:, idx, :], start=True, stop=True),
                         reads=[B.dXX, B.dTm], writes=[dB[pb + 2]])
                yield
                S.op("dve", lambda e: e.tensor_tensor(out=Tm[:].rearrange("p i s -> p (i s)"), in0=Tm[:].rearrange("p i s -> p (i s)"),
                                                      in1=PSA[:, pb + 2, :], op=ALU.add),
                     reads=[dB[pb + 2], B.dTm], writes=[B.dTm])
                yield
            pW = PSA[:, pb + 3, 0:256].rearrange("p (h s) -> p h s", s=64)
            pU = PSA[:, pb + 2, 0:256].rearrange("p (h s) -> p h s", s=64)
            pYS = PSA[:, pb + 3, :].rearrange("p (h w s) -> p h w s", w=2, s=64)
            ST, WT, UT, tS, v_in, bkT, pc = B.ST, B.WT, B.UT, B.tS, B.v_in, B.bkT, B.pc
            for c in corder:
                gc = blk * NCB + c
                for h in range(4):
                    vh = v_in[:, c, h * 64:(h + 1) * 64]
                    S.op("pe", lambda e: e.matmul(pW[:, h, :], lhsT=AR[:, h, c, 0, :], rhs=ST[:, h, :], start=True, stop=False),
                         reads=[B.dAR, B.dST], writes=[dB[pb + 3]])
                    S.op("pe", lambda e: e.matmul(pW[:, h, :], lhsT=GM[:, c, h, 128:192], rhs=vh, start=False, stop=True),
                         reads=[B.dGM, B.dv], writes=[dB[pb + 3]])
                yield
                S.op("act", lambda e: e.copy(out=WT[:], in_=pW), reads=[dB[pb + 3]], writes=[B.dWT])
                yield
                for h in range(4):
                    S.op("pe", lambda e: e.matmul(pU[:, h, :], lhsT=Tm[:, c * 4 + h, :], rhs=WT[:, h, :], start=True, stop=True),
                         reads=[B.dTm, B.dWT], writes=[dB[pb + 2]])
                yield
                S.op("act", lambda e: e.copy(out=UT[:], in_=pU), reads=[dB[pb + 2]], writes=[B.dUT])
                yield
                for h in range(4):
                    vh = v_in[:, c, h * 64:(h + 1) * 64]
                    S.op("pe", lambda e: e.matmul(pYS[:, h, 0, :], lhsT=AR[:, h, c, 1, :], rhs=ST[:, h, :], start=True, stop=False),
                         reads=[B.dAR, B.dST], writes=[dB[pb + 3]])
                    S.op("pe", lambda e: e.matmul(pYS[:, h, 0, :], lhsT=GM[:, c, h, 64:128], rhs=UT[:, h, :], start=False, stop=False),
                         reads=[B.dGM, B.dUT], writes=[dB[pb + 3]])
                    S.op("pe", lambda e: e.matmul(pYS[:, h, 0, :], lhsT=GM[:, c, h, 192:256], rhs=vh, start=False, stop=True),
                         reads=[B.dGM, B.dv], writes=[dB[pb + 3]])
                    S.op("pe", lambda e: e.matmul(pYS[:, h, 1, :], lhsT=bkT[:, c, h, 0, :], rhs=UT[:, h, :], start=True, stop=False),
                         reads=[B.dbk, B.dUT], writes=[dB[pb + 3]])
                    S.op("pe", lambda e: e.matmul(pYS[:, h, 1, :], lhsT=bkT[:, c, h, 1, :], rhs=vh, start=False, stop=True),
                         reads=[B.dbk, B.dv], writes=[dB[pb + 3]])
                S.op("dve", lambda e: e.tensor_tensor(out=tS[:], in0=ST[:], in1=pc[:, :, c:c + 1].to_broadcast([64, 4, 64]), op=ALU.mult),
                     reads=[B.dST, B.dpc], writes=[B.dtS])
                yield
                S.op("dve", lambda e: e.tensor_tensor(out=ST[:], in0=tS[:], in1=pYS[:, :, 1, :], op=ALU.add), reads=[B.dtS, dB[pb + 3]], writes=[B.dST])
                ysl = yacc[:, gc, :].rearrange("p (h s) -> p h s", s=64)
                S.op("dve", lambda e: e.tensor_tensor(out=ysl, in0=ysl, in1=pYS[:, :, 0, :], op=ALU.add), reads=[dB[pb + 3], dy], writes=[dy])
                yield

        sets = [mkset(0), mkset(1)]
        import os
        nb = int(os.environ.get('A3_BLOCKS', NHB))
        order = [list(range(NHB)), [1, 0] + list(range(NHB - 1, 1, -1))]

        def chain(e_):
            B = sets[e_]
            S.op("dve", lambda e: e.memset(B.ST[:], 0.0), writes=[B.dST])
            for blk in order[e_][:nb]:
                yield from block_gen(e_, B, blk)

        gens = [chain(0), chain(1)]
        alive = [True, True]
        while any(alive):
            for e_ in range(2):
                if alive[e_]:
                    try:
                        next(gens[e_])
                    except StopIteration:
                        alive[e_] = False

        B = sets[0]
        tmpv = fl(B.tmp[:]).rearrange("p (c n) -> p c n", n=256)
        e1v = fl(B.E1[:]).rearrange("p (c n) -> p c n", n=256)
        gtb_ = fl(B.E3[:]).bitcast(BF16)[:, 0:NCB * 256].rearrange("p (c n) -> p c n", n=256); dgtb = B.dE3
        obr_ = fl(B.E4[:]).bitcast(BF16)[:, 0:NCB * 256].rearrange("p (c n) -> p c n", n=256); dobr = B.dE4
        st1 = S.sb("st1", [64, 4, NI], F32, st); dst1 = S.dep()
        v16 = lambda ap: ap.rearrange("p c (h s) -> p (c h) s", s=64)
        for blk in range(NHB):
            t0 = blk * BT
            S.drain_dma("sp", keep=4)
            yb = yacc[:, blk * NCB:(blk + 1) * NCB, :]
            S.dma("sp", B.v_in[:], T["s_v"][t0:t0 + BT, :].rearrange("(c s) n -> s c n", s=64), writes=[B.dv])
            S.dma("sp", gtb_, T["s_gate"][t0:t0 + BT, 0:256].rearrange("(c s) n -> s c n", s=64), writes=[dgtb])
            S.op("dve", lambda e: e.tensor_reduce(out=st1[:, 0, :], in_=v16(yb), axis=AX.X, op=ALU.add), reads=[dy], writes=[dst1])
            S.op("dve", lambda e: e.tensor_tensor(out=tmpv, in0=yb, in1=yb, op=ALU.mult), reads=[dy], writes=[B.dtmp])
            S.op("dve", lambda e: e.tensor_reduce(out=st1[:, 1, :], in_=v16(tmpv), axis=AX.X, op=ALU.add), reads=[B.dtmp], writes=[dst1])
            S.op("dve", lambda e: e.tensor_scalar(out=st1[:, 0:2, :], in0=st1[:, 0:2, :], scalar1=1.0 / 64, scalar2=None, op0=ALU.mult),
                 reads=[dst1], writes=[dst1])
            S.op("dve", lambda e: e.tensor_tensor(out=st1[:, 2, :], in0=st1[:, 0, :], in1=st1[:, 0, :], op=ALU.mult), reads=[dst1], writes=[dst1])
            S.op("dve", lambda e: e.tensor_tensor(out=st1[:, 3, :], in0=st1[:, 1, :], in1=st1[:, 2, :], op=ALU.subtract), reads=[dst1], writes=[dst1])
            _rstd(S, st1[:, 3, :], NI, 1.0, 64e-5, [], dst1)
            S.op("dve", lambda e: e.tensor_tensor(out=v16(tmpv), in0=v16(yb), in1=st1[:, 0, :].unsqueeze(2).to_broadcast([64, NI, 64]), op=ALU.subtract),
                 reads=[dy, dst1], writes=[B.dtmp])
            S.op("dve", lambda e: e.tensor_tensor(out=v16(tmpv), in0=v16(tmpv), in1=st1[:, 3, :].unsqueeze(2).to_broadcast([64, NI, 64]), op=ALU.mult),
                 reads=[B.dtmp, dst1], writes=[B.dtmp])
            S.op("dve", lambda e: e.tensor_tensor(out=tmpv, in0=tmpv, in1=gng[:, 0, :].unsqueeze(1).to_broadcast([64, NCB, 256]), op=ALU.mult),
                 reads=[B.dtmp, dconst], writes=[B.dtmp])
            S.op("dve", lambda e: e.tensor_tensor(out=tmpv, in0=tmpv, in1=gng[:, 1, :].unsqueeze(1).to_broadcast([64, NCB, 256]), op=ALU.add),
                 reads=[B.dtmp, dconst], writes=[B.dtmp])
            bv = bacc[:, blk * NCB:(blk + 1) * NCB, :].rearrange("p c h -> p (c h)").unsqueeze(2).to_broadcast([64, NI, 64])
            S.op("dve", lambda e: e.tensor_tensor(out=v16(e1v), in0=v16(B.v_in[:]), in1=bv, op=ALU.mult), reads=[B.dv, dba], writes=[B.dE1])
            S.op("dve", lambda e: e.tensor_tensor(out=tmpv, in0=tmpv, in1=e1v, op=ALU.add), reads=[B.dtmp, B.dE1], writes=[B.dtmp])
            S.op("dve", lambda e: e.tensor_tensor(out=obr_, in0=tmpv, in1=gtb_, op=ALU.mult), reads=[B.dtmp, dgtb], writes=[dobr])
            S.dma("sp", T["br_rwkv"][t0:t0 + BT, :].rearrange("(c s) n -> s c n", s=64), obr_, reads=[dobr])
        S.barrier()


def build_A(phases="123"):
    nc = bass.Bass("TRN2", target_bir_lowering=False)
    T = {}

    def din(name, shape, dt=F32):
        T[name] = nc.dram_tensor(name, list(shape), dt, kind="ExternalInput").ap()

    def dscr(name, shape, dt):
        T[name] = nc.dram_tensor(name, list(shape), dt, kind="Internal").ap()

    def dout(name, shape, dt):
        T[name] = nc.dram_tensor(name, list(shape), dt, kind="ExternalOutput").ap()

    din("xf", [NT, D]); din("cT", [128, 16, 2]); din("wmod", [D, 4096]); din("bmod", [1, 4096]); din("gpre", [1, D])
    din("gqn", [1, 256]); din("w_fm", [D, 704]); din("w_tm", [D, 2304]); din("rope", [NT, 192])
    din("lamp", [1, 256]); din("lami", [1, 1]); din("subg", [1, 128]); din("rprm", [64, 7, 4])
    din("wup", [2, 96, 256]); din("aup", [2, 96, 256]); din("gn", [1, 512])
    dscr("s_rk", [8, 64, NT], F32); dscr("s_wa", [2, 96, NT], F32); dscr("s_v", [NT, 256], F32)
    dscr("s_dqkT", [8, 64, NT], BF16); dscr("s_gqkT", [3, 128, NT], BF16)
    dscr("s_dv", [NT, 2, 129], BF16); dscr("s_gv", [NT, 129], BF16); dscr("s_gate", [NT, 768], BF16)
    dout("hT", [16, 128, NT], BF16); dout("br_att", [NT, 512], BF16); dout("br_rwkv", [NT, 256], BF16)
    with contextlib.ExitStack() as st:
        S = Sched(nc, st)
        if "1" in phases:
            _phase_A1(S, nc, T)
        if "2" in phases:
            _phase_A2(S, nc, T)
        if "3" in phases:
            _phase_A3i(S, nc, T)
        S.barrier()
        print("build_A: ninst", S.ninst, "nsem", S.nsem, {k: v for k, v in S.cnt.items()})
    return nc


OFF = {"cv_val": 0, "cv_glu": 1024, "cv_gate": 2048, "rk_r": 3072, "rk_k": 4096, "rk_v": 5120, "rk_wl": 6144, "rk_al": 6240,
       "rk_gate": 6336, "df_q": 7360, "df_k": 8384, "df_v": 9408, "df_gate": 10432, "gq_q": 11456, "gq_k": 12480, "gq_v": 12736,
       "gq_gate": 12992, "merge": 14016}


def _rope_table():
    tab = np.zeros((NT, 192), np.float32)
    tab[:, 0:32] = 1.0
    tab[:, 64:128] = 1.0
    t = np.arange(4096)
    row = (t // 64).astype(np.float32); col = (t % 64).astype(np.float32)
    for half, c0, s0 in ((32, 0, 32), (64, 64, 128)):
        inv = (10000.0 ** (-np.arange(0, half, 2, dtype=np.float32) / half)).astype(np.float32)
        ang = np.concatenate([row[:, None] * inv, col[:, None] * inv], axis=-1).astype(np.float32)
        tab[NCTX:, c0:c0 + half] = np.cos(ang)
        tab[NCTX:, s0:s0 + half] = np.sin(ang)
    return tab


def _cT(c_ctx, cb):
    both = np.stack([c_ctx, cb], axis=-1)
    return np.ascontiguousarray(both.reshape(16, 128, 2).transpose(1, 0, 2))


def inputs_A(inp, li, b, q, xfull, rope):
    w_in = inp["w_in"][li]
    cs = lambda name, a, n: w_in[:, OFF[name] + a:OFF[name] + a + n]
    kv = q // 2
    w_fm = np.concatenate([cs("rk_r", 256 * q, 256), cs("rk_k", 256 * q, 256), cs("rk_wl", 0, 96), cs("rk_al", 0, 96)], axis=1)
    w_tm = np.concatenate([cs("rk_v", 256 * q, 256), cs("rk_gate", 256 * q, 256),
                           cs("df_q", 256 * q, 256), cs("df_k", 256 * q, 256), cs("df_v", 256 * q, 256), cs("df_gate", 256 * q, 256),
                           cs("gq_q", 256 * q, 256), cs("gq_k", 128 * kv, 128), cs("gq_v", 128 * kv, 128), cs("gq_gate", 256 * q, 256)], axis=1)
    sl = slice(256 * q, 256 * q + 256)
    hm = lambda v: v[sl].reshape(4, 64).T
    rprm = np.stack([hm(inp["rwkv_w0"][li, 0]), hm(inp["rwkv_w0"][li, 1]), hm(inp["rwkv_a0"][li, 0]), hm(inp["rwkv_a0"][li, 1]),
                     hm(inp["rwkv_k_k"][li]), hm(inp["rwkv_k_a"][li]), hm(inp["rwkv_r_k"][li].reshape(-1))], axis=1)
    lam_init = 0.8 - 0.6 * math.exp(-0.3 * li)
    return {
        "xf": np.ascontiguousarray(xfull[b]), "cT": _cT(inp["c_ctx"], inp["c"][b]),
        "wmod": np.ascontiguousarray(inp["w_mod"][li][:, 0:4096]), "bmod": np.ascontiguousarray(inp["b_mod"][li][None, 0:4096]),
        "gpre": np.ascontiguousarray(inp["norm_pre_g"][li][None]), "gqn": np.ascontiguousarray(inp["gqa_qk_norm_g"][li].reshape(1, 256)),
        "w_fm": np.ascontiguousarray(w_fm), "w_tm": np.ascontiguousarray(w_tm), "rope": rope,
        "lamp": np.ascontiguousarray(inp["diff_lam"][li].reshape(1, 256)), "lami": np.full((1, 1), lam_init, np.float32),
        "subg": np.ascontiguousarray(inp["diff_subln_g"][li][None]), "rprm": np.ascontiguousarray(rprm.astype(np.float32)),
        "wup": np.ascontiguousarray(inp["rwkv_w_up"][li][:, :, sl]), "aup": np.ascontiguousarray(inp["rwkv_a_up"][li][:, :, sl]),
        "gn": np.ascontiguousarray(np.concatenate([inp["rwkv_gn_g"][li][sl], inp["rwkv_gn_b"][li][sl]])[None]),
    }


NOWN = 1088
NEXT = 1152
MBLK = [(0, 64, 15), (64, 384, 109), (448, 384, 493), (832, 256, 877)]
LNBLK = [(0, 384), (384, 384), (768, 320)]


def build_B():
    nc = bass.Bass("TRN2", target_bir_lowering=False)
    T = {}

    def din(name, shape, dt=F32):
        T[name] = nc.dram_tensor(name, list(shape), dt, kind="ExternalInput").ap()

    din("hTx", [128, 16, NEXT], BF16); din("mask", [1, NEXT]); din("brT", [128, 3, 8, NOWN], BF16); din("x_own", [NOWN, D])
    din("w_cv", [D, 3072]); din("cvp", [128, 8, 34]); din("Wl", [D, 8192]); din("Wb", [4, W, D]); din("Wout", [D, D])
    din("bg", [128, 4, 16]); din("cT", [128, 16, 2]); din("wmodg", [D, D]); din("bmodg", [1, D]); din("gpost", [1, D])
    T["s_cv"] = nc.dram_tensor("s_cv", [8, 128, NOWN], BF16, kind="Internal").ap()
    T["xo"] = nc.dram_tensor("xo", [NOWN, D], F32, kind="ExternalOutput").ap()
    with contextlib.ExitStack() as st0:
        S = Sched(nc, st0)
        big = S.sb("big", [128, 16 * NOWN], BF16, st0); dbig = S.dep()
        mergedT = big[:].rearrange("p (c t) -> p c t", t=NOWN)
        conv_all = big[:].bitcast(F32).rearrange("p (c t) -> p c t", t=NOWN)
        with contextlib.ExitStack() as stm:
            hTx = S.sb("hTx_sb", [128, 16, NEXT], BF16, stm); dhTx = S.dep()
            for k4 in range(4):
                S.dma("sp", hTx[:, k4 * 4:(k4 + 1) * 4, :], T["hTx"][:, k4 * 4:(k4 + 1) * 4, :], writes=[dhTx])
            with contextlib.ExitStack() as st:
                maskb = S.sb("maskb", [128, NEXT], F32, st); dmask = S.dep()
                S.dma("sp", maskb[:], T["mask"][0:1, :].to_broadcast([128, NEXT]), writes=[dmask])
                cvp = S.sb("cvp_sb", [128, 8, 34], F32, st); dcvp = S.dep()
                S.dma("sp", cvp[:], T["cvp"][:, :, :], writes=[dcvp])
                ones = S.sb("ones128", [128, 128], F32, st); dones = S.dep()
                S.op("dve", lambda e: e.memset(ones[:], 1.0), writes=[dones])
                wck = [S.sb(f"wck{k}", [128, 16, 128], BF16, st) for k in range(3)]; dwck = [S.dep() for _ in range(3)]
                u = S.sb("u", [128, NEXT], F32, st); du = S.dep()
                sg = S.sb("sg", [128, 384], F32, st); dsg = S.dep()
                cgx = S.sb("cgx", [128, 8, NEXT], BF16, st); dcg = S.dep()
                sqt = S.sb("sqt", [128, NOWN], F32, st); dsq = S.dep()
                meanb = S.sb("meanb", [128, NOWN], F32, st); dmean = S.dep()
                rstdb = S.sb("rstdb", [128, NOWN], F32, st); drstd = S.dep()
                cst = S.sb("cst", [128, NOWN], BF16, st); dcst = S.dep()
                pcv = [S.ps(f"pcv{i}", [128, 512], F32, st) for i in range(6)]; dpcv = [S.pdep() for _ in range(6)]
                wcv = T["w_cv"].rearrange("(kc p) n -> p kc n", p=128)
                for cc in range(8):
                    for k in range(3):
                        c0 = k * 1024 + cc * 128
                        for k4 in range(2):
                            S.dma("pool", wck[k][:, k4 * 8:(k4 + 1) * 8, :], wcv[:, k4 * 8:(k4 + 1) * 8, c0:c0 + 128], writes=[dwck[k]])
                    for tb in range(3):
                        ts_ = slice(tb * 384, (tb + 1) * 384)
                        pgl, dgl = pcv[0 + tb % 2], dpcv[0 + tb % 2]
                        pv, dpv = pcv[2 + tb % 2], dpcv[2 + tb % 2]
                        pgt, dgt_ = pcv[4 + tb % 2], dpcv[4 + tb % 2]
                        for (k, p_, dp_) in ((1, pgl, dgl), (0, pv, dpv), (2, pgt, dgt_)):
                            for kc in range(16):
                                S.op("pe", lambda e: e.matmul(p_[:, 0:384], lhsT=wck[k][:, kc, :], rhs=hTx[:, kc, ts_], start=(kc == 0), stop=(kc == 15)),
                                     reads=[dwck[k], dhTx], writes=[dp_])
                        S.op("act", lambda e: e.activation(out=sg[:], in_=pgl[:, 0:384], func=AF.Sigmoid), reads=[dgl], writes=[dsg])
                        S.op("dve", lambda e: e.tensor_tensor(out=sg[:], in0=sg[:], in1=maskb[:, ts_], op=ALU.mult), reads=[dsg, dmask], writes=[dsg])
                        S.op("dve", lambda e: e.tensor_tensor(out=u[:, ts_], in0=pv[:, 0:384], in1=sg[:], op=ALU.mult), reads=[dpv, dsg], writes=[du])
                        S.op("act", lambda e: e.activation(out=cgx[:, cc, ts_], in_=pgt[:, 0:384], func=AF.Silu), reads=[dgt_], writes=[dcg])
                    for (o0_, n_, e0_) in ((0, 64, 0), (64, 1024, 94)):
                        acc = conv_all[:, cc, o0_:o0_ + n_]
                        S.op("dve", lambda e: e.tensor_scalar(out=acc, in0=u[:, e0_:e0_ + n_], scalar1=cvp[:, cc, 0:1], scalar2=cvp[:, cc, 31:32],
                                                              op0=ALU.mult, op1=ALU.add), reads=[du, dcvp], writes=[dbig])
                        for j in range(1, 31):
                            S.op("dve", lambda e: e.scalar_tensor_tensor(out=acc, in0=u[:, e0_ + j:e0_ + j + n_], scalar=cvp[:, cc, j:j + 1], in1=acc,
                                                                         op0=ALU.mult, op1=ALU.add), reads=[du, dcvp, dbig], writes=[dbig])
                for cc in range(8):
                    S.op("dve", lambda e: e.tensor_tensor(out=sqt[:], in0=conv_all[:, cc, :], in1=conv_all[:, cc, :], op=ALU.mult), reads=[dbig], writes=[dsq])
                    for bi, (o_, n_) in enumerate(LNBLK):
                        S.op("pe", lambda e: e.matmul(pcv[bi][:, 0:n_], lhsT=ones[:], rhs=conv_all[:, cc, o_:o_ + n_], start=(cc == 0), stop=(cc == 7)),
                             reads=[dones, dbig], writes=[dpcv[bi]])
                        S.op("pe", lambda e: e.matmul(pcv[3 + bi][:, 0:n_], lhsT=ones[:], rhs=sqt[:, o_:o_ + n_], start=(cc == 0), stop=(cc == 7)),
                             reads=[dones, dsq], writes=[dpcv[3 + bi]])
                for bi, (o_, n_) in enumerate(LNBLK):
                    S.op("dve", lambda e: e.tensor_scalar(out=meanb[:, o_:o_ + n_], in0=pcv[bi][:, 0:n_], scalar1=1.0 / W, scalar2=None, op0=ALU.mult),
                         reads=[dpcv[bi]], writes=[dmean])
                    S.op("dve", lambda e: e.tensor_scalar(out=rstdb[:, o_:o_ + n_], in0=pcv[3 + bi][:, 0:n_], scalar1=1.0 / W, scalar2=None, op0=ALU.mult),
                         reads=[dpcv[3 + bi]], writes=[drstd])
                S.op("dve", lambda e: e.tensor_tensor(out=sqt[:], in0=meanb[:], in1=meanb[:], op=ALU.mult), reads=[dmean], writes=[dsq])
                S.op("dve", lambda e: e.tensor_tensor(out=rstdb[:], in0=rstdb[:], in1=sqt[:], op=ALU.subtract), reads=[drstd, dsq], writes=[drstd])
                _rstd(S, rstdb[:], NOWN, 1.0, 1e-5, [], drstd)
                for cc in range(8):
                    cv = conv_all[:, cc, :]
                    S.op("dve", lambda e: e.tensor_tensor(out=cv, in0=cv, in1=meanb[:], op=ALU.subtract), reads=[dbig, dmean], writes=[dbig])
                    S.op("dve", lambda e: e.tensor_tensor(out=cv, in0=cv, in1=rstdb[:], op=ALU.mult), reads=[dbig, drstd], writes=[dbig])
                    S.op("act", lambda e: e.activation(out=sqt[:], in_=cv, func=AF.Silu, bias=cvp[:, cc, 33:34], scale=cvp[:, cc, 32:33]),
                         reads=[dbig, dcvp], writes=[dsq])
                    S.op("dve", lambda e: e.tensor_tensor(out=cst[:, 0:64], in0=sqt[:, 0:64], in1=cgx[:, cc, 15:79], op=ALU.mult), reads=[dsq, dcg], writes=[dcst])
                    S.op("dve", lambda e: e.tensor_tensor(out=cst[:, 64:NOWN], in0=sqt[:, 64:NOWN], in1=cgx[:, cc, 109:1133], op=ALU.mult), reads=[dsq, dcg], writes=[dcst])
                    S.dma("sp", T["s_cv"][cc, :, :], cst[:], reads=[dcst])
                S.barrier()
            with contextlib.ExitStack() as st:
                brT4 = S.sb("brT4", [128, 4, 8, NOWN], BF16, st); dbr = S.dep()
                S.dma("sp", brT4[:, 0, :, :], T["s_cv"].rearrange("c p t -> p c t"), writes=[dbr])
                for j in range(3):
                    S.dma("sp", brT4[:, 1 + j, :, :], T["brT"][:, j, :, :], writes=[dbr])
                bg = S.sb("bg_sb", [128, 4, 16], F32, st); dbg = S.dep()
                S.dma("sp", bg[:], T["bg"][:, :, :], writes=[dbg])
                wl = [S.sb(f"wl{i}", [128, 16, 4, 128], BF16, st) for i in range(2)]; dwl = [S.dep() for _ in range(2)]
                wb = [S.sb(f"wb{i}", [128, 8, 4, 128], BF16, st) for i in range(2)]; dwb = [S.dep() for _ in range(2)]
                gsb = S.sb("gsb", [128, 384], F32, st); dgs = S.dep()
                macc = S.sb("macc", [128, 384], F32, st); dma_ = S.dep()
                mtmp = S.sb("mtmp", [128, 384], F32, st); dmt = S.dep()
                pl = [S.ps(f"pl{i}", [128, 512], F32, st) for i in range(2)]; dpl = [S.pdep() for _ in range(2)]
                pp = [S.ps(f"pp{i}", [128, 512], F32, st) for i in range(2)]; dpp = [S.pdep() for _ in range(2)]
                Wlv = T["Wl"].rearrange("(kc p) n -> p kc n", p=128)
                it = 0
                for dc in range(16):
                    b2 = dc % 2
                    for j in range(4):
                        c0 = j * D + dc * 128
                        S.dma("pool", wl[b2][:, :, j, :], Wlv[:, :, c0:c0 + 128], writes=[dwl[b2]])
                        S.dma("pool", wb[b2][:, :, j, :], T["Wb"][j].rearrange("(cc p) n -> p cc n", p=128)[:, :, dc * 128:(dc + 1) * 128], writes=[dwb[b2]])
                    for (o_, n_, e_) in MBLK:
                        for j in range(4):
                            p1, d1 = pl[it % 2], dpl[it % 2]
                            p2, d2 = pp[it % 2], dpp[it % 2]
                            it += 1
                            for kc in range(16):
                                S.op("pe", lambda e: e.matmul(p1[:, 0:n_], lhsT=wl[b2][:, kc, j, :], rhs=hTx[:, kc, e_:e_ + n_], start=(kc == 0), stop=(kc == 15)),
                                     reads=[dwl[b2], dhTx], writes=[d1])
                            for cc in range(8):
                                S.op("pe", lambda e: e.matmul(p2[:, 0:n_], lhsT=wb[b2][:, cc, j, :], rhs=brT4[:, j, cc, o_:o_ + n_], start=(cc == 0), stop=(cc == 7)),
                                     reads=[dwb[b2], dbr], writes=[d2])
                            S.op("act", lambda e: e.activation(out=gsb[:, 0:n_], in_=p1[:, 0:n_], func=AF.Sigmoid, bias=bg[:, j, dc:dc + 1]),
                                 reads=[d1, dbg], writes=[dgs])
                            if j == 0:
                                S.op("dve", lambda e: e.tensor_tensor(out=macc[:, 0:n_], in0=p2[:, 0:n_], in1=gsb[:, 0:n_], op=ALU.mult), reads=[d2, dgs], writes=[dma_])
                            else:
                                S.op("dve", lambda e: e.tensor_tensor(out=mtmp[:, 0:n_], in0=p2[:, 0:n_], in1=gsb[:, 0:n_], op=ALU.mult), reads=[d2, dgs], writes=[dmt])
                                dst = macc[:, 0:n_] if j < 3 else mergedT[:, dc, o_:o_ + n_]
                                S.op("dve", lambda e: e.tensor_tensor(out=dst, in0=macc[:, 0:n_], in1=mtmp[:, 0:n_], op=ALU.add),
                                     reads=[dma_, dmt], writes=[dma_ if j < 3 else dbig])
                S.barrier()
        with contextlib.ExitStack() as st:
            modb = [S.sb(f"modg{i}", [128, D], F32, st) for i in range(2)]; dmodb = S.dep()
            _mod_prologue(S, nc, T["cT"], T["wmodg"], T["bmodg"], D, modb, dmodb)
            gpb = S.sb("gpb", [128, D], F32, st); dgp = S.dep()
            S.dma("sp", gpb[:], T["gpost"][0:1, :].to_broadcast([128, D]), writes=[dgp])
            for wh in range(2):
                S.op("dve", lambda e: e.tensor_tensor(out=modb[wh][:], in0=modb[wh][:], in1=gpb[:], op=ALU.mult), reads=[dmodb, dgp], writes=[dmodb])
            Wo = S.sb("Wo", [128, 16, D], BF16, st); dWo = S.dep()
            Wov = T["Wout"].rearrange("(kc p) n -> p kc n", p=128)
            for kc in range(16):
                S.dma("pool", Wo[:, kc:kc + 1, :], Wov[:, kc:kc + 1, :], writes=[dWo])
            xt = [S.sb(f"xt{i}", [128, D], F32, st) for i in range(2)]; dxt = [S.dep() for _ in range(2)]
            yb = S.sb("yb", [128, D], F32, st); dyb = S.dep()
            jk = S.sb("jk", [128, D], BF16, st); djk = S.dep()
            ss = S.sb("ss3", [128, 4], F32, st); dss = S.dep()
            py = [S.ps(f"py{i}", [128, 512], F32, st) for i in range(2)]; dpy = [S.pdep() for _ in range(2)]
            tiles = [(0, 64, 0)] + [(64 + 128 * i, 128, 1) for i in range(8)]
            for ti, (o_, n_, wh) in enumerate(tiles):
                S.drain_dma("sp", keep=4)
                x_ = xt[ti % 2]; dx_ = dxt[ti % 2]
                S.dma("sp", x_[0:n_, :], T["x_own"][o_:o_ + n_, :], writes=[dx_])
                for cg in range(4):
                    p_, dp_ = py[cg % 2], dpy[cg % 2]
                    for kc in range(16):
                        S.op("pe", lambda e: e.matmul(p_[0:n_, :], lhsT=mergedT[:, kc, o_:o_ + n_], rhs=Wo[:, kc, cg * 512:(cg + 1) * 512], start=(kc == 0), stop=(kc == 15)),
                             reads=[dbig, dWo], writes=[dp_])
                    S.op("act", lambda e: e.copy(out=yb[0:n_, cg * 512:(cg + 1) * 512], in_=p_[0:n_, :]), reads=[dp_], writes=[dyb])
                S.op("act", lambda e: e.activation(out=jk[0:n_, :], in_=yb[0:n_, :], func=AF.Square, accum_out=ss[0:n_, 0:1]), reads=[dyb], writes=[djk, dss])
                _rstd(S, ss[0:n_, 0:1], 1, 1.0 / D, EPS, [], dss)
                S.op("dve", lambda e: e.scalar_tensor_tensor(out=yb[0:n_, :], in0=yb[0:n_, :], scalar=ss[0:n_, 0:1], in1=modb[wh][0:n_, :],
                                                             op0=ALU.mult, op1=ALU.mult), reads=[dyb, dss, dmodb], writes=[dyb])
                S.op("dve", lambda e: e.tensor_tensor(out=yb[0:n_, :], in0=yb[0:n_, :], in1=x_[0:n_, :], op=ALU.add), reads=[dyb, dx_], writes=[dyb])
                S.dma("sp", T["xo"][o_:o_ + n_, :], yb[0:n_, :], reads=[dyb])
            S.barrier()
        print("build_B: ninst", S.ninst, "nsem", S.nsem, {k: v for k, v in S.cnt.items()})
    return nc


def _bf16(a):
    import ml_dtypes
    return np.ascontiguousarray(np.asarray(a).astype(ml_dtypes.bfloat16))


def _tok_maps(q):
    ctx_idx = np.arange(64 * q - 15, 64 * q + 79)
    lat_idx = np.arange(1024 * q - 15, 1024 * q + 1039)
    tok = np.full(NEXT, -1, np.int64)
    v = (ctx_idx >= 0) & (ctx_idx < NCTX)
    tok[0:94][v] = ctx_idx[v]
    v2 = (lat_idx >= 0) & (lat_idx < 4096)
    tok[94:94 + 1054][v2] = NCTX + lat_idx[v2]
    own = np.concatenate([np.arange(64 * q, 64 * q + 64), NCTX + np.arange(1024 * q, 1024 * q + 1024)])
    return tok, own


def shared_B(inp, li):
    w_in = inp["w_in"][li]
    cvp = np.concatenate([inp["conv_w"][li].T, inp["conv_b"][li][:, None], inp["conv_ln_g"][li][:, None], inp["conv_ln_b"][li][:, None]], axis=1)
    return {
        "w_cv": np.ascontiguousarray(w_in[:, 0:3072]),
        "cvp": np.ascontiguousarray(cvp.reshape(8, 128, 34).transpose(1, 0, 2).astype(np.float32)),
        "Wl": np.ascontiguousarray(w_in[:, OFF["merge"]:OFF["merge"] + 8192]),
        "Wb": np.ascontiguousarray(inp["w_branch"][li]),
        "Wout": np.ascontiguousarray(inp["w_out"][li]),
        "bg": np.ascontiguousarray(inp["b_gate"][li].reshape(4, 16, 128).transpose(2, 0, 1)),
        "wmodg": np.ascontiguousarray(inp["w_mod"][li][:, 4096:6144]),
        "bmodg": np.ascontiguousarray(inp["b_mod"][li][None, 4096:6144]),
        "gpost": np.ascontiguousarray(inp["norm_post_g"][li][None]),
    }


def inputs_B(inp, shared, b, q, xfull, hT_b, br_b):
    tok, own = _tok_maps(q)
    valid = tok >= 0
    hTx = np.zeros((16, 128, NEXT), hT_b.dtype)
    hTx[:, :, valid] = hT_b[:, :, tok[valid]]
    brT = br_b[own].reshape(NOWN, 3, 8, 128).transpose(3, 1, 2, 0)
    m = dict(shared)
    m.update({
        "hTx": np.ascontiguousarray(hTx.transpose(1, 0, 2)), "mask": valid.astype(np.float32)[None],
        "brT": np.ascontiguousarray(brT), "x_own": np.ascontiguousarray(xfull[b][own]),
        "cT": _cT(inp["c_ctx"], inp["c"][b]),
    })
    return m


def _gather_A(resA):
    out = []
    for b in range(2):
        hT_b = np.asarray(resA[4 * b]["hT"])
        parts = []
        for j in range(3):
            cols = []
            for q in range(4):
                r = resA[4 * b + q]
                if j == 0:
                    cols.append(np.asarray(r["br_rwkv"]))
                else:
                    cols.append(np.asarray(r["br_att"])[:, (j - 1) * 256:j * 256])
            parts.append(np.concatenate(cols, axis=1))
        out.append((hT_b, np.stack(parts, axis=1)))
    return out


_NC = {}


def kernel(**inputs):
    inp = {k: np.asarray(v) for k, v in inputs.items()}
    if "A" not in _NC:
        _NC["A"] = build_A()
        _NC["B"] = build_B()
    rope = _rope_table()
    xfull = np.concatenate([inp["ctx"], inp["x"]], axis=1).astype(np.float32)
    cores = list(range(8))
    for li in range(4):
        mapsA = [inputs_A(inp, li, c // 4, c % 4, xfull, rope) for c in cores]
        resA = run_bass_kernel_spmd(_NC["A"], mapsA, core_ids=cores).results
        del mapsA
        gA = _gather_A(resA)
        del resA
        sh = shared_B(inp, li)
        mapsB = [inputs_B(inp, sh, c // 4, c % 4, xfull, gA[c // 4][0], gA[c // 4][1]) for c in cores]
        resB = run_bass_kernel_spmd(_NC["B"], mapsB, core_ids=cores).results
        del mapsB
        xnew = np.empty_like(xfull)
        for c in cores:
            _, own = _tok_maps(c % 4)
            xnew[c // 4][own] = np.asarray(resB[c]["xo"])
        xfull = xnew
    return np.ascontiguousarray(xfull[:, NCTX:, :]).astype(np.float32)
```

```python
import contextlib
import math
import numpy as np
import concourse.bass as bass
import concourse.mybir as mybir
from concourse.bass_utils import run_bass_kernel_spmd

F32 = mybir.dt.float32
BF16 = mybir.dt.bfloat16
AF = mybir.ActivationFunctionType
ALU = mybir.AluOpType
AX = mybir.AxisListType

SEM_LIM = 30000
D = 2048
NT = 4352
NTILE = 34
NBLK = 17
NCTX = 256
W = 1024
EPS = 1e-6


class Dep:
    __slots__ = ("name", "w", "rs", "wsem", "wcnt", "rsem", "rcnt", "excl")

    def __init__(self, name="", excl=False):
        self.name = name
        self.excl = excl
        self.w = None
        self.rs = []
        self.wsem = None
        self.wcnt = 0
        self.rsem = None
        self.rcnt = 0


class Sched:
    def __init__(self, nc, stack):
        self.nc = nc
        self.stack = stack
        self.eng = {"pe": nc.tensor, "dve": nc.vector, "act": nc.scalar, "pool": nc.gpsimd, "sp": nc.sync}
        self.sems = {k: [] for k in self.eng}
        self.cnt = {k: 0 for k in self.eng}
        self.known = {k: {} for k in self.eng}
        self.nsem = 0
        self.ninst = 0
        self.deps = []
        self.dticks = []

    def dep(self, name="", excl=False):
        d = Dep(name, excl)
        self.deps.append(d)
        return d

    def pdep(self, name=""):
        return self.dep(name, excl=True)

    def new_sem(self, name):
        self.nsem += 1
        return self.stack.enter_context(self.nc.semaphore(f"{name}_{self.nsem}"))

    def sb(self, name, shape, dt, stack=None):
        return (stack or self.stack).enter_context(self.nc.sbuf_tensor(name, list(shape), dt))

    def ps(self, name, shape, dt=F32, stack=None):
        return (stack or self.stack).enter_context(self.nc.psum_tensor(name, list(shape), dt))

    def _wait(self, e, tick):
        sem, val, src = tick
        kn = self.known[e]
        if kn.get(id(sem), 0) >= val:
            return
        self.eng[e].wait_ge(sem, val)
        kn[id(sem)] = val
        if src in self.sems:
            for s in self.sems[src]:
                if s is sem:
                    break
                kn[id(s)] = SEM_LIM

    def _deps(self, e, reads, writes, dma=False):
        for d in reads:
            if d.w is not None:
                self._wait(e, d.w)
            if d.excl:
                for r in d.rs:
                    if r[2] != e:
                        self._wait(e, r)
        for d in writes:
            if d.w is not None and (dma or d.w[2] != e or e != "pe"):
                self._wait(e, d.w)
            for r in d.rs:
                self._wait(e, r)

    def op(self, e, fn, reads=(), writes=()):
        self._deps(e, reads, writes)
        c = self.cnt[e]
        if c % SEM_LIM == 0:
            self.sems[e].append(self.new_sem(e))
        sem = self.sems[e][-1]
        val = c % SEM_LIM + 1
        self.cnt[e] = c + 1
        ins = fn(self.eng[e])
        ins.then_inc(sem, 1)
        self.ninst += 1
        tick = (sem, val, e)
        for d in reads:
            d.rs.append(tick)
        for d in writes:
            d.w = tick
            d.rs = []
        return tick

    def dma(self, e, out, in_, reads=(), writes=(), **kw):
        self._deps(e, reads, writes, dma=True)
        if writes:
            d0 = writes[0]
            if d0.wsem is None:
                d0.wsem = self.new_sem("dw")
            d0.wcnt += 16
            tick = (d0.wsem, d0.wcnt, "dma")
        else:
            d0 = reads[0]
            if d0.rsem is None:
                d0.rsem = self.new_sem("dr")
            d0.rcnt += 16
            tick = (d0.rsem, d0.rcnt, "dma")
        ins = self.eng[e].dma_start(out=out, in_=in_, **kw)
        ins.then_inc(tick[0], 16)
        self.ninst += 1
        self.dticks.append(tick)
        for d in reads:
            d.rs.append(tick)
        for d in writes:
            d.w = tick
            d.rs = []
        return tick

    def wait_all(self, e, deps):
        for d in deps:
            if d.w is not None:
                self._wait(e, d.w)
            for r in d.rs:
                self._wait(e, r)

    def drain_dma(self, e, keep=0):
        n = len(self.dticks) - keep
        for t in self.dticks[:max(n, 0)]:
            self._wait(e, t)
        self.dticks = self.dticks[max(n, 0):]

    def barrier(self):
        for e in self.eng:
            self.wait_all(e, self.deps)
        for d in self.deps:
            d.rs = d.rs[-8:]


def _mod_prologue(S, nc, cT, wmod, bmod, ncols, modb, dmodb):
    with contextlib.ExitStack() as st:
        ct = S.sb("m_ct", [128, 16, 2], F32, st); dct = S.dep()
        cs = S.sb("m_cs", [128, 16, 2], F32, st); dcs = S.dep()
        rep = S.sb("m_rep", [128, 2, 16, 128], F32, st); drep = S.dep()
        ones1 = S.sb("m_ones", [1, 128], F32, st); dones = S.dep()
        bm = S.sb("m_bm", [1, ncols], F32, st); dbm = S.dep()
        wt = [S.sb(f"m_wt{i}", [128, 16, 512], F32, st) for i in range(2)]
        dwt = [S.dep() for _ in range(2)]
        pm = [S.ps(f"m_pm{i}", [128, 512], F32, st) for i in range(2)]
        dpm = [S.pdep() for _ in range(2)]
        S.dma("sp", ct[:], cT[:, :, :], writes=[dct])
        S.dma("sp", bm[:], bmod[:, :], writes=[dbm])
        S.op("dve", lambda e: e.memset(ones1[:], 1.0), writes=[dones])
        S.op("act", lambda e: e.activation(out=cs[:], in_=ct[:], func=AF.Silu), reads=[dct], writes=[dcs])
        for wh in range(2):
            for kc in range(16):
                S.op("dve", lambda e: e.tensor_copy(out=rep[:, wh, kc, :], in_=cs[:, kc, wh:wh + 1].to_broadcast([128, 128])),
                     reads=[dcs], writes=[drep])
        ng = ncols // 512
        wv = wmod.rearrange("(kc p) n -> p kc n", p=128)
        for g in range(ng):
            b = g % 2
            for h4 in range(4):
                S.dma("sp", wt[b][:, h4 * 4:(h4 + 1) * 4, :], wv[:, h4 * 4:(h4 + 1) * 4, g * 512:(g + 1) * 512], writes=[dwt[b]])
            for wh in range(2):
                for kc in range(16):
                    S.op("pe", lambda e: e.matmul(pm[wh][:], lhsT=rep[:, wh, kc, :], rhs=wt[b][:, kc, :], start=(kc == 0), stop=False),
                         reads=[drep, dwt[b]], writes=[dpm[wh]])
                S.op("pe", lambda e: e.matmul(pm[wh][:], lhsT=ones1[:], rhs=bm[:, g * 512:(g + 1) * 512], start=False, stop=True),
                     reads=[dones, dbm], writes=[dpm[wh]])
                S.op("act", lambda e: e.copy(out=modb[wh][:, g * 512:(g + 1) * 512], in_=pm[wh][:]), reads=[dpm[wh]], writes=[dmodb])
        S.barrier()


def _make_ident(S, st, n, dt, name):
    f = S.sb(name + "_f", [n, n], F32, st)
    df = S.dep()
    S.op("pool", lambda e: e.memset(f[:], 1.0), writes=[df])
    S.op("pool", lambda e: e.affine_select(out=f[:], in_=f[:], pattern=[[-1, n]], compare_op=ALU.is_equal, fill=0.0,
                                           base=0, channel_multiplier=1), reads=[df], writes=[df])
    if dt == F32:
        return f, df
    b = S.sb(name + "_b", [n, n], dt, st)
    db = S.dep()
    S.op("dve", lambda e: e.tensor_copy(out=b[:], in_=f[:]), reads=[df], writes=[db])
    return b, db


def _rope(S, src, dst, cos, sin, G, P, t1, t2, reads, writes, dtmp):
    sv = src.rearrange("p g (i t) -> p g i t", t=2)
    dv = dst.rearrange("p g (i t) -> p g i t", t=2)
    cb = cos.unsqueeze(1).to_broadcast([128, G, P])
    sbb = sin.unsqueeze(1).to_broadcast([128, G, P])
    S.op("dve", lambda e: e.tensor_tensor(out=t1, in0=sv[:, :, :, 0], in1=cb, op=ALU.mult), reads=reads, writes=[dtmp])
    S.op("dve", lambda e: e.tensor_tensor(out=t2, in0=sv[:, :, :, 1], in1=sbb, op=ALU.mult), reads=reads, writes=[dtmp])
    S.op("dve", lambda e: e.tensor_tensor(out=dv[:, :, :, 0], in0=t1, in1=t2, op=ALU.subtract), reads=[dtmp], writes=writes)
    S.op("dve", lambda e: e.tensor_tensor(out=t1, in0=sv[:, :, :, 0], in1=sbb, op=ALU.mult), reads=reads + [dtmp], writes=[dtmp])
    S.op("dve", lambda e: e.tensor_tensor(out=t2, in0=sv[:, :, :, 1], in1=cb, op=ALU.mult), reads=reads + [dtmp], writes=[dtmp])
    S.op("dve", lambda e: e.tensor_tensor(out=dv[:, :, :, 1], in0=t1, in1=t2, op=ALU.add), reads=[dtmp], writes=writes)


def _rstd(S, ss, n, scale, eps, reads, dss):
    S.op("dve", lambda e: e.tensor_scalar(out=ss, in0=ss, scalar1=scale, scalar2=eps, op0=ALU.mult, op1=ALU.add),
         reads=reads + [dss], writes=[dss])
    S.op("act", lambda e: e.activation(out=ss, in_=ss, func=AF.Sqrt), reads=[dss], writes=[dss])
    S.op("dve", lambda e: e.reciprocal(out=ss, in_=ss), reads=[dss], writes=[dss])


def _phase_A1(S, nc, T):
    with contextlib.ExitStack() as st:
        identb, did = _make_ident(S, st, 128, BF16, "a1id")
        modb = [S.sb(f"modb{i}", [128, 4096], F32, st) for i in range(2)]
        dmodb = S.dep()
        _mod_prologue(S, nc, T["cT"], T["wmod"], T["bmod"], 4096, modb, dmodb)
        with contextlib.ExitStack() as st2:
            gb = S.sb("gb", [128, 2048], F32, st2); dgb = S.dep()
            S.dma("sp", gb[:], T["gpre"][0:1, :].to_broadcast([128, 2048]), writes=[dgb])
            for wh in range(2):
                S.op("dve", lambda e: e.scalar_tensor_tensor(out=modb[wh][:, 2048:4096], in0=modb[wh][:, 2048:4096], scalar=1.0,
                                                             in1=gb[:], op0=ALU.add, op1=ALU.mult), reads=[dmodb, dgb], writes=[dmodb])
            S.barrier()
        gqn = S.sb("gqn_sb", [128, 2, 128], F32, st); dgqn = S.dep()
        S.dma("sp", gqn[:].rearrange("p a b -> p (a b)"), T["gqn"][0:1, :].to_broadcast([128, 256]), writes=[dgqn])
        wfm = S.sb("wfm", [128, 16, 704], BF16, st); dwfm = S.dep()
        wtm = S.sb("wtm", [128, 16, 2304], BF16, st); dwtm = S.dep()
        wfv = T["w_fm"].rearrange("(kc p) n -> p kc n", p=128)
        wtv = T["w_tm"].rearrange("(kc p) n -> p kc n", p=128)
        for k4 in range(8):
            S.dma("pool", wfm[:, k4 * 2:(k4 + 1) * 2, :], wfv[:, k4 * 2:(k4 + 1) * 2, :], writes=[dwfm])
        for k4 in range(16):
            S.dma("pool", wtm[:, k4:(k4 + 1), :], wtv[:, k4:(k4 + 1), :], writes=[dwtm])
        xb = [S.sb(f"xb{i}", [128, 2048], F32, st) for i in range(2)]; dxb = [S.dep() for _ in range(2)]
        rpb = [S.sb(f"rp{i}", [128, 192], F32, st) for i in range(2)]; drp = [S.dep() for _ in range(2)]
        hb = S.sb("hb", [128, 2048], BF16, st); dhb = S.dep()
        ss = S.sb("ss", [128, 4], F32, st); dss = S.dep()
        hTb = [S.sb(f"hTb{i}", [128, 16, 256], BF16, st) for i in range(2)]; dhT = [S.dep() for _ in range(2)]
        rkst = S.sb("rkst", [64, 8, 256], F32, st); drkst = S.dep()
        wast = S.sb("wast", [96, 2, 256], F32, st); dwast = S.dep()
        stf = S.sb("stf", [128, 2304], F32, st); dstf = [S.dep() for _ in range(5)]
        gst = S.sb("gst", [128, 768], BF16, st); dgst = S.dep()
        qkd = S.sb("qkd", [128, 8, 64], BF16, st); dqkd = S.dep()
        qkTd = S.sb("qkTd", [64, 8, 128], BF16, st); dqkTd = S.dep()
        qkg = S.sb("qkg", [128, 3, 128], BF16, st); dqkg = S.dep()
        qkgn = S.sb("qkgn", [128, 3, 128], F32, st); dqkgn = S.dep()
        qkTg = S.sb("qkTg", [128, 3, 128], BF16, st); dqkTg = S.dep()
        vdst = S.sb("vdst", [128, 2, 129], BF16, st); dvd = S.dep()
        vgst = S.sb("vgst", [128, 129], BF16, st); dvg = S.dep()
        rt1 = S.sb("rt1", [128, 8, 32], F32, st); rt2 = S.sb("rt2", [128, 8, 32], F32, st); drt = S.dep()
        sqg = S.sb("sqg", [128, 3, 128], F32, st); dsqg = S.dep()
        ssg = S.sb("ssg", [128, 4], F32, st); dssg = S.dep()
        pf = S.ps("pf", [64, 8, 256], F32, st); dpf = S.pdep()
        pw = S.ps("pw", [96, 2, 256], F32, st); dpw = S.pdep()
        pg = [S.ps(f"pg{i}", [128, 512], F32, st) for i in range(2)]; dpg = [S.pdep() for _ in range(2)]
        ptb = S.ps("ptb", [128, 8, 128], BF16, st); dptb = S.pdep()
        S.op("dve", lambda e: e.memset(vdst[:], 1.0), writes=[dvd])
        S.op("dve", lambda e: e.memset(vgst[:], 1.0), writes=[dvg])

        s_rk = T["s_rk"].rearrange("g c t -> c g t")
        s_wa = T["s_wa"].rearrange("j c t -> c j t")
        s_dqk = T["s_dqkT"].rearrange("g c t -> c g t")
        s_gqk = T["s_gqkT"].rearrange("g c t -> c g t")
        hTo = T["hT"].rearrange("kc p t -> p kc t")
        import os
        nblk = int(os.environ.get('A1_BLOCKS', NBLK))

        def prep(blk):
            wh = 0 if blk == 0 else 1
            S.drain_dma("sp", keep=12)
            hT = hTb[blk % 2]; dh = dhT[blk % 2]
            for ti in range(2):
                t = 2 * blk + ti
                tok0 = t * 128
                xt = xb[t % 2]; dx = dxb[t % 2]; rp = rpb[t % 2]; dr = drp[t % 2]
                S.dma("sp", xt[:], T["xf"][tok0:tok0 + 128, :], writes=[dx])
                S.op("act", lambda e: e.activation(out=hb[:], in_=xt[:], func=AF.Square, accum_out=ss[:, 0:1]),
                     reads=[dx], writes=[dhb, dss])
                _rstd(S, ss[:, 0:1], 1, 1.0 / D, EPS, [], dss)
                S.op("dve", lambda e: e.scalar_tensor_tensor(out=xt[:], in0=xt[:], scalar=ss[:, 0:1], in1=modb[wh][:, 2048:4096],
                                                             op0=ALU.mult, op1=ALU.mult), reads=[dx, dss, dmodb], writes=[dx])
                S.op("dve", lambda e: e.tensor_tensor(out=hb[:], in0=xt[:], in1=modb[wh][:, 0:2048], op=ALU.add),
                     reads=[dx, dmodb], writes=[dhb])
                for half in range(2):
                    for j in range(8):
                        kc = half * 8 + j
                        S.op("pe", lambda e: e.transpose(out=ptb[:, j, :], in_=hb[:, kc * 128:(kc + 1) * 128], identity=identb[:]),
                             reads=[dhb, did], writes=[dptb])
                    S.op("act", lambda e: e.copy(out=hT[:, half * 8:(half + 1) * 8, ti * 128:(ti + 1) * 128], in_=ptb[:]),
                         reads=[dptb], writes=[dh])
            S.dma("sp", hTo[:, :, blk * 256:(blk + 1) * 256], hT[:], reads=[dh])

        def mm(blk):
            hT = hTb[blk % 2]; dh = dhT[blk % 2]
            for g in range(8):
                for kc in range(16):
                    S.op("pe", lambda e: e.matmul(pf[:, g, :], lhsT=wfm[:, kc, g * 64:(g + 1) * 64], rhs=hT[:, kc, :],
                                                  start=(kc == 0), stop=(kc == 15)), reads=[dwfm, dh], writes=[dpf])
            S.op("act", lambda e: e.copy(out=rkst[:], in_=pf[:]), reads=[dpf], writes=[drkst])
            S.dma("sp", s_rk[:, :, blk * 256:(blk + 1) * 256], rkst[:], reads=[drkst])
            for j in range(2):
                for kc in range(16):
                    S.op("pe", lambda e: e.matmul(pw[:, j, :], lhsT=wfm[:, kc, 512 + j * 96:512 + (j + 1) * 96], rhs=hT[:, kc, :],
                                                  start=(kc == 0), stop=(kc == 15)), reads=[dwfm, dh], writes=[dpw])
            S.op("act", lambda e: e.activation(out=wast[:, 0, :], in_=pw[:, 0, :], func=AF.Tanh), reads=[dpw], writes=[dwast])
            S.op("act", lambda e: e.copy(out=wast[:, 1, :], in_=pw[:, 1, :]), reads=[dpw], writes=[dwast])
            S.dma("sp", s_wa[:, :, blk * 256:(blk + 1) * 256], wast[:], reads=[dwast])
            for ti in range(2):
                t = 2 * blk + ti
                tok0 = t * 128
                rp = rpb[t % 2]; dr = drp[t % 2]
                S.dma("sp", rp[:], T["rope"][tok0:tok0 + 128, :], writes=[dr])
                for gi in range(5):
                    ncol = 512 if gi < 4 else 256
                    p = pg[gi % 2]; dp = dpg[gi % 2]
                    for kc in range(16):
                        S.op("pe", lambda e: e.matmul(p[:, 0:ncol], lhsT=hT[:, kc, ti * 128:(ti + 1) * 128],
                                                      rhs=wtm[:, kc, gi * 512:gi * 512 + ncol], start=(kc == 0), stop=(kc == 15)),
                             reads=[dwtm, dh], writes=[dp])
                    S.op("act", lambda e: e.copy(out=stf[:, gi * 512:gi * 512 + ncol], in_=p[:, 0:ncol]), reads=[dp], writes=[dstf[gi]])
                S.dma("sp", T["s_v"][tok0:tok0 + 128, :], stf[:, 0:256], reads=[dstf[0]])
                S.op("act", lambda e: e.activation(out=gst[:, 0:256], in_=stf[:, 256:512], func=AF.Silu), reads=[dstf[0]], writes=[dgst])
                _rope(S, stf[:, 512:1024].rearrange("p (g d) -> p g d", g=8), qkd[:], rp[:, 0:32], rp[:, 32:64], 8, 32,
                      rt1[:], rt2[:], [dstf[1], dr], [dqkd], drt)
                for g in range(8):
                    S.op("pe", lambda e: e.transpose(out=ptb[0:64, g, :], in_=qkd[:, g, :], identity=identb[:]),
                         reads=[dqkd, did], writes=[dptb])
                S.op("act", lambda e: e.copy(out=qkTd[:], in_=ptb[0:64, :, :]), reads=[dptb], writes=[dqkTd])
                S.dma("sp", s_dqk[:, :, tok0:tok0 + 128], qkTd[:], reads=[dqkTd])
                S.op("act", lambda e: e.copy(out=vdst[:, :, 0:128], in_=stf[:, 1024:1280].rearrange("p (h d) -> p h d", h=2)),
                     reads=[dstf[2]], writes=[dvd])
                S.dma("sp", T["s_dv"][tok0:tok0 + 128, :, :], vdst[:], reads=[dvd])
                S.op("act", lambda e: e.activation(out=gst[:, 256:512], in_=stf[:, 1280:1536], func=AF.Silu), reads=[dstf[2]], writes=[dgst])
                src3 = stf[:, 1536:1920].rearrange("p (g d) -> p g d", g=3)
                S.op("dve", lambda e: e.tensor_tensor(out=sqg[:], in0=src3, in1=src3, op=ALU.mult), reads=[dstf[3]], writes=[dsqg])
                S.op("dve", lambda e: e.tensor_reduce(out=ssg[:, 0:3], in_=sqg[:], axis=AX.X, op=ALU.add), reads=[dsqg], writes=[dssg])
                _rstd(S, ssg[:, 0:3], 3, 1.0 / 128, EPS, [], dssg)
                for i in range(3):
                    S.op("dve", lambda e: e.scalar_tensor_tensor(out=qkgn[:, i, :], in0=src3[:, i, :], scalar=ssg[:, i:i + 1],
                                                                 in1=gqn[:, (0 if i < 2 else 1), :], op0=ALU.mult, op1=ALU.mult),
                         reads=[dstf[3], dssg, dgqn], writes=[dqkgn])
                _rope(S, qkgn[:], qkg[:], rp[:, 64:128], rp[:, 128:192], 3, 64,
                      rt1[:].rearrange("p a b -> p (a b)")[:, 0:192].rearrange("p (a b) -> p a b", a=3),
                      rt2[:].rearrange("p a b -> p (a b)")[:, 0:192].rearrange("p (a b) -> p a b", a=3),
                      [dqkgn, dr], [dqkg], drt)
                for g in range(3):
                    S.op("pe", lambda e: e.transpose(out=ptb[:, g, :], in_=qkg[:, g, :], identity=identb[:]),
                         reads=[dqkg, did], writes=[dptb])
                S.op("act", lambda e: e.copy(out=qkTg[:], in_=ptb[:, 0:3, :]), reads=[dptb], writes=[dqkTg])
                S.dma("sp", s_gqk[:, :, tok0:tok0 + 128], qkTg[:], reads=[dqkTg])
                S.op("act", lambda e: e.copy(out=vgst[:, 0:128], in_=stf[:, 1920:2048]), reads=[dstf[3]], writes=[dvg])
                S.dma("sp", T["s_gv"][tok0:tok0 + 128, :], vgst[:], reads=[dvg])
                S.op("act", lambda e: e.activation(out=gst[:, 512:768], in_=stf[:, 2048:2304], func=AF.Silu), reads=[dstf[4]], writes=[dgst])
                S.dma("sp", T["s_gate"][tok0:tok0 + 128, :], gst[:], reads=[dgst])

        if nblk > 0:
            prep(0)
        for blk in range(nblk):
            if blk + 1 < nblk:
                prep(blk + 1)
            mm(blk)
        S.barrier()


def _phase_A2(S, nc, T):
    with contextlib.ExitStack() as st:
        KTd = S.sb("KTd", [64, 4, NT], BF16, st); dKTd = S.dep()
        Vd = S.sb("Vd", [128, NTILE, 2, 129], BF16, st); dVd = S.dep()
        KTg = S.sb("KTg", [128, NT], BF16, st); dKTg = S.dep()
        Vg = S.sb("Vg", [128, NTILE, 129], BF16, st); dVg = S.dep()
        for g in range(4):
            S.dma("sp", KTd[:, g, :], T["s_dqkT"][4 + g, :, :], writes=[dKTd])
        S.dma("sp", KTg[:], T["s_gqkT"][2, :, :], writes=[dKTg])
        for c in range(2):
            S.dma("sp", Vd[:, c * 17:(c + 1) * 17, :, :], T["s_dv"].rearrange("(t p) h d -> p t h d", p=128)[:, c * 17:(c + 1) * 17, :, :], writes=[dVd])
        S.dma("sp", Vg[:], T["s_gv"].rearrange("(t p) d -> p t d", p=128), writes=[dVg])
        lamp = S.sb("lamp_sb", [128, 4, 64], F32, st); dlam = S.dep()
        lam = S.sb("lam", [128, 8], F32, st)
        S.dma("sp", lamp[:].rearrange("p a b -> p (a b)"), T["lamp"][0:1, :].to_broadcast([128, 256]), writes=[dlam])
        S.dma("sp", lam[:, 4:5], T["lami"][0:1, 0:1].to_broadcast([128, 1]), writes=[dlam])
        prod = S.sb("lprod", [128, 2, 64], F32, st)
        S.op("dve", lambda e: e.tensor_tensor(out=prod[:, 0, :], in0=lamp[:, 0, :], in1=lamp[:, 1, :], op=ALU.mult), reads=[dlam], writes=[dlam])
        S.op("dve", lambda e: e.tensor_tensor(out=prod[:, 1, :], in0=lamp[:, 2, :], in1=lamp[:, 3, :], op=ALU.mult), reads=[dlam], writes=[dlam])
        S.op("dve", lambda e: e.tensor_reduce(out=lam[:, 0:2], in_=prod[:], axis=AX.X, op=ALU.add), reads=[dlam], writes=[dlam])
        S.op("act", lambda e: e.activation(out=lam[:, 2:4], in_=lam[:, 0:2], func=AF.Exp), reads=[dlam], writes=[dlam])
        S.op("dve", lambda e: e.tensor_tensor(out=lam[:, 5:6], in0=lam[:, 2:3], in1=lam[:, 3:4], op=ALU.subtract), reads=[dlam], writes=[dlam])
        S.op("dve", lambda e: e.tensor_tensor(out=lam[:, 6:7], in0=lam[:, 5:6], in1=lam[:, 4:5], op=ALU.add), reads=[dlam], writes=[dlam])
        gsub = S.sb("gsub", [128, 128], F32, st); dgsub = S.dep()
        S.dma("sp", gsub[:], T["subg"][0:1, :].to_broadcast([128, 128]), writes=[dgsub])
        S.op("dve", lambda e: e.tensor_scalar(out=lam[:, 7:8], in0=lam[:, 4:5], scalar1=-1.0, scalar2=1.0, op0=ALU.mult, op1=ALU.add),
             reads=[dlam], writes=[dlam])
        S.op("dve", lambda e: e.tensor_scalar(out=gsub[:], in0=gsub[:], scalar1=lam[:, 7:8], scalar2=None, op0=ALU.mult),
             reads=[dlam, dgsub], writes=[dgsub])

        QTd = [S.sb(f"QTd{i}", [64, 4, 256], BF16, st) for i in range(2)]; dQd = [S.dep() for _ in range(2)]
        QTg = [S.sb(f"QTg{i}", [128, 2, 256], BF16, st) for i in range(2)]; dQg = [S.dep() for _ in range(2)]
        gt = [S.sb(f"gt{i}", [128, 2, 768], BF16, st) for i in range(2)]; dgt = [S.dep() for _ in range(2)]
        PT = [S.sb(f"PT{i}", [128, 2, 256], BF16, st) for i in range(3)]; dPT = [S.dep() for _ in range(3)]
        pss = [S.ps(f"pss{i}", [128, 2, 256], F32, st) for i in range(3)]; dpss = [S.pdep() for _ in range(3)]
        acc = [[S.ps(f"acc{j}{q}", [128, 512], F32, st) for q in range(2)] for j in range(2)]
        dacc = [[S.pdep() for q in range(2)] for j in range(2)]
        rs = S.sb("rs", [128, 8], F32, st); drs = S.dep()
        accs = S.sb("accs", [128, 4, 129], F32, st); daccs = S.dep()
        o0 = S.sb("o0", [128, 128], F32, st); do0 = S.dep()
        dd = S.sb("dd", [128, 128], F32, st); ddd = S.dep()
        junk = S.sb("junk", [128, 128], F32, st); djunk = S.dep()
        brs = [S.sb(f"brs{i}", [128, 512], BF16, st) for i in range(2)]; dbrs = [S.dep() for _ in range(2)]
        s_dqk = T["s_dqkT"].rearrange("g c t -> c g t")
        s_gqk = T["s_gqkT"].rearrange("g c t -> c g t")
        steps = []
        for qb in range(NBLK):
            kts = list(range(2)) if qb == 0 else list(range(NTILE))
            for u in range(3):
                for ki, kt in enumerate(kts):
                    steps.append((qb, u, ki, kt, len(kts)))

        def emit_S(i):
            qb, u, ki, kt, nk = steps[i]
            b2 = qb % 2
            q0 = qb * 256
            if u == 0 and ki == 0:
                S.drain_dma("sp", keep=8)
                S.dma("sp", QTd[b2][:], s_dqk[:, 0:4, q0:q0 + 256], writes=[dQd[b2]])
                S.dma("sp", QTg[b2][:], s_gqk[:, 0:2, q0:q0 + 256], writes=[dQg[b2]])
                S.dma("sp", gt[b2][:], T["s_gate"].rearrange("(t p) n -> p t n", p=128)[:, 2 * qb:2 * qb + 2, :], writes=[dgt[b2]])
            ps_ = pss[i % 3]; dps_ = dpss[i % 3]
            for j in range(2):
                if u < 2:
                    S.op("pe", lambda e: e.matmul(ps_[:, j, :], lhsT=KTd[:, u * 2 + j, kt * 128:(kt + 1) * 128], rhs=QTd[b2][:, u * 2 + j, :],
                                                  start=True, stop=True), reads=[dKTd, dQd[b2]], writes=[dps_])
                else:
                    S.op("pe", lambda e: e.matmul(ps_[:, j, :], lhsT=KTg[:, kt * 128:(kt + 1) * 128], rhs=QTg[b2][:, j, :],
                                                  start=True, stop=True), reads=[dKTg, dQg[b2]], writes=[dps_])

        def emit_EPV(i):
            qb, u, ki, kt, nk = steps[i]
            ps_ = pss[i % 3]; dps_ = dpss[i % 3]; pt_ = PT[i % 3]; dpt_ = dPT[i % 3]
            sc = (64 ** -0.5) if u < 2 else (128 ** -0.5)
            S.op("act", lambda e: e.activation(out=pt_[:], in_=ps_[:], func=AF.Exp, scale=sc), reads=[dps_], writes=[dpt_])
            for j in range(2):
                for q in range(2):
                    rhs = Vd[:, kt, u, :] if u < 2 else Vg[:, kt, :]
                    S.op("pe", lambda e: e.matmul(acc[j][q][:, 0:129], lhsT=pt_[:, j, q * 128:(q + 1) * 128], rhs=rhs,
                                                  start=(ki == 0), stop=(ki == nk - 1)),
                         reads=[dpt_, dVd if u < 2 else dVg], writes=[dacc[j][q]])

        def finalize(qb, u):
            b2 = qb % 2
            q0 = qb * 256
            for j in range(2):
                for q in range(2):
                    S.op("dve", lambda e: e.tensor_copy(out=accs[:, j * 2 + q, :], in_=acc[j][q][:, 0:129]), reads=[dacc[j][q]], writes=[daccs])
            for q in range(2):
                bs = brs[q]; dbs = dbrs[q]
                if u < 2:
                    S.op("dve", lambda e: e.reciprocal(out=rs[:, 0:1], in_=accs[:, 0 * 2 + q, 128:129]), reads=[daccs], writes=[drs])
                    S.op("dve", lambda e: e.reciprocal(out=rs[:, 1:2], in_=accs[:, 1 * 2 + q, 128:129]), reads=[daccs], writes=[drs])
                    S.op("dve", lambda e: e.scalar_tensor_tensor(out=rs[:, 2:3], in0=rs[:, 1:2], scalar=-1.0, in1=lam[:, 6:7],
                                                                 op0=ALU.mult, op1=ALU.mult), reads=[drs, dlam], writes=[drs])
                    S.op("dve", lambda e: e.tensor_scalar(out=o0[:], in0=accs[:, 0 * 2 + q, 0:128], scalar1=rs[:, 0:1], scalar2=None, op0=ALU.mult),
                         reads=[daccs, drs], writes=[do0])
                    S.op("dve", lambda e: e.scalar_tensor_tensor(out=dd[:], in0=accs[:, 1 * 2 + q, 0:128], scalar=rs[:, 2:3], in1=o0[:],
                                                                 op0=ALU.mult, op1=ALU.add), reads=[daccs, drs, do0], writes=[ddd])
                    S.op("act", lambda e: e.activation(out=junk[:], in_=dd[:], func=AF.Square, accum_out=rs[:, 3:4]),
                         reads=[ddd], writes=[djunk, drs])
                    _rstd(S, rs[:, 3:4], 1, 1.0 / 128, EPS, [], drs)
                    S.op("dve", lambda e: e.scalar_tensor_tensor(out=o0[:], in0=dd[:], scalar=rs[:, 3:4], in1=gsub[:],
                                                                 op0=ALU.mult, op1=ALU.mult), reads=[ddd, drs, dgsub], writes=[do0])
                    S.op("dve", lambda e: e.tensor_tensor(out=bs[:, u * 128:(u + 1) * 128], in0=o0[:],
                                                          in1=gt[b2][:, q, 256 + u * 128:256 + (u + 1) * 128], op=ALU.mult),
                         reads=[do0, dgt[b2]], writes=[dbs])
                else:
                    for j in range(2):
                        S.op("dve", lambda e: e.reciprocal(out=rs[:, 4 + j:5 + j], in_=accs[:, j * 2 + q, 128:129]), reads=[daccs], writes=[drs])
                        S.op("dve", lambda e: e.scalar_tensor_tensor(out=bs[:, 256 + j * 128:256 + (j + 1) * 128], in0=accs[:, j * 2 + q, 0:128],
                                                                     scalar=rs[:, 4 + j:5 + j], in1=gt[b2][:, q, 512 + j * 128:512 + (j + 1) * 128],
                                                                     op0=ALU.mult, op1=ALU.mult), reads=[daccs, drs, dgt[b2]], writes=[dbs])
                    S.dma("sp", T["br_att"][q0 + q * 128:q0 + (q + 1) * 128, :], bs[:], reads=[dbs])

        emit_S(0)
        emit_S(1)
        for i in range(len(steps)):
            if i + 2 < len(steps):
                emit_S(i + 2)
            emit_EPV(i)
            qb, u, ki, kt, nk = steps[i]
            if ki == nk - 1:
                finalize(qb, u)
        S.barrier()


def _phase_A3(S, nc, T):
    NCH = NT // 64
    with contextlib.ExitStack() as st:
        identf, didf = _make_ident(S, st, 64, F32, "a3id")
        ones64 = S.sb("ones64", [64, 64], F32, st); dconst = S.dep()
        S.op("dve", lambda e: e.memset(ones64[:], 1.0), writes=[dconst])
        mk = S.sb("mk", [64, 4, 64], F32, st)
        S.op("pool", lambda e: e.memset(mk[:], 1.0), writes=[dconst])
        for i, (stp, cm, cmp_) in enumerate([(1, -1, ALU.is_gt), (1, -1, ALU.is_ge), (-1, 1, ALU.is_gt), (-1, 1, ALU.is_ge)]):
            S.op("pool", lambda e: e.affine_select(out=mk[:, i, :], in_=mk[:, i, :], pattern=[[stp, 64]], compare_op=cmp_, fill=0.0,
                                                   base=0, channel_multiplier=cm), reads=[dconst], writes=[dconst])
        MSK = S.sb("MSK", [64, 2, 320], F32, st)
        for e_ in range(2):
            order = [0, 1, 0, 1, 2] if e_ == 0 else [2, 3, 2, 3, 0]
            for j, m in enumerate(order):
                S.op("dve", lambda e: e.tensor_copy(out=MSK[:, e_, j * 64:(j + 1) * 64], in_=mk[:, m, :]), reads=[dconst], writes=[dconst])
        rmask = S.sb("rmask", [64, 16, 64], F32, st)
        S.op("dve", lambda e: e.memset(rmask[:], 1.0), writes=[dconst])
        S.op("dve", lambda e: e.memset(rmask[:, :, 0:1], 0.0), reads=[dconst], writes=[dconst])
        prm = S.sb("prm", [64, 7, 4], F32, st)
        S.dma("sp", prm[:], T["rprm"][:, :, :], writes=[dconst])
        omk = S.sb("omk", [64, 4], F32, st)
        S.op("dve", lambda e: e.tensor_scalar(out=omk[:], in0=prm[:, 5, :], scalar1=-1.0, scalar2=1.0, op0=ALU.mult, op1=ALU.add),
             reads=[dconst], writes=[dconst])
        wup = S.sb("wup_sb", [96, 2, 256], F32, st)
        aup = S.sb("aup_sb", [96, 2, 256], F32, st)
        S.dma("sp", wup[:], T["wup"].rearrange("e r c -> r e c"), writes=[dconst])
        S.dma("sp", aup[:], T["aup"].rearrange("e r c -> r e c"), writes=[dconst])
        gng = S.sb("gng", [64, 2, 256], F32, st)
        S.dma("sp", gng[:].rearrange("p a b -> p (a b)"), T["gn"][0:1, :].to_broadcast([64, 512]), writes=[dconst])

        yacc = S.sb("yacc", [64, NCH, 256], F32, st); dy = S.dep()
        bacc = S.sb("bacc", [64, NCH, 4], F32, st); dba = S.dep()
        ST = S.sb("ST", [64, 4, 64], F32, st); dST = S.dep()
        PSA = S.ps("PSA", [64, 8, 512], F32, st); dB = [S.pdep() for _ in range(8)]

        def t4(name):
            return S.sb(name, [64, 4, 256], F32, st), S.dep()
        rk_in = S.sb("rk_in", [64, 8, 256], F32, st); drk = S.dep()
        wa_in = S.sb("wa_in", [96, 2, 256], F32, st); dwa = S.dep()
        v_in, dv = t4("v_in")
        lw, dlw = t4("lw"); aa, daa = t4("aa"); kk, dkk = t4("kk"); kd, dkd = t4("kd"); be, dbe = t4("be")
        L, dL = t4("L"); Ld, dLd = t4("Ld"); tmp, dtmp = t4("tmp")
        E1, dE1 = t4("E1"); E2, dE2 = t4("E2"); E3, dE3 = t4("E3"); E4, dE4 = t4("E4")
        bt, dbt = t4("bt"); kt_, dkt = t4("kt_"); bh, dbh = be, dbe; kh, dkh = kd, dkd
        AR = S.sb("AR", [64, 4, 4, 2, 64], F32, st); dAR = S.dep()
        ltot = S.sb("ltot", [64, 4, 4], F32, st); dlt = S.dep()
        pc = S.sb("pc", [64, 4, 4], F32, st); dpc = S.dep()
        bkT = S.sb("bkT", [64, 4, 4, 2, 64], F32, st); dbk = S.dep()
        GM = S.sb("GM", [64, 4, 4, 320], F32, st); dGM = S.dep()
        XX0 = S.sb("XX0", [64, 16, 2, 64], F32, st); XX = [XX0, XX0]; dXX0 = S.dep(); dXX = [dXX0, dXX0]
        Tm = S.sb("Tm", [64, 16, 64], F32, st); dTm = S.dep()
        WT = S.sb("WT", [64, 4, 64], F32, st); dWT = S.dep()
        UT = S.sb("UT", [64, 4, 64], F32, st); dUT = S.dep()
        tS = S.sb("tS", [64, 4, 64], F32, st); dtS = S.dep()

        s_rk = T["s_rk"].rearrange("g c t -> c g t")
        s_wa = T["s_wa"].rearrange("j c t -> c j t")
        v4 = lambda ap: ap.rearrange("p h (c s) -> p h c s", s=64)
        fl = lambda ap: ap.rearrange("p h t -> p (h t)")

        for e_ in range(2):
            S.op("dve", lambda e: e.memset(ST[:], 0.0), reads=[dST], writes=[dST])
            blocks = list(range(NBLK)) if e_ == 0 else [0] + list(range(NBLK - 1, 0, -1))
            corder = [0, 1, 2, 3] if e_ == 0 else [3, 2, 1, 0]
            import os
            blocks = blocks[:int(os.environ.get('A3_BLOCKS', NBLK))]
            for blk in blocks:
                t0 = blk * 256
                S.drain_dma("sp", keep=4)
                S.dma("sp", rk_in[:], s_rk[:, :, t0:t0 + 256], writes=[drk])
                S.dma("sp", wa_in[:], s_wa[:, :, t0:t0 + 256], writes=[dwa])
                S.dma("sp", v_in[:], T["s_v"][t0:t0 + 256, :].rearrange("(c s) n -> s c n", s=64), writes=[dv])
                r_ = rk_in[:, 0:4, :]; k_ = rk_in[:, 4:8, :]
                pwp = PSA[:, 0:2, :].rearrange("p b (h t) -> p (b h) t", h=2)
                pap = PSA[:, 2:4, :].rearrange("p b (h t) -> p (b h) t", h=2)
                for h in range(4):
                    S.op("pe", lambda e: e.matmul(pwp[:, h, :], lhsT=wup[:, e_, h * 64:(h + 1) * 64], rhs=wa_in[:, 0, :], start=True, stop=True),
                         reads=[dconst, dwa], writes=[dB[h // 2]])
                for h in range(4):
                    S.op("pe", lambda e: e.matmul(pap[:, h, :], lhsT=aup[:, e_, h * 64:(h + 1) * 64], rhs=wa_in[:, 1, :], start=True, stop=True),
                         reads=[dconst, dwa], writes=[dB[2 + h // 2]])
                for h in range(4):
                    S.op("act", lambda e: e.activation(out=lw[:, h, :], in_=pwp[:, h, :], func=AF.Sigmoid, bias=prm[:, e_, h:h + 1]),
                         reads=[dB[h // 2], dconst], writes=[dlw])
                for h in range(4):
                    S.op("act", lambda e: e.activation(out=aa[:, h, :], in_=pap[:, h, :], func=AF.Sigmoid, bias=prm[:, 2 + e_, h:h + 1]),
                         reads=[dB[2 + h // 2], dconst], writes=[daa])
                S.op("dve", lambda e: e.tensor_scalar(out=lw[:], in0=lw[:], scalar1=-0.6065306597126334, scalar2=None, op0=ALU.mult),
                     reads=[dlw], writes=[dlw])
                S.op("dve", lambda e: e.tensor_tensor(out=kk[:], in0=k_, in1=prm[:, 4, :].unsqueeze(2).to_broadcast([64, 4, 256]), op=ALU.mult),
                     reads=[drk, dconst], writes=[dkk])
                S.op("dve", lambda e: e.tensor_tensor(out=tmp[:], in0=kk[:], in1=kk[:], op=ALU.mult), reads=[dkk], writes=[dtmp])
                for half in range(2):
                    S.op("pe", lambda e: e.matmul(PSA[:, 4 + half, :], lhsT=ones64[:], rhs=fl(tmp[:])[:, half * 512:(half + 1) * 512],
                                                  start=True, stop=True), reads=[dconst, dtmp], writes=[dB[4 + half]])
                S.op("dve", lambda e: e.tensor_scalar(out=fl(tmp[:]), in0=PSA[:, 4:6, :].rearrange("p b t -> p (b t)"), scalar1=1e-24, scalar2=None,
                                                      op0=ALU.max), reads=[dB[4], dB[5]], writes=[dtmp])
                S.op("act", lambda e: e.activation(out=tmp[:], in_=tmp[:], func=AF.Sqrt), reads=[dtmp], writes=[dtmp])
                S.op("dve", lambda e: e.reciprocal(out=tmp[:], in_=tmp[:]), reads=[dtmp], writes=[dtmp])
                S.op("dve", lambda e: e.tensor_tensor(out=kk[:], in0=kk[:], in1=tmp[:], op=ALU.mult), reads=[dkk, dtmp], writes=[dkk])
                S.op("dve", lambda e: e.tensor_tensor(out=tmp[:], in0=aa[:], in1=prm[:, 5, :].unsqueeze(2).to_broadcast([64, 4, 256]), op=ALU.mult),
                     reads=[daa, dconst], writes=[dtmp])
                S.op("dve", lambda e: e.tensor_tensor(out=tmp[:], in0=tmp[:], in1=omk[:].unsqueeze(2).to_broadcast([64, 4, 256]), op=ALU.add),
                     reads=[dtmp, dconst], writes=[dtmp])
                S.op("dve", lambda e: e.tensor_tensor(out=kd[:], in0=tmp[:], in1=k_, op=ALU.mult), reads=[dtmp, drk], writes=[dkd])
                S.op("pool", lambda e: e.tensor_tensor(out=be[:], in0=kk[:], in1=aa[:], op=ALU.mult), reads=[dkk, daa], writes=[dbe])
                S.op("dve", lambda e: e.tensor_tensor_scan(out=fl(L[:]), data0=rmask[:].rearrange("p a b -> p (a b)"), data1=fl(lw[:]),
                                                           initial=0.0, op0=ALU.mult, op1=ALU.add), reads=[dlw, dconst], writes=[dL])
                S.op("dve", lambda e: e.tensor_copy(out=ltot[:], in_=v4(L[:])[:, :, :, 63]), reads=[dL], writes=[dlt])
                if e_ == 0:
                    Lc, dLc = L, dL
                else:
                    S.op("dve", lambda e: e.tensor_tensor(out=tmp[:], in0=lw[:], in1=L[:], op=ALU.subtract), reads=[dlw, dL], writes=[dtmp])
                    S.op("dve", lambda e: e.tensor_tensor(out=v4(Ld[:]), in0=v4(tmp[:]), in1=ltot[:].unsqueeze(3).to_broadcast([64, 4, 4, 64]),
                                                          op=ALU.add), reads=[dtmp, dlt], writes=[dLd])
                    Lc, dLc = Ld, dLd
                S.op("pool", lambda e: e.tensor_tensor(out=E1[:], in0=Lc[:], in1=lw[:], op=ALU.subtract), reads=[dLc, dlw], writes=[dE1])
                S.op("act", lambda e: e.activation(out=E1[:], in_=E1[:], func=AF.Exp), reads=[dE1], writes=[dE1])
                S.op("act", lambda e: e.activation(out=E2[:], in_=Lc[:], func=AF.Exp), reads=[dLc], writes=[dE2])
                S.op("act", lambda e: e.activation(out=E3[:], in_=Lc[:], func=AF.Exp, scale=-1.0), reads=[dLc], writes=[dE3])
                S.op("dve", lambda e: e.tensor_tensor(out=v4(tmp[:]), in0=ltot[:].unsqueeze(3).to_broadcast([64, 4, 4, 64]), in1=v4(Lc[:]),
                                                      op=ALU.subtract), reads=[dlt, dLc], writes=[dtmp])
                S.op("act", lambda e: e.activation(out=E4[:], in_=tmp[:], func=AF.Exp), reads=[dtmp], writes=[dE4])
                S.op("act", lambda e: e.activation(out=pc[:], in_=ltot[:], func=AF.Exp), reads=[dlt], writes=[dpc])
                S.op("dve", lambda e: e.scalar_tensor_tensor(out=AR[:, :, :, 0, :], in0=v4(kk[:]), scalar=-1.0, in1=v4(E1[:]),
                                                             op0=ALU.mult, op1=ALU.mult), reads=[dkk, dE1], writes=[dAR])
                S.op("dve", lambda e: e.tensor_tensor(out=AR[:, :, :, 1, :], in0=v4(r_), in1=v4(E2[:]), op=ALU.mult), reads=[drk, dE2], writes=[dAR])
                S.op("dve", lambda e: e.tensor_tensor(out=kt_[:], in0=kd[:], in1=E3[:], op=ALU.mult), reads=[dkd, dE3], writes=[dkt])
                S.op("pool", lambda e: e.tensor_tensor(out=bt[:], in0=be[:], in1=E3[:], op=ALU.mult), reads=[dbe, dE3], writes=[dbt])
                S.op("dve", lambda e: e.tensor_tensor(out=tmp[:], in0=r_, in1=kd[:], op=ALU.mult), reads=[drk, dkd], writes=[dtmp])
                S.op("dve", lambda e: e.tensor_tensor(out=tmp[:], in0=tmp[:], in1=prm[:, 6, :].unsqueeze(2).to_broadcast([64, 4, 256]), op=ALU.mult),
                     reads=[dtmp, dconst], writes=[dtmp])
                pbo2 = PSA[:, 6, 0:32].rearrange("p (c h w) -> p c h w", h=4, w=2)
                pbo = pbo2[:, :, :, 0]
                for c in range(4):
                    for h in range(4):
                        S.op("pe", lambda e: e.matmul(pbo2[:, c, h, :], lhsT=tmp[:, h, c * 64:(c + 1) * 64], rhs=ones64[:, 0:2], start=True, stop=True),
                             reads=[dtmp, dconst], writes=[dB[6]])
                S.op("dve", lambda e: e.tensor_tensor(out=bh[:], in0=be[:], in1=E4[:], op=ALU.mult), reads=[dbe, dE4], writes=[dbh])
                S.op("pool", lambda e: e.tensor_tensor(out=kh[:], in0=kd[:], in1=E4[:], op=ALU.mult), reads=[dkd, dE4], writes=[dkh])
                bsl = bacc[:, blk * 4:(blk + 1) * 4, :]
                if e_ == 0:
                    S.op("dve", lambda e: e.tensor_copy(out=bsl, in_=pbo), reads=[dB[6]], writes=[dba])
                else:
                    S.op("dve", lambda e: e.tensor_tensor(out=bsl, in0=bsl, in1=pbo, op=ALU.add), reads=[dB[6], dba], writes=[dba])
                ptr = PSA[:, 0:4, :].rearrange("p b (i s) -> p (b i) s", s=64)
                for c in range(4):
                    for h in range(4):
                        for w_, (src, dsrc) in enumerate([(bh, dbh), (kh, dkh)]):
                            idx = (c * 4 + h) * 2 + w_
                            S.op("pe", lambda e: e.transpose(out=ptr[:, idx, :], in_=src[:, h, c * 64:(c + 1) * 64], identity=identf[:]),
                                 reads=[dsrc, didf], writes=[dB[idx // 8]])
                for half in range(2):
                    S.op("act", lambda e: e.copy(out=bkT[:, half * 2:(half + 1) * 2].rearrange("p c h w s -> p (c h w s)"),
                                                 in_=PSA[:, half * 2:(half + 1) * 2, :].rearrange("p b t -> p (b t)")),
                         reads=[dB[half * 2], dB[half * 2 + 1]], writes=[dbk])
                for c in range(0 if 'g' in os.environ.get('A3_SKIP', '') else 4):
                    bb = (c % 2) * 4
                    cs = slice(c * 64, (c + 1) * 64)
                    for h in range(4):
                        arh = AR[:, h, c, :, :].rearrange("p a s -> p (a s)")
                        S.op("pe", lambda e: e.matmul(PSA[:, bb + h, 0:128], lhsT=bt[:, h, cs], rhs=arh, start=True, stop=True),
                             reads=[dbt, dAR], writes=[dB[bb + h]])
                        S.op("pe", lambda e: e.matmul(PSA[:, bb + h, 128:256], lhsT=kt_[:, h, cs], rhs=arh, start=True, stop=True),
                             reads=[dkt, dAR], writes=[dB[bb + h]])
                        S.op("pe", lambda e: e.matmul(PSA[:, bb + h, 256:320], lhsT=AR[:, h, c, 0, :], rhs=bt[:, h, cs], start=True, stop=True),
                             reads=[dbt, dAR], writes=[dB[bb + h]])
                    S.op("dve", lambda e: e.tensor_tensor(out=GM[:, c, :, :], in0=PSA[:, bb:bb + 4, 0:320],
                                                          in1=MSK[:, e_, :].unsqueeze(1).to_broadcast([64, 4, 320]), op=ALU.mult),
                         reads=[dB[bb], dB[bb + 1], dB[bb + 2], dB[bb + 3], dconst], writes=[dGM])
                GMf = GM[:].rearrange("p c h n -> p (c h) n")
                S.op("dve", lambda e: e.tensor_tensor(out=Tm[:], in0=GMf[:, :, 0:64], in1=identf[:].unsqueeze(1).to_broadcast([64, 16, 64]), op=ALU.add),
                     reads=[dGM, didf], writes=[dTm])
                pxx = PSA[:, 0:4, :].rearrange("p b (i w s) -> p (b i) w s", w=2, s=64)
                ptm = PSA[:, 4:6, :].rearrange("p b (i s) -> p (b i) s", s=64)
                for it_ in range(0 if 'd' in os.environ.get('A3_SKIP', '') else 5):
                    pp = it_ % 2
                    for idx in range(16):
                        if it_ == 0:
                            Xc = GMf[:, idx, 0:64]; XTc = GMf[:, idx, 256:320]; dsrc = dGM
                        else:
                            Xc = XX[1 - pp][:, idx, 0, :]; XTc = XX[1 - pp][:, idx, 1, :]; dsrc = dXX[1 - pp]
                        S.op("pe", lambda e: e.matmul(pxx[:, idx, 0, :], lhsT=XTc, rhs=Xc, start=True, stop=True), reads=[dsrc], writes=[dB[idx // 4]])
                        S.op("pe", lambda e: e.matmul(pxx[:, idx, 1, :], lhsT=Xc, rhs=XTc, start=True, stop=True), reads=[dsrc], writes=[dB[idx // 4]])
                    S.op("act", lambda e: e.copy(out=XX[pp][:].rearrange("p i w s -> p (i w s)"), in_=PSA[:, 0:4, :].rearrange("p b t -> p (b t)")),
                         reads=[dB[0], dB[1], dB[2], dB[3]], writes=[dXX[pp]])
                    for idx in range(16):
                        S.op("pe", lambda e: e.matmul(ptm[:, idx, :], lhsT=XX[pp][:, idx, 1, :], rhs=Tm[:, idx, :], start=True, stop=True),
                             reads=[dXX[pp], dTm], writes=[dB[4 + idx // 8]])
                    S.op("dve", lambda e: e.tensor_tensor(out=Tm[:].rearrange("p i s -> p (i s)"), in0=Tm[:].rearrange("p i s -> p (i s)"),
                                                          in1=PSA[:, 4:6, :].rearrange("p b t -> p (b t)"), op=ALU.add),
                         reads=[dB[4], dB[5], dTm], writes=[dTm])
                pW = PSA[:, 6, 0:256].rearrange("p (h s) -> p h s", s=64)
                pU = PSA[:, 7, 0:256].rearrange("p (h s) -> p h s", s=64)
                pYS = PSA[:, 6, :].rearrange("p (h w s) -> p h w s", w=2, s=64)
                for c in ([] if 's' in os.environ.get('A3_SKIP', '') else corder):
                    gc = blk * 4 + c
                    for h in range(4):
                        vh = v_in[:, c, h * 64:(h + 1) * 64]
                        S.op("pe", lambda e: e.matmul(pW[:, h, :], lhsT=AR[:, h, c, 0, :], rhs=ST[:, h, :], start=True, stop=False),
                             reads=[dAR, dST], writes=[dB[6]])
                        S.op("pe", lambda e: e.matmul(pW[:, h, :], lhsT=GM[:, c, h, 128:192], rhs=vh, start=False, stop=True),
                             reads=[dGM, dv], writes=[dB[6]])
                    S.op("act", lambda e: e.copy(out=WT[:], in_=pW), reads=[dB[6]], writes=[dWT])
                    for h in range(4):
                        S.op("pe", lambda e: e.matmul(pU[:, h, :], lhsT=Tm[:, c * 4 + h, :], rhs=WT[:, h, :], start=True, stop=True),
                             reads=[dTm, dWT], writes=[dB[7]])
                    S.op("act", lambda e: e.copy(out=UT[:], in_=pU), reads=[dB[7]], writes=[dUT])
                    if 'y' in os.environ.get('A3_SKIP', ''):
                        continue
                    for h in range(4):
                        vh = v_in[:, c, h * 64:(h + 1) * 64]
                        S.op("pe", lambda e: e.matmul(pYS[:, h, 0, :], lhsT=AR[:, h, c, 1, :], rhs=ST[:, h, :], start=True, stop=False),
                             reads=[dAR, dST], writes=[dB[6]])
                        S.op("pe", lambda e: e.matmul(pYS[:, h, 0, :], lhsT=GM[:, c, h, 64:128], rhs=UT[:, h, :], start=False, stop=False),
                             reads=[dGM, dUT], writes=[dB[6]])
                        S.op("pe", lambda e: e.matmul(pYS[:, h, 0, :], lhsT=GM[:, c, h, 192:256], rhs=vh, start=False, stop=True),
                             reads=[dGM, dv], writes=[dB[6]])
                        S.op("pe", lambda e: e.matmul(pYS[:, h, 1, :], lhsT=bkT[:, c, h, 0, :], rhs=UT[:, h, :], start=True, stop=False),
                             reads=[dbk, dUT], writes=[dB[6]])
                        S.op("pe", lambda e: e.matmul(pYS[:, h, 1, :], lhsT=bkT[:, c, h, 1, :], rhs=vh, start=False, stop=True),
                             reads=[dbk, dv], writes=[dB[6]])
                    ysl = yacc[:, gc, :].rearrange("p (h s) -> p h s", s=64)
                    if e_ == 0:
                        S.op("dve", lambda e: e.tensor_copy(out=ysl, in_=pYS[:, :, 0, :]), reads=[dB[6]], writes=[dy])
                    else:
                        S.op("dve", lambda e: e.tensor_tensor(out=ysl, in0=ysl, in1=pYS[:, :, 0, :], op=ALU.add), reads=[dB[6], dy], writes=[dy])
                    S.op("dve", lambda e: e.tensor_tensor(out=tS[:], in0=ST[:], in1=pc[:, :, c:c + 1].to_broadcast([64, 4, 64]), op=ALU.mult),
                         reads=[dST, dpc], writes=[dtS])
                    S.op("dve", lambda e: e.tensor_tensor(out=ST[:], in0=tS[:], in1=pYS[:, :, 1, :], op=ALU.add), reads=[dtS, dB[6]], writes=[dST])
        gtb_ = fl(E3[:]).bitcast(BF16)[:, 0:1024].rearrange("p (c n) -> p c n", n=256); dgtb = dE3
        obr_ = fl(E4[:]).bitcast(BF16)[:, 0:1024].rearrange("p (c n) -> p c n", n=256); dobr = dE4
        st1 = S.sb("st1", [64, 4, 16], F32, st); dst1 = S.dep()
        v16 = lambda ap: ap.rearrange("p c (h s) -> p (c h) s", s=64)
        for blk in range(NBLK):
            t0 = blk * 256
            S.drain_dma("sp", keep=4)
            yb = yacc[:, blk * 4:(blk + 1) * 4, :]
            S.dma("sp", v_in[:], T["s_v"][t0:t0 + 256, :].rearrange("(c s) n -> s c n", s=64), writes=[dv])
            S.dma("sp", gtb_, T["s_gate"][t0:t0 + 256, 0:256].rearrange("(c s) n -> s c n", s=64), writes=[dgtb])
            S.op("dve", lambda e: e.tensor_reduce(out=st1[:, 0, :], in_=v16(yb), axis=AX.X, op=ALU.add), reads=[dy], writes=[dst1])
            S.op("dve", lambda e: e.tensor_tensor(out=tmp[:], in0=yb, in1=yb, op=ALU.mult), reads=[dy], writes=[dtmp])
            S.op("dve", lambda e: e.tensor_reduce(out=st1[:, 1, :], in_=v16(tmp[:]), axis=AX.X, op=ALU.add), reads=[dtmp], writes=[dst1])
            S.op("dve", lambda e: e.tensor_scalar(out=st1[:, 0:2, :], in0=st1[:, 0:2, :], scalar1=1.0 / 64, scalar2=None, op0=ALU.mult),
                 reads=[dst1], writes=[dst1])
            S.op("dve", lambda e: e.tensor_tensor(out=st1[:, 2, :], in0=st1[:, 0, :], in1=st1[:, 0, :], op=ALU.mult), reads=[dst1], writes=[dst1])
            S.op("dve", lambda e: e.tensor_tensor(out=st1[:, 3, :], in0=st1[:, 1, :], in1=st1[:, 2, :], op=ALU.subtract), reads=[dst1], writes=[dst1])
            _rstd(S, st1[:, 3, :], 16, 1.0, 64e-5, [], dst1)
            S.op("dve", lambda e: e.tensor_tensor(out=v16(tmp[:]), in0=v16(yb), in1=st1[:, 0, :].unsqueeze(2).to_broadcast([64, 16, 64]), op=ALU.subtract),
                 reads=[dy, dst1], writes=[dtmp])
            S.op("dve", lambda e: e.tensor_tensor(out=v16(tmp[:]), in0=v16(tmp[:]), in1=st1[:, 3, :].unsqueeze(2).to_broadcast([64, 16, 64]), op=ALU.mult),
                 reads=[dtmp, dst1], writes=[dtmp])
            S.op("dve", lambda e: e.tensor_tensor(out=tmp[:], in0=tmp[:], in1=gng[:, 0, :].unsqueeze(1).to_broadcast([64, 4, 256]), op=ALU.mult),
                 reads=[dtmp, dconst], writes=[dtmp])
            S.op("dve", lambda e: e.tensor_tensor(out=tmp[:], in0=tmp[:], in1=gng[:, 1, :].unsqueeze(1).to_broadcast([64, 4, 256]), op=ALU.add),
                 reads=[dtmp, dconst], writes=[dtmp])
            bv = bacc[:, blk * 4:(blk + 1) * 4, :].rearrange("p c h -> p (c h)").unsqueeze(2).to_broadcast([64, 16, 64])
            S.op("dve", lambda e: e.tensor_tensor(out=v16(E1[:]), in0=v16(v_in[:]), in1=bv, op=ALU.mult), reads=[dv, dba], writes=[dE1])
            S.op("dve", lambda e: e.tensor_tensor(out=tmp[:], in0=tmp[:], in1=E1[:], op=ALU.add), reads=[dtmp, dE1], writes=[dtmp])
            S.op("dve", lambda e: e.tensor_tensor(out=obr_, in0=tmp[:], in1=gtb_, op=ALU.mult), reads=[dtmp, dgtb], writes=[dobr])
            S.dma("sp", T["br_rwkv"][t0:t0 + 256, :].rearrange("(c s) n -> s c n", s=64), obr_, reads=[dobr])
        S.barrier()


def _phase_A3i(S, nc, T):
    NCH = NT // 64
    BT = 128
    NCB = 2
    NHB = NT // BT
    NI = 4 * NCB
    with contextlib.ExitStack() as st:
        identf, didf = _make_ident(S, st, 64, F32, "a3id")
        ones64 = S.sb("ones64", [64, 64], F32, st); dconst = S.dep()
        S.op("dve", lambda e: e.memset(ones64[:], 1.0), writes=[dconst])
        mk = S.sb("mk", [64, 4, 64], F32, st)
        S.op("pool", lambda e: e.memset(mk[:], 1.0), writes=[dconst])
        for i, (stp, cm, cmp_) in enumerate([(1, -1, ALU.is_gt), (1, -1, ALU.is_ge), (-1, 1, ALU.is_gt), (-1, 1, ALU.is_ge)]):
            S.op("pool", lambda e: e.affine_select(out=mk[:, i, :], in_=mk[:, i, :], pattern=[[stp, 64]], compare_op=cmp_, fill=0.0,
                                                   base=0, channel_multiplier=cm), reads=[dconst], writes=[dconst])
        MSK = S.sb("MSK", [64, 2, 320], F32, st)
        for e_ in range(2):
            order = [0, 1, 0, 1, 2] if e_ == 0 else [2, 3, 2, 3, 0]
            for j, m in enumerate(order):
                S.op("dve", lambda e: e.tensor_copy(out=MSK[:, e_, j * 64:(j + 1) * 64], in_=mk[:, m, :]), reads=[dconst], writes=[dconst])
        rmask = S.sb("rmask", [64, NI, 64], F32, st)
        S.op("dve", lambda e: e.memset(rmask[:], 1.0), writes=[dconst])
        S.op("dve", lambda e: e.memset(rmask[:, :, 0:1], 0.0), reads=[dconst], writes=[dconst])
        prm = S.sb("prm", [64, 7, 4], F32, st)
        S.dma("sp", prm[:], T["rprm"][:, :, :], writes=[dconst])
        omk = S.sb("omk", [64, 4], F32, st)
        S.op("dve", lambda e: e.tensor_scalar(out=omk[:], in0=prm[:, 5, :], scalar1=-1.0, scalar2=1.0, op0=ALU.mult, op1=ALU.add),
             reads=[dconst], writes=[dconst])
        wup = S.sb("wup_sb", [96, 2, 256], F32, st)
        aup = S.sb("aup_sb", [96, 2, 256], F32, st)
        S.dma("sp", wup[:], T["wup"].rearrange("e r c -> r e c"), writes=[dconst])
        S.dma("sp", aup[:], T["aup"].rearrange("e r c -> r e c"), writes=[dconst])
        gng = S.sb("gng", [64, 2, 256], F32, st)
        S.dma("sp", gng[:].rearrange("p a b -> p (a b)"), T["gn"][0:1, :].to_broadcast([64, 512]), writes=[dconst])

        yacc = S.sb("yacc", [64, NCH, 256], F32, st); dy = S.dep()
        bacc = S.sb("bacc", [64, NCH, 4], F32, st); dba = S.dep()
        S.op("dve", lambda e: e.memset(yacc[:], 0.0), writes=[dy])
        S.op("dve", lambda e: e.memset(bacc[:], 0.0), writes=[dba])
        PSA = S.ps("PSA", [64, 8, 512], F32, st); dB = [S.pdep() for _ in range(8)]

        s_rk = T["s_rk"].rearrange("g c t -> c g t")
        s_wa = T["s_wa"].rearrange("j c t -> c j t")
        v4 = lambda ap: ap.rearrange("p h (c s) -> p h c s", s=64)
        fl = lambda ap: ap.rearrange("p h t -> p (h t)")

        class Bset:
            pass

        def mkset(e_):
            B = Bset()
            sf = f"_{e_}"

            def t4(name):
                return S.sb(name + sf, [64, 4, BT], F32, st), S.dep()
            B.ST = S.sb("ST" + sf, [64, 4, 64], F32, st); B.dST = S.dep()
            B.rk_in = S.sb("rk_in" + sf, [64, 8, BT], F32, st); B.drk = S.dep()
            B.wa_in = S.sb("wa_in" + sf, [96, 2, BT], F32, st); B.dwa = S.dep()
            B.v_in = S.sb("v_in" + sf, [64, NCB, 256], F32, st); B.dv = S.dep()
            B.lw, B.dlw = t4("lw"); B.aa, B.daa = t4("aa"); B.kk, B.dkk = t4("kk"); B.kd, B.dkd = t4("kd"); B.be, B.dbe = t4("be")
            B.L, B.dL = t4("L"); B.Ld, B.dLd = t4("Ld"); B.tmp, B.dtmp = t4("tmp")
            B.E1, B.dE1 = t4("E1"); B.E2, B.dE2 = t4("E2"); B.E3, B.dE3 = t4("E3"); B.E4, B.dE4 = t4("E4")
            B.bt, B.dbt = t4("bt"); B.kt_, B.dkt = t4("kt_")
            B.AR = S.sb("AR" + sf, [64, 4, NCB, 2, 64], F32, st); B.dAR = S.dep()
            B.ltot = S.sb("ltot" + sf, [64, 4, NCB], F32, st); B.dlt = S.dep()
            B.pc = S.sb("pc" + sf, [64, 4, NCB], F32, st); B.dpc = S.dep()
            B.bkT = S.sb("bkT" + sf, [64, NCB, 4, 2, 64], F32, st); B.dbk = S.dep()
            B.GM = S.sb("GM" + sf, [64, NCB, 4, 320], F32, st); B.dGM = S.dep()
            B.XX0 = S.sb("XX0" + sf, [64, NI, 2, 64], F32, st); B.dXX = S.dep()
            B.Tm = S.sb("Tm" + sf, [64, NI, 64], F32, st); B.dTm = S.dep()
            B.WT = S.sb("WT" + sf, [64, 4, 64], F32, st); B.dWT = S.dep()
            B.UT = S.sb("UT" + sf, [64, 4, 64], F32, st); B.dUT = S.dep()
            B.tS = S.sb("tS" + sf, [64, 4, 64], F32, st); B.dtS = S.dep()
            return B

        def block_gen(e_, B, blk):
            pb = 4 * e_
            t0 = blk * BT
            corder = list(range(NCB)) if e_ == 0 else list(range(NCB - 1, -1, -1))
            S.drain_dma("sp", keep=8)
            S.dma("sp", B.rk_in[:], s_rk[:, :, t0:t0 + BT], writes=[B.drk])
            S.dma("sp", B.wa_in[:], s_wa[:, :, t0:t0 + BT], writes=[B.dwa])
            S.dma("sp", B.v_in[:], T["s_v"][t0:t0 + BT, :].rearrange("(c s) n -> s c n", s=64), writes=[B.dv])
            r_ = B.rk_in[:, 0:4, :]; k_ = B.rk_in[:, 4:8, :]
            lw, aa, kk, kd, be, L, Ld, tmp = B.lw, B.aa, B.kk, B.kd, B.be, B.L, B.Ld, B.tmp
            E1, E2, E3, E4, bt, kt_, AR, GM, Tm, XX0 = B.E1, B.E2, B.E3, B.E4, B.bt, B.kt_, B.AR, B.GM, B.Tm, B.XX0
            bh, kh = be, kd
            pwp = PSA[:, pb, :].rearrange("p (h t) -> p h t", h=4)
            pap = PSA[:, pb + 1, :].rearrange("p (h t) -> p h t", h=4)
            for h in range(4):
                S.op("pe", lambda e: e.matmul(pwp[:, h, :], lhsT=wup[:, e_, h * 64:(h + 1) * 64], rhs=B.wa_in[:, 0, :], start=True, stop=True),
                     reads=[dconst, B.dwa], writes=[dB[pb]])
            for h in range(4):
                S.op("pe", lambda e: e.matmul(pap[:, h, :], lhsT=aup[:, e_, h * 64:(h + 1) * 64], rhs=B.wa_in[:, 1, :], start=True, stop=True),
                     reads=[dconst, B.dwa], writes=[dB[pb + 1]])
            S.op("dve", lambda e: e.tensor_tensor(out=kk[:], in0=k_, in1=prm[:, 4, :].unsqueeze(2).to_broadcast([64, 4, BT]), op=ALU.mult),
                 reads=[B.drk, dconst], writes=[B.dkk])
            S.op("dve", lambda e: e.tensor_tensor(out=tmp[:], in0=kk[:], in1=kk[:], op=ALU.mult), reads=[B.dkk], writes=[B.dtmp])
            S.op("pe", lambda e: e.matmul(PSA[:, pb + 2, :], lhsT=ones64[:], rhs=fl(tmp[:]), start=True, stop=True),
                 reads=[dconst, B.dtmp], writes=[dB[pb + 2]])
            yield
            for h in range(4):
                S.op("act", lambda e: e.activation(out=lw[:, h, :], in_=pwp[:, h, :], func=AF.Sigmoid, bias=prm[:, e_, h:h + 1]),
                     reads=[dB[pb], dconst], writes=[B.dlw])
            for h in range(4):
                S.op("act", lambda e: e.activation(out=aa[:, h, :], in_=pap[:, h, :], func=AF.Sigmoid, bias=prm[:, 2 + e_, h:h + 1]),
                     reads=[dB[pb + 1], dconst], writes=[B.daa])
            S.op("dve", lambda e: e.tensor_scalar(out=fl(tmp[:]), in0=PSA[:, pb + 2, :], scalar1=1e-24, scalar2=None, op0=ALU.max),
                 reads=[dB[pb + 2]], writes=[B.dtmp])
            yield
            S.op("act", lambda e: e.activation(out=tmp[:], in_=tmp[:], func=AF.Sqrt), reads=[B.dtmp], writes=[B.dtmp])
            S.op("dve", lambda e: e.tensor_scalar(out=lw[:], in0=lw[:], scalar1=-0.6065306597126334, scalar2=None, op0=ALU.mult),
                 reads=[B.dlw], writes=[B.dlw])
            yield
            S.op("dve", lambda e: e.reciprocal(out=tmp[:], in_=tmp[:]), reads=[B.dtmp], writes=[B.dtmp])
            S.op("dve", lambda e: e.tensor_tensor(out=kk[:], in0=kk[:], in1=tmp[:], op=ALU.mult), reads=[B.dkk, B.dtmp], writes=[B.dkk])
            S.op("dve", lambda e: e.tensor_tensor(out=tmp[:], in0=aa[:], in1=prm[:, 5, :].unsqueeze(2).to_broadcast([64, 4, BT]), op=ALU.mult),
                 reads=[B.daa, dconst], writes=[B.dtmp])
            S.op("dve", lambda e: e.tensor_tensor(out=tmp[:], in0=tmp[:], in1=omk[:].unsqueeze(2).to_broadcast([64, 4, BT]), op=ALU.add),
                 reads=[B.dtmp, dconst], writes=[B.dtmp])
            S.op("dve", lambda e: e.tensor_tensor(out=kd[:], in0=tmp[:], in1=k_, op=ALU.mult), reads=[B.dtmp, B.drk], writes=[B.dkd])
            S.op("pool", lambda e: e.tensor_tensor(out=be[:], in0=kk[:], in1=aa[:], op=ALU.mult), reads=[B.dkk, B.daa], writes=[B.dbe])
            yield
            S.op("dve", lambda e: e.tensor_tensor_scan(out=fl(L[:]), data0=rmask[:].rearrange("p a b -> p (a b)"), data1=fl(lw[:]),
                                                       initial=0.0, op0=ALU.mult, op1=ALU.add), reads=[B.dlw, dconst], writes=[B.dL])
            S.op("dve", lambda e: e.tensor_copy(out=B.ltot[:], in_=v4(L[:])[:, :, :, 63]), reads=[B.dL], writes=[B.dlt])
            if e_ == 0:
                Lc, dLc = L, B.dL
            else:
                S.op("dve", lambda e: e.tensor_tensor(out=tmp[:], in0=lw[:], in1=L[:], op=ALU.subtract), reads=[B.dlw, B.dL], writes=[B.dtmp])
                S.op("dve", lambda e: e.tensor_tensor(out=v4(Ld[:]), in0=v4(tmp[:]), in1=B.ltot[:].unsqueeze(3).to_broadcast([64, 4, NCB, 64]),
                                                      op=ALU.add), reads=[B.dtmp, B.dlt], writes=[B.dLd])
                Lc, dLc = Ld, B.dLd
            S.op("pool", lambda e: e.tensor_tensor(out=E1[:], in0=Lc[:], in1=lw[:], op=ALU.subtract), reads=[dLc, B.dlw], writes=[B.dE1])
            S.op("dve", lambda e: e.tensor_tensor(out=v4(tmp[:]), in0=B.ltot[:].unsqueeze(3).to_broadcast([64, 4, NCB, 64]), in1=v4(Lc[:]),
                                                  op=ALU.subtract), reads=[B.dlt, dLc], writes=[B.dtmp])
            yield
            S.op("act", lambda e: e.activation(out=E1[:], in_=E1[:], func=AF.Exp), reads=[B.dE1], writes=[B.dE1])
            S.op("act", lambda e: e.activation(out=E2[:], in_=Lc[:], func=AF.Exp), reads=[dLc], writes=[B.dE2])
            S.op("act", lambda e: e.activation(out=E3[:], in_=Lc[:], func=AF.Exp, scale=-1.0), reads=[dLc], writes=[B.dE3])
            S.op("act", lambda e: e.activation(out=E4[:], in_=tmp[:], func=AF.Exp), reads=[B.dtmp], writes=[B.dE4])
            S.op("act", lambda e: e.activation(out=B.pc[:], in_=B.ltot[:], func=AF.Exp), reads=[B.dlt], writes=[B.dpc])
            yield
            S.op("dve", lambda e: e.scalar_tensor_tensor(out=AR[:, :, :, 0, :], in0=v4(kk[:]), scalar=-1.0, in1=v4(E1[:]),
                                                         op0=ALU.mult, op1=ALU.mult), reads=[B.dkk, B.dE1], writes=[B.dAR])
            S.op("dve", lambda e: e.tensor_tensor(out=AR[:, :, :, 1, :], in0=v4(r_), in1=v4(E2[:]), op=ALU.mult), reads=[B.drk, B.dE2], writes=[B.dAR])
            S.op("dve", lambda e: e.tensor_tensor(out=kt_[:], in0=kd[:], in1=E3[:], op=ALU.mult), reads=[B.dkd, B.dE3], writes=[B.dkt])
            S.op("pool", lambda e: e.tensor_tensor(out=bt[:], in0=be[:], in1=E3[:], op=ALU.mult), reads=[B.dbe, B.dE3], writes=[B.dbt])
            S.op("dve", lambda e: e.tensor_tensor(out=tmp[:], in0=r_, in1=kd[:], op=ALU.mult), reads=[B.drk, B.dkd], writes=[B.dtmp])
            S.op("dve", lambda e: e.tensor_tensor(out=tmp[:], in0=tmp[:], in1=prm[:, 6, :].unsqueeze(2).to_broadcast([64, 4, BT]), op=ALU.mult),
                 reads=[B.dtmp, dconst], writes=[B.dtmp])
            pbo2 = PSA[:, pb + 3, 0:NCB * 8].rearrange("p (c h w) -> p c h w", h=4, w=2)
            pbo = pbo2[:, :, :, 0]
            for c in range(NCB):
                for h in range(4):
                    S.op("pe", lambda e: e.matmul(pbo2[:, c, h, :], lhsT=tmp[:, h, c * 64:(c + 1) * 64], rhs=ones64[:, 0:2], start=True, stop=True),
                         reads=[B.dtmp, dconst], writes=[dB[pb + 3]])
            yield
            S.op("dve", lambda e: e.tensor_tensor(out=bh[:], in0=be[:], in1=E4[:], op=ALU.mult), reads=[B.dbe, B.dE4], writes=[B.dbe])
            S.op("pool", lambda e: e.tensor_tensor(out=kh[:], in0=kd[:], in1=E4[:], op=ALU.mult), reads=[B.dkd, B.dE4], writes=[B.dkd])
            bsl = bacc[:, blk * NCB:(blk + 1) * NCB, :]
            S.op("dve", lambda e: e.tensor_tensor(out=bsl, in0=bsl, in1=pbo, op=ALU.add), reads=[dB[pb + 3], dba], writes=[dba])
            yield
            ptr = PSA[:, pb:pb + 2, :].rearrange("p b (i s) -> p (b i) s", s=64)
            for c in range(NCB):
                for h in range(4):
                    for w_, (src, dsrc) in enumerate([(bh, B.dbe), (kh, B.dkd)]):
                        idx = (c * 4 + h) * 2 + w_
                        S.op("pe", lambda e: e.transpose(out=ptr[:, idx, :], in_=src[:, h, c * 64:(c + 1) * 64], identity=identf[:]),
                             reads=[dsrc, didf], writes=[dB[pb + idx // 8]])
            yield
            S.op("act", lambda e: e.copy(out=B.bkT[:].rearrange("p c h w s -> p (c h w s)"),
                                         in_=PSA[:, pb:pb + 2, :].rearrange("p b t -> p (b t)")),
                 reads=[dB[pb], dB[pb + 1]], writes=[B.dbk])
            yield
            for c in range(NCB):
                cs = slice(c * 64, (c + 1) * 64)
                for h in range(4):
                    arh = AR[:, h, c, :, :].rearrange("p a s -> p (a s)")
                    S.op("pe", lambda e: e.matmul(PSA[:, pb + h, 0:128], lhsT=bt[:, h, cs], rhs=arh, start=True, stop=True),
                         reads=[B.dbt, B.dAR], writes=[dB[pb + h]])
                    S.op("pe", lambda e: e.matmul(PSA[:, pb + h, 128:256], lhsT=kt_[:, h, cs], rhs=arh, start=True, stop=True),
                         reads=[B.dkt, B.dAR], writes=[dB[pb + h]])
                    S.op("pe", lambda e: e.matmul(PSA[:, pb + h, 256:320], lhsT=AR[:, h, c, 0, :], rhs=bt[:, h, cs], start=True, stop=True),
                         reads=[B.dbt, B.dAR], writes=[dB[pb + h]])
                yield
                S.op("dve", lambda e: e.tensor_tensor(out=GM[:, c, :, :], in0=PSA[:, pb:pb + 4, 0:320],
                                                      in1=MSK[:, e_, :].unsqueeze(1).to_broadcast([64, 4, 320]), op=ALU.mult),
                     reads=[dB[pb], dB[pb + 1], dB[pb + 2], dB[pb + 3], dconst], writes=[B.dGM])
                yield
            GMf = GM[:].rearrange("p c h n -> p (c h) n")
            S.op("dve", lambda e: e.tensor_tensor(out=Tm[:], in0=GMf[:, :, 0:64], in1=identf[:].unsqueeze(1).to_broadcast([64, NI, 64]), op=ALU.add),
                 reads=[B.dGM, didf], writes=[B.dTm])
            pxx = PSA[:, pb:pb + 2, :].rearrange("p b (i w s) -> p (b i) w s", w=2, s=64)
            ptm = PSA[:, pb + 2, :].rearrange("p (i s) -> p i s", s=64)
            for it_ in range(5):
                for idx in range(NI):
                    if it_ == 0:
                        Xc = GMf[:, idx, 0:64]; XTc = GMf[:, idx, 256:320]; dsrc = B.dGM
                    else:
                        Xc = XX0[:, idx, 0, :]; XTc = XX0[:, idx, 1, :]; dsrc = B.dXX
                    S.op("pe", lambda e: e.matmul(pxx[:, idx, 0, :], lhsT=XTc, rhs=Xc, start=True, stop=True), reads=[dsrc], writes=[dB[pb + idx // 4]])
                    S.op("pe", lambda e: e.matmul(pxx[:, idx, 1, :], lhsT=Xc, rhs=XTc, start=True, stop=True), reads=[dsrc], writes=[dB[pb + idx // 4]])
                yield
                S.op("act", lambda e: e.copy(out=XX0[:].rearrange("p i w s -> p (i w s)"), in_=PSA[:, pb:pb + 2, :].rearrange("p b t -> p (b t)")),
                     reads=[dB[pb], dB[pb + 1]], writes=[B.dXX])
                yield
                for idx in range(NI):
                    S.op("pe", lambda e: e.matmul(ptm[:, idx, :], lhsT=XX0[:, idx, 1, :], rhs=Tm[:, idx, :], start=True, stop=True),
                         reads=[B.dXX, B.dTm], writes=[dB[pb + 2]])
                yield
                S.op("dve", lambda e: e.tensor_tensor(out=Tm[:].rearrange("p i s -> p (i s)"), in0=Tm[:].rearrange("p i s -> p (i s)"),
                                                      in1=PSA[:, pb + 2, :], op=ALU.add),
                     reads=[dB[pb + 2], B.dTm], writes=[B.dTm])
                yield
            pW = PSA[:, pb + 3, 0:256].rearrange("p (h s) -> p h s", s=64)
            pU = PSA[:, pb + 2, 0:256].rearrange("p (h s) -> p h s", s=64)
            pYS = PSA[:, pb + 3, :].rearrange("p (h w s) -> p h w s", w=2, s=64)
            ST, WT, UT, tS, v_in, bkT, pc = B.ST, B.WT, B.UT, B.tS, B.v_in, B.bkT, B.pc
            for c in corder:
                gc = blk * NCB + c
                for h in range(4):
                    vh = v_in[:, c, h * 64:(h + 1) * 64]
                    S.op("pe", lambda e: e.matmul(pW[:, h, :], lhsT=AR[:, h, c, 0, :], rhs=ST[:, h, :], start=True, stop=False),
                         reads=[B.dAR, B.dST], writes=[dB[pb + 3]])
                    S.op("pe", lambda e: e.matmul(pW[:, h, :], lhsT=GM[:, c, h, 128:192], rhs=vh, start=False, stop=True),
                         reads=[B.dGM, B.dv], writes=[dB[pb + 3]])
                yield
                S.op("act", lambda e: e.copy(out=WT[:], in_=pW), reads=[dB[pb + 3]], writes=[B.dWT])
                yield
                for h in range(4):
                    S.op("pe", lambda e: e.matmul(pU[:, h, :], lhsT=Tm[:, c * 4 + h, :], rhs=WT[:, h, :], start=True, stop=True),
                         reads=[B.dTm, B.dWT], writes=[dB[pb + 2]])
                yield
                S.op("act", lambda e: e.copy(out=UT[:], in_=pU), reads=[dB[pb + 2]], writes=[B.dUT])
                yield
                for h in range(4):
                    vh = v_in[:, c, h * 64:(h + 1) * 64]
                    S.op("pe", lambda e: e.matmul(pYS[:, h, 0, :], lhsT=AR[:, h, c, 1, :], rhs=ST[:, h, :], start=True, stop=False),
                         reads=[B.dAR, B.dST], writes=[dB[pb + 3]])
                    S.op("pe", lambda e: e.matmul(pYS[:, h, 0, :], lhsT=GM[:, c, h, 64:128], rhs=UT[:, h, :], start=False, stop=False),
                         reads=[B.dGM, B.dUT], writes=[dB[pb + 3]])
                    S.op("pe", lambda e: e.matmul(pYS[:, h, 0, :], lhsT=GM[:, c, h, 192:256], rhs=vh, start=False, stop=True),
                         reads=[B.dGM, B.dv], writes=[dB[pb + 3]])
                    S.op("pe", lambda e: e.matmul(pYS[:, h, 1, :], lhsT=bkT[:, c, h, 0, :], rhs=UT[:, h, :], start=True, stop=False),
                         reads=[B.dbk, B.dUT], writes=[dB[pb + 3]])
                    S.op("pe", lambda e: e.matmul(pYS[:, h, 1, :], lhsT=bkT[:, c, h, 1, :], rhs=vh, start=False, stop=True),
                         reads=[B.dbk, B.dv], writes=[dB[pb + 3]])
                S.op("dve", lambda e: e.tensor_tensor(out=tS[:], in0=ST[:], in1=pc[:, :, c:c + 1].to_broadcast([64, 4, 64]), op=ALU.mult),
                     reads=[B.dST, B.dpc], writes=[B.dtS])
                yield
                S.op("dve", lambda e: e.tensor_tensor(out=ST[:], in0=tS[:], in1=pYS[:, :, 1, :], op=ALU.add), reads=[B.dtS, dB[pb + 3]], writes=[B.dST])
                ysl = yacc[:, gc, :].rearrange("p (h s) -> p h s", s=64)
                S.op("dve", lambda e: e.tensor_tensor(out=ysl, in0=ysl, in1=pYS[:, :, 0, :], op=ALU.add), reads=[dB[pb + 3], dy], writes=[dy])
                yield

        sets = [mkset(0), mkset(1)]
        import os
        nb = int(os.environ.get('A3_BLOCKS', NHB))
        order = [list(range(NHB)), [1, 0] + list(range(NHB - 1, 1, -1))]

        def chain(e_):
            B = sets[e_]
            S.op("dve", lambda e: e.memset(B.ST[:], 0.0), writes=[B.dST])
            for blk in order[e_][:nb]:
                yield from block_gen(e_, B, blk)

        gens = [chain(0), chain(1)]
        alive = [True, True]
        while any(alive):
            for e_ in range(2):
                if alive[e_]:
                    try:
                        next(gens[e_])
                    except StopIteration:
                        alive[e_] = False

        B = sets[0]
        tmpv = fl(B.tmp[:]).rearrange("p (c n) -> p c n", n=256)
        e1v = fl(B.E1[:]).rearrange("p (c n) -> p c n", n=256)
        gtb_ = fl(B.E3[:]).bitcast(BF16)[:, 0:NCB * 256].rearrange("p (c n) -> p c n", n=256); dgtb = B.dE3
        obr_ = fl(B.E4[:]).bitcast(BF16)[:, 0:NCB * 256].rearrange("p (c n) -> p c n", n=256); dobr = B.dE4
        st1 = S.sb("st1", [64, 4, NI], F32, st); dst1 = S.dep()
        v16 = lambda ap: ap.rearrange("p c (h s) -> p (c h) s", s=64)
        for blk in range(NHB):
            t0 = blk * BT
            S.drain_dma("sp", keep=4)
            yb = yacc[:, blk * NCB:(blk + 1) * NCB, :]
            S.dma("sp", B.v_in[:], T["s_v"][t0:t0 + BT, :].rearrange("(c s) n -> s c n", s=64), writes=[B.dv])
            S.dma("sp", gtb_, T["s_gate"][t0:t0 + BT, 0:256].rearrange("(c s) n -> s c n", s=64), writes=[dgtb])
            S.op("dve", lambda e: e.tensor_reduce(out=st1[:, 0, :], in_=v16(yb), axis=AX.X, op=ALU.add), reads=[dy], writes=[dst1])
            S.op("dve", lambda e: e.tensor_tensor(out=tmpv, in0=yb, in1=yb, op=ALU.mult), reads=[dy], writes=[B.dtmp])
            S.op("dve", lambda e: e.tensor_reduce(out=st1[:, 1, :], in_=v16(tmpv), axis=AX.X, op=ALU.add), reads=[B.dtmp], writes=[dst1])
            S.op("dve", lambda e: e.tensor_scalar(out=st1[:, 0:2, :], in0=st1[:, 0:2, :], scalar1=1.0 / 64, scalar2=None, op0=ALU.mult),
                 reads=[dst1], writes=[dst1])
            S.op("dve", lambda e: e.tensor_tensor(out=st1[:, 2, :], in0=st1[:, 0, :], in1=st1[:, 0, :], op=ALU.mult), reads=[dst1], writes=[dst1])
            S.op("dve", lambda e: e.tensor_tensor(out=st1[:, 3, :], in0=st1[:, 1, :], in1=st1[:, 2, :], op=ALU.subtract), reads=[dst1], writes=[dst1])
            _rstd(S, st1[:, 3, :], NI, 1.0, 64e-5, [], dst1)
            S.op("dve", lambda e: e.tensor_tensor(out=v16(tmpv), in0=v16(yb), in1=st1[:, 0, :].unsqueeze(2).to_broadcast([64, NI, 64]), op=ALU.subtract),
                 reads=[dy, dst1], writes=[B.dtmp])
            S.op("dve", lambda e: e.tensor_tensor(out=v16(tmpv), in0=v16(tmpv), in1=st1[:, 3, :].unsqueeze(2).to_broadcast([64, NI, 64]), op=ALU.mult),
                 reads=[B.dtmp, dst1], writes=[B.dtmp])
            S.op("dve", lambda e: e.tensor_tensor(out=tmpv, in0=tmpv, in1=gng[:, 0, :].unsqueeze(1).to_broadcast([64, NCB, 256]), op=ALU.mult),
                 reads=[B.dtmp, dconst], writes=[B.dtmp])
            S.op("dve", lambda e: e.tensor_tensor(out=tmpv, in0=tmpv, in1=gng[:, 1, :].unsqueeze(1).to_broadcast([64, NCB, 256]), op=ALU.add),
                 reads=[B.dtmp, dconst], writes=[B.dtmp])
            bv = bacc[:, blk * NCB:(blk + 1) * NCB, :].rearrange("p c h -> p (c h)").unsqueeze(2).to_broadcast([64, NI, 64])
            S.op("dve", lambda e: e.tensor_tensor(out=v16(e1v), in0=v16(B.v_in[:]), in1=bv, op=ALU.mult), reads=[B.dv, dba], writes=[B.dE1])
            S.op("dve", lambda e: e.tensor_tensor(out=tmpv, in0=tmpv, in1=e1v, op=ALU.add), reads=[B.dtmp, B.dE1], writes=[B.dtmp])
            S.op("dve", lambda e: e.tensor_tensor(out=obr_, in0=tmpv, in1=gtb_, op=ALU.mult), reads=[B.dtmp, dgtb], writes=[dobr])
            S.dma("sp", T["br_rwkv"][t0:t0 + BT, :].rearrange("(c s) n -> s c n", s=64), obr_, reads=[dobr])
        S.barrier()


def build_A(phases="123"):
    nc = bass.Bass("TRN2", target_bir_lowering=False)
    T = {}

    def din(name, shape, dt=F32):
        T[name] = nc.dram_tensor(name, list(shape), dt, kind="ExternalInput").ap()

    def dscr(name, shape, dt):
        T[name] = nc.dram_tensor(name, list(shape), dt, kind="Internal").ap()

    def dout(name, shape, dt):
        T[name] = nc.dram_tensor(name, list(shape), dt, kind="ExternalOutput").ap()

    din("xf", [NT, D]); din("cT", [128, 16, 2]); din("wmod", [D, 4096]); din("bmod", [1, 4096]); din("gpre", [1, D])
    din("gqn", [1, 256]); din("w_fm", [D, 704]); din("w_tm", [D, 2304]); din("rope", [NT, 192])
    din("lamp", [1, 256]); din("lami", [1, 1]); din("subg", [1, 128]); din("rprm", [64, 7, 4])
    din("wup", [2, 96, 256]); din("aup", [2, 96, 256]); din("gn", [1, 512])
    dscr("s_rk", [8, 64, NT], F32); dscr("s_wa", [2, 96, NT], F32); dscr("s_v", [NT, 256], F32)
    dscr("s_dqkT", [8, 64, NT], BF16); dscr("s_gqkT", [3, 128, NT], BF16)
    dscr("s_dv", [NT, 2, 129], BF16); dscr("s_gv", [NT, 129], BF16); dscr("s_gate", [NT, 768], BF16)
    dout("hT", [16, 128, NT], BF16); dout("br_att", [NT, 512], BF16); dout("br_rwkv", [NT, 256], BF16)
    with contextlib.ExitStack() as st:
        S = Sched(nc, st)
        if "1" in phases:
            _phase_A1(S, nc, T)
        if "2" in phases:
            _phase_A2(S, nc, T)
        if "3" in phases:
            _phase_A3i(S, nc, T)
        S.barrier()
        print("build_A: ninst", S.ninst, "nsem", S.nsem, {k: v for k, v in S.cnt.items()})
    return nc


OFF = {"cv_val": 0, "cv_glu": 1024, "cv_gate": 2048, "rk_r": 3072, "rk_k": 4096, "rk_v": 5120, "rk_wl": 6144, "rk_al": 6240,
       "rk_gate": 6336, "df_q": 7360, "df_k": 8384, "df_v": 9408, "df_gate": 10432, "gq_q": 11456, "gq_k": 12480, "gq_v": 12736,
       "gq_gate": 12992, "merge": 14016}


def _rope_table():
    tab = np.zeros((NT, 192), np.float32)
    tab[:, 0:32] = 1.0
    tab[:, 64:128] = 1.0
    t = np.arange(4096)
    row = (t // 64).astype(np.float32); col = (t % 64).astype(np.float32)
    for half, c0, s0 in ((32, 0, 32), (64, 64, 128)):
        inv = (10000.0 ** (-np.arange(0, half, 2, dtype=np.float32) / half)).astype(np.float32)
        ang = np.concatenate([row[:, None] * inv, col[:, None] * inv], axis=-1).astype(np.float32)
        tab[NCTX:, c0:c0 + half] = np.cos(ang)
        tab[NCTX:, s0:s0 + half] = np.sin(ang)
    return tab


def _cT(c_ctx, cb):
    both = np.stack([c_ctx, cb], axis=-1)
    return np.ascontiguousarray(both.reshape(16, 128, 2).transpose(1, 0, 2))


def inputs_A(inp, li, b, q, xfull, rope):
    w_in = inp["w_in"][li]
    cs = lambda name, a, n: w_in[:, OFF[name] + a:OFF[name] + a + n]
    kv = q // 2
    w_fm = np.concatenate([cs("rk_r", 256 * q, 256), cs("rk_k", 256 * q, 256), cs("rk_wl", 0, 96), cs("rk_al", 0, 96)], axis=1)
    w_tm = np.concatenate([cs("rk_v", 256 * q, 256), cs("rk_gate", 256 * q, 256),
                           cs("df_q", 256 * q, 256), cs("df_k", 256 * q, 256), cs("df_v", 256 * q, 256), cs("df_gate", 256 * q, 256),
                           cs("gq_q", 256 * q, 256), cs("gq_k", 128 * kv, 128), cs("gq_v", 128 * kv, 128), cs("gq_gate", 256 * q, 256)], axis=1)
    sl = slice(256 * q, 256 * q + 256)
    hm = lambda v: v[sl].reshape(4, 64).T
    rprm = np.stack([hm(inp["rwkv_w0"][li, 0]), hm(inp["rwkv_w0"][li, 1]), hm(inp["rwkv_a0"][li, 0]), hm(inp["rwkv_a0"][li, 1]),
                     hm(inp["rwkv_k_k"][li]), hm(inp["rwkv_k_a"][li]), hm(inp["rwkv_r_k"][li].reshape(-1))], axis=1)
    lam_init = 0.8 - 0.6 * math.exp(-0.3 * li)
    return {
        "xf": np.ascontiguousarray(xfull[b]), "cT": _cT(inp["c_ctx"], inp["c"][b]),
        "wmod": np.ascontiguousarray(inp["w_mod"][li][:, 0:4096]), "bmod": np.ascontiguousarray(inp["b_mod"][li][None, 0:4096]),
        "gpre": np.ascontiguousarray(inp["norm_pre_g"][li][None]), "gqn": np.ascontiguousarray(inp["gqa_qk_norm_g"][li].reshape(1, 256)),
        "w_fm": np.ascontiguousarray(w_fm), "w_tm": np.ascontiguousarray(w_tm), "rope": rope,
        "lamp": np.ascontiguousarray(inp["diff_lam"][li].reshape(1, 256)), "lami": np.full((1, 1), lam_init, np.float32),
        "subg": np.ascontiguousarray(inp["diff_subln_g"][li][None]), "rprm": np.ascontiguousarray(rprm.astype(np.float32)),
        "wup": np.ascontiguousarray(inp["rwkv_w_up"][li][:, :, sl]), "aup": np.ascontiguousarray(inp["rwkv_a_up"][li][:, :, sl]),
        "gn": np.ascontiguousarray(np.concatenate([inp["rwkv_gn_g"][li][sl], inp["rwkv_gn_b"][li][sl]])[None]),
    }


NOWN = 1088
NEXT = 1152
MBLK = [(0, 64, 15), (64, 384, 109), (448, 384, 493), (832, 256, 877)]
LNBLK = [(0, 384), (384, 384), (768, 320)]


def build_B():
    nc = bass.Bass("TRN2", target_bir_lowering=False)
    T = {}

    def din(name, shape, dt=F32):
        T[name] = nc.dram_tensor(name, list(shape), dt, kind="ExternalInput").ap()

    din("hTx", [128, 16, NEXT], BF16); din("mask", [1, NEXT]); din("brT", [128, 3, 8, NOWN], BF16); din("x_own", [NOWN, D])
    din("w_cv", [D, 3072]); din("cvp", [128, 8, 34]); din("Wl", [D, 8192]); din("Wb", [4, W, D]); din("Wout", [D, D])
    din("bg", [128, 4, 16]); din("cT", [128, 16, 2]); din("wmodg", [D, D]); din("bmodg", [1, D]); din("gpost", [1, D])
    T["s_cv"] = nc.dram_tensor("s_cv", [8, 128, NOWN], BF16, kind="Internal").ap()
    T["xo"] = nc.dram_tensor("xo", [NOWN, D], F32, kind="ExternalOutput").ap()
    with contextlib.ExitStack() as st0:
        S = Sched(nc, st0)
        big = S.sb("big", [128, 16 * NOWN], BF16, st0); dbig = S.dep()
        mergedT = big[:].rearrange("p (c t) -> p c t", t=NOWN)
        conv_all = big[:].bitcast(F32).rearrange("p (c t) -> p c t", t=NOWN)
        with contextlib.ExitStack() as stm:
            hTx = S.sb("hTx_sb", [128, 16, NEXT], BF16, stm); dhTx = S.dep()
            for k4 in range(4):
                S.dma("sp", hTx[:, k4 * 4:(k4 + 1) * 4, :], T["hTx"][:, k4 * 4:(k4 + 1) * 4, :], writes=[dhTx])
            with contextlib.ExitStack() as st:
                maskb = S.sb("maskb", [128, NEXT], F32, st); dmask = S.dep()
                S.dma("sp", maskb[:], T["mask"][0:1, :].to_broadcast([128, NEXT]), writes=[dmask])
                cvp = S.sb("cvp_sb", [128, 8, 34], F32, st); dcvp = S.dep()
                S.dma("sp", cvp[:], T["cvp"][:, :, :], writes=[dcvp])
                ones = S.sb("ones128", [128, 128], F32, st); dones = S.dep()
                S.op("dve", lambda e: e.memset(ones[:], 1.0), writes=[dones])
                wck = [S.sb(f"wck{k}", [128, 16, 128], BF16, st) for k in range(3)]; dwck = [S.dep() for _ in range(3)]
                u = S.sb("u", [128, NEXT], F32, st); du = S.dep()
                sg = S.sb("sg", [128, 384], F32, st); dsg = S.dep()
                cgx = S.sb("cgx", [128, 8, NEXT], BF16, st); dcg = S.dep()
                sqt = S.sb("sqt", [128, NOWN], F32, st); dsq = S.dep()
                meanb = S.sb("meanb", [128, NOWN], F32, st); dmean = S.dep()
                rstdb = S.sb("rstdb", [128, NOWN], F32, st); drstd = S.dep()
                cst = S.sb("cst", [128, NOWN], BF16, st); dcst = S.dep()
                pcv = [S.ps(f"pcv{i}", [128, 512], F32, st) for i in range(6)]; dpcv = [S.pdep() for _ in range(6)]
                wcv = T["w_cv"].rearrange("(kc p) n -> p kc n", p=128)
                for cc in range(8):
                    for k in range(3):
                        c0 = k * 1024 + cc * 128
                        for k4 in range(2):
                            S.dma("pool", wck[k][:, k4 * 8:(k4 + 1) * 8, :], wcv[:, k4 * 8:(k4 + 1) * 8, c0:c0 + 128], writes=[dwck[k]])
                    for tb in range(3):
                        ts_ = slice(tb * 384, (tb + 1) * 384)
                        pgl, dgl = pcv[0 + tb % 2], dpcv[0 + tb % 2]
                        pv, dpv = pcv[2 + tb % 2], dpcv[2 + tb % 2]
                        pgt, dgt_ = pcv[4 + tb % 2], dpcv[4 + tb % 2]
                        for (k, p_, dp_) in ((1, pgl, dgl), (0, pv, dpv), (2, pgt, dgt_)):
                            for kc in range(16):
                                S.op("pe", lambda e: e.matmul(p_[:, 0:384], lhsT=wck[k][:, kc, :], rhs=hTx[:, kc, ts_], start=(kc == 0), stop=(kc == 15)),
                                     reads=[dwck[k], dhTx], writes=[dp_])
                        S.op("act", lambda e: e.activation(out=sg[:], in_=pgl[:, 0:384], func=AF.Sigmoid), reads=[dgl], writes=[dsg])
                        S.op("dve", lambda e: e.tensor_tensor(out=sg[:], in0=sg[:], in1=maskb[:, ts_], op=ALU.mult), reads=[dsg, dmask], writes=[dsg])
                        S.op("dve", lambda e: e.tensor_tensor(out=u[:, ts_], in0=pv[:, 0:384], in1=sg[:], op=ALU.mult), reads=[dpv, dsg], writes=[du])
                        S.op("act", lambda e: e.activation(out=cgx[:, cc, ts_], in_=pgt[:, 0:384], func=AF.Silu), reads=[dgt_], writes=[dcg])
                    for (o0_, n_, e0_) in ((0, 64, 0), (64, 1024, 94)):
                        acc = conv_all[:, cc, o0_:o0_ + n_]
                        S.op("dve", lambda e: e.tensor_scalar(out=acc, in0=u[:, e0_:e0_ + n_], scalar1=cvp[:, cc, 0:1], scalar2=cvp[:, cc, 31:32],
                                                              op0=ALU.mult, op1=ALU.add), reads=[du, dcvp], writes=[dbig])
                        for j in range(1, 31):
                            S.op("dve", lambda e: e.scalar_tensor_tensor(out=acc, in0=u[:, e0_ + j:e0_ + j + n_], scalar=cvp[:, cc, j:j + 1], in1=acc,
                                                                         op0=ALU.mult, op1=ALU.add), reads=[du, dcvp, dbig], writes=[dbig])
                for cc in range(8):
                    S.op("dve", lambda e: e.tensor_tensor(out=sqt[:], in0=conv_all[:, cc, :], in1=conv_all[:, cc, :], op=ALU.mult), reads=[dbig], writes=[dsq])
                    for bi, (o_, n_) in enumerate(LNBLK):
                        S.op("pe", lambda e: e.matmul(pcv[bi][:, 0:n_], lhsT=ones[:], rhs=conv_all[:, cc, o_:o_ + n_], start=(cc == 0), stop=(cc == 7)),
                             reads=[dones, dbig], writes=[dpcv[bi]])
                        S.op("pe", lambda e: e.matmul(pcv[3 + bi][:, 0:n_], lhsT=ones[:], rhs=sqt[:, o_:o_ + n_], start=(cc == 0), stop=(cc == 7)),
                             reads=[dones, dsq], writes=[dpcv[3 + bi]])
                for bi, (o_, n_) in enumerate(LNBLK):
                    S.op("dve", lambda e: e.tensor_scalar(out=meanb[:, o_:o_ + n_], in0=pcv[bi][:, 0:n_], scalar1=1.0 / W, scalar2=None, op0=ALU.mult),
                         reads=[dpcv[bi]], writes=[dmean])
                    S.op("dve", lambda e: e.tensor_scalar(out=rstdb[:, o_:o_ + n_], in0=pcv[3 + bi][:, 0:n_], scalar1=1.0 / W, scalar2=None, op0=ALU.mult),
                         reads=[dpcv[3 + bi]], writes=[drstd])
                S.op("dve", lambda e: e.tensor_tensor(out=sqt[:], in0=meanb[:], in1=meanb[:], op=ALU.mult), reads=[dmean], writes=[dsq])
                S.op("dve", lambda e: e.tensor_tensor(out=rstdb[:], in0=rstdb[:], in1=sqt[:], op=ALU.subtract), reads=[drstd, dsq], writes=[drstd])
                _rstd(S, rstdb[:], NOWN, 1.0, 1e-5, [], drstd)
                for cc in range(8):
                    cv = conv_all[:, cc, :]
                    S.op("dve", lambda e: e.tensor_tensor(out=cv, in0=cv, in1=meanb[:], op=ALU.subtract), reads=[dbig, dmean], writes=[dbig])
                    S.op("dve", lambda e: e.tensor_tensor(out=cv, in0=cv, in1=rstdb[:], op=ALU.mult), reads=[dbig, drstd], writes=[dbig])
                    S.op("act", lambda e: e.activation(out=sqt[:], in_=cv, func=AF.Silu, bias=cvp[:, cc, 33:34], scale=cvp[:, cc, 32:33]),
                         reads=[dbig, dcvp], writes=[dsq])
                    S.op("dve", lambda e: e.tensor_tensor(out=cst[:, 0:64], in0=sqt[:, 0:64], in1=cgx[:, cc, 15:79], op=ALU.mult), reads=[dsq, dcg], writes=[dcst])
                    S.op("dve", lambda e: e.tensor_tensor(out=cst[:, 64:NOWN], in0=sqt[:, 64:NOWN], in1=cgx[:, cc, 109:1133], op=ALU.mult), reads=[dsq, dcg], writes=[dcst])
                    S.dma("sp", T["s_cv"][cc, :, :], cst[:], reads=[dcst])
                S.barrier()
            with contextlib.ExitStack() as st:
                brT4 = S.sb("brT4", [128, 4, 8, NOWN], BF16, st); dbr = S.dep()
                S.dma("sp", brT4[:, 0, :, :], T["s_cv"].rearrange("c p t -> p c t"), writes=[dbr])
                for j in range(3):
                    S.dma("sp", brT4[:, 1 + j, :, :], T["brT"][:, j, :, :], writes=[dbr])
                bg = S.sb("bg_sb", [128, 4, 16], F32, st); dbg = S.dep()
                S.dma("sp", bg[:], T["bg"][:, :, :], writes=[dbg])
                wl = [S.sb(f"wl{i}", [128, 16, 4, 128], BF16, st) for i in range(2)]; dwl = [S.dep() for _ in range(2)]
                wb = [S.sb(f"wb{i}", [128, 8, 4, 128], BF16, st) for i in range(2)]; dwb = [S.dep() for _ in range(2)]
                gsb = S.sb("gsb", [128, 384], F32, st); dgs = S.dep()
                macc = S.sb("macc", [128, 384], F32, st); dma_ = S.dep()
                mtmp = S.sb("mtmp", [128, 384], F32, st); dmt = S.dep()
                pl = [S.ps(f"pl{i}", [128, 512], F32, st) for i in range(2)]; dpl = [S.pdep() for _ in range(2)]
                pp = [S.ps(f"pp{i}", [128, 512], F32, st) for i in range(2)]; dpp = [S.pdep() for _ in range(2)]
                Wlv = T["Wl"].rearrange("(kc p) n -> p kc n", p=128)
                it = 0
                for dc in range(16):
                    b2 = dc % 2
                    for j in range(4):
                        c0 = j * D + dc * 128
                        S.dma("pool", wl[b2][:, :, j, :], Wlv[:, :, c0:c0 + 128], writes=[dwl[b2]])
                        S.dma("pool", wb[b2][:, :, j, :], T["Wb"][j].rearrange("(cc p) n -> p cc n", p=128)[:, :, dc * 128:(dc + 1) * 128], writes=[dwb[b2]])
                    for (o_, n_, e_) in MBLK:
                        for j in range(4):
                            p1, d1 = pl[it % 2], dpl[it % 2]
                            p2, d2 = pp[it % 2], dpp[it % 2]
                            it += 1
                            for kc in range(16):
                                S.op("pe", lambda e: e.matmul(p1[:, 0:n_], lhsT=wl[b2][:, kc, j, :], rhs=hTx[:, kc, e_:e_ + n_], start=(kc == 0), stop=(kc == 15)),
                                     reads=[dwl[b2], dhTx], writes=[d1])
                            for cc in range(8):
                                S.op("pe", lambda e: e.matmul(p2[:, 0:n_], lhsT=wb[b2][:, cc, j, :], rhs=brT4[:, j, cc, o_:o_ + n_], start=(cc == 0), stop=(cc == 7)),
                                     reads=[dwb[b2], dbr], writes=[d2])
                            S.op("act", lambda e: e.activation(out=gsb[:, 0:n_], in_=p1[:, 0:n_], func=AF.Sigmoid, bias=bg[:, j, dc:dc + 1]),
                                 reads=[d1, dbg], writes=[dgs])
                            if j == 0:
                                S.op("dve", lambda e: e.tensor_tensor(out=macc[:, 0:n_], in0=p2[:, 0:n_], in1=gsb[:, 0:n_], op=ALU.mult), reads=[d2, dgs], writes=[dma_])
                            else:
                                S.op("dve", lambda e: e.tensor_tensor(out=mtmp[:, 0:n_], in0=p2[:, 0:n_], in1=gsb[:, 0:n_], op=ALU.mult), reads=[d2, dgs], writes=[dmt])
                                dst = macc[:, 0:n_] if j < 3 else mergedT[:, dc, o_:o_ + n_]
                                S.op("dve", lambda e: e.tensor_tensor(out=dst, in0=macc[:, 0:n_], in1=mtmp[:, 0:n_], op=ALU.add),
                                     reads=[dma_, dmt], writes=[dma_ if j < 3 else dbig])
                S.barrier()
        with contextlib.ExitStack() as st:
            modb = [S.sb(f"modg{i}", [128, D], F32, st) for i in range(2)]; dmodb = S.dep()
            _mod_prologue(S, nc, T["cT"], T["wmodg"], T["bmodg"], D, modb, dmodb)
            gpb = S.sb("gpb", [128, D], F32, st); dgp = S.dep()
            S.dma("sp", gpb[:], T["gpost"][0:1, :].to_broadcast([128, D]), writes=[dgp])
            for wh in range(2):
                S.op("dve", lambda e: e.tensor_tensor(out=modb[wh][:], in0=modb[wh][:], in1=gpb[:], op=ALU.mult), reads=[dmodb, dgp], writes=[dmodb])
            Wo = S.sb("Wo", [128, 16, D], BF16, st); dWo = S.dep()
            Wov = T["Wout"].rearrange("(kc p) n -> p kc n", p=128)
            for kc in range(16):
                S.dma("pool", Wo[:, kc:kc + 1, :], Wov[:, kc:kc + 1, :], writes=[dWo])
            xt = [S.sb(f"xt{i}", [128, D], F32, st) for i in range(2)]; dxt = [S.dep() for _ in range(2)]
            yb = S.sb("yb", [128, D], F32, st); dyb = S.dep()
            jk = S.sb("jk", [128, D], BF16, st); djk = S.dep()
            ss = S.sb("ss3", [128, 4], F32, st); dss = S.dep()
            py = [S.ps(f"py{i}", [128, 512], F32, st) for i in range(2)]; dpy = [S.pdep() for _ in range(2)]
            tiles = [(0, 64, 0)] + [(64 + 128 * i, 128, 1) for i in range(8)]
            for ti, (o_, n_, wh) in enumerate(tiles):
                S.drain_dma("sp", keep=4)
                x_ = xt[ti % 2]; dx_ = dxt[ti % 2]
                S.dma("sp", x_[0:n_, :], T["x_own"][o_:o_ + n_, :], writes=[dx_])
                for cg in range(4):
                    p_, dp_ = py[cg % 2], dpy[cg % 2]
                    for kc in range(16):
                        S.op("pe", lambda e: e.matmul(p_[0:n_, :], lhsT=mergedT[:, kc, o_:o_ + n_], rhs=Wo[:, kc, cg * 512:(cg + 1) * 512], start=(kc == 0), stop=(kc == 15)),
                             reads=[dbig, dWo], writes=[dp_])
                    S.op("act", lambda e: e.copy(out=yb[0:n_, cg * 512:(cg + 1) * 512], in_=p_[0:n_, :]), reads=[dp_], writes=[dyb])
                S.op("act", lambda e: e.activation(out=jk[0:n_, :], in_=yb[0:n_, :], func=AF.Square, accum_out=ss[0:n_, 0:1]), reads=[dyb], writes=[djk, dss])
                _rstd(S, ss[0:n_, 0:1], 1, 1.0 / D, EPS, [], dss)
                S.op("dve", lambda e: e.scalar_tensor_tensor(out=yb[0:n_, :], in0=yb[0:n_, :], scalar=ss[0:n_, 0:1], in1=modb[wh][0:n_, :],
                                                             op0=ALU.mult, op1=ALU.mult), reads=[dyb, dss, dmodb], writes=[dyb])
                S.op("dve", lambda e: e.tensor_tensor(out=yb[0:n_, :], in0=yb[0:n_, :], in1=x_[0:n_, :], op=ALU.add), reads=[dyb, dx_], writes=[dyb])
                S.dma("sp", T["xo"][o_:o_ + n_, :], yb[0:n_, :], reads=[dyb])
            S.barrier()
        print("build_B: ninst", S.ninst, "nsem", S.nsem, {k: v for k, v in S.cnt.items()})
    return nc


def _bf16(a):
    import ml_dtypes
    return np.ascontiguousarray(np.asarray(a).astype(ml_dtypes.bfloat16))


def _tok_maps(q):
    ctx_idx = np.arange(64 * q - 15, 64 * q + 79)
    lat_idx = np.arange(1024 * q - 15, 1024 * q + 1039)
    tok = np.full(NEXT, -1, np.int64)
    v = (ctx_idx >= 0) & (ctx_idx < NCTX)
    tok[0:94][v] = ctx_idx[v]
    v2 = (lat_idx >= 0) & (lat_idx < 4096)
    tok[94:94 + 1054][v2] = NCTX + lat_idx[v2]
    own = np.concatenate([np.arange(64 * q, 64 * q + 64), NCTX + np.arange(1024 * q, 1024 * q + 1024)])
    return tok, own


def shared_B(inp, li):
    w_in = inp["w_in"][li]
    cvp = np.concatenate([inp["conv_w"][li].T, inp["conv_b"][li][:, None], inp["conv_ln_g"][li][:, None], inp["conv_ln_b"][li][:, None]], axis=1)
    return {
        "w_cv": np.ascontiguousarray(w_in[:, 0:3072]),
        "cvp": np.ascontiguousarray(cvp.reshape(8, 128, 34).transpose(1, 0, 2).astype(np.float32)),
        "Wl": np.ascontiguousarray(w_in[:, OFF["merge"]:OFF["merge"] + 8192]),
        "Wb": np.ascontiguousarray(inp["w_branch"][li]),
        "Wout": np.ascontiguousarray(inp["w_out"][li]),
        "bg": np.ascontiguousarray(inp["b_gate"][li].reshape(4, 16, 128).transpose(2, 0, 1)),
        "wmodg": np.ascontiguousarray(inp["w_mod"][li][:, 4096:6144]),
        "bmodg": np.ascontiguousarray(inp["b_mod"][li][None, 4096:6144]),
        "gpost": np.ascontiguousarray(inp["norm_post_g"][li][None]),
    }


def inputs_B(inp, shared, b, q, xfull, hT_b, br_b):
    tok, own = _tok_maps(q)
    valid = tok >= 0
    hTx = np.zeros((16, 128, NEXT), hT_b.dtype)
    hTx[:, :, valid] = hT_b[:, :, tok[valid]]
    brT = br_b[own].reshape(NOWN, 3, 8, 128).transpose(3, 1, 2, 0)
    m = dict(shared)
    m.update({
        "hTx": np.ascontiguousarray(hTx.transpose(1, 0, 2)), "mask": valid.astype(np.float32)[None],
        "brT": np.ascontiguousarray(brT), "x_own": np.ascontiguousarray(xfull[b][own]),
        "cT": _cT(inp["c_ctx"], inp["c"][b]),
    })
    return m


def _gather_A(resA):
    out = []
    for b in range(2):
        hT_b = np.asarray(resA[4 * b]["hT"])
        parts = []
        for j in range(3):
            cols = []
            for q in range(4):
                r = resA[4 * b + q]
                if j == 0:
                    cols.append(np.asarray(r["br_rwkv"]))
                else:
                    cols.append(np.asarray(r["br_att"])[:, (j - 1) * 256:j * 256])
            parts.append(np.concatenate(cols, axis=1))
        out.append((hT_b, np.stack(parts, axis=1)))
    return out


_NC = {}


def kernel(**inputs):
    inp = {k: np.asarray(v) for k, v in inputs.items()}
    if "A" not in _NC:
        _NC["A"] = build_A()
        _NC["B"] = build_B()
    rope = _rope_table()
    xfull = np.concatenate([inp["ctx"], inp["x"]], axis=1).astype(np.float32)
    cores = list(range(8))
    for li in range(4):
        mapsA = [inputs_A(inp, li, c // 4, c % 4, xfull, rope) for c in cores]
        resA = run_bass_kernel_spmd(_NC["A"], mapsA, core_ids=cores).results
        del mapsA
        gA = _gather_A(resA)
        del resA
        sh = shared_B(inp, li)
        mapsB = [inputs_B(inp, sh, c // 4, c % 4, xfull, gA[c // 4][0], gA[c // 4][1]) for c in cores]
        resB = run_bass_kernel_spmd(_NC["B"], mapsB, core_ids=cores).results
        del mapsB
        xnew = np.empty_like(xfull)
        for c in cores:
            _, own = _tok_maps(c % 4)
            xnew[c // 4][own] = np.asarray(resB[c]["xo"])
        xfull = xnew
    return np.ascontiguousarray(xfull[:, NCTX:, :]).astype(np.float32)
```

```python
import contextlib
import math
import numpy as np
import concourse.bass as bass
import concourse.mybir as mybir
from concourse.bass_utils import run_bass_kernel_spmd

F32 = mybir.dt.float32
BF16 = mybir.dt.bfloat16
AF = mybir.ActivationFunctionType
ALU = mybir.AluOpType
AX = mybir.AxisListType

SEM_LIM = 30000
D = 2048
NT = 4352
NTILE = 34
NBLK = 17
NCTX = 256
W = 1024
EPS = 1e-6


class Dep:
    __slots__ = ("name", "w", "rs", "wsem", "wcnt", "rsem", "rcnt", "excl")

    def __init__(self, name="", excl=False):
        self.name = name
        self.excl = excl
        self.w = None
        self.rs = []
        self.wsem = None
        self.wcnt = 0
        self.rsem = None
        self.rcnt = 0


class Sched:
    def __init__(self, nc, stack):
        self.nc = nc
        self.stack = stack
        self.eng = {"pe": nc.tensor, "dve": nc.vector, "act": nc.scalar, "pool": nc.gpsimd, "sp": nc.sync}
        self.sems = {k: [] for k in self.eng}
        self.cnt = {k: 0 for k in self.eng}
        self.known = {k: {} for k in self.eng}
        self.nsem = 0
        self.ninst = 0
        self.deps = []
        self.dticks = []

    def dep(self, name="", excl=False):
        d = Dep(name, excl)
        self.deps.append(d)
        return d

    def pdep(self, name=""):
        return self.dep(name, excl=True)

    def new_sem(self, name):
        self.nsem += 1
        return self.stack.enter_context(self.nc.semaphore(f"{name}_{self.nsem}"))

    def sb(self, name, shape, dt, stack=None):
        return (stack or self.stack).enter_context(self.nc.sbuf_tensor(name, list(shape), dt))

    def ps(self, name, shape, dt=F32, stack=None):
        return (stack or self.stack).enter_context(self.nc.psum_tensor(name, list(shape), dt))

    def _wait(self, e, tick):
        sem, val, src = tick
        kn = self.known[e]
        if kn.get(id(sem), 0) >= val:
            return
        self.eng[e].wait_ge(sem, val)
        kn[id(sem)] = val
        if src in self.sems:
            for s in self.sems[src]:
                if s is sem:
                    break
                kn[id(s)] = SEM_LIM

    def _deps(self, e, reads, writes, dma=False):
        for d in reads:
            if d.w is not None:
                self._wait(e, d.w)
            if d.excl:
                for r in d.rs:
                    if r[2] != e:
                        self._wait(e, r)
        for d in writes:
            if d.w is not None and (dma or d.w[2] != e or e != "pe"):
                self._wait(e, d.w)
            for r in d.rs:
                self._wait(e, r)

    def op(self, e, fn, reads=(), writes=()):
        self._deps(e, reads, writes)
        c = self.cnt[e]
        if c % SEM_LIM == 0:
            self.sems[e].append(self.new_sem(e))
        sem = self.sems[e][-1]
        val = c % SEM_LIM + 1
        self.cnt[e] = c + 1
        ins = fn(self.eng[e])
        ins.then_inc(sem, 1)
        self.ninst += 1
        tick = (sem, val, e)
        for d in reads:
            d.rs.append(tick)
        for d in writes:
            d.w = tick
            d.rs = []
        return tick

    def dma(self, e, out, in_, reads=(), writes=(), **kw):
        self._deps(e, reads, writes, dma=True)
        if writes:
            d0 = writes[0]
            if d0.wsem is None:
                d0.wsem = self.new_sem("dw")
            d0.wcnt += 16
            tick = (d0.wsem, d0.wcnt, "dma")
        else:
            d0 = reads[0]
            if d0.rsem is None:
                d0.rsem = self.new_sem("dr")
            d0.rcnt += 16
            tick = (d0.rsem, d0.rcnt, "dma")
        ins = self.eng[e].dma_start(out=out, in_=in_, **kw)
        ins.then_inc(tick[0], 16)
        self.ninst += 1
        self.dticks.append(tick)
        for d in reads:
            d.rs.append(tick)
        for d in writes:
            d.w = tick
            d.rs = []
        return tick

    def wait_all(self, e, deps):
        for d in deps:
            if d.w is not None:
                self._wait(e, d.w)
            for r in d.rs:
                self._wait(e, r)

    def drain_dma(self, e, keep=0):
        n = len(self.dticks) - keep
        for t in self.dticks[:max(n, 0)]:
            self._wait(e, t)
        self.dticks = self.dticks[max(n, 0):]

    def barrier(self):
        for e in self.eng:
            self.wait_all(e, self.deps)
        for d in self.deps:
            d.rs = d.rs[-8:]


def _mod_prologue(S, nc, cT, wmod, bmod, ncols, modb, dmodb):
    with contextlib.ExitStack() as st:
        ct = S.sb("m_ct", [128, 16, 2], F32, st); dct = S.dep()
        cs = S.sb("m_cs", [128, 16, 2], F32, st); dcs = S.dep()
        rep = S.sb("m_rep", [128, 2, 16, 128], F32, st); drep = S.dep()
        ones1 = S.sb("m_ones", [1, 128], F32, st); dones = S.dep()
        bm = S.sb("m_bm", [1, ncols], F32, st); dbm = S.dep()
        wt = [S.sb(f"m_wt{i}", [128, 16, 512], F32, st) for i in range(2)]
        dwt = [S.dep() for _ in range(2)]
        pm = [S.ps(f"m_pm{i}", [128, 512], F32, st) for i in range(2)]
        dpm = [S.pdep() for _ in range(2)]
        S.dma("sp", ct[:], cT[:, :, :], writes=[dct])
        S.dma("sp", bm[:], bmod[:, :], writes=[dbm])
        S.op("dve", lambda e: e.memset(ones1[:], 1.0), writes=[dones])
        S.op("act", lambda e: e.activation(out=cs[:], in_=ct[:], func=AF.Silu), reads=[dct], writes=[dcs])
        for wh in range(2):
            for kc in range(16):
                S.op("dve", lambda e: e.tensor_copy(out=rep[:, wh, kc, :], in_=cs[:, kc, wh:wh + 1].to_broadcast([128, 128])),
                     reads=[dcs], writes=[drep])
        ng = ncols // 512
        wv = wmod.rearrange("(kc p) n -> p kc n", p=128)
        for g in range(ng):
            b = g % 2
            for h4 in range(4):
                S.dma("sp", wt[b][:, h4 * 4:(h4 + 1) * 4, :], wv[:, h4 * 4:(h4 + 1) * 4, g * 512:(g + 1) * 512], writes=[dwt[b]])
            for wh in range(2):
                for kc in range(16):
                    S.op("pe", lambda e: e.matmul(pm[wh][:], lhsT=rep[:, wh, kc, :], rhs=wt[b][:, kc, :], start=(kc == 0), stop=False),
                         reads=[drep, dwt[b]], writes=[dpm[wh]])
                S.op("pe", lambda e: e.matmul(pm[wh][:], lhsT=ones1[:], rhs=bm[:, g * 512:(g + 1) * 512], start=False, stop=True),
                     reads=[dones, dbm], writes=[dpm[wh]])
                S.op("act", lambda e: e.copy(out=modb[wh][:, g * 512:(g + 1) * 512], in_=pm[wh][:]), reads=[dpm[wh]], writes=[dmodb])
        S.barrier()


def _make_ident(S, st, n, dt, name):
    f = S.sb(name + "_f", [n, n], F32, st)
    df = S.dep()
    S.op("pool", lambda e: e.memset(f[:], 1.0), writes=[df])
    S.op("pool", lambda e: e.affine_select(out=f[:], in_=f[:], pattern=[[-1, n]], compare_op=ALU.is_equal, fill=0.0,
                                           base=0, channel_multiplier=1), reads=[df], writes=[df])
    if dt == F32:
        return f, df
    b = S.sb(name + "_b", [n, n], dt, st)
    db = S.dep()
    S.op("dve", lambda e: e.tensor_copy(out=b[:], in_=f[:]), reads=[df], writes=[db])
    return b, db


def _rope(S, src, dst, cos, sin, G, P, t1, t2, reads, writes, dtmp):
    sv = src.rearrange("p g (i t) -> p g i t", t=2)
    dv = dst.rearrange("p g (i t) -> p g i t", t=2)
    cb = cos.unsqueeze(1).to_broadcast([128, G, P])
    sbb = sin.unsqueeze(1).to_broadcast([128, G, P])
    S.op("dve", lambda e: e.tensor_tensor(out=t1, in0=sv[:, :, :, 0], in1=cb, op=ALU.mult), reads=reads, writes=[dtmp])
    S.op("dve", lambda e: e.tensor_tensor(out=t2, in0=sv[:, :, :, 1], in1=sbb, op=ALU.mult), reads=reads, writes=[dtmp])
    S.op("dve", lambda e: e.tensor_tensor(out=dv[:, :, :, 0], in0=t1, in1=t2, op=ALU.subtract), reads=[dtmp], writes=writes)
    S.op("dve", lambda e: e.tensor_tensor(out=t1, in0=sv[:, :, :, 0], in1=sbb, op=ALU.mult), reads=reads + [dtmp], writes=[dtmp])
    S.op("dve", lambda e: e.tensor_tensor(out=t2, in0=sv[:, :, :, 1], in1=cb, op=ALU.mult), reads=reads + [dtmp], writes=[dtmp])
    S.op("dve", lambda e: e.tensor_tensor(out=dv[:, :, :, 1], in0=t1, in1=t2, op=ALU.add), reads=[dtmp], writes=writes)


def _rstd(S, ss, n, scale, eps, reads, dss):
    S.op("dve", lambda e: e.tensor_scalar(out=ss, in0=ss, scalar1=scale, scalar2=eps, op0=ALU.mult, op1=ALU.add),
         reads=reads + [dss], writes=[dss])
    S.op("act", lambda e: e.activation(out=ss, in_=ss, func=AF.Sqrt), reads=[dss], writes=[dss])
    S.op("dve", lambda e: e.reciprocal(out=ss, in_=ss), reads=[dss], writes=[dss])


def _phase_A1(S, nc, T):
    with contextlib.ExitStack() as st:
        identb, did = _make_ident(S, st, 128, BF16, "a1id")
        modb = [S.sb(f"modb{i}", [128, 4096], F32, st) for i in range(2)]
        dmodb = S.dep()
        _mod_prologue(S, nc, T["cT"], T["wmod"], T["bmod"], 4096, modb, dmodb)
        with contextlib.ExitStack() as st2:
            gb = S.sb("gb", [128, 2048], F32, st2); dgb = S.dep()
            S.dma("sp", gb[:], T["gpre"][0:1, :].to_broadcast([128, 2048]), writes=[dgb])
            for wh in range(2):
                S.op("dve", lambda e: e.scalar_tensor_tensor(out=modb[wh][:, 2048:4096], in0=modb[wh][:, 2048:4096], scalar=1.0,
                                                             in1=gb[:], op0=ALU.add, op1=ALU.mult), reads=[dmodb, dgb], writes=[dmodb])
            S.barrier()
        gqn = S.sb("gqn_sb", [128, 2, 128], F32, st); dgqn = S.dep()
        S.dma("sp", gqn[:].rearrange("p a b -> p (a b)"), T["gqn"][0:1, :].to_broadcast([128, 256]), writes=[dgqn])
        wfm = S.sb("wfm", [128, 16, 704], BF16, st); dwfm = S.dep()
        wtm = S.sb("wtm", [128, 16, 2304], BF16, st); dwtm = S.dep()
        wfv = T["w_fm"].rearrange("(kc p) n -> p kc n", p=128)
        wtv = T["w_tm"].rearrange("(kc p) n -> p kc n", p=128)
        for k4 in range(8):
            S.dma("pool", wfm[:, k4 * 2:(k4 + 1) * 2, :], wfv[:, k4 * 2:(k4 + 1) * 2, :], writes=[dwfm])
        for k4 in range(16):
            S.dma("pool", wtm[:, k4:(k4 + 1), :], wtv[:, k4:(k4 + 1), :], writes=[dwtm])
        xb = [S.sb(f"xb{i}", [128, 2048], F32, st) for i in range(2)]; dxb = [S.dep() for _ in range(2)]
        rpb = [S.sb(f"rp{i}", [128, 192], F32, st) for i in range(2)]; drp = [S.dep() for _ in range(2)]
        hb = S.sb("hb", [128, 2048], BF16, st); dhb = S.dep()
        ss = S.sb("ss", [128, 4], F32, st); dss = S.dep()
        hTb = [S.sb(f"hTb{i}", [128, 16, 256], BF16, st) for i in range(2)]; dhT = [S.dep() for _ in range(2)]
        rkst = S.sb("rkst", [64, 8, 256], F32, st); drkst = S.dep()
        wast = S.sb("wast", [96, 2, 256], F32, st); dwast = S.dep()
        stf = S.sb("stf", [128, 2304], F32, st); dstf = [S.dep() for _ in range(5)]
        gst = S.sb("gst", [128, 768], BF16, st); dgst = S.dep()
        qkd = S.sb("qkd", [128, 8, 64], BF16, st); dqkd = S.dep()
        qkTd = S.sb("qkTd", [64, 8, 128], BF16, st); dqkTd = S.dep()
        qkg = S.sb("qkg", [128, 3, 128], BF16, st); dqkg = S.dep()
        qkgn = S.sb("qkgn", [128, 3, 128], F32, st); dqkgn = S.dep()
        qkTg = S.sb("qkTg", [128, 3, 128], BF16, st); dqkTg = S.dep()
        vdst = S.sb("vdst", [128, 2, 129], BF16, st); dvd = S.dep()
        vgst = S.sb("vgst", [128, 129], BF16, st); dvg = S.dep()
        rt1 = S.sb("rt1", [128, 8, 32], F32, st); rt2 = S.sb("rt2", [128, 8, 32], F32, st); drt = S.dep()
        sqg = S.sb("sqg", [128, 3, 128], F32, st); dsqg = S.dep()
        ssg = S.sb("ssg", [128, 4], F32, st); dssg = S.dep()
        pf = S.ps("pf", [64, 8, 256], F32, st); dpf = S.pdep()
        pw = S.ps("pw", [96, 2, 256], F32, st); dpw = S.pdep()
        pg = [S.ps(f"pg{i}", [128, 512], F32, st) for i in range(2)]; dpg = [S.pdep() for _ in range(2)]
        ptb = S.ps("ptb", [128, 8, 128], BF16, st); dptb = S.pdep()
        S.op("dve", lambda e: e.memset(vdst[:], 1.0), writes=[dvd])
        S.op("dve", lambda e: e.memset(vgst[:], 1.0), writes=[dvg])

        s_rk = T["s_rk"].rearrange("g c t -> c g t")
        s_wa = T["s_wa"].rearrange("j c t -> c j t")
        s_dqk = T["s_dqkT"].rearrange("g c t -> c g t")
        s_gqk = T["s_gqkT"].rearrange("g c t -> c g t")
        hTo = T["hT"].rearrange("kc p t -> p kc t")
        import os
        nblk = int(os.environ.get('A1_BLOCKS', NBLK))

        def prep(blk):
            wh = 0 if blk == 0 else 1
            S.drain_dma("sp", keep=12)
            hT = hTb[blk % 2]; dh = dhT[blk % 2]
            for ti in range(2):
                t = 2 * blk + ti
                tok0 = t * 128
                xt = xb[t % 2]; dx = dxb[t % 2]; rp = rpb[t % 2]; dr = drp[t % 2]
                S.dma("sp", xt[:], T["xf"][tok0:tok0 + 128, :], writes=[dx])
                S.op("act", lambda e: e.activation(out=hb[:], in_=xt[:], func=AF.Square, accum_out=ss[:, 0:1]),
                     reads=[dx], writes=[dhb, dss])
                _rstd(S, ss[:, 0:1], 1, 1.0 / D, EPS, [], dss)
                S.op("dve", lambda e: e.scalar_tensor_tensor(out=xt[:], in0=xt[:], scalar=ss[:, 0:1], in1=modb[wh][:, 2048:4096],
                                                             op0=ALU.mult, op1=ALU.mult), reads=[dx, dss, dmodb], writes=[dx])
                S.op("dve", lambda e: e.tensor_tensor(out=hb[:], in0=xt[:], in1=modb[wh][:, 0:2048], op=ALU.add),
                     reads=[dx, dmodb], writes=[dhb])
                for half in range(2):
                    for j in range(8):
                        kc = half * 8 + j
                        S.op("pe", lambda e: e.transpose(out=ptb[:, j, :], in_=hb[:, kc * 128:(kc + 1) * 128], identity=identb[:]),
                             reads=[dhb, did], writes=[dptb])
                    S.op("act", lambda e: e.copy(out=hT[:, half * 8:(half + 1) * 8, ti * 128:(ti + 1) * 128], in_=ptb[:]),
                         reads=[dptb], writes=[dh])
            S.dma("sp", hTo[:, :, blk * 256:(blk + 1) * 256], hT[:], reads=[dh])

        def mm(blk):
            hT = hTb[blk % 2]; dh = dhT[blk % 2]
            for g in range(8):
                for kc in range(16):
                    S.op("pe", lambda e: e.matmul(pf[:, g, :], lhsT=wfm[:, kc, g * 64:(g + 1) * 64], rhs=hT[:, kc, :],
                                                  start=(kc == 0), stop=(kc == 15)), reads=[dwfm, dh], writes=[dpf])
            S.op("act", lambda e: e.copy(out=rkst[:], in_=pf[:]), reads=[dpf], writes=[drkst])
            S.dma("sp", s_rk[:, :, blk * 256:(blk + 1) * 256], rkst[:], reads=[drkst])
            for j in range(2):
                for kc in range(16):
                    S.op("pe", lambda e: e.matmul(pw[:, j, :], lhsT=wfm[:, kc, 512 + j * 96:512 + (j + 1) * 96], rhs=hT[:, kc, :],
                                                  start=(kc == 0), stop=(kc == 15)), reads=[dwfm, dh], writes=[dpw])
            S.op("act", lambda e: e.activation(out=wast[:, 0, :], in_=pw[:, 0, :], func=AF.Tanh), reads=[dpw], writes=[dwast])
            S.op("act", lambda e: e.copy(out=wast[:, 1, :], in_=pw[:, 1, :]), reads=[dpw], writes=[dwast])
            S.dma("sp", s_wa[:, :, blk * 256:(blk + 1) * 256], wast[:], reads=[dwast])
            for ti in range(2):
                t = 2 * blk + ti
                tok0 = t * 128
                rp = rpb[t % 2]; dr = drp[t % 2]
                S.dma("sp", rp[:], T["rope"][tok0:tok0 + 128, :], writes=[dr])
                for gi in range(5):
                    ncol = 512 if gi < 4 else 256
                    p = pg[gi % 2]; dp = dpg[gi % 2]
                    for kc in range(16):
                        S.op("pe", lambda e: e.matmul(p[:, 0:ncol], lhsT=hT[:, kc, ti * 128:(ti + 1) * 128],
                                                      rhs=wtm[:, kc, gi * 512:gi * 512 + ncol], start=(kc == 0), stop=(kc == 15)),
                             reads=[dwtm, dh], writes=[dp])
                    S.op("act", lambda e: e.copy(out=stf[:, gi * 512:gi * 512 + ncol], in_=p[:, 0:ncol]), reads=[dp], writes=[dstf[gi]])
                S.dma("sp", T["s_v"][tok0:tok0 + 128, :], stf[:, 0:256], reads=[dstf[0]])
                S.op("act", lambda e: e.activation(out=gst[:, 0:256], in_=stf[:, 256:512], func=AF.Silu), reads=[dstf[0]], writes=[dgst])
                _rope(S, stf[:, 512:1024].rearrange("p (g d) -> p g d", g=8), qkd[:], rp[:, 0:32], rp[:, 32:64], 8, 32,
                      rt1[:], rt2[:], [dstf[1], dr], [dqkd], drt)
                for g in range(8):
                    S.op("pe", lambda e: e.transpose(out=ptb[0:64, g, :], in_=qkd[:, g, :], identity=identb[:]),
                         reads=[dqkd, did], writes=[dptb])
                S.op("act", lambda e: e.copy(out=qkTd[:], in_=ptb[0:64, :, :]), reads=[dptb], writes=[dqkTd])
                S.dma("sp", s_dqk[:, :, tok0:tok0 + 128], qkTd[:], reads=[dqkTd])
                S.op("act", lambda e: e.copy(out=vdst[:, :, 0:128], in_=stf[:, 1024:1280].rearrange("p (h d) -> p h d", h=2)),
                     reads=[dstf[2]], writes=[dvd])
                S.dma("sp", T["s_dv"][tok0:tok0 + 128, :, :], vdst[:], reads=[dvd])
                S.op("act", lambda e: e.activation(out=gst[:, 256:512], in_=stf[:, 1280:1536], func=AF.Silu), reads=[dstf[2]], writes=[dgst])
                src3 = stf[:, 1536:1920].rearrange("p (g d) -> p g d", g=3)
                S.op("dve", lambda e: e.tensor_tensor(out=sqg[:], in0=src3, in1=src3, op=ALU.mult), reads=[dstf[3]], writes=[dsqg])
                S.op("dve", lambda e: e.tensor_reduce(out=ssg[:, 0:3], in_=sqg[:], axis=AX.X, op=ALU.add), reads=[dsqg], writes=[dssg])
                _rstd(S, ssg[:, 0:3], 3, 1.0 / 128, EPS, [], dssg)
                for i in range(3):
                    S.op("dve", lambda e: e.scalar_tensor_tensor(out=qkgn[:, i, :], in0=src3[:, i, :], scalar=ssg[:, i:i + 1],
                                                                 in1=gqn[:, (0 if i < 2 else 1), :], op0=ALU.mult, op1=ALU.mult),
                         reads=[dstf[3], dssg, dgqn], writes=[dqkgn])
                _rope(S, qkgn[:], qkg[:], rp[:, 64:128], rp[:, 128:192], 3, 64,
                      rt1[:].rearrange("p a b -> p (a b)")[:, 0:192].rearrange("p (a b) -> p a b", a=3),
                      rt2[:].rearrange("p a b -> p (a b)")[:, 0:192].rearrange("p (a b) -> p a b", a=3),
                      [dqkgn, dr], [dqkg], drt)
                for g in range(3):
                    S.op("pe", lambda e: e.transpose(out=ptb[:, g, :], in_=qkg[:, g, :], identity=identb[:]),
                         reads=[dqkg, did], writes=[dptb])
                S.op("act", lambda e: e.copy(out=qkTg[:], in_=ptb[:, 0:3, :]), reads=[dptb], writes=[dqkTg])
                S.dma("sp", s_gqk[:, :, tok0:tok0 + 128], qkTg[:], reads=[dqkTg])
                S.op("act", lambda e: e.copy(out=vgst[:, 0:128], in_=stf[:, 1920:2048]), reads=[dstf[3]], writes=[dvg])
                S.dma("sp", T["s_gv"][tok0:tok0 + 128, :], vgst[:], reads=[dvg])
                S.op("act", lambda e: e.activation(out=gst[:, 512:768], in_=stf[:, 2048:2304], func=AF.Silu), reads=[dstf[4]], writes=[dgst])
                S.dma("sp", T["s_gate"][tok0:tok0 + 128, :], gst[:], reads=[dgst])

        if nblk > 0:
            prep(0)
        for blk in range(nblk):
            if blk + 1 < nblk:
                prep(blk + 1)
            mm(blk)
        S.barrier()


def _phase_A2(S, nc, T):
    with contextlib.ExitStack() as st:
        KTd = S.sb("KTd", [64, 4, NT], BF16, st); dKTd = S.dep()
        Vd = S.sb("Vd", [128, NTILE, 2, 129], BF16, st); dVd = S.dep()
        KTg = S.sb("KTg", [128, NT], BF16, st); dKTg = S.dep()
        Vg = S.sb("Vg", [128, NTILE, 129], BF16, st); dVg = S.dep()
        for g in range(4):
            S.dma("sp", KTd[:, g, :], T["s_dqkT"][4 + g, :, :], writes=[dKTd])
        S.dma("sp", KTg[:], T["s_gqkT"][2, :, :], writes=[dKTg])
        for c in range(2):
            S.dma("sp", Vd[:, c * 17:(c + 1) * 17, :, :], T["s_dv"].rearrange("(t p) h d -> p t h d", p=128)[:, c * 17:(c + 1) * 17, :, :], writes=[dVd])
        S.dma("sp", Vg[:], T["s_gv"].rearrange("(t p) d -> p t d", p=128), writes=[dVg])
        lamp = S.sb("lamp_sb", [128, 4, 64], F32, st); dlam = S.dep()
        lam = S.sb("lam", [128, 8], F32, st)
        S.dma("sp", lamp[:].rearrange("p a b -> p (a b)"), T["lamp"][0:1, :].to_broadcast([128, 256]), writes=[dlam])
        S.dma("sp", lam[:, 4:5], T["lami"][0:1, 0:1].to_broadcast([128, 1]), writes=[dlam])
        prod = S.sb("lprod", [128, 2, 64], F32, st)
        S.op("dve", lambda e: e.tensor_tensor(out=prod[:, 0, :], in0=lamp[:, 0, :], in1=lamp[:, 1, :], op=ALU.mult), reads=[dlam], writes=[dlam])
        S.op("dve", lambda e: e.tensor_tensor(out=prod[:, 1, :], in0=lamp[:, 2, :], in1=lamp[:, 3, :], op=ALU.mult), reads=[dlam], writes=[dlam])
        S.op("dve", lambda e: e.tensor_reduce(out=lam[:, 0:2], in_=prod[:], axis=AX.X, op=ALU.add), reads=[dlam], writes=[dlam])
        S.op("act", lambda e: e.activation(out=lam[:, 2:4], in_=lam[:, 0:2], func=AF.Exp), reads=[dlam], writes=[dlam])
        S.op("dve", lambda e: e.tensor_tensor(out=lam[:, 5:6], in0=lam[:, 2:3], in1=lam[:, 3:4], op=ALU.subtract), reads=[dlam], writes=[dlam])
        S.op("dve", lambda e: e.tensor_tensor(out=lam[:, 6:7], in0=lam[:, 5:6], in1=lam[:, 4:5], op=ALU.add), reads=[dlam], writes=[dlam])
        gsub = S.sb("gsub", [128, 128], F32, st); dgsub = S.dep()
        S.dma("sp", gsub[:], T["subg"][0:1, :].to_broadcast([128, 128]), writes=[dgsub])
        S.op("dve", lambda e: e.tensor_scalar(out=lam[:, 7:8], in0=lam[:, 4:5], scalar1=-1.0, scalar2=1.0, op0=ALU.mult, op1=ALU.add),
             reads=[dlam], writes=[dlam])
        S.op("dve", lambda e: e.tensor_scalar(out=gsub[:], in0=gsub[:], scalar1=lam[:, 7:8], scalar2=None, op0=ALU.mult),
             reads=[dlam, dgsub], writes=[dgsub])

        QTd = [S.sb(f"QTd{i}", [64, 4, 256], BF16, st) for i in range(2)]; dQd = [S.dep() for _ in range(2)]
        QTg = [S.sb(f"QTg{i}", [128, 2, 256], BF16, st) for i in range(2)]; dQg = [S.dep() for _ in range(2)]
        gt = [S.sb(f"gt{i}", [128, 2, 768], BF16, st) for i in range(2)]; dgt = [S.dep() for _ in range(2)]
        PT = [S.sb(f"PT{i}", [128, 2, 256], BF16, st) for i in range(3)]; dPT = [S.dep() for _ in range(3)]
        pss = [S.ps(f"pss{i}", [128, 2, 256], F32, st) for i in range(3)]; dpss = [S.pdep() for _ in range(3)]
        acc = [[S.ps(f"acc{j}{q}", [128, 512], F32, st) for q in range(2)] for j in range(2)]
        dacc = [[S.pdep() for q in range(2)] for j in range(2)]
        rs = S.sb("rs", [128, 8], F32, st); drs = S.dep()
        accs = S.sb("accs", [128, 4, 129], F32, st); daccs = S.dep()
        o0 = S.sb("o0", [128, 128], F32, st); do0 = S.dep()
        dd = S.sb("dd", [128, 128], F32, st); ddd = S.dep()
        junk = S.sb("junk", [128, 128], F32, st); djunk = S.dep()
        brs = [S.sb(f"brs{i}", [128, 512], BF16, st) for i in range(2)]; dbrs = [S.dep() for _ in range(2)]
        s_dqk = T["s_dqkT"].rearrange("g c t -> c g t")
        s_gqk = T["s_gqkT"].rearrange("g c t -> c g t")
        steps = []
        for qb in range(NBLK):
            kts = list(range(2)) if qb == 0 else list(range(NTILE))
            for u in range(3):
                for ki, kt in enumerate(kts):
                    steps.append((qb, u, ki, kt, len(kts)))

        def emit_S(i):
            qb, u, ki, kt, nk = steps[i]
            b2 = qb % 2
            q0 = qb * 256
            if u == 0 and ki == 0:
                S.drain_dma("sp", keep=8)
                S.dma("sp", QTd[b2][:], s_dqk[:, 0:4, q0:q0 + 256], writes=[dQd[b2]])
                S.dma("sp", QTg[b2][:], s_gqk[:, 0:2, q0:q0 + 256], writes=[dQg[b2]])
                S.dma("sp", gt[b2][:], T["s_gate"].rearrange("(t p) n -> p t n", p=128)[:, 2 * qb:2 * qb + 2, :], writes=[dgt[b2]])
            ps_ = pss[i % 3]; dps_ = dpss[i % 3]
            for j in range(2):
                if u < 2:
                    S.op("pe", lambda e: e.matmul(ps_[:, j, :], lhsT=KTd[:, u * 2 + j, kt * 128:(kt + 1) * 128], rhs=QTd[b2][:, u * 2 + j, :],
                                                  start=True, stop=True), reads=[dKTd, dQd[b2]], writes=[dps_])
                else:
                    S.op("pe", lambda e: e.matmul(ps_[:, j, :], lhsT=KTg[:, kt * 128:(kt + 1) * 128], rhs=QTg[b2][:, j, :],
                                                  start=True, stop=True), reads=[dKTg, dQg[b2]], writes=[dps_])

        def emit_EPV(i):
            qb, u, ki, kt, nk = steps[i]
            ps_ = pss[i % 3]; dps_ = dpss[i % 3]; pt_ = PT[i % 3]; dpt_ = dPT[i % 3]
            sc = (64 ** -0.5) if u < 2 else (128 ** -0.5)
            S.op("act", lambda e: e.activation(out=pt_[:], in_=ps_[:], func=AF.Exp, scale=sc), reads=[dps_], writes=[dpt_])
            for j in range(2):
                for q in range(2):
                    rhs = Vd[:, kt, u, :] if u < 2 else Vg[:, kt, :]
                    S.op("pe", lambda e: e.matmul(acc[j][q][:, 0:129], lhsT=pt_[:, j, q * 128:(q + 1) * 128], rhs=rhs,
                                                  start=(ki == 0), stop=(ki == nk - 1)),
                         reads=[dpt_, dVd if u < 2 else dVg], writes=[dacc[j][q]])

        def finalize(qb, u):
            b2 = qb % 2
            q0 = qb * 256
            for j in range(2):
                for q in range(2):
                    S.op("dve", lambda e: e.tensor_copy(out=accs[:, j * 2 + q, :], in_=acc[j][q][:, 0:129]), reads=[dacc[j][q]], writes=[daccs])
            for q in range(2):
                bs = brs[q]; dbs = dbrs[q]
                if u < 2:
                    S.op("dve", lambda e: e.reciprocal(out=rs[:, 0:1], in_=accs[:, 0 * 2 + q, 128:129]), reads=[daccs], writes=[drs])
                    S.op("dve", lambda e: e.reciprocal(out=rs[:, 1:2], in_=accs[:, 1 * 2 + q, 128:129]), reads=[daccs], writes=[drs])
                    S.op("dve", lambda e: e.scalar_tensor_tensor(out=rs[:, 2:3], in0=rs[:, 1:2], scalar=-1.0, in1=lam[:, 6:7],
                                                                 op0=ALU.mult, op1=ALU.mult), reads=[drs, dlam], writes=[drs])
                    S.op("dve", lambda e: e.tensor_scalar(out=o0[:], in0=accs[:, 0 * 2 + q, 0:128], scalar1=rs[:, 0:1], scalar2=None, op0=ALU.mult),
                         reads=[daccs, drs], writes=[do0])
                    S.op("dve", lambda e: e.scalar_tensor_tensor(out=dd[:], in0=accs[:, 1 * 2 + q, 0:128], scalar=rs[:, 2:3], in1=o0[:],
                                                                 op0=ALU.mult, op1=ALU.add), reads=[daccs, drs, do0], writes=[ddd])
                    S.op("act", lambda e: e.activation(out=junk[:], in_=dd[:], func=AF.Square, accum_out=rs[:, 3:4]),
                         reads=[ddd], writes=[djunk, drs])
                    _rstd(S, rs[:, 3:4], 1, 1.0 / 128, EPS, [], drs)
                    S.op("dve", lambda e: e.scalar_tensor_tensor(out=o0[:], in0=dd[:], scalar=rs[:, 3:4], in1=gsub[:],
                                                                 op0=ALU.mult, op1=ALU.mult), reads=[ddd, drs, dgsub], writes=[do0])
                    S.op("dve", lambda e: e.tensor_tensor(out=bs[:, u * 128:(u + 1) * 128], in0=o0[:],
                                                          in1=gt[b2][:, q, 256 + u * 128:256 + (u + 1) * 128], op=ALU.mult),
                         reads=[do0, dgt[b2]], writes=[dbs])
                else:
                    for j in range(2):
                        S.op("dve", lambda e: e.reciprocal(out=rs[:, 4 + j:5 + j], in_=accs[:, j * 2 + q, 128:129]), reads=[daccs], writes=[drs])
                        S.op("dve", lambda e: e.scalar_tensor_tensor(out=bs[:, 256 + j * 128:256 + (j + 1) * 128], in0=accs[:, j * 2 + q, 0:128],
                                                                     scalar=rs[:, 4 + j:5 + j], in1=gt[b2][:, q, 512 + j * 128:512 + (j + 1) * 128],
                                                                     op0=ALU.mult, op1=ALU.mult), reads=[daccs, drs, dgt[b2]], writes=[dbs])
                    S.dma("sp", T["br_att"][q0 + q * 128:q0 + (q + 1) * 128, :], bs[:], reads=[dbs])

        emit_S(0)
        emit_S(1)
        for i in range(len(steps)):
            if i + 2 < len(steps):
                emit_S(i + 2)
            emit_EPV(i)
            qb, u, ki, kt, nk = steps[i]
            if ki == nk - 1:
                finalize(qb, u)
        S.barrier()


def _phase_A3(S, nc, T):
    NCH = NT // 64
    with contextlib.ExitStack() as st:
        identf, didf = _make_ident(S, st, 64, F32, "a3id")
        ones64 = S.sb("ones64", [64, 64], F32, st); dconst = S.dep()
        S.op("dve", lambda e: e.memset(ones64[:], 1.0), writes=[dconst])
        mk = S.sb("mk", [64, 4, 64], F32, st)
        S.op("pool", lambda e: e.memset(mk[:], 1.0), writes=[dconst])
        for i, (stp, cm, cmp_) in enumerate([(1, -1, ALU.is_gt), (1, -1, ALU.is_ge), (-1, 1, ALU.is_gt), (-1, 1, ALU.is_ge)]):
            S.op("pool", lambda e: e.affine_select(out=mk[:, i, :], in_=mk[:, i, :], pattern=[[stp, 64]], compare_op=cmp_, fill=0.0,
                                                   base=0, channel_multiplier=cm), reads=[dconst], writes=[dconst])
        MSK = S.sb("MSK", [64, 2, 320], F32, st)
        for e_ in range(2):
            order = [0, 1, 0, 1, 2] if e_ == 0 else [2, 3, 2, 3, 0]
            for j, m in enumerate(order):
                S.op("dve", lambda e: e.tensor_copy(out=MSK[:, e_, j * 64:(j + 1) * 64], in_=mk[:, m, :]), reads=[dconst], writes=[dconst])
        rmask = S.sb("rmask", [64, 16, 64], F32, st)
        S.op("dve", lambda e: e.memset(rmask[:], 1.0), writes=[dconst])
        S.op("dve", lambda e: e.memset(rmask[:, :, 0:1], 0.0), reads=[dconst], writes=[dconst])
        prm = S.sb("prm", [64, 7, 4], F32, st)
        S.dma("sp", prm[:], T["rprm"][:, :, :], writes=[dconst])
        omk = S.sb("omk", [64, 4], F32, st)
        S.op("dve", lambda e: e.tensor_scalar(out=omk[:], in0=prm[:, 5, :], scalar1=-1.0, scalar2=1.0, op0=ALU.mult, op1=ALU.add),
             reads=[dconst], writes=[dconst])
        wup = S.sb("wup_sb", [96, 2, 256], F32, st)
        aup = S.sb("aup_sb", [96, 2, 256], F32, st)
        S.dma("sp", wup[:], T["wup"].rearrange("e r c -> r e c"), writes=[dconst])
        S.dma("sp", aup[:], T["aup"].rearrange("e r c -> r e c"), writes=[dconst])
        gng = S.sb("gng", [64, 2, 256], F32, st)
        S.dma("sp", gng[:].rearrange("p a b -> p (a b)"), T["gn"][0:1, :].to_broadcast([64, 512]), writes=[dconst])

        yacc = S.sb("yacc", [64, NCH, 256], F32, st); dy = S.dep()
        bacc = S.sb("bacc", [64, NCH, 4], F32, st); dba = S.dep()
        ST = S.sb("ST", [64, 4, 64], F32, st); dST = S.dep()
        PSA = S.ps("PSA", [64, 8, 512], F32, st); dB = [S.pdep() for _ in range(8)]

        def t4(name):
            return S.sb(name, [64, 4, 256], F32, st), S.dep()
        rk_in = S.sb("rk_in", [64, 8, 256], F32, st); drk = S.dep()
        wa_in = S.sb("wa_in", [96, 2, 256], F32, st); dwa = S.dep()
        v_in, dv = t4("v_in")
        lw, dlw = t4("lw"); aa, daa = t4("aa"); kk, dkk = t4("kk"); kd, dkd = t4("kd"); be, dbe = t4("be")
        L, dL = t4("L"); Ld, dLd = t4("Ld"); tmp, dtmp = t4("tmp")
        E1, dE1 = t4("E1"); E2, dE2 = t4("E2"); E3, dE3 = t4("E3"); E4, dE4 = t4("E4")
        bt, dbt = t4("bt"); kt_, dkt = t4("kt_"); bh, dbh = be, dbe; kh, dkh = kd, dkd
        AR = S.sb("AR", [64, 4, 4, 2, 64], F32, st); dAR = S.dep()
        ltot = S.sb("ltot", [64, 4, 4], F32, st); dlt = S.dep()
        pc = S.sb("pc", [64, 4, 4], F32, st); dpc = S.dep()
        bkT = S.sb("bkT", [64, 4, 4, 2, 64], F32, st); dbk = S.dep()
        GM = S.sb("GM", [64, 4, 4, 320], F32, st); dGM = S.dep()
        XX0 = S.sb("XX0", [64, 16, 2, 64], F32, st); XX = [XX0, XX0]; dXX0 = S.dep(); dXX = [dXX0, dXX0]
        Tm = S.sb("Tm", [64, 16, 64], F32, st); dTm = S.dep()
        WT = S.sb("WT", [64, 4, 64], F32, st); dWT = S.dep()
        UT = S.sb("UT", [64, 4, 64], F32, st); dUT = S.dep()
        tS = S.sb("tS", [64, 4, 64], F32, st); dtS = S.dep()

        s_rk = T["s_rk"].rearrange("g c t -> c g t")
        s_wa = T["s_wa"].rearrange("j c t -> c j t")
        v4 = lambda ap: ap.rearrange("p h (c s) -> p h c s", s=64)
        fl = lambda ap: ap.rearrange("p h t -> p (h t)")

        for e_ in range(2):
            S.op("dve", lambda e: e.memset(ST[:], 0.0), reads=[dST], writes=[dST])
            blocks = list(range(NBLK)) if e_ == 0 else [0] + list(range(NBLK - 1, 0, -1))
            corder = [0, 1, 2, 3] if e_ == 0 else [3, 2, 1, 0]
            import os
            blocks = blocks[:int(os.environ.get('A3_BLOCKS', NBLK))]
            for blk in blocks:
                t0 = blk * 256
                S.drain_dma("sp", keep=4)
                S.dma("sp", rk_in[:], s_rk[:, :, t0:t0 + 256], writes=[drk])
                S.dma("sp", wa_in[:], s_wa[:, :, t0:t0 + 256], writes=[dwa])
                S.dma("sp", v_in[:], T["s_v"][t0:t0 + 256, :].rearrange("(c s) n -> s c n", s=64), writes=[dv])
                r_ = rk_in[:, 0:4, :]; k_ = rk_in[:, 4:8, :]
                pwp = PSA[:, 0:2, :].rearrange("p b (h t) -> p (b h) t", h=2)
                pap = PSA[:, 2:4, :].rearrange("p b (h t) -> p (b h) t", h=2)
                for h in range(4):
                    S.op("pe", lambda e: e.matmul(pwp[:, h, :], lhsT=wup[:, e_, h * 64:(h + 1) * 64], rhs=wa_in[:, 0, :], start=True, stop=True),
                         reads=[dconst, dwa], writes=[dB[h // 2]])
                for h in range(4):
                    S.op("pe", lambda e: e.matmul(pap[:, h, :], lhsT=aup[:, e_, h * 64:(h + 1) * 64], rhs=wa_in[:, 1, :], start=True, stop=True),
                         reads=[dconst, dwa], writes=[dB[2 + h // 2]])
                for h in range(4):
                    S.op("act", lambda e: e.activation(out=lw[:, h, :], in_=pwp[:, h, :], func=AF.Sigmoid, bias=prm[:, e_, h:h + 1]),
                         reads=[dB[h // 2], dconst], writes=[dlw])
                for h in range(4):
                    S.op("act", lambda e: e.activation(out=aa[:, h, :], in_=pap[:, h, :], func=AF.Sigmoid, bias=prm[:, 2 + e_, h:h + 1]),
                         reads=[dB[2 + h // 2], dconst], writes=[daa])
                S.op("dve", lambda e: e.tensor_scalar(out=lw[:], in0=lw[:], scalar1=-0.6065306597126334, scalar2=None, op0=ALU.mult),
                     reads=[dlw], writes=[dlw])
                S.op("dve", lambda e: e.tensor_tensor(out=kk[:], in0=k_, in1=prm[:, 4, :].unsqueeze(2).to_broadcast([64, 4, 256]), op=ALU.mult),
                     reads=[drk, dconst], writes=[dkk])
                S.op("dve", lambda e: e.tensor_tensor(out=tmp[:], in0=kk[:], in1=kk[:], op=ALU.mult), reads=[dkk], writes=[dtmp])
                for half in range(2):
                    S.op("pe", lambda e: e.matmul(PSA[:, 4 + half, :], lhsT=ones64[:], rhs=fl(tmp[:])[:, half * 512:(half + 1) * 512],
                                                  start=True, stop=True), reads=[dconst, dtmp], writes=[dB[4 + half]])
                S.op("dve", lambda e: e.tensor_scalar(out=fl(tmp[:]), in0=PSA[:, 4:6, :].rearrange("p b t -> p (b t)"), scalar1=1e-24, scalar2=None,
                                                      op0=ALU.max), reads=[dB[4], dB[5]], writes=[dtmp])
                S.op("act", lambda e: e.activation(out=tmp[:], in_=tmp[:], func=AF.Sqrt), reads=[dtmp], writes=[dtmp])
                S.op("dve", lambda e: e.reciprocal(out=tmp[:], in_=tmp[:]), reads=[dtmp], writes=[dtmp])
                S.op("dve", lambda e: e.tensor_tensor(out=kk[:], in0=kk[:], in1=tmp[:], op=ALU.mult), reads=[dkk, dtmp], writes=[dkk])
                S.op("dve", lambda e: e.tensor_tensor(out=tmp[:], in0=aa[:], in1=prm[:, 5, :].unsqueeze(2).to_broadcast([64, 4, 256]), op=ALU.mult),
                     reads=[daa, dconst], writes=[dtmp])
                S.op("dve", lambda e: e.tensor_tensor(out=tmp[:], in0=tmp[:], in1=omk[:].unsqueeze(2).to_broadcast([64, 4, 256]), op=ALU.add),
                     reads=[dtmp, dconst], writes=[dtmp])
                S.op("dve", lambda e: e.tensor_tensor(out=kd[:], in0=tmp[:], in1=k_, op=ALU.mult), reads=[dtmp, drk], writes=[dkd])
                S.op("pool", lambda e: e.tensor_tensor(out=be[:], in0=kk[:], in1=aa[:], op=ALU.mult), reads=[dkk, daa], writes=[dbe])
                S.op("dve", lambda e: e.tensor_tensor_scan(out=fl(L[:]), data0=rmask[:].rearrange("p a b -> p (a b)"), data1=fl(lw[:]),
                                                           initial=0.0, op0=ALU.mult, op1=ALU.add), reads=[dlw, dconst], writes=[dL])
                S.op("dve", lambda e: e.tensor_copy(out=ltot[:], in_=v4(L[:])[:, :, :, 63]), reads=[dL], writes=[dlt])
                if e_ == 0:
                    Lc, dLc = L, dL
                else:
                    S.op("dve", lambda e: e.tensor_tensor(out=tmp[:], in0=lw[:], in1=L[:], op=ALU.subtract), reads=[dlw, dL], writes=[dtmp])
                    S.op("dve", lambda e: e.tensor_tensor(out=v4(Ld[:]), in0=v4(tmp[:]), in1=ltot[:].unsqueeze(3).to_broadcast([64, 4, 4, 64]),
                                                          op=ALU.add), reads=[dtmp, dlt], writes=[dLd])
                    Lc, dLc = Ld, dLd
                S.op("pool", lambda e: e.tensor_tensor(out=E1[:], in0=Lc[:], in1=lw[:], op=ALU.subtract), reads=[dLc, dlw], writes=[dE1])
                S.op("act", lambda e: e.activation(out=E1[:], in_=E1[:], func=AF.Exp), reads=[dE1], writes=[dE1])
                S.op("act", lambda e: e.activation(out=E2[:], in_=Lc[:], func=AF.Exp), reads=[dLc], writes=[dE2])
                S.op("act", lambda e: e.activation(out=E3[:], in_=Lc[:], func=AF.Exp, scale=-1.0), reads=[dLc], writes=[dE3])
                S.op("dve", lambda e: e.tensor_tensor(out=v4(tmp[:]), in0=ltot[:].unsqueeze(3).to_broadcast([64, 4, 4, 64]), in1=v4(Lc[:]),
                                                      op=ALU.subtract), reads=[dlt, dLc], writes=[dtmp])
                S.op("act", lambda e: e.activation(out=E4[:], in_=tmp[:], func=AF.Exp), reads=[dtmp], writes=[dE4])
                S.op("act", lambda e: e.activation(out=pc[:], in_=ltot[:], func=AF.Exp), reads=[dlt], writes=[dpc])
                S.op("dve", lambda e: e.scalar_tensor_tensor(out=AR[:, :, :, 0, :], in0=v4(kk[:]), scalar=-1.0, in1=v4(E1[:]),
                                                             op0=ALU.mult, op1=ALU.mult), reads=[dkk, dE1], writes=[dAR])
                S.op("dve", lambda e: e.tensor_tensor(out=AR[:, :, :, 1, :], in0=v4(r_), in1=v4(E2[:]), op=ALU.mult), reads=[drk, dE2], writes=[dAR])
                S.op("dve", lambda e: e.tensor_tensor(out=kt_[:], in0=kd[:], in1=E3[:], op=ALU.mult), reads=[dkd, dE3], writes=[dkt])
                S.op("pool", lambda e: e.tensor_tensor(out=bt[:], in0=be[:], in1=E3[:], op=ALU.mult), reads=[dbe, dE3], writes=[dbt])
                S.op("dve", lambda e: e.tensor_tensor(out=tmp[:], in0=r_, in1=kd[:], op=ALU.mult), reads=[drk, dkd], writes=[dtmp])
                S.op("dve", lambda e: e.tensor_tensor(out=tmp[:], in0=tmp[:], in1=prm[:, 6, :].unsqueeze(2).to_broadcast([64, 4, 256]), op=ALU.mult),
                     reads=[dtmp, dconst], writes=[dtmp])
                pbo2 = PSA[:, 6, 0:32].rearrange("p (c h w) -> p c h w", h=4, w=2)
                pbo = pbo2[:, :, :, 0]
                for c in range(4):
                    for h in range(4):
                        S.op("pe", lambda e: e.matmul(pbo2[:, c, h, :], lhsT=tmp[:, h, c * 64:(c + 1) * 64], rhs=ones64[:, 0:2], start=True, stop=True),
                             reads=[dtmp, dconst], writes=[dB[6]])
                S.op("dve", lambda e: e.tensor_tensor(out=bh[:], in0=be[:], in1=E4[:], op=ALU.mult), reads=[dbe, dE4], writes=[dbh])
                S.op("pool", lambda e: e.tensor_tensor(out=kh[:], in0=kd[:], in1=E4[:], op=ALU.mult), reads=[dkd, dE4], writes=[dkh])
                bsl = bacc[:, blk * 4:(blk + 1) * 4, :]
                if e_ == 0:
                    S.op("dve", lambda e: e.tensor_copy(out=bsl, in_=pbo), reads=[dB[6]], writes=[dba])
                else:
                    S.op("dve", lambda e: e.tensor_tensor(out=bsl, in0=bsl, in1=pbo, op=ALU.add), reads=[dB[6], dba], writes=[dba])
                ptr = PSA[:, 0:4, :].rearrange("p b (i s) -> p (b i) s", s=64)
                for c in range(4):
                    for h in range(4):
                        for w_, (src, dsrc) in enumerate([(bh, dbh), (kh, dkh)]):
                            idx = (c * 4 + h) * 2 + w_
                            S.op("pe", lambda e: e.transpose(out=ptr[:, idx, :], in_=src[:, h, c * 64:(c + 1) * 64], identity=identf[:]),
                                 reads=[dsrc, didf], writes=[dB[idx // 8]])
                for half in range(2):
                    S.op("act", lambda e: e.copy(out=bkT[:, half * 2:(half + 1) * 2].rearrange("p c h w s -> p (c h w s)"),
                                                 in_=PSA[:, half * 2:(half + 1) * 2, :].rearrange("p b t -> p (b t)")),
                         reads=[dB[half * 2], dB[half * 2 + 1]], writes=[dbk])
                for c in range(0 if 'g' in os.environ.get('A3_SKIP', '') else 4):
                    bb = (c % 2) * 4
                    cs = slice(c * 64, (c + 1) * 64)
                    for h in range(4):
                        arh = AR[:, h, c, :, :].rearrange("p a s -> p (a s)")
                        S.op("pe", lambda e: e.matmul(PSA[:, bb + h, 0:128], lhsT=bt[:, h, cs], rhs=arh, start=True, stop=True),
                             reads=[dbt, dAR], writes=[dB[bb + h]])
                        S.op("pe", lambda e: e.matmul(PSA[:, bb + h, 128:256], lhsT=kt_[:, h, cs], rhs=arh, start=True, stop=True),
                             reads=[dkt, dAR], writes=[dB[bb + h]])
                        S.op("pe", lambda e: e.matmul(PSA[:, bb + h, 256:320], lhsT=AR[:, h, c, 0, :], rhs=bt[:, h, cs], start=True, stop=True),
                             reads=[dbt, dAR], writes=[dB[bb + h]])
                    S.op("dve", lambda e: e.tensor_tensor(out=GM[:, c, :, :], in0=PSA[:, bb:bb + 4, 0:320],
                                                          in1=MSK[:, e_, :].unsqueeze(1).to_broadcast([64, 4, 320]), op=ALU.mult),
                         reads=[dB[bb], dB[bb + 1], dB[bb + 2], dB[bb + 3], dconst], writes=[dGM])
                GMf = GM[:].rearrange("p c h n -> p (c h) n")
                S.op("dve", lambda e: e.tensor_tensor(out=Tm[:], in0=GMf[:, :, 0:64], in1=identf[:].unsqueeze(1).to_broadcast([64, 16, 64]), op=ALU.add),
                     reads=[dGM, didf], writes=[dTm])
                pxx = PSA[:, 0:4, :].rearrange("p b (i w s) -> p (b i) w s", w=2, s=64)
                ptm = PSA[:, 4:6, :].rearrange("p b (i s) -> p (b i) s", s=64)
                for it_ in range(0 if 'd' in os.environ.get('A3_SKIP', '') else 5):
                    pp = it_ % 2
                    for idx in range(16):
                        if it_ == 0:
                            Xc = GMf[:, idx, 0:64]; XTc = GMf[:, idx, 256:320]; dsrc = dGM
                        else:
                            Xc = XX[1 - pp][:, idx, 0, :]; XTc = XX[1 - pp][:, idx, 1, :]; dsrc = dXX[1 - pp]
                        S.op("pe", lambda e: e.matmul(pxx[:, idx, 0, :], lhsT=XTc, rhs=Xc, start=True, stop=True), reads=[dsrc], writes=[dB[idx // 4]])
                        S.op("pe", lambda e: e.matmul(pxx[:, idx, 1, :], lhsT=Xc, rhs=XTc, start=True, stop=True), reads=[dsrc], writes=[dB[idx // 4]])
                    S.op("act", lambda e: e.copy(out=XX[pp][:].rearrange("p i w s -> p (i w s)"), in_=PSA[:, 0:4, :].rearrange("p b t -> p (b t)")),
                         reads=[dB[0], dB[1], dB[2], dB[3]], writes=[dXX[pp]])
                    for idx in range(16):
                        S.op("pe", lambda e: e.matmul(ptm[:, idx, :], lhsT=XX[pp][:, idx, 1, :], rhs=Tm[:, idx, :], start=True, stop=True),
                             reads=[dXX[pp], dTm], writes=[dB[4 + idx // 8]])
                    S.op("dve", lambda e: e.tensor_tensor(out=Tm[:].rearrange("p i s -> p (i s)"), in0=Tm[:].rearrange("p i s -> p (i s)"),
                                                          in1=PSA[:, 4:6, :].rearrange("p b t -> p (b t)"), op=ALU.add),
                         reads=[dB[4], dB[5], dTm], writes=[dTm])
                pW = PSA[:, 6, 0:256].rearrange("p (h s) -> p h s", s=64)
                pU = PSA[:, 7, 0:256].rearrange("p (h s) -> p h s", s=64)
                pYS = PSA[:, 6, :].rearrange("p (h w s) -> p h w s", w=2, s=64)
                for c in ([] if 's' in os.environ.get('A3_SKIP', '') else corder):
                    gc = blk * 4 + c
                    for h in range(4):
                        vh = v_in[:, c, h * 64:(h + 1) * 64]
                        S.op("pe", lambda e: e.matmul(pW[:, h, :], lhsT=AR[:, h, c, 0, :], rhs=ST[:, h, :], start=True, stop=False),
                             reads=[dAR, dST], writes=[dB[6]])
                        S.op("pe", lambda e: e.matmul(pW[:, h, :], lhsT=GM[:, c, h, 128:192], rhs=vh, start=False, stop=True),
                             reads=[dGM, dv], writes=[dB[6]])
                    S.op("act", lambda e: e.copy(out=WT[:], in_=pW), reads=[dB[6]], writes=[dWT])
                    for h in range(4):
                        S.op("pe", lambda e: e.matmul(pU[:, h, :], lhsT=Tm[:, c * 4 + h, :], rhs=WT[:, h, :], start=True, stop=True),
                             reads=[dTm, dWT], writes=[dB[7]])
                    S.op("act", lambda e: e.copy(out=UT[:], in_=pU), reads=[dB[7]], writes=[dUT])
                    if 'y' in os.environ.get('A3_SKIP', ''):
                        continue
                    for h in range(4):
                        vh = v_in[:, c, h * 64:(h + 1) * 64]
                        S.op("pe", lambda e: e.matmul(pYS[:, h, 0, :], lhsT=AR[:, h, c, 1, :], rhs=ST[:, h, :], start=True, stop=False),
                             reads=[dAR, dST], writes=[dB[6]])
                        S.op("pe", lambda e: e.matmul(pYS[:, h, 0, :], lhsT=GM[:, c, h, 64:128], rhs=UT[:, h, :], start=False, stop=False),
                             reads=[dGM, dUT], writes=[dB[6]])
                        S.op("pe", lambda e: e.matmul(pYS[:, h, 0, :], lhsT=GM[:, c, h, 192:256], rhs=vh, start=False, stop=True),
                             reads=[dGM, dv], writes=[dB[6]])
                        S.op("pe", lambda e: e.matmul(pYS[:, h, 1, :], lhsT=bkT[:, c, h, 0, :], rhs=UT[:, h, :], start=True, stop=False),
                             reads=[dbk, dUT], writes=[dB[6]])
                        S.op("pe", lambda e: e.matmul(pYS[:, h, 1, :], lhsT=bkT[:, c, h, 1, :], rhs=vh, start=False, stop=True),
                             reads=[dbk, dv], writes=[dB[6]])
                    ysl = yacc[:, gc, :].rearrange("p (h s) -> p h s", s=64)
                    if e_ == 0:
                        S.op("dve", lambda e: e.tensor_copy(out=ysl, in_=pYS[:, :, 0, :]), reads=[dB[6]], writes=[dy])
                    else:
                        S.op("dve", lambda e: e.tensor_tensor(out=ysl, in0=ysl, in1=pYS[:, :, 0, :], op=ALU.add), reads=[dB[6], dy], writes=[dy])
                    S.op("dve", lambda e: e.tensor_tensor(out=tS[:], in0=ST[:], in1=pc[:, :, c:c + 1].to_broadcast([64, 4, 64]), op=ALU.mult),
                         reads=[dST, dpc], writes=[dtS])
                    S.op("dve", lambda e: e.tensor_tensor(out=ST[:], in0=tS[:], in1=pYS[:, :, 1, :], op=ALU.add), reads=[dtS, dB[6]], writes=[dST])
        gtb_ = fl(E3[:]).bitcast(BF16)[:, 0:1024].rearrange("p (c n) -> p c n", n=256); dgtb = dE3
        obr_ = fl(E4[:]).bitcast(BF16)[:, 0:1024].rearrange("p (c n) -> p c n", n=256); dobr = dE4
        st1 = S.sb("st1", [64, 4, 16], F32, st); dst1 = S.dep()
        v16 = lambda ap: ap.rearrange("p c (h s) -> p (c h) s", s=64)
        for blk in range(NBLK):
            t0 = blk * 256
            S.drain_dma("sp", keep=4)
            yb = yacc[:, blk * 4:(blk + 1) * 4, :]
            S.dma("sp", v_in[:], T["s_v"][t0:t0 + 256, :].rearrange("(c s) n -> s c n", s=64), writes=[dv])
            S.dma("sp", gtb_, T["s_gate"][t0:t0 + 256, 0:256].rearrange("(c s) n -> s c n", s=64), writes=[dgtb])
            S.op("dve", lambda e: e.tensor_reduce(out=st1[:, 0, :], in_=v16(yb), axis=AX.X, op=ALU.add), reads=[dy], writes=[dst1])
            S.op("dve", lambda e: e.tensor_tensor(out=tmp[:], in0=yb, in1=yb, op=ALU.mult), reads=[dy], writes=[dtmp])
            S.op("dve", lambda e: e.tensor_reduce(out=st1[:, 1, :], in_=v16(tmp[:]), axis=AX.X, op=ALU.add), reads=[dtmp], writes=[dst1])
            S.op("dve", lambda e: e.tensor_scalar(out=st1[:, 0:2, :], in0=st1[:, 0:2, :], scalar1=1.0 / 64, scalar2=None, op0=ALU.mult),
                 reads=[dst1], writes=[dst1])
            S.op("dve", lambda e: e.tensor_tensor(out=st1[:, 2, :], in0=st1[:, 0, :], in1=st1[:, 0, :], op=ALU.mult), reads=[dst1], writes=[dst1])
            S.op("dve", lambda e: e.tensor_tensor(out=st1[:, 3, :], in0=st1[:, 1, :], in1=st1[:, 2, :], op=ALU.subtract), reads=[dst1], writes=[dst1])
            _rstd(S, st1[:, 3, :], 16, 1.0, 64e-5, [], dst1)
            S.op("dve", lambda e: e.tensor_tensor(out=v16(tmp[:]), in0=v16(yb), in1=st1[:, 0, :].unsqueeze(2).to_broadcast([64, 16, 64]), op=ALU.subtract),
                 reads=[dy, dst1], writes=[dtmp])
            S.op("dve", lambda e: e.tensor_tensor(out=v16(tmp[:]), in0=v16(tmp[:]), in1=st1[:, 3, :].unsqueeze(2).to_broadcast([64, 16, 64]), op=ALU.mult),
                 reads=[dtmp, dst1], writes=[dtmp])
            S.op("dve", lambda e: e.tensor_tensor(out=tmp[:], in0=tmp[:], in1=gng[:, 0, :].unsqueeze(1).to_broadcast([64, 4, 256]), op=ALU.mult),
                 reads=[dtmp, dconst], writes=[dtmp])
            S.op("dve", lambda e: e.tensor_tensor(out=tmp[:], in0=tmp[:], in1=gng[:, 1, :].unsqueeze(1).to_broadcast([64, 4, 256]), op=ALU.add),
                 reads=[dtmp, dconst], writes=[dtmp])
            bv = bacc[:, blk * 4:(blk + 1) * 4, :].rearrange("p c h -> p (c h)").unsqueeze(2).to_broadcast([64, 16, 64])
            S.op("dve", lambda e: e.tensor_tensor(out=v16(E1[:]), in0=v16(v_in[:]), in1=bv, op=ALU.mult), reads=[dv, dba], writes=[dE1])
            S.op("dve", lambda e: e.tensor_tensor(out=tmp[:], in0=tmp[:], in1=E1[:], op=ALU.add), reads=[dtmp, dE1], writes=[dtmp])
            S.op("dve", lambda e: e.tensor_tensor(out=obr_, in0=tmp[:], in1=gtb_, op=ALU.mult), reads=[dtmp, dgtb], writes=[dobr])
            S.dma("sp", T["br_rwkv"][t0:t0 + 256, :].rearrange("(c s) n -> s c n", s=64), obr_, reads=[dobr])
        S.barrier()


def _phase_A3i(S, nc, T):
    NCH = NT // 64
    BT = 128
    NCB = 2
    NHB = NT // BT
    NI = 4 * NCB
    with contextlib.ExitStack() as st:
        identf, didf = _make_ident(S, st, 64, F32, "a3id")
        ones64 = S.sb("ones64", [64, 64], F32, st); dconst = S.dep()
        S.op("dve", lambda e: e.memset(ones64[:], 1.0), writes=[dconst])
        mk = S.sb("mk", [64, 4, 64], F32, st)
        S.op("pool", lambda e: e.memset(mk[:], 1.0), writes=[dconst])
        for i, (stp, cm, cmp_) in enumerate([(1, -1, ALU.is_gt), (1, -1, ALU.is_ge), (-1, 1, ALU.is_gt), (-1, 1, ALU.is_ge)]):
            S.op("pool", lambda e: e.affine_select(out=mk[:, i, :], in_=mk[:, i, :], pattern=[[stp, 64]], compare_op=cmp_, fill=0.0,
                                                   base=0, channel_multiplier=cm), reads=[dconst], writes=[dconst])
        MSK = S.sb("MSK", [64, 2, 320], F32, st)
        for e_ in range(2):
            order = [0, 1, 0, 1, 2] if e_ == 0 else [2, 3, 2, 3, 0]
            for j, m in enumerate(order):
                S.op("dve", lambda e: e.tensor_copy(out=MSK[:, e_, j * 64:(j + 1) * 64], in_=mk[:, m, :]), reads=[dconst], writes=[dconst])
        rmask = S.sb("rmask", [64, NI, 64], F32, st)
        S.op("dve", lambda e: e.memset(rmask[:], 1.0), writes=[dconst])
        S.op("dve", lambda e: e.memset(rmask[:, :, 0:1], 0.0), reads=[dconst], writes=[dconst])
        prm = S.sb("prm", [64, 7, 4], F32, st)
        S.dma("sp", prm[:], T["rprm"][:, :, :], writes=[dconst])
        omk = S.sb("omk", [64, 4], F32, st)
        S.op("dve", lambda e: e.tensor_scalar(out=omk[:], in0=prm[:, 5, :], scalar1=-1.0, scalar2=1.0, op0=ALU.mult, op1=ALU.add),
             reads=[dconst], writes=[dconst])
        wup = S.sb("wup_sb", [96, 2, 256], F32, st)
        aup = S.sb("aup_sb", [96, 2, 256], F32, st)
        S.dma("sp", wup[:], T["wup"].rearrange("e r c -> r e c"), writes=[dconst])
        S.dma("sp", aup[:], T["aup"].rearrange("e r c -> r e c"), writes=[dconst])
        gng = S.sb("gng", [64, 2, 256], F32, st)
        S.dma("sp", gng[:].rearrange("p a b -> p (a b)"), T["gn"][0:1, :].to_broadcast([64, 512]), writes=[dconst])

        yacc = S.sb("yacc", [64, NCH, 256], F32, st); dy = S.dep()
        bacc = S.sb("bacc", [64, NCH, 4], F32, st); dba = S.dep()
        S.op("dve", lambda e: e.memset(yacc[:], 0.0), writes=[dy])
        S.op("dve", lambda e: e.memset(bacc[:], 0.0), writes=[dba])
        PSA = S.ps("PSA", [64, 8, 512], F32, st); dB = [S.pdep() for _ in range(8)]

        s_rk = T["s_rk"].rearrange("g c t -> c g t")
        s_wa = T["s_wa"].rearrange("j c t -> c j t")
        v4 = lambda ap: ap.rearrange("p h (c s) -> p h c s", s=64)
        fl = lambda ap: ap.rearrange("p h t -> p (h t)")

        class Bset:
            pass

        def mkset(e_):
            B = Bset()
            sf = f"_{e_}"

            def t4(name):
                return S.sb(name + sf, [64, 4, BT], F32, st), S.dep()
            B.ST = S.sb("ST" + sf, [64, 4, 64], F32, st); B.dST = S.dep()
            B.rk_in = S.sb("rk_in" + sf, [64, 8, BT], F32, st); B.drk = S.dep()
            B.wa_in = S.sb("wa_in" + sf, [96, 2, BT], F32, st); B.dwa = S.dep()
            B.v_in = S.sb("v_in" + sf, [64, NCB, 256], F32, st); B.dv = S.dep()
            B.lw, B.dlw = t4("lw"); B.aa, B.daa = t4("aa"); B.kk, B.dkk = t4("kk"); B.kd, B.dkd = t4("kd"); B.be, B.dbe = t4("be")
            B.L, B.dL = t4("L"); B.Ld, B.dLd = t4("Ld"); B.tmp, B.dtmp = t4("tmp")
            B.E1, B.dE1 = t4("E1"); B.E2, B.dE2 = t4("E2"); B.E3, B.dE3 = t4("E3"); B.E4, B.dE4 = t4("E4")
            B.bt, B.dbt = t4("bt"); B.kt_, B.dkt = t4("kt_")
            B.AR = S.sb("AR" + sf, [64, 4, NCB, 2, 64], F32, st); B.dAR = S.dep()
            B.ltot = S.sb("ltot" + sf, [64, 4, NCB], F32, st); B.dlt = S.dep()
            B.pc = S.sb("pc" + sf, [64, 4, NCB], F32, st); B.dpc = S.dep()
            B.bkT = S.sb("bkT" + sf, [64, NCB, 4, 2, 64], F32, st); B.dbk = S.dep()
            B.GM = S.sb("GM" + sf, [64, NCB, 4, 320], F32, st); B.dGM = S.dep()
            B.XX0 = S.sb("XX0" + sf, [64, NI, 2, 64], F32, st); B.dXX = S.dep()
            B.Tm = S.sb("Tm" + sf, [64, NI, 64], F32, st); B.dTm = S.dep()
            B.WT = S.sb("WT" + sf, [64, 4, 64], F32, st); B.dWT = S.dep()
            B.UT = S.sb("UT" + sf, [64, 4, 64], F32, st); B.dUT = S.dep()
            B.tS = S.sb("tS" + sf, [64, 4, 64], F32, st); B.dtS = S.dep()
            return B

        def block_gen(e_, B, blk):
            pb = 4 * e_
            t0 = blk * BT
            corder = list(range(NCB)) if e_ == 0 else list(range(NCB - 1, -1, -1))
            S.drain_dma("sp", keep=8)
            S.dma("sp", B.rk_in[:], s_rk[:, :, t0:t0 + BT], writes=[B.drk])
            S.dma("sp", B.wa_in[:], s_wa[:, :, t0:t0 + BT], writes=[B.dwa])
            S.dma("sp", B.v_in[:], T["s_v"][t0:t0 + BT, :].rearrange("(c s) n -> s c n", s=64), writes=[B.dv])
            r_ = B.rk_in[:, 0:4, :]; k_ = B.rk_in[:, 4:8, :]
            lw, aa, kk, kd, be, L, Ld, tmp = B.lw, B.aa, B.kk, B.kd, B.be, B.L, B.Ld, B.tmp
            E1, E2, E3, E4, bt, kt_, AR, GM, Tm, XX0 = B.E1, B.E2, B.E3, B.E4, B.bt, B.kt_, B.AR, B.GM, B.Tm, B.XX0
            bh, kh = be, kd
            pwp = PSA[:, pb, :].rearrange("p (h t) -> p h t", h=4)
            pap = PSA[:, pb + 1, :].rearrange("p (h t) -> p h t", h=4)
            for h in range(4):
                S.op("pe", lambda e: e.matmul(pwp[:, h, :], lhsT=wup[:, e_, h * 64:(h + 1) * 64], rhs=B.wa_in[:, 0, :], start=True, stop=True),
                     reads=[dconst, B.dwa], writes=[dB[pb]])
            for h in range(4):
                S.op("pe", lambda e: e.matmul(pap[:, h, :], lhsT=aup[:, e_, h * 64:(h + 1) * 64], rhs=B.wa_in[:, 1, :], start=True, stop=True),
                     reads=[dconst, B.dwa], writes=[dB[pb + 1]])
            S.op("dve", lambda e: e.tensor_tensor(out=kk[:], in0=k_, in1=prm[:, 4, :].unsqueeze(2).to_broadcast([64, 4, BT]), op=ALU.mult),
                 reads=[B.drk, dconst], writes=[B.dkk])
            S.op("dve", lambda e: e.tensor_tensor(out=tmp[:], in0=kk[:], in1=kk[:], op=ALU.mult), reads=[B.dkk], writes=[B.dtmp])
            S.op("pe", lambda e: e.matmul(PSA[:, pb + 2, :], lhsT=ones64[:], rhs=fl(tmp[:]), start=True, stop=True),
                 reads=[dconst, B.dtmp], writes=[dB[pb + 2]])
            yield
            for h in range(4):
                S.op("act", lambda e: e.activation(out=lw[:, h, :], in_=pwp[:, h, :], func=AF.Sigmoid, bias=prm[:, e_, h:h + 1]),
                     reads=[dB[pb], dconst], writes=[B.dlw])
            for h in range(4):
                S.op("act", lambda e: e.activation(out=aa[:, h, :], in_=pap[:, h, :], func=AF.Sigmoid, bias=prm[:, 2 + e_, h:h + 1]),
                     reads=[dB[pb + 1], dconst], writes=[B.daa])
            S.op("dve", lambda e: e.tensor_scalar(out=fl(tmp[:]), in0=PSA[:, pb + 2, :], scalar1=1e-24, scalar2=None, op0=ALU.max),
                 reads=[dB[pb + 2]], writes=[B.dtmp])
            yield
            S.op("act", lambda e: e.activation(out=tmp[:], in_=tmp[:], func=AF.Sqrt), reads=[B.dtmp], writes=[B.dtmp])
            S.op("dve", lambda e: e.tensor_scalar(out=lw[:], in0=lw[:], scalar1=-0.6065306597126334, scalar2=None, op0=ALU.mult),
                 reads=[B.dlw], writes=[B.dlw])
            yield
            S.op("dve", lambda e: e.reciprocal(out=tmp[:], in_=tmp[:]), reads=[B.dtmp], writes=[B.dtmp])
            S.op("dve", lambda e: e.tensor_tensor(out=kk[:], in0=kk[:], in1=tmp[:], op=ALU.mult), reads=[B.dkk, B.dtmp], writes=[B.dkk])
            S.op("dve", lambda e: e.tensor_tensor(out=tmp[:], in0=aa[:], in1=prm[:, 5, :].unsqueeze(2).to_broadcast([64, 4, BT]), op=ALU.mult),
                 reads=[B.daa, dconst], writes=[B.dtmp])
            S.op("dve", lambda e: e.tensor_tensor(out=tmp[:], in0=tmp[:], in1=omk[:].unsqueeze(2).to_broadcast([64, 4, BT]), op=ALU.add),
                 reads=[B.dtmp, dconst], writes=[B.dtmp])
            S.op("dve", lambda e: e.tensor_tensor(out=kd[:], in0=tmp[:], in1=k_, op=ALU.mult), reads=[B.dtmp, B.drk], writes=[B.dkd])
            S.op("pool", lambda e: e.tensor_tensor(out=be[:], in0=kk[:], in1=aa[:], op=ALU.mult), reads=[B.dkk, B.daa], writes=[B.dbe])
            yield
            S.op("dve", lambda e: e.tensor_tensor_scan(out=fl(L[:]), data0=rmask[:].rearrange("p a b -> p (a b)"), data1=fl(lw[:]),
                                                       initial=0.0, op0=ALU.mult, op1=ALU.add), reads=[B.dlw, dconst], writes=[B.dL])
            S.op("dve", lambda e: e.tensor_copy(out=B.ltot[:], in_=v4(L[:])[:, :, :, 63]), reads=[B.dL], writes=[B.dlt])
            if e_ == 0:
                Lc, dLc = L, B.dL
            else:
                S.op("dve", lambda e: e.tensor_tensor(out=tmp[:], in0=lw[:], in1=L[:], op=ALU.subtract), reads=[B.dlw, B.dL], writes=[B.dtmp])
                S.op("dve", lambda e: e.tensor_tensor(out=v4(Ld[:]), in0=v4(tmp[:]), in1=B.ltot[:].unsqueeze(3).to_broadcast([64, 4, NCB, 64]),
                                                      op=ALU.add), reads=[B.dtmp, B.dlt], writes=[B.dLd])
                Lc, dLc = Ld, B.dLd
            S.op("pool", lambda e: e.tensor_tensor(out=E1[:], in0=Lc[:], in1=lw[:], op=ALU.subtract), reads=[dLc, B.dlw], writes=[B.dE1])
            S.op("dve", lambda e: e.tensor_tensor(out=v4(tmp[:]), in0=B.ltot[:].unsqueeze(3).to_broadcast([64, 4, NCB, 64]), in1=v4(Lc[:]),
                                                  op=ALU.subtract), reads=[B.dlt, dLc], writes=[B.dtmp])
            yield
            S.op("act", lambda e: e.activation(out=E1[:], in_=E1[:], func=AF.Exp), reads=[B.dE1], writes=[B.dE1])
            S.op("act", lambda e: e.activation(out=E2[:], in_=Lc[:], func=AF.Exp), reads=[dLc], writes=[B.dE2])
            S.op("act", lambda e: e.activation(out=E3[:], in_=Lc[:], func=AF.Exp, scale=-1.0), reads=[dLc], writes=[B.dE3])
            S.op("act", lambda e: e.activation(out=E4[:], in_=tmp[:], func=AF.Exp), reads=[B.dtmp], writes=[B.dE4])
            S.op("act", lambda e: e.activation(out=B.pc[:], in_=B.ltot[:], func=AF.Exp), reads=[B.dlt], writes=[B.dpc])
            yield
            S.op("dve", lambda e: e.scalar_tensor_tensor(out=AR[:, :, :, 0, :], in0=v4(kk[:]), scalar=-1.0, in1=v4(E1[:]),
                                                         op0=ALU.mult, op1=ALU.mult), reads=[B.dkk, B.dE1], writes=[B.dAR])
            S.op("dve", lambda e: e.tensor_tensor(out=AR[:, :, :, 1, :], in0=v4(r_), in1=v4(E2[:]), op=ALU.mult), reads=[B.drk, B.dE2], writes=[B.dAR])
            S.op("dve", lambda e: e.tensor_tensor(out=kt_[:], in0=kd[:], in1=E3[:], op=ALU.mult), reads=[B.dkd, B.dE3], writes=[B.dkt])
            S.op("pool", lambda e: e.tensor_tensor(out=bt[:], in0=be[:], in1=E3[:], op=ALU.mult), reads=[B.dbe, B.dE3], writes=[B.dbt])
            S.op("dve", lambda e: e.tensor_tensor(out=tmp[:], in0=r_, in1=kd[:], op=ALU.mult), reads=[B.drk, B.dkd], writes=[B.dtmp])
            S.op("dve", lambda e: e.tensor_tensor(out=tmp[:], in0=tmp[:], in1=prm[:, 6, :].unsqueeze(2).to_broadcast([64, 4, BT]), op=ALU.mult),
                 reads=[B.dtmp, dconst], writes=[B.dtmp])
            pbo2 = PSA[:, pb + 3, 0:NCB * 8].rearrange("p (c h w) -> p c h w", h=4, w=2)
            pbo = pbo2[:, :, :, 0]
            for c in range(NCB):
                for h in range(4):
                    S.op("pe", lambda e: e.matmul(pbo2[:, c, h, :], lhsT=tmp[:, h, c * 64:(c + 1) * 64], rhs=ones64[:, 0:2], start=True, stop=True),
                         reads=[B.dtmp, dconst], writes=[dB[pb + 3]])
            yield
            S.op("dve", lambda e: e.tensor_tensor(out=bh[:], in0=be[:], in1=E4[:], op=ALU.mult), reads=[B.dbe, B.dE4], writes=[B.dbe])
            S.op("pool", lambda e: e.tensor_tensor(out=kh[:], in0=kd[:], in1=E4[:], op=ALU.mult), reads=[B.dkd, B.dE4], writes=[B.dkd])
            bsl = bacc[:, blk * NCB:(blk + 1) * NCB, :]
            S.op("dve", lambda e: e.tensor_tensor(out=bsl, in0=bsl, in1=pbo, op=ALU.add), reads=[dB[pb + 3], dba], writes=[dba])
            yield
            ptr = PSA[:, pb:pb + 2, :].rearrange("p b (i s) -> p (b i) s", s=64)
            for c in range(NCB):
                for h in range(4):
                    for w_, (src, dsrc) in enumerate([(bh, B.dbe), (kh, B.dkd)]):
                        idx = (c * 4 + h) * 2 + w_
                        S.op("pe", lambda e: e.transpose(out=ptr[:, idx, :], in_=src[:, h, c * 64:(c + 1) * 64], identity=identf[:]),
                             reads=[dsrc, didf], writes=[dB[pb + idx // 8]])
            yield
            S.op("act", lambda e: e.copy(out=B.bkT[:].rearrange("p c h w s -> p (c h w s)"),
                                         in_=PSA[:, pb:pb + 2, :].rearrange("p b t -> p (b t)")),
                 reads=[dB[pb], dB[pb + 1]], writes=[B.dbk])
            yield
            for c in range(NCB):
                cs = slice(c * 64, (c + 1) * 64)
                for h in range(4):
                    arh = AR[:, h, c, :, :].rearrange("p a s -> p (a s)")
                    S.op("pe", lambda e: e.matmul(PSA[:, pb + h, 0:128], lhsT=bt[:, h, cs], rhs=arh, start=True, stop=True),
                         reads=[B.dbt, B.dAR], writes=[dB[pb + h]])
                    S.op("pe", lambda e: e.matmul(PSA[:, pb + h, 128:256], lhsT=kt_[:, h, cs], rhs=arh, start=True, stop=True),
                         reads=[B.dkt, B.dAR], writes=[dB[pb + h]])
                    S.op("pe", lambda e: e.matmul(PSA[:, pb + h, 256:320], lhsT=AR[:, h, c, 0, :], rhs=bt[:, h, cs], start=True, stop=True),
                         reads=[B.dbt, B.dAR], writes=[dB[pb + h]])
                yield
                S.op("dve", lambda e: e.tensor_tensor(out=GM[:, c, :, :], in0=PSA[:, pb:pb + 4, 0:320],
                                                      in1=MSK[:, e_, :].unsqueeze(1).to_broadcast([64, 4, 320]), op=ALU.mult),
                     reads=[dB[pb], dB[pb + 1], dB[pb + 2], dB[pb + 3], dconst], writes=[B.dGM])
                yield
            GMf = GM[:].rearrange("p c h n -> p (c h) n")
            S.op("dve", lambda e: e.tensor_tensor(out=Tm[:], in0=GMf[:, :, 0:64], in1=identf[:].unsqueeze(1).to_broadcast([64, NI, 64]), op=ALU.add),
                 reads=[B.dGM, didf], writes=[B.dTm])
            pxx = PSA[:, pb:pb + 2, :].rearrange("p b (i w s) -> p (b i) w s", w=2, s=64)
            ptm = PSA[:, pb + 2, :].rearrange("p (i s) -> p i s", s=64)
            for it_ in range(5):
                for idx in range(NI):
                    if it_ == 0:
                        Xc = GMf[:, idx, 0:64]; XTc = GMf[:, idx, 256:320]; dsrc = B.dGM
                    else:
                        Xc = XX0[:, idx, 0, :]; XTc = XX0[:, idx, 1, :]; dsrc = B.dXX
                    S.op("pe", lambda e: e.matmul(pxx[:, idx, 0, :], lhsT=XTc, rhs=Xc, start=True, stop=True), reads=[dsrc], writes=[dB[pb + idx // 4]])
                    S.op("pe", lambda e: e.matmul(pxx[:, idx, 1, :], lhsT=Xc, rhs=XTc, start=True, stop=True), reads=[dsrc], writes=[dB[pb + idx // 4]])
                yield
                S.op("act", lambda e: e.copy(out=XX0[:].rearrange("p i w s -> p (i w s)"), in_=PSA[:, pb:pb + 2, :].rearrange("p b t -> p (b t)")),
                     reads=[dB[pb], dB[pb + 1]], writes=[B.dXX])
                yield
                for idx in range(NI):
                    S.op("pe", lambda e: e.matmul(ptm[:, idx, :], lhsT=XX0[:, idx, 1, :], rhs=Tm[:, idx, :], start=True, stop=True),
                         reads=[B.dXX, B.dTm], writes=[dB[pb + 2]])
                yield
                S.op("dve", lambda e: e.tensor_tensor(out=Tm[:].rearrange("p i s -> p (i s)"), in0=Tm[:].rearrange("p i s -> p (i s)"),
                                                      in1=PSA[:, pb + 2, :], op=ALU.add),
                     reads=[dB[pb + 2], B.dTm], writes=[B.dTm])
                yield
            pW = PSA[:, pb + 3, 0:256].rearrange("p (h s) -> p h s", s=64)
            pU = PSA[:, pb + 2, 0:256].rearrange("p (h s) -> p h s", s=64)
            pYS = PSA[:, pb + 3, :].rearrange("p (h w s) -> p h w s", w=2, s=64)
            ST, WT, UT, tS, v_in, bkT, pc = B.ST, B.WT, B.UT, B.tS, B.v_in, B.bkT, B.pc
            for c in corder:
                gc = blk * NCB + c
                for h in range(4):
                    vh = v_in[:, c, h * 64:(h + 1) * 64]
                    S.op("pe", lambda e: e.matmul(pW[:, h, :], lhsT=AR[:, h, c, 0, :], rhs=ST[:, h, :], start=True, stop=False),
                         reads=[B.dAR, B.dST], writes=[dB[pb + 3]])
                    S.op("pe", lambda e: e.matmul(pW[:, h, :], lhsT=GM[:, c, h, 128:192], rhs=vh, start=False, stop=True),
                         reads=[B.dGM, B.dv], writes=[dB[pb + 3]])
                yield
                S.op("act", lambda e: e.copy(out=WT[:], in_=pW), reads=[dB[pb + 3]], writes=[B.dWT])
                yield
                for h in range(4):
                    S.op("pe", lambda e: e.matmul(pU[:, h, :], lhsT=Tm[:, c * 4 + h, :], rhs=WT[:, h, :], start=True, stop=True),
                         reads=[B.dTm, B.dWT], writes=[dB[pb + 2]])
                yield
                S.op("act", lambda e: e.copy(out=UT[:], in_=pU), reads=[dB[pb + 2]], writes=[B.dUT])
                yield
                for h in range(4):
                    vh = v_in[:, c, h * 64:(h + 1) * 64]
                    S.op("pe", lambda e: e.matmul(pYS[:, h, 0, :], lhsT=AR[:, h, c, 1, :], rhs=ST[:, h, :], start=True, stop=False),
                         reads=[B.dAR, B.dST], writes=[dB[pb + 3]])
                    S.op("pe", lambda e: e.matmul(pYS[:, h, 0, :], lhsT=GM[:, c, h, 64:128], rhs=UT[:, h, :], start=False, stop=False),
                         reads=[B.dGM, B.dUT], writes=[dB[pb + 3]])
                    S.op("pe", lambda e: e.matmul(pYS[:, h, 0, :], lhsT=GM[:, c, h, 192:256], rhs=vh, start=False, stop=True),
                         reads=[B.dGM, B.dv], writes=[dB[pb + 3]])
                    S.op("pe", lambda e: e.matmul(pYS[:, h, 1, :], lhsT=bkT[:, c, h, 0, :], rhs=UT[:, h, :], start=True, stop=False),
                         reads=[B.dbk, B.dUT], writes=[dB[pb + 3]])
                    S.op("pe", lambda e: e.matmul(pYS[:, h, 1, :], lhsT=bkT[:, c, h, 1, :], rhs=vh, start=False, stop=True),
                         reads=[B.dbk, B.dv], writes=[dB[pb + 3]])
                S.op("dve", lambda e: e.tensor_tensor(out=tS[:], in0=ST[:], in1=pc[:, :, c:c + 1].to_broadcast([64, 4, 64]), op=ALU.mult),
                     reads=[B.dST, B.dpc], writes=[B.dtS])
                yield
                S.op("dve", lambda e: e.tensor_tensor(out=ST[:], in0=tS[:], in1=pYS[:, :, 1, :], op=ALU.add), reads=[B.dtS, dB[pb + 3]], writes=[B.dST])
                ysl = yacc[:, gc, :].rearrange("p (h s) -> p h s", s=64)
                S.op("dve", lambda e: e.tensor_tensor(out=ysl, in0=ysl, in1=pYS[:, :, 0, :], op=ALU.add), reads=[dB[pb + 3], dy], writes=[dy])
                yield

        sets = [mkset(0), mkset(1)]
        import os
        nb = int(os.environ.get('A3_BLOCKS', NHB))
        order = [list(range(NHB)), [1, 0] + list(range(NHB - 1, 1, -1))]

        def chain(e_):
            B = sets[e_]
            S.op("dve", lambda e: e.memset(B.ST[:], 0.0), writes=[B.dST])
            for blk in order[e_][:nb]:
                yield from block_gen(e_, B, blk)

        gens = [chain(0), chain(1)]
        alive = [True, True]
        while any(alive):
            for e_ in range(2):
                if alive[e_]:
                    try:
                        next(gens[e_])
                    except StopIteration:
                        alive[e_] = False

        B = sets[0]
        tmpv = fl(B.tmp[:]).rearrange("p (c n) -> p c n", n=256)
        e1v = fl(B.E1[:]).rearrange("p (c n) -> p c n", n=256)
        gtb_ = fl(B.E3[:]).bitcast(BF16)[:, 0:NCB * 256].rearrange("p (c n) -> p c n", n=256); dgtb = B.dE3
        obr_ = fl(B.E4[:]).bitcast(BF16)[:, 0:NCB * 256].rearrange("p (c n) -> p c n", n=256); dobr = B.dE4
        st1 = S.sb("st1", [64, 4, NI], F32, st); dst1 = S.dep()
        v16 = lambda ap: ap.rearrange("p c (h s) -> p (c h) s", s=64)
        for blk in range(NHB):
            t0 = blk * BT
            S.drain_dma("sp", keep=4)
            yb = yacc[:, blk * NCB:(blk + 1) * NCB, :]
            S.dma("sp", B.v_in[:], T["s_v"][t0:t0 + BT, :].rearrange("(c s) n -> s c n", s=64), writes=[B.dv])
            S.dma("sp", gtb_, T["s_gate"][t0:t0 + BT, 0:256].rearrange("(c s) n -> s c n", s=64), writes=[dgtb])
            S.op("dve", lambda e: e.tensor_reduce(out=st1[:, 0, :], in_=v16(yb), axis=AX.X, op=ALU.add), reads=[dy], writes=[dst1])
            S.op("dve", lambda e: e.tensor_tensor(out=tmpv, in0=yb, in1=yb, op=ALU.mult), reads=[dy], writes=[B.dtmp])
            S.op("dve", lambda e: e.tensor_reduce(out=st1[:, 1, :], in_=v16(tmpv), axis=AX.X, op=ALU.add), reads=[B.dtmp], writes=[dst1])
            S.op("dve", lambda e: e.tensor_scalar(out=st1[:, 0:2, :], in0=st1[:, 0:2, :], scalar1=1.0 / 64, scalar2=None, op0=ALU.mult),
                 reads=[dst1], writes=[dst1])
            S.op("dve", lambda e: e.tensor_tensor(out=st1[:, 2, :], in0=st1[:, 0, :], in1=st1[:, 0, :], op=ALU.mult), reads=[dst1], writes=[dst1])
            S.op("dve", lambda e: e.tensor_tensor(out=st1[:, 3, :], in0=st1[:, 1, :], in1=st1[:, 2, :], op=ALU.subtract), reads=[dst1], writes=[dst1])
            _rstd(S, st1[:, 3, :], NI, 1.0, 64e-5, [], dst1)
            S.op("dve", lambda e: e.tensor_tensor(out=v16(tmpv), in0=v16(yb), in1=st1[:, 0, :].unsqueeze(2).to_broadcast([64, NI, 64]), op=ALU.subtract),
                 reads=[dy, dst1], writes=[B.dtmp])
            S.op("dve", lambda e: e.tensor_tensor(out=v16(tmpv), in0=v16(tmpv), in1=st1[:, 3, :].unsqueeze(2).to_broadcast([64, NI, 64]), op=ALU.mult),
                 reads=[B.dtmp, dst1], writes=[B.dtmp])
            S.op("dve", lambda e: e.tensor_tensor(out=tmpv, in0=tmpv, in1=gng[:, 0, :].unsqueeze(1).to_broadcast([64, NCB, 256]), op=ALU.mult),
                 reads=[B.dtmp, dconst], writes=[B.dtmp])
            S.op("dve", lambda e: e.tensor_tensor(out=tmpv, in0=tmpv, in1=gng[:, 1, :].unsqueeze(1).to_broadcast([64, NCB, 256]), op=ALU.add),
                 reads=[B.dtmp, dconst], writes=[B.dtmp])
            bv = bacc[:, blk * NCB:(blk + 1) * NCB, :].rearrange("p c h -> p (c h)").unsqueeze(2).to_broadcast([64, NI, 64])
            S.op("dve", lambda e: e.tensor_tensor(out=v16(e1v), in0=v16(B.v_in[:]), in1=bv, op=ALU.mult), reads=[B.dv, dba], writes=[B.dE1])
            S.op("dve", lambda e: e.tensor_tensor(out=tmpv, in0=tmpv, in1=e1v, op=ALU.add), reads=[B.dtmp, B.dE1], writes=[B.dtmp])
            S.op("dve", lambda e: e.tensor_tensor(out=obr_, in0=tmpv, in1=gtb_, op=ALU.mult), reads=[B.dtmp, dgtb], writes=[dobr])
            S.dma("sp", T["br_rwkv"][t0:t0 + BT, :].rearrange("(c s) n -> s c n", s=64), obr_, reads=[dobr])
        S.barrier()


def build_A(phases="123"):
    nc = bass.Bass("TRN2", target_bir_lowering=False)
    T = {}

    def din(name, shape, dt=F32):
        T[name] = nc.dram_tensor(name, list(shape), dt, kind="ExternalInput").ap()

    def dscr(name, shape, dt):
        T[name] = nc.dram_tensor(name, list(shape), dt, kind="Internal").ap()

    def dout(name, shape, dt):
        T[name] = nc.dram_tensor(name, list(shape), dt, kind="ExternalOutput").ap()

    din("xf", [NT, D]); din("cT", [128, 16, 2]); din("wmod", [D, 4096]); din("bmod", [1, 4096]); din("gpre", [1, D])
    din("gqn", [1, 256]); din("w_fm", [D, 704]); din("w_tm", [D, 2304]); din("rope", [NT, 192])
    din("lamp", [1, 256]); din("lami", [1, 1]); din("subg", [1, 128]); din("rprm", [64, 7, 4])
    din("wup", [2, 96, 256]); din("aup", [2, 96, 256]); din("gn", [1, 512])
    dscr("s_rk", [8, 64, NT], F32); dscr("s_wa", [2, 96, NT], F32); dscr("s_v", [NT, 256], F32)
    dscr("s_dqkT", [8, 64, NT], BF16); dscr("s_gqkT", [3, 128, NT], BF16)
    dscr("s_dv", [NT, 2, 129], BF16); dscr("s_gv", [NT, 129], BF16); dscr("s_gate", [NT, 768], BF16)
    dout("hT", [16, 128, NT], BF16); dout("br_att", [NT, 512], BF16); dout("br_rwkv", [NT, 256], BF16)
    with contextlib.ExitStack() as st:
        S = Sched(nc, st)
        if "1" in phases:
            _phase_A1(S, nc, T)
        if "2" in phases:
            _phase_A2(S, nc, T)
        if "3" in phases:
            _phase_A3i(S, nc, T)
        S.barrier()
        print("build_A: ninst", S.ninst, "nsem", S.nsem, {k: v for k, v in S.cnt.items()})
    return nc


OFF = {"cv_val": 0, "cv_glu": 1024, "cv_gate": 2048, "rk_r": 3072, "rk_k": 4096, "rk_v": 5120, "rk_wl": 6144, "rk_al": 6240,
       "rk_gate": 6336, "df_q": 7360, "df_k": 8384, "df_v": 9408, "df_gate": 10432, "gq_q": 11456, "gq_k": 12480, "gq_v": 12736,
       "gq_gate": 12992, "merge": 14016}


def _rope_table():
    tab = np.zeros((NT, 192), np.float32)
    tab[:, 0:32] = 1.0
    tab[:, 64:128] = 1.0
    t = np.arange(4096)
    row = (t // 64).astype(np.float32); col = (t % 64).astype(np.float32)
    for half, c0, s0 in ((32, 0, 32), (64, 64, 128)):
        inv = (10000.0 ** (-np.arange(0, half, 2, dtype=np.float32) / half)).astype(np.float32)
        ang = np.concatenate([row[:, None] * inv, col[:, None] * inv], axis=-1).astype(np.float32)
        tab[NCTX:, c0:c0 + half] = np.cos(ang)
        tab[NCTX:, s0:s0 + half] = np.sin(ang)
    return tab


def _cT(c_ctx, cb):
    both = np.stack([c_ctx, cb], axis=-1)
    return np.ascontiguousarray(both.reshape(16, 128, 2).transpose(1, 0, 2))


def inputs_A(inp, li, b, q, xfull, rope):
    w_in = inp["w_in"][li]
    cs = lambda name, a, n: w_in[:, OFF[name] + a:OFF[name] + a + n]
    kv = q // 2
    w_fm = np.concatenate([cs("rk_r", 256 * q, 256), cs("rk_k", 256 * q, 256), cs("rk_wl", 0, 96), cs("rk_al", 0, 96)], axis=1)
    w_tm = np.concatenate([cs("rk_v", 256 * q, 256), cs("rk_gate", 256 * q, 256),
                           cs("df_q", 256 * q, 256), cs("df_k", 256 * q, 256), cs("df_v", 256 * q, 256), cs("df_gate", 256 * q, 256),
                           cs("gq_q", 256 * q, 256), cs("gq_k", 128 * kv, 128), cs("gq_v", 128 * kv, 128), cs("gq_gate", 256 * q, 256)], axis=1)
    sl = slice(256 * q, 256 * q + 256)
    hm = lambda v: v[sl].reshape(4, 64).T
    rprm = np.stack([hm(inp["rwkv_w0"][li, 0]), hm(inp["rwkv_w0"][li, 1]), hm(inp["rwkv_a0"][li, 0]), hm(inp["rwkv_a0"][li, 1]),
                     hm(inp["rwkv_k_k"][li]), hm(inp["rwkv_k_a"][li]), hm(inp["rwkv_r_k"][li].reshape(-1))], axis=1)
    lam_init = 0.8 - 0.6 * math.exp(-0.3 * li)
    return {
        "xf": np.ascontiguousarray(xfull[b]), "cT": _cT(inp["c_ctx"], inp["c"][b]),
        "wmod": np.ascontiguousarray(inp["w_mod"][li][:, 0:4096]), "bmod": np.ascontiguousarray(inp["b_mod"][li][None, 0:4096]),
        "gpre": np.ascontiguousarray(inp["norm_pre_g"][li][None]), "gqn": np.ascontiguousarray(inp["gqa_qk_norm_g"][li].reshape(1, 256)),
        "w_fm": np.ascontiguousarray(w_fm), "w_tm": np.ascontiguousarray(w_tm), "rope": rope,
        "lamp": np.ascontiguousarray(inp["diff_lam"][li].reshape(1, 256)), "lami": np.full((1, 1), lam_init, np.float32),
        "subg": np.ascontiguousarray(inp["diff_subln_g"][li][None]), "rprm": np.ascontiguousarray(rprm.astype(np.float32)),
        "wup": np.ascontiguousarray(inp["rwkv_w_up"][li][:, :, sl]), "aup": np.ascontiguousarray(inp["rwkv_a_up"][li][:, :, sl]),
        "gn": np.ascontiguousarray(np.concatenate([inp["rwkv_gn_g"][li][sl], inp["rwkv_gn_b"][li][sl]])[None]),
    }


NOWN = 1088
NEXT = 1152
MBLK = [(0, 64, 15), (64, 384, 109), (448, 384, 493), (832, 256, 877)]
LNBLK = [(0, 384), (384, 384), (768, 320)]


def build_B():
    nc = bass.Bass("TRN2", target_bir_lowering=False)
    T = {}

    def din(name, shape, dt=F32):
        T[name] = nc.dram_tensor(name, list(shape), dt, kind="ExternalInput").ap()

    din("hTx", [128, 16, NEXT], BF16); din("mask", [1, NEXT]); din("brT", [128, 3, 8, NOWN], BF16); din("x_own", [NOWN, D])
    din("w_cv", [D, 3072]); din("cvp", [128, 8, 34]); din("Wl", [D, 8192]); din("Wb", [4, W, D]); din("Wout", [D, D])
    din("bg", [128, 4, 16]); din("cT", [128, 16, 2]); din("wmodg", [D, D]); din("bmodg", [1, D]); din("gpost", [1, D])
    T["s_cv"] = nc.dram_tensor("s_cv", [8, 128, NOWN], BF16, kind="Internal").ap()
    T["xo"] = nc.dram_tensor("xo", [NOWN, D], F32, kind="ExternalOutput").ap()
    with contextlib.ExitStack() as st0:
        S = Sched(nc, st0)
        big = S.sb("big", [128, 16 * NOWN], BF16, st0); dbig = S.dep()
        mergedT = big[:].rearrange("p (c t) -> p c t", t=NOWN)
        conv_all = big[:].bitcast(F32).rearrange("p (c t) -> p c t", t=NOWN)
        with contextlib.ExitStack() as stm:
            hTx = S.sb("hTx_sb", [128, 16, NEXT], BF16, stm); dhTx = S.dep()
            for k4 in range(4):
                S.dma("sp", hTx[:, k4 * 4:(k4 + 1) * 4, :], T["hTx"][:, k4 * 4:(k4 + 1) * 4, :], writes=[dhTx])
            with contextlib.ExitStack() as st:
                maskb = S.sb("maskb", [128, NEXT], F32, st); dmask = S.dep()
                S.dma("sp", maskb[:], T["mask"][0:1, :].to_broadcast([128, NEXT]), writes=[dmask])
                cvp = S.sb("cvp_sb", [128, 8, 34], F32, st); dcvp = S.dep()
                S.dma("sp", cvp[:], T["cvp"][:, :, :], writes=[dcvp])
                ones = S.sb("ones128", [128, 128], F32, st); dones = S.dep()
                S.op("dve", lambda e: e.memset(ones[:], 1.0), writes=[dones])
                wck2 = [[S.sb(f"wck{k}_{i}", [128, 16, 128], BF16, st) for k in range(3)] for i in range(2)]
                dwck2 = [[S.dep() for _ in range(3)] for i in range(2)]
                u = S.sb("u", [128, NEXT], BF16, st); du = S.dep()
                identb, didb = _make_ident(S, st, 128, BF16, "b1id")
                dg = S.sb("dg", [128, 31, 128], BF16, st); ddg = S.dep()
                pconv = [S.ps(f"pconv{i}", [128, 512], F32, st) for i in range(2)]; dpconv = [S.pdep() for _ in range(2)]
                sg = S.sb("sg", [128, 384], F32, st); dsg = S.dep()
                cgx = S.sb("cgx", [128, 8, NEXT], BF16, st); dcg = S.dep()
                sqt = S.sb("sqt", [128, NOWN], F32, st); dsq = S.dep()
                meanb = S.sb("meanb", [128, NOWN], F32, st); dmean = S.dep()
                rstdb = S.sb("rstdb", [128, NOWN], F32, st); drstd = S.dep()
                cst = S.sb("cst", [128, NOWN], BF16, st); dcst = S.dep()
                pcv = [S.ps(f"pcv{i}", [128, 512], F32, st) for i in range(6)]; dpcv = [S.pdep() for _ in range(6)]
                wcv = T["w_cv"].rearrange("(kc p) n -> p kc n", p=128)
                for cc in range(8):
                    wck = wck2[cc % 2]; dwck = dwck2[cc % 2]
                    for k in range(3):
                        c0 = k * 1024 + cc * 128
                        for k4 in range(2):
                            S.dma("pool", wck[k][:, k4 * 8:(k4 + 1) * 8, :], wcv[:, k4 * 8:(k4 + 1) * 8, c0:c0 + 128], writes=[dwck[k]])
                    for tb in range(3):
                        ts_ = slice(tb * 384, (tb + 1) * 384)
                        pgl, dgl = pcv[0 + tb % 2], dpcv[0 + tb % 2]
                        pv, dpv = pcv[2 + tb % 2], dpcv[2 + tb % 2]
                        pgt, dgt_ = pcv[4 + tb % 2], dpcv[4 + tb % 2]
                        for (k, p_, dp_) in ((1, pgl, dgl), (0, pv, dpv), (2, pgt, dgt_)):
                            for kc in range(16):
                                S.op("pe", lambda e: e.matmul(p_[:, 0:384], lhsT=wck[k][:, kc, :], rhs=hTx[:, kc, ts_], start=(kc == 0), stop=(kc == 15)),
                                     reads=[dwck[k], dhTx], writes=[dp_])
                        S.op("act", lambda e: e.activation(out=sg[:], in_=pgl[:, 0:384], func=AF.Sigmoid), reads=[dgl], writes=[dsg])
                        S.op("dve", lambda e: e.tensor_tensor(out=sg[:], in0=sg[:], in1=maskb[:, ts_], op=ALU.mult), reads=[dsg, dmask], writes=[dsg])
                        S.op("dve", lambda e: e.tensor_tensor(out=u[:, ts_], in0=pv[:, 0:384], in1=sg[:], op=ALU.mult), reads=[dpv, dsg], writes=[du])
                        S.op("act", lambda e: e.activation(out=cgx[:, cc, ts_], in_=pgt[:, 0:384], func=AF.Silu), reads=[dgt_], writes=[dcg])
                    for j in range(31):
                        S.op("dve", lambda e: e.tensor_scalar(out=dg[:, j, :], in0=identb[:], scalar1=cvp[:, cc, j:j + 1], scalar2=None, op0=ALU.mult),
                             reads=[didb, dcvp], writes=[ddg])
                    for bi_, (o0_, n_, e0_) in enumerate(((0, 64, 0), (64, 512, 94), (576, 512, 606))):
                        pc_, dpc_ = pconv[bi_ % 2], dpconv[bi_ % 2]
                        for j in range(31):
                            S.op("pe", lambda e: e.matmul(pc_[:, 0:n_], lhsT=dg[:, j, :], rhs=u[:, e0_ + j:e0_ + j + n_], start=(j == 0), stop=(j == 30)),
                                 reads=[ddg, du], writes=[dpc_])
                        S.op("dve", lambda e: e.tensor_scalar(out=conv_all[:, cc, o0_:o0_ + n_], in0=pc_[:, 0:n_], scalar1=cvp[:, cc, 31:32], scalar2=None,
                                                              op0=ALU.add), reads=[dpc_, dcvp], writes=[dbig])
                for cc in range(8):
                    S.op("dve", lambda e: e.tensor_tensor(out=sqt[:], in0=conv_all[:, cc, :], in1=conv_all[:, cc, :], op=ALU.mult), reads=[dbig], writes=[dsq])
                    for bi, (o_, n_) in enumerate(LNBLK):
                        S.op("pe", lambda e: e.matmul(pcv[bi][:, 0:n_], lhsT=ones[:], rhs=conv_all[:, cc, o_:o_ + n_], start=(cc == 0), stop=(cc == 7)),
                             reads=[dones, dbig], writes=[dpcv[bi]])
                        S.op("pe", lambda e: e.matmul(pcv[3 + bi][:, 0:n_], lhsT=ones[:], rhs=sqt[:, o_:o_ + n_], start=(cc == 0), stop=(cc == 7)),
                             reads=[dones, dsq], writes=[dpcv[3 + bi]])
                for bi, (o_, n_) in enumerate(LNBLK):
                    S.op("dve", lambda e: e.tensor_scalar(out=meanb[:, o_:o_ + n_], in0=pcv[bi][:, 0:n_], scalar1=1.0 / W, scalar2=None, op0=ALU.mult),
                         reads=[dpcv[bi]], writes=[dmean])
                    S.op("dve", lambda e: e.tensor_scalar(out=rstdb[:, o_:o_ + n_], in0=pcv[3 + bi][:, 0:n_], scalar1=1.0 / W, scalar2=None, op0=ALU.mult),
                         reads=[dpcv[3 + bi]], writes=[drstd])
                S.op("dve", lambda e: e.tensor_tensor(out=sqt[:], in0=meanb[:], in1=meanb[:], op=ALU.mult), reads=[dmean], writes=[dsq])
                S.op("dve", lambda e: e.tensor_tensor(out=rstdb[:], in0=rstdb[:], in1=sqt[:], op=ALU.subtract), reads=[drstd, dsq], writes=[drstd])
                _rstd(S, rstdb[:], NOWN, 1.0, 1e-5, [], drstd)
                for cc in range(8):
                    cv = conv_all[:, cc, :]
                    S.op("dve", lambda e: e.tensor_tensor(out=cv, in0=cv, in1=meanb[:], op=ALU.subtract), reads=[dbig, dmean], writes=[dbig])
                    S.op("dve", lambda e: e.tensor_tensor(out=cv, in0=cv, in1=rstdb[:], op=ALU.mult), reads=[dbig, drstd], writes=[dbig])
                    S.op("act", lambda e: e.activation(out=sqt[:], in_=cv, func=AF.Silu, bias=cvp[:, cc, 33:34], scale=cvp[:, cc, 32:33]),
                         reads=[dbig, dcvp], writes=[dsq])
                    S.op("dve", lambda e: e.tensor_tensor(out=cst[:, 0:64], in0=sqt[:, 0:64], in1=cgx[:, cc, 15:79], op=ALU.mult), reads=[dsq, dcg], writes=[dcst])
                    S.op("dve", lambda e: e.tensor_tensor(out=cst[:, 64:NOWN], in0=sqt[:, 64:NOWN], in1=cgx[:, cc, 109:1133], op=ALU.mult), reads=[dsq, dcg], writes=[dcst])
                    S.dma("sp", T["s_cv"][cc, :, :], cst[:], reads=[dcst])
                S.barrier()
            with contextlib.ExitStack() as st:
                brT4 = S.sb("brT4", [128, 4, 8, NOWN], BF16, st); dbr = S.dep()
                S.dma("sp", brT4[:, 0, :, :], T["s_cv"].rearrange("c p t -> p c t"), writes=[dbr])
                for j in range(3):
                    S.dma("sp", brT4[:, 1 + j, :, :], T["brT"][:, j, :, :], writes=[dbr])
                bg = S.sb("bg_sb", [128, 4, 16], F32, st); dbg = S.dep()
                S.dma("sp", bg[:], T["bg"][:, :, :], writes=[dbg])
                wl = [S.sb(f"wl{i}", [128, 16, 4, 128], BF16, st) for i in range(2)]; dwl = [S.dep() for _ in range(2)]
                wb = [S.sb(f"wb{i}", [128, 8, 4, 128], BF16, st) for i in range(2)]; dwb = [S.dep() for _ in range(2)]
                gsb = S.sb("gsb", [128, 384], F32, st); dgs = S.dep()
                macc = S.sb("macc", [128, 384], F32, st); dma_ = S.dep()
                mtmp = S.sb("mtmp", [128, 384], F32, st); dmt = S.dep()
                pl = [S.ps(f"pl{i}", [128, 512], F32, st) for i in range(2)]; dpl = [S.pdep() for _ in range(2)]
                pp = [S.ps(f"pp{i}", [128, 512], F32, st) for i in range(2)]; dpp = [S.pdep() for _ in range(2)]
                Wlv = T["Wl"].rearrange("(kc p) n -> p kc n", p=128)
                it = 0
                for dc in range(16):
                    b2 = dc % 2
                    for j in range(4):
                        c0 = j * D + dc * 128
                        S.dma("pool", wl[b2][:, :, j, :], Wlv[:, :, c0:c0 + 128], writes=[dwl[b2]])
                        S.dma("pool", wb[b2][:, :, j, :], T["Wb"][j].rearrange("(cc p) n -> p cc n", p=128)[:, :, dc * 128:(dc + 1) * 128], writes=[dwb[b2]])
                    for (o_, n_, e_) in MBLK:
                        for j in range(4):
                            p1, d1 = pl[it % 2], dpl[it % 2]
                            p2, d2 = pp[it % 2], dpp[it % 2]
                            it += 1
                            for kc in range(16):
                                S.op("pe", lambda e: e.matmul(p1[:, 0:n_], lhsT=wl[b2][:, kc, j, :], rhs=hTx[:, kc, e_:e_ + n_], start=(kc == 0), stop=(kc == 15)),
                                     reads=[dwl[b2], dhTx], writes=[d1])
                            for cc in range(8):
                                S.op("pe", lambda e: e.matmul(p2[:, 0:n_], lhsT=wb[b2][:, cc, j, :], rhs=brT4[:, j, cc, o_:o_ + n_], start=(cc == 0), stop=(cc == 7)),
                                     reads=[dwb[b2], dbr], writes=[d2])
                            S.op("act", lambda e: e.activation(out=gsb[:, 0:n_], in_=p1[:, 0:n_], func=AF.Sigmoid, bias=bg[:, j, dc:dc + 1]),
                                 reads=[d1, dbg], writes=[dgs])
                            if j == 0:
                                S.op("dve", lambda e: e.tensor_tensor(out=macc[:, 0:n_], in0=p2[:, 0:n_], in1=gsb[:, 0:n_], op=ALU.mult), reads=[d2, dgs], writes=[dma_])
                            else:
                                S.op("dve", lambda e: e.tensor_tensor(out=mtmp[:, 0:n_], in0=p2[:, 0:n_], in1=gsb[:, 0:n_], op=ALU.mult), reads=[d2, dgs], writes=[dmt])
                                dst = macc[:, 0:n_] if j < 3 else mergedT[:, dc, o_:o_ + n_]
                                S.op("dve", lambda e: e.tensor_tensor(out=dst, in0=macc[:, 0:n_], in1=mtmp[:, 0:n_], op=ALU.add),
                                     reads=[dma_, dmt], writes=[dma_ if j < 3 else dbig])
                S.barrier()
        with contextlib.ExitStack() as st:
            modb = [S.sb(f"modg{i}", [128, D], F32, st) for i in range(2)]; dmodb = S.dep()
            _mod_prologue(S, nc, T["cT"], T["wmodg"], T["bmodg"], D, modb, dmodb)
            gpb = S.sb("gpb", [128, D], F32, st); dgp = S.dep()
            S.dma("sp", gpb[:], T["gpost"][0:1, :].to_broadcast([128, D]), writes=[dgp])
            for wh in range(2):
                S.op("dve", lambda e: e.tensor_tensor(out=modb[wh][:], in0=modb[wh][:], in1=gpb[:], op=ALU.mult), reads=[dmodb, dgp], writes=[dmodb])
            Wo = S.sb("Wo", [128, 16, D], BF16, st); dWo = S.dep()
            Wov = T["Wout"].rearrange("(kc p) n -> p kc n", p=128)
            for kc in range(16):
                S.dma("pool", Wo[:, kc:kc + 1, :], Wov[:, kc:kc + 1, :], writes=[dWo])
            xt = [S.sb(f"xt{i}", [128, D], F32, st) for i in range(2)]; dxt = [S.dep() for _ in range(2)]
            yb = S.sb("yb", [128, D], F32, st); dyb = S.dep()
            jk = S.sb("jk", [128, D], BF16, st); djk = S.dep()
            ss = S.sb("ss3", [128, 4], F32, st); dss = S.dep()
            py = [S.ps(f"py{i}", [128, 512], F32, st) for i in range(2)]; dpy = [S.pdep() for _ in range(2)]
            tiles = [(0, 64, 0)] + [(64 + 128 * i, 128, 1) for i in range(8)]
            for ti, (o_, n_, wh) in enumerate(tiles):
                S.drain_dma("sp", keep=4)
                x_ = xt[ti % 2]; dx_ = dxt[ti % 2]
                S.dma("sp", x_[0:n_, :], T["x_own"][o_:o_ + n_, :], writes=[dx_])
                for cg in range(4):
                    p_, dp_ = py[cg % 2], dpy[cg % 2]
                    for kc in range(16):
                        S.op("pe", lambda e: e.matmul(p_[0:n_, :], lhsT=mergedT[:, kc, o_:o_ + n_], rhs=Wo[:, kc, cg * 512:(cg + 1) * 512], start=(kc == 0), stop=(kc == 15)),
                             reads=[dbig, dWo], writes=[dp_])
                    S.op("act", lambda e: e.copy(out=yb[0:n_, cg * 512:(cg + 1) * 512], in_=p_[0:n_, :]), reads=[dp_], writes=[dyb])
                S.op("act", lambda e: e.activation(out=jk[0:n_, :], in_=yb[0:n_, :], func=AF.Square, accum_out=ss[0:n_, 0:1]), reads=[dyb], writes=[djk, dss])
                _rstd(S, ss[0:n_, 0:1], 1, 1.0 / D, EPS, [], dss)
                S.op("dve", lambda e: e.scalar_tensor_tensor(out=yb[0:n_, :], in0=yb[0:n_, :], scalar=ss[0:n_, 0:1], in1=modb[wh][0:n_, :],
                                                             op0=ALU.mult, op1=ALU.mult), reads=[dyb, dss, dmodb], writes=[dyb])
                S.op("dve", lambda e: e.tensor_tensor(out=yb[0:n_, :], in0=yb[0:n_, :], in1=x_[0:n_, :], op=ALU.add), reads=[dyb, dx_], writes=[dyb])
                S.dma("sp", T["xo"][o_:o_ + n_, :], yb[0:n_, :], reads=[dyb])
            S.barrier()
        print("build_B: ninst", S.ninst, "nsem", S.nsem, {k: v for k, v in S.cnt.items()})
    return nc


def _bf16(a):
    import ml_dtypes
    return np.ascontiguousarray(np.asarray(a).astype(ml_dtypes.bfloat16))


def _tok_maps(q):
    ctx_idx = np.arange(64 * q - 15, 64 * q + 79)
    lat_idx = np.arange(1024 * q - 15, 1024 * q + 1039)
    tok = np.full(NEXT, -1, np.int64)
    v = (ctx_idx >= 0) & (ctx_idx < NCTX)
    tok[0:94][v] = ctx_idx[v]
    v2 = (lat_idx >= 0) & (lat_idx < 4096)
    tok[94:94 + 1054][v2] = NCTX + lat_idx[v2]
    own = np.concatenate([np.arange(64 * q, 64 * q + 64), NCTX + np.arange(1024 * q, 1024 * q + 1024)])
    return tok, own


def shared_B(inp, li):
    w_in = inp["w_in"][li]
    cvp = np.concatenate([inp["conv_w"][li].T, inp["conv_b"][li][:, None], inp["conv_ln_g"][li][:, None], inp["conv_ln_b"][li][:, None]], axis=1)
    return {
        "w_cv": np.ascontiguousarray(w_in[:, 0:3072]),
        "cvp": np.ascontiguousarray(cvp.reshape(8, 128, 34).transpose(1, 0, 2).astype(np.float32)),
        "Wl": np.ascontiguousarray(w_in[:, OFF["merge"]:OFF["merge"] + 8192]),
        "Wb": np.ascontiguousarray(inp["w_branch"][li]),
        "Wout": np.ascontiguousarray(inp["w_out"][li]),
        "bg": np.ascontiguousarray(inp["b_gate"][li].reshape(4, 16, 128).transpose(2, 0, 1)),
        "wmodg": np.ascontiguousarray(inp["w_mod"][li][:, 4096:6144]),
        "bmodg": np.ascontiguousarray(inp["b_mod"][li][None, 4096:6144]),
        "gpost": np.ascontiguousarray(inp["norm_post_g"][li][None]),
    }


def inputs_B(inp, shared, b, q, xfull, hT_b, br_b):
    tok, own = _tok_maps(q)
    valid = tok >= 0
    hTx = np.zeros((16, 128, NEXT), hT_b.dtype)
    hTx[:, :, valid] = hT_b[:, :, tok[valid]]
    brT = br_b[own].reshape(NOWN, 3, 8, 128).transpose(3, 1, 2, 0)
    m = dict(shared)
    m.update({
        "hTx": np.ascontiguousarray(hTx.transpose(1, 0, 2)), "mask": valid.astype(np.float32)[None],
        "brT": np.ascontiguousarray(brT), "x_own": np.ascontiguousarray(xfull[b][own]),
        "cT": _cT(inp["c_ctx"], inp["c"][b]),
    })
    return m


def _gather_A(resA):
    out = []
    for b in range(2):
        hT_b = np.asarray(resA[4 * b]["hT"])
        parts = []
        for j in range(3):
            cols = []
            for q in range(4):
                r = resA[4 * b + q]
                if j == 0:
                    cols.append(np.asarray(r["br_rwkv"]))
                else:
                    cols.append(np.asarray(r["br_att"])[:, (j - 1) * 256:j * 256])
            parts.append(np.concatenate(cols, axis=1))
        out.append((hT_b, np.stack(parts, axis=1)))
    return out


_NC = {}


def kernel(**inputs):
    inp = {k: np.asarray(v) for k, v in inputs.items()}
    if "A" not in _NC:
        _NC["A"] = build_A()
        _NC["B"] = build_B()
    rope = _rope_table()
    xfull = np.concatenate([inp["ctx"], inp["x"]], axis=1).astype(np.float32)
    cores = list(range(8))
    for li in range(4):
        mapsA = [inputs_A(inp, li, c // 4, c % 4, xfull, rope) for c in cores]
        resA = run_bass_kernel_spmd(_NC["A"], mapsA, core_ids=cores).results
        del mapsA
        gA = _gather_A(resA)
        del resA
        sh = shared_B(inp, li)
        mapsB = [inputs_B(inp, sh, c // 4, c % 4, xfull, gA[c // 4][0], gA[c // 4][1]) for c in cores]
        resB = run_bass_kernel_spmd(_NC["B"], mapsB, core_ids=cores).results
        del mapsB
        xnew = np.empty_like(xfull)
        for c in cores:
            _, own = _tok_maps(c % 4)
            xnew[c // 4][own] = np.asarray(resB[c]["xo"])
        xfull = xnew
    return np.ascontiguousarray(xfull[:, NCTX:, :]).astype(np.float32)
```
